# Optimizing a Trainium2 kernel written in Bass

```python
import math
import jax, jax.numpy as jnp
from jax import lax
import numpy as np

D_MODEL = 4096
BATCH = 4
SEQ = 2048
DEPTH = 2
DEC_BATCH = 32
DEC_SEQ = 1
PAST_LEN = 16384
PAGE_SIZE = 128

HEAD_DIM = 128
A_HEADS = (3 * D_MODEL) // (8 * HEAD_DIM)
A_KV_HEADS = A_HEADS // 3
A_GROUP = A_HEADS // A_KV_HEADS
A_WIDTH = A_HEADS * HEAD_DIM
A_KV_WIDTH = A_KV_HEADS * HEAD_DIM
WINDOW = 128
ATTN_BLOCK = 128
M_WIDTH = (3 * D_MODEL) // 8
M_HEADS = 6
M_DV = M_WIDTH // M_HEADS
M_DK = M_DV // 2
M_QK_WIDTH = M_HEADS * M_DK
M_CHUNK = 64
C_WIDTH = D_MODEL // 4
C_GROUPS = 8
C_GROUP_WIDTH = C_WIDTH // C_GROUPS
C_CHUNK = 128

D_MIX = A_WIDTH + M_WIDTH + C_WIDTH
IN_SPLITS = (A_WIDTH, A_KV_WIDTH, A_KV_WIDTH, A_WIDTH,
             M_QK_WIDTH, M_QK_WIDTH, M_WIDTH, M_HEADS, M_HEADS, M_WIDTH, M_WIDTH,
             C_WIDTH, C_WIDTH, C_WIDTH)
D_IN = 2 * A_WIDTH + 2 * A_KV_WIDTH + 2 * M_QK_WIDTH + 3 * M_WIDTH + 2 * M_HEADS + 3 * C_WIDTH
EPS = 1e-6

kernel_name = 'hymba_swa_mlstm_chunkmlp_step'

F32 = jnp.float32


def rmsnorm(x, g):
    xf = x.astype(F32)
    y = xf * lax.rsqrt(jnp.mean(xf * xf, axis=-1, keepdims=True) + EPS) * g.astype(F32)
    return y.astype(x.dtype)


def layernorm(x, g, b):
    mu = jnp.mean(x, axis=-1, keepdims=True)
    xc = x - mu
    return xc * lax.rsqrt(jnp.mean(xc * xc, axis=-1, keepdims=True) + EPS) * g.astype(F32) + b.astype(F32)


def split_projection(p):
    cuts, off = [], 0
    for size in IN_SPLITS[:-1]:
        off += size
        cuts.append(off)
    return jnp.split(p, cuts, axis=-1)


def alibi_slopes():
    return 2.0 ** (-8.0 * jnp.arange(1, A_HEADS + 1, dtype=F32) / A_HEADS)


def sink_attention(q, k, v, dist, valid, sinks):
    s = jnp.einsum('...qkgd,...skd->...kgqs', q, k).astype(F32) * (HEAD_DIM ** -0.5)
    slopes = alibi_slopes().reshape(A_KV_HEADS, A_GROUP, 1, 1)
    logits = jnp.where(valid, s - slopes * dist, -jnp.inf)
    sink = jnp.broadcast_to(sinks.astype(F32).reshape(A_KV_HEADS, A_GROUP, 1, 1), logits.shape[:-1] + (1,))
    p = jax.nn.softmax(jnp.concatenate([logits, sink], axis=-1), axis=-1)[..., :-1]
    return jnp.einsum('...kgqs,...skd->...qkgd', p.astype(v.dtype), v)


def swa_banded(q, k, v, sinks):
    B, T = k.shape[:2]
    nb = T // ATTN_BLOCK

    def band(x):
        cur = x.reshape(B, nb, ATTN_BLOCK, A_KV_HEADS, HEAD_DIM)
        prev = jnp.pad(x, ((0, 0), (ATTN_BLOCK, 0), (0, 0), (0, 0)))[:, :T].reshape(cur.shape)
        return jnp.concatenate([prev, cur], axis=2)

    qi = jnp.arange(ATTN_BLOCK)[:, None]
    sj = jnp.arange(2 * ATTN_BLOCK)[None, :]
    dist = qi + ATTN_BLOCK - sj
    s_abs = jnp.arange(nb)[:, None, None] * ATTN_BLOCK - ATTN_BLOCK + sj[None]
    valid = (dist >= 0) & (dist <= WINDOW) & (s_abs >= 0)
    qb = q.reshape(B, nb, ATTN_BLOCK, A_KV_HEADS, A_GROUP, HEAD_DIM)
    out = sink_attention(qb, band(k), band(v), dist.astype(F32), valid[:, None, None], sinks)
    return out.reshape(B, T, A_WIDTH)


def swa_cached(q, k, v, win_k, win_v, sinks):
    B, T = k.shape[:2]
    wb = win_k.shape[1]
    keys = jnp.concatenate([win_k.astype(k.dtype), k], axis=1)
    vals = jnp.concatenate([win_v.astype(v.dtype), v], axis=1)
    kpos = jnp.concatenate([jnp.arange(wb) - wb, jnp.arange(T)])
    dist = jnp.arange(T)[:, None] - kpos[None, :]
    valid = (dist >= 0) & (dist <= WINDOW)
    out = sink_attention(q.reshape(B, T, A_KV_HEADS, A_GROUP, HEAD_DIM), keys, vals,
                         dist.astype(F32), valid, sinks)
    return out.reshape(B, T, A_WIDTH)


def mlstm_chunkwise(q, k, v, i_pre, f_pre, C0, n0, m0, chunk):
    B, T, H = q.shape[:3]
    L = math.gcd(T, chunk)
    nc = T // L

    def to_chunks(x):
        return jnp.moveaxis(x.reshape((B, nc, L, H) + x.shape[3:]), (1, 2), (0, 3))

    xs = (to_chunks(q), to_chunks(k), to_chunks(v), to_chunks(i_pre), to_chunks(jax.nn.log_sigmoid(f_pre)))
    causal = jnp.tril(jnp.ones((L, L), dtype=bool))

    def step(carry, xc):
        C, n, m = carry
        qc, kc, vc, ic, lf = xc
        a = jnp.cumsum(lf, axis=-1)
        D = jnp.where(causal, a[..., :, None] - a[..., None, :] + ic[..., None, :], -jnp.inf)
        inter = a + m[..., None]
        mt = jnp.maximum(inter, jnp.max(D, axis=-1))
        w = jnp.exp(D - mt[..., None]) * jnp.einsum('bhtd,bhsd->bhts', qc, kc)
        wi = jnp.exp(inter - mt)
        num = wi[..., None] * jnp.einsum('bhtd,bhde->bhte', qc, C) + jnp.einsum('bhts,bhse->bhte', w, vc)
        den = wi * jnp.einsum('bhtd,bhd->bht', qc, n) + jnp.sum(w, axis=-1)
        h = num / jnp.maximum(jnp.abs(den), jnp.exp(-mt))[..., None]
        aL = a[..., -1]
        g = aL[..., None] - a + ic
        m_new = jnp.maximum(aL + m, jnp.max(g, axis=-1))
        ws = jnp.exp(g - m_new[..., None])
        wc = jnp.exp(aL + m - m_new)
        C_new = wc[..., None, None] * C + jnp.einsum('bhs,bhsd,bhse->bhde', ws, kc, vc)
        n_new = wc[..., None] * n + jnp.einsum('bhs,bhsd->bhd', ws, kc)
        return (C_new, n_new, m_new), h

    (C, n, m), hs = lax.scan(step, (C0, n0, m0), xs)
    h = jnp.moveaxis(hs, (0, 3), (1, 2)).reshape(B, T, H, v.shape[-1])
    return h, C, n, m


def chunk_spatial_gate(u, v, w_s, b_s):
    B, T, _ = v.shape
    nc = -(-T // C_CHUNK)
    pad = nc * C_CHUNK - T
    vp = jnp.pad(v, ((0, 0), (0, pad), (0, 0))).reshape(B, nc, C_CHUNK, C_GROUPS, C_GROUP_WIDTH)
    wm = jnp.where(jnp.tril(jnp.ones((C_CHUNK, C_CHUNK), dtype=bool)), w_s.astype(F32), 0.0)
    s = jnp.einsum('gts,bnsgc->bntgc', wm, vp) + b_s.astype(F32).T[:, :, None]
    s = s.reshape(B, nc * C_CHUNK, C_WIDTH)[:, :T]
    return u * s


def mixer(hn, w_in, b_if, attn_sinks, m_head_gain, c_ln_gain, c_ln_bias, c_w_s, c_b_s, w_out,
          win_k=None, win_v=None, C0=None, n0=None, m0=None):
    B, T, _ = hn.shape
    aq, ak, av, az, mq, mk, mv, mi, mf, mo, mz, cu, cv, cz = split_projection(
        jnp.einsum('btd,de->bte', hn, w_in))
    st_dtype = hn.dtype if C0 is None else C0.dtype

    k = ak.reshape(B, T, A_KV_HEADS, HEAD_DIM)
    v = av.reshape(B, T, A_KV_HEADS, HEAD_DIM)
    if win_k is None:
        a_out = swa_banded(aq, k, v, attn_sinks)
        wb = min(WINDOW, T)
        k_rows, v_rows = k[:, T - wb:], v[:, T - wb:]
    else:
        a_out = swa_cached(aq, k, v, win_k, win_v, attn_sinks)
        k_rows, v_rows = k, v

    if C0 is None:
        C0 = jnp.zeros((B, M_HEADS, M_DK, M_DV), F32)
        n0 = jnp.zeros((B, M_HEADS, M_DK), F32)
        m0 = jnp.zeros((B, M_HEADS), F32)
    b_if = b_if.astype(F32)
    h, C, n, m = mlstm_chunkwise(
        mq.reshape(B, T, M_HEADS, M_DK).astype(F32),
        mk.reshape(B, T, M_HEADS, M_DK).astype(F32) * (M_DK ** -0.5),
        mv.reshape(B, T, M_HEADS, M_DV).astype(F32),
        mi.astype(F32) + b_if[0], mf.astype(F32) + b_if[1],
        C0.astype(F32), n0.astype(F32), m0.astype(F32), M_CHUNK)
    h = h * lax.rsqrt(jnp.mean(h * h, axis=-1, keepdims=True) + EPS) * m_head_gain.astype(F32).reshape(M_HEADS, M_DV)
    m_out = (jax.nn.sigmoid(mo.astype(F32)).reshape(B, T, M_HEADS, M_DV) * h).reshape(B, T, M_WIDTH)

    u = jax.nn.gelu(cu.astype(F32), approximate=False)
    vv = layernorm(jax.nn.gelu(cv.astype(F32), approximate=False), c_ln_gain, c_ln_bias)
    c_out = chunk_spatial_gate(u, vv, c_w_s, c_b_s)
    cv_rows = vv[:, T - ((T - 1) % C_CHUNK + 1):]

    y = jnp.concatenate([a_out.astype(F32) * jax.nn.silu(az.astype(F32)),
                         m_out * jax.nn.silu(mz.astype(F32)),
                         c_out * jax.nn.silu(cz.astype(F32))], axis=-1).astype(hn.dtype)
    out = jnp.einsum('bte,ed->btd', y, w_out).astype(hn.dtype)
    states = (k_rows, v_rows, C.astype(st_dtype), n.astype(st_dtype), m.astype(st_dtype), cv_rows.astype(hn.dtype))
    return out, states


def setup_inputs(seed: int = 0) -> dict:
    key = jax.random.key(seed)
    ks = jax.random.split(key, 20)
    wb = min(WINDOW, PAST_LEN)

    def nrm(k, shape, scale):
        return scale * jax.random.normal(k, shape, F32)

    b_if = jnp.stack([nrm(ks[0], (DEPTH, M_HEADS), 0.1),
                      3.0 + nrm(ks[1], (DEPTH, M_HEADS), 0.5)], axis=1)
    return {
        'x_prompt': nrm(ks[2], (BATCH, SEQ, D_MODEL), 1.0),
        'x_sample': nrm(ks[3], (DEC_BATCH, DEC_SEQ, D_MODEL), 1.0),
        'cache_win_k': nrm(ks[4], (DEPTH, DEC_BATCH, wb, A_KV_HEADS, HEAD_DIM), 1.0),
        'cache_win_v': nrm(ks[5], (DEPTH, DEC_BATCH, wb, A_KV_HEADS, HEAD_DIM), 1.0),
        'state_mlstm_C': nrm(ks[6], (DEPTH, DEC_BATCH, M_HEADS, M_DK, M_DV), 0.3),
        'state_mlstm_n': nrm(ks[7], (DEPTH, DEC_BATCH, M_HEADS, M_DK), 0.3),
        'state_mlstm_m': nrm(ks[8], (DEPTH, DEC_BATCH, M_HEADS), 1.0),
        'norm_gain': 1.0 + nrm(ks[9], (DEPTH, D_MODEL), 0.02),
        'w_in': nrm(ks[10], (DEPTH, D_MODEL, D_IN), D_MODEL ** -0.5),
        'b_if': b_if,
        'attn_sinks': nrm(ks[11], (DEPTH, A_HEADS), 0.5),
        'm_head_gain': 1.0 + nrm(ks[12], (DEPTH, M_WIDTH), 0.02),
        'c_ln_gain': 1.0 + nrm(ks[13], (DEPTH, C_WIDTH), 0.02),
        'c_ln_bias': nrm(ks[14], (DEPTH, C_WIDTH), 0.02),
        'c_w_s': nrm(ks[15], (DEPTH, C_GROUPS, C_CHUNK, C_CHUNK), C_CHUNK ** -0.5),
        'c_b_s': 1.0 + nrm(ks[16], (DEPTH, C_GROUPS, C_CHUNK), 0.1),
        'w_out': nrm(ks[17], (DEPTH, D_MIX, D_MODEL), D_MIX ** -0.5),
        'final_gain': 1.0 + nrm(ks[18], (D_MODEL,), 0.02),
    }


def reference(x_prompt, x_sample, cache_win_k, cache_win_v, state_mlstm_C, state_mlstm_n, state_mlstm_m,
              norm_gain, w_in, b_if, attn_sinks, m_head_gain, c_ln_gain, c_ln_bias, c_w_s, c_b_s,
              w_out, final_gain):
    hp, hs = x_prompt, x_sample
    p_states, s_states = [], []
    for l in range(DEPTH):
        params = (w_in[l], b_if[l], attn_sinks[l], m_head_gain[l], c_ln_gain[l], c_ln_bias[l],
                  c_w_s[l], c_b_s[l], w_out[l])
        out_p, st_p = mixer(rmsnorm(hp, norm_gain[l]), *params)
        out_s, st_s = mixer(rmsnorm(hs, norm_gain[l]), *params, cache_win_k[l], cache_win_v[l],
                            state_mlstm_C[l], state_mlstm_n[l], state_mlstm_m[l])
        hp = hp + out_p
        hs = hs + out_s
        p_states.append(st_p)
        s_states.append(st_s)
    y_prompt = rmsnorm(hp, final_gain)
    y_sample = rmsnorm(hs, final_gain)
    k_p, v_p, C_p, n_p, m_p, cv_p = [jnp.stack([st[i] for st in p_states]) for i in range(6)]
    k_s, v_s, C_s, n_s, m_s, cv_s = [jnp.stack([st[i] for st in s_states]) for i in range(6)]
    return (y_prompt, y_sample, k_p, v_p, k_s, v_s, C_p, n_p, m_p, C_s, n_s, m_s, cv_p, cv_s)
```

```python
import contextlib
import numpy as np
import concourse.bass as bass
import concourse.mybir as mybir
from concourse.bass_utils import run_bass_kernel_spmd

F32 = mybir.dt.float32
BF16 = mybir.dt.bfloat16
ALU = mybir.AluOpType
AF = mybir.ActivationFunctionType
AX = mybir.AxisListType

D = 4096
SEQ = 2048
NS = 32
DIN = 13324
NG_IN = 53
NG_OUT = 16
TT = 256
NBLK = TT // 128
NTILE = SEQ // TT
EPS = 1e-6
SLOPES = [2.0 ** (-8.0 * (h + 1) / 12.0) for h in range(12)]
QSCALE = 128.0 ** -0.5
G_AQ, G_AK, G_AV, G_AZ = 0, 6, 8, 10
G_MQ, G_MK, G_MV, G_MO, G_MZ = 16, 19, 22, 28, 34
G_CU, G_CV, G_CZ, G_GATE = 40, 44, 48, 52
SBUF_LIMIT = 182 * 1024


class Region:
    __slots__ = ("name", "last_w", "readers", "excl")

    def __init__(self, name, excl=False):
        self.name = name
        self.last_w = None
        self.readers = []
        self.excl = excl


class EngineCtx:
    def __init__(self, name, sem):
        self.name = name
        self.sem = sem
        self.count = 0
        self.known = {}
        self.instrs = []


class Prog:
    def __init__(self, nc, stack, n_dma_sems=24):
        self.nc = nc
        self.eng = {}
        self.sems = {}
        for name in ("pe", "act", "dve", "pool", "sp"):
            sem = stack.enter_context(nc.semaphore("s_" + name))
            self.eng[name] = EngineCtx(name, sem)
            self.sems["e_" + name] = sem
        self.dma_pool = {}
        for q in ("sp", "pool"):
            lst = []
            for i in range(n_dma_sems):
                key = "d_%s_%d" % (q, i)
                self.sems[key] = stack.enter_context(nc.semaphore(key))
                lst.append([key, 0])
            self.dma_pool[q] = [lst, 0]
        self.n_instr = 0

    def _need(self, e, tok, waits):
        if tok is None:
            return
        key, val, src = tok
        if src == "pe" and e.name == "pe":
            return
        if e.known.get(key, 0) >= val:
            return
        if waits.get(key, 0) < val:
            waits[key] = val

    def _deps(self, e, reads, writes):
        waits = {}
        for r in reads:
            if r.excl:
                writes = list(writes) + [r]
                continue
            self._need(e, r.last_w, waits)
        for r in writes:
            self._need(e, r.last_w, waits)
            for t in r.readers:
                self._need(e, t, waits)
        return waits

    def _commit(self, tok, reads, writes):
        for r in reads:
            if r.excl:
                r.last_w = tok
                r.readers = []
                continue
            r.readers.append(tok)
            if len(r.readers) > 48:
                best = {}
                for t in r.readers:
                    if best.get(t[0], (None, -1))[1] < t[1]:
                        best[t[0]] = t
                r.readers = list(best.values())
        for r in writes:
            r.last_w = tok
            r.readers = []

    def _emit_waits(self, e, waits):
        for key, val in waits.items():
            e.instrs.append(("wait", self.sems[key], val))
            e.known[key] = val

    def op(self, engname, fn, reads=(), writes=()):
        return self.group(engname, [fn], reads, writes)

    def group(self, engname, fns, reads=(), writes=()):
        e = self.eng[engname]
        self._emit_waits(e, self._deps(e, reads, writes))
        e.count += 1
        tok = ("e_" + engname, e.count, engname)
        for fn in fns[:-1]:
            e.instrs.append(("op", fn, None, 0))
        e.instrs.append(("op", fns[-1], e.sem, 1))
        self._commit(tok, reads, writes)
        self.n_instr += len(fns)
        return tok

    def dma(self, qname, out, in_, reads=(), writes=()):
        e = self.eng[qname]
        waits = self._deps(e, reads, writes)
        pool, idx = self.dma_pool[qname]
        ent = pool[idx % len(pool)]
        self.dma_pool[qname][1] = idx + 1
        key, cum = ent
        if cum > 0 and e.known.get(key, 0) < cum and waits.get(key, 0) < cum:
            waits[key] = cum
        self._emit_waits(e, waits)
        ent[1] = cum + 16
        tok = (key, cum + 16, "dma")

        def fn(h, out=out, in_=in_):
            return h.dma_start(out=out, in_=in_)
        e.instrs.append(("op", fn, self.sems[key], 16))
        self._commit(tok, reads, writes)
        self.n_instr += 1
        return tok

    def wait_tok(self, engname, tok):
        e = self.eng[engname]
        waits = {}
        self._need(e, tok, waits)
        self._emit_waits(e, waits)

    def barrier(self, bar_out, bar_in, bar_region):
        sp = self.eng["sp"]
        waits = {}
        snap = {}
        for name in ("pe", "act", "dve"):
            c = self.eng[name].count
            snap["e_" + name] = c
            if c > 0:
                self._need(sp, ("e_" + name, c, name), waits)
        for key, cum in self.dma_pool["sp"][0]:
            snap[key] = cum
            if cum > 0:
                self._need(sp, (key, cum, "dma"), waits)
        self._emit_waits(sp, waits)
        tok = self.dma("sp", bar_out, bar_in, writes=[bar_region])
        for name in ("pe", "act", "dve"):
            self.wait_tok(name, tok)
            e = self.eng[name]
            for k, v in snap.items():
                if e.known.get(k, 0) < v:
                    e.known[k] = v

    def final_wait(self, engname, regions):
        e = self.eng[engname]
        waits = {}
        for r in regions:
            self._need(e, r.last_w, waits)
            for t in r.readers:
                self._need(e, t, waits)
        self._emit_waits(e, waits)

    def replay(self):
        nc = self.nc
        with nc.Block() as block:
            def mk(e):
                def body(h):
                    for ins in e.instrs:
                        if ins[0] == "wait":
                            h.wait_ge(ins[1], ins[2])
                        elif ins[2] is None:
                            ins[1](h)
                        else:
                            ins[1](h).then_inc(ins[2], ins[3])
                return body
            block.tensor(mk(self.eng["pe"]))
            block.scalar(mk(self.eng["act"]))
            block.vector(mk(self.eng["dve"]))
            block.gpsimd(mk(self.eng["pool"]))
            block.sync(mk(self.eng["sp"]))


class KB:
    def __init__(self, do_sample=True, debug=False):
        self.do_sample = do_sample
        self.debug = debug
        self.nc = bass.Bass("TRN2", target_bir_lowering=False)
        self.uid = 0
        self.sb_bytes = 0
        self.sb_peak = 0

    def sb(self, stack, shape, dt, name="t"):
        self.uid += 1
        n = 1
        for s in shape[1:]:
            n *= s
        nbytes = n * (4 if dt == F32 else 2)
        self.sb_bytes += nbytes
        self.sb_peak = max(self.sb_peak, self.sb_bytes)
        assert self.sb_bytes <= SBUF_LIMIT, ("SBUF overflow", name, self.sb_bytes)
        t = stack.enter_context(self.nc.sbuf_tensor("%s_%d" % (name, self.uid), list(shape), dt))

        def rel():
            self.sb_bytes -= nbytes
        stack.callback(rel)
        return t

    def dram_in(self, name, shape, dt=F32):
        return self.nc.dram_tensor(name, list(shape), dt, kind="ExternalInput").ap()

    def dram_out(self, name, shape, dt=F32):
        return self.nc.dram_tensor(name, list(shape), dt, kind="ExternalOutput").ap()

    def dram_scr(self, name, shape, dt=F32):
        return self.nc.dram_tensor(name, list(shape), dt, kind="Internal").ap()

    def build(self):
        nc = self.nc
        I = {}
        I["xp"] = self.dram_in("xp", [SEQ, D])
        I["xs"] = self.dram_in("xs", [NS, D])
        I["w_in"] = self.dram_in("w_in", [2, NG_IN, 128, 32 * 256])
        I["w_out"] = self.dram_in("w_out", [2, NG_OUT, 128, 32 * 256])
        I["gainT"] = self.dram_in("gainT", [2, 128, 32])
        I["fgain"] = self.dram_in("fgain", [D])
        I["bif"] = self.dram_in("bif", [24])
        I["sinks"] = self.dram_in("sinks", [24])
        I["mhgT"] = self.dram_in("mhgT", [2, 128, 12])
        I["lng"] = self.dram_in("lng", [2, 1024])
        I["lnb"] = self.dram_in("lnb", [2, 1024])
        I["wsT"] = self.dram_in("wsT", [2, 128, 8 * 128])
        I["bs"] = self.dram_in("bs", [2, 1024])
        I["ws00"] = self.dram_in("ws00", [16])
        I["bs0"] = self.dram_in("bs0", [16])
        I["ck"] = self.dram_in("ck", [2, 128, 128 * 128])
        I["cv"] = self.dram_in("cv", [2, 128, 128 * 128])
        I["sC"] = self.dram_in("sC", [2, NS, 6, 128, 256])
        I["sn"] = self.dram_in("sn", [2, NS, 6 * 128])
        I["sm"] = self.dram_in("sm", [2, NS, 6])
        I["c_ident"] = self.dram_in("c_ident", [128, 128])
        I["c_dm"] = self.dram_in("c_dm", [128, 256])
        I["c_mask6"] = self.dram_in("c_mask6", [128, 768])
        I["c_tri2"] = self.dram_in("c_tri2", [128, 128])
        I["c_onesc"] = self.dram_in("c_onesc", [128, 256])
        I["c_cmask"] = self.dram_in("c_cmask", [128, 2])
        I["c_tril"] = self.dram_in("c_tril", [128, 128])
        I["c_dist"] = self.dram_in("c_dist", [129])
        I["c_slp"] = self.dram_in("c_slp", [128, 3])
        I["sk4"] = self.dram_in("sk4", [2, 128, 3])
        I["c_i32"] = self.dram_in("c_i32", [128, NS * NS])
        I["mhg"] = self.dram_in("mhg", [2, 1536])
        O = {}
        O["yp"] = self.dram_out("yp", [SEQ, D])
        O["ys"] = self.dram_out("ys", [NS, D])
        O["kp"] = self.dram_out("kp", [2, 128, 512])
        O["vp"] = self.dram_out("vp", [2, 128, 512])
        O["ks"] = self.dram_out("ks", [2, NS, 512])
        O["vs"] = self.dram_out("vs", [2, NS, 512])
        O["Cp"] = self.dram_out("Cp", [2, 128, 6 * 256])
        O["np"] = self.dram_out("np", [2, 128, 6])
        O["mp"] = self.dram_out("mp", [2, 1, 6])
        O["Cs"] = self.dram_out("Cs", [2, NS, 6, 128, 256])
        O["ns"] = self.dram_out("ns", [2, NS, 6 * 128])
        O["ms"] = self.dram_out("ms", [2, NS, 6])
        O["cvp"] = self.dram_out("cvp", [2, 128, 1024])
        O["cvs"] = self.dram_out("cvs", [2, NS, 1024])
        if self.debug:
            O["dbg_y"] = self.dram_out("dbg_y", [128, 32 * TT])
            O["dbg_h1"] = self.dram_out("dbg_h1", [TT, D])
        self.I, self.O = I, O
        S = {}
        S["wb_in"] = self.dram_scr("wb_in", [2, NG_IN, 128, 32 * 256], BF16)
        S["wb_out"] = self.dram_scr("wb_out", [2, NG_OUT, 128, 32 * 256], BF16)
        S["h1"] = self.dram_scr("h1", [SEQ, D])
        S["h2"] = self.dram_scr("h2", [SEQ, D])
        S["hs1"] = self.dram_scr("hs1", [NS, D])
        S["hs2"] = self.dram_scr("hs2", [NS, D])
        S["bar"] = self.dram_scr("bar", [2, 16])
        S["bq"] = self.dram_scr("bq", [NS, 1536])
        S["bk"] = self.dram_scr("bk", [NS, 512])
        S["bv"] = self.dram_scr("bv", [NS, 512])
        S["bo"] = self.dram_scr("bo", [NS, 1536])
        self.S = S

        with contextlib.ExitStack() as st:
            self.st = st
            P = self.P = Prog(nc, st)
            self.R = {}
            for k in list(O.keys()) + ["h1", "h2", "hs1", "hs2", "bar", "xin", "bnc"]:
                self.R[k] = Region(k)
            self.Rw = {}
            self.pb = [st.enter_context(nc.psum_tensor("pb%d" % i, [128, 512], F32)) for i in range(7)]
            self.Rpb = [Region("pb%d" % i, excl=True) for i in range(7)]
            self.tb = st.enter_context(nc.psum_tensor("tb0", [128, 1024], BF16))
            self.Rtb = Region("tb0", excl=True)
            self.acc_i = 0
            self.emit_all()
            P.replay()
        return nc

    def const(self, name, shape, dt, src_ap, q="sp"):
        t = self.sb(self.st, shape, dt, name)
        r = Region(name)
        if dt == F32 or q == "sp":
            self.P.dma("sp", t[:], src_ap, writes=[r])
        else:
            self.P.dma("pool", t[:], src_ap, writes=[r])
        return t, r

    def barrier(self):
        self.P.barrier(self.S["bar"][0:1, :], self.S["bar"][1:2, :], self.R["bar"])

    def next_acc(self):
        i = self.acc_i % 4
        self.acc_i += 1
        return self.pb[i], self.Rpb[i]

    def emit_all(self):
        P, nc, I, O, S, st = self.P, self.nc, self.I, self.O, self.S, self.st
        sbp = lambda shape, dt, name: self.sb(st, shape, dt, name)
        for l in range(2):
            for g in range(NG_IN):
                r = Region("wbin%d_%d" % (l, g))
                self.Rw[("in", l, g)] = r
                P.dma("pool", S["wb_in"][l, g], I["w_in"][l, g], writes=[r])
            for g in range(NG_OUT):
                r = Region("wbout%d_%d" % (l, g))
                self.Rw[("out", l, g)] = r
                P.dma("pool", S["wb_out"][l, g], I["w_out"][l, g], writes=[r])
        identf, Ridf = self.const("identf", [128, 128], F32, I["c_ident"])
        self.identf, self.Ridf = identf, Ridf
        identb = sbp([128, 128], BF16, "identb")
        Ridb = Region("identb")
        P.op("dve", lambda h: h.tensor_copy(identb[:], identf[:]), reads=[Ridf], writes=[Ridb])
        self.identb, self.Ridb = identb, Ridb
        self.dm, self.Rdm = self.const("dm", [128, 256], F32, I["c_dm"])
        self.mask6, self.Rmask6 = self.const("mask6", [128, 768], F32, I["c_mask6"])
        self.tri2, self.Rtri2 = self.const("tri2", [128, 128], F32, I["c_tri2"])
        self.onesc, self.Ronesc = self.const("onesc", [128, 256], F32, I["c_onesc"])
        self.cmask, self.Rcmask = self.const("cmask", [128, 2], F32, I["c_cmask"])
        self.bifb, self.Rbifb = self.const("bifb", [128, 24], F32, I["bif"].partition_broadcast(128))
        self.sinkb, self.Rsinkb = self.const("sinkb", [128, 24], F32, I["sinks"].partition_broadcast(128))
        onesf = sbp([128, 128], F32, "onesf")
        self.Ronesf = Region("onesf")
        P.op("dve", lambda h: h.memset(onesf[:], 1.0), writes=[self.Ronesf])
        self.onesf = onesf
        onesb = sbp([128, 2], BF16, "onesb")
        self.Ronesb = Region("onesb")
        P.op("dve", lambda h: h.memset(onesb[:], 1.0), writes=[self.Ronesb])
        self.onesb = onesb
        tril, Rtril = self.const("tril", [128, 128], F32, I["c_tril"])
        self.hnT = sbp([128, 32, TT], BF16, "hnT")
        self.RhnT = Region("hnT")
        self.yT = sbp([128, 32, TT], BF16, "yT")
        self.RyT = [Region("yT_A"), Region("yT_M"), Region("yT_C")]
        self.wbuf = [sbp([128, 32, 256], BF16, "wbuf") for _ in range(3)]
        self.Rwbuf = [Region("wbuf%d" % i) for i in range(3)]
        self.w_i = 0
        self.Cst = sbp([128, 6, 256], F32, "Cst")
        self.Cb = [sbp([128, 6, 256], BF16, "Cb") for _ in range(2)]
        self.nst = sbp([128, 8], F32, "nst")
        self.nb = [sbp([128, 8], BF16, "nb") for _ in range(2)]
        self.mst = sbp([128, 8], F32, "mst")
        self.RC = Region("Cst")
        self.RCb = [Region("Cb0"), Region("Cb1")]
        self.Rn = Region("nst")
        self.Rnb = [Region("nb0"), Region("nb1")]
        self.Rm = Region("mst")
        self.kcar = sbp([128, 4, 128], BF16, "kcar")
        self.vcar = sbp([128, 512], BF16, "vcar")
        self.Rkcar, self.Rvcar = Region("kcar"), Region("vcar")
        self.gainT = sbp([128, 32], F32, "gainT")
        self.RgainT = Region("gainT")
        self.mhgT = sbp([128, 12], F32, "mhgT")
        self.RmhgT = Region("mhgT")
        self.wmT = sbp([128, 8, 128], BF16, "wmT")
        self.RwmT = Region("wmT")
        self.bsb = sbp([128, 1024], F32, "bsb")
        self.Rbsb = Region("bsb")
        self.lngb = sbp([128, 1024], F32, "lngb")
        self.lnbb = sbp([128, 1024], F32, "lnbb")
        self.Rln = Region("ln")

        for l in range(2):
            self.l = l
            P.dma("sp", self.gainT[:], I["gainT"][l], writes=[self.RgainT])
            P.dma("sp", self.mhgT[:], I["mhgT"][l], writes=[self.RmhgT])
            P.dma("sp", self.bsb[:], I["bs"][l].partition_broadcast(128), writes=[self.Rbsb])
            P.dma("sp", self.lngb[:], I["lng"][l].partition_broadcast(128), writes=[self.Rln])
            P.dma("sp", self.lnbb[:], I["lnb"][l].partition_broadcast(128), writes=[self.Rln])
            with contextlib.ExitStack() as ph:
                wtmp = self.sb(ph, [128, 8, 128], F32, "wtmp")
                Rwtmp = Region("wtmp")
                P.dma("sp", wtmp[:].rearrange("p a b -> p (a b)"), I["wsT"][l], writes=[Rwtmp])
                P.op("dve", lambda h, wtmp=wtmp: h.tensor_tensor(self.wmT[:], wtmp[:], tril[:].unsqueeze(1).to_broadcast([128, 8, 128]), ALU.mult),
                     reads=[Rwtmp, Rtril], writes=[self.RwmT])
                self.barrier()
            P.op("dve", lambda h: h.memset(self.Cst[:], 0.0), writes=[self.RC])
            P.op("dve", lambda h: h.memset(self.Cb[0][:], 0.0), writes=[self.RCb[0]])
            P.op("dve", lambda h: h.memset(self.nst[:], 0.0), writes=[self.Rn])
            P.op("dve", lambda h: h.memset(self.nb[0][:], 0.0), writes=[self.Rnb[0]])
            P.op("dve", lambda h: h.memset(self.mst[:], 0.0), writes=[self.Rm])
            src = I["xp"] if l == 0 else S["h1"]
            Rsrc = self.R["xin"] if l == 0 else self.R["h1"]
            dst = S["h1"] if l == 0 else S["h2"]
            Rdst = self.R["h1"] if l == 0 else self.R["h2"]
            for t in range(NTILE):
                self.t = t
                self.phase_norm(src[t * TT:(t + 1) * TT, :], Rsrc, NBLK, 128)
                self.phase_A()
                self.phase_M()
                self.phase_C()
                if self.debug and l == 0 and t == 0:
                    self.dbg_dump_y()
                self.phase_O(src[t * TT:(t + 1) * TT, :], Rsrc, dst[t * TT:(t + 1) * TT, :], Rdst, 128, NBLK)
                if self.debug and l == 0 and t == 0:
                    self.P.dma("sp", self.O["dbg_h1"], dst[0:TT, :], reads=[Rdst], writes=[self.R["dbg_h1"]])
                if l == 1:
                    self.phase_final(dst[t * TT:(t + 1) * TT, :], Rdst, O["yp"][t * TT:(t + 1) * TT, :], self.R["yp"], 128, NBLK)
            P.dma("sp", O["Cp"][l], self.Cst[:].rearrange("p a b -> p (a b)"), reads=[self.RC], writes=[self.R["Cp"]])
            P.dma("sp", O["np"][l], self.nst[:, 0:6], reads=[self.Rn], writes=[self.R["np"]])
            P.dma("sp", O["mp"][l], self.mst[0:1, 0:6], reads=[self.Rm], writes=[self.R["mp"]])
            if self.do_sample:
                self.sample_layer()
        outs = [self.R[k] for k in O.keys()]
        P.final_wait("sp", outs)

    def wnext(self, kind, g):
        i = self.w_i % 3
        self.w_i += 1
        src = self.S["wb_in"] if kind == "in" else self.S["wb_out"]
        self.P.dma("sp", self.wbuf[i][:].rearrange("p a b -> p (a b)"), src[self.l, g],
                   reads=[self.Rw[(kind, self.l, g)]], writes=[self.Rwbuf[i]])
        return self.wbuf[i], self.Rwbuf[i]

    def wstream(self, kind, groups):
        pend = []
        for idx, g in enumerate(groups):
            if idx == 0:
                pend.append(self.wnext(kind, g))
            if idx + 1 < len(groups):
                pend.append(self.wnext(kind, groups[idx + 1]))
            buf, reg = pend.pop(0)
            yield g, buf, reg

    def mm_feat(self, wbuf, Rw, j, ntok):
        acc, Racc = self.next_acc()
        hnT = self.hnT
        fns = [(lambda h, kc=kc: h.matmul(acc[:, 0:ntok], wbuf[:, kc, j * 128:(j + 1) * 128], hnT[:, kc, 0:ntok],
                                          start=(kc == 0), stop=(kc == 31))) for kc in range(32)]
        self.P.group("pe", fns, reads=[Rw, self.RhnT], writes=[Racc])
        return acc, Racc

    def mm_tok(self, wbuf, Rw, t0, nt, ncols):
        acc, Racc = self.next_acc()
        hnT = self.hnT
        fns = [(lambda h, kc=kc: h.matmul(acc[0:nt, 0:ncols], hnT[:, kc, t0:t0 + nt], wbuf[:, kc, 0:ncols],
                                          start=(kc == 0), stop=(kc == 31))) for kc in range(32)]
        self.P.group("pe", fns, reads=[Rw, self.RhnT], writes=[Racc])
        return acc, Racc

    def phase_norm(self, src, Rsrc, nblk, np_):
        P = self.P
        with contextlib.ExitStack() as ph:
            hb = [self.sb(ph, [128, D], F32, "hb") for _ in range(2)]
            hn = [self.sb(ph, [128, D], BF16, "hn") for _ in range(2)]
            junk = self.sb(ph, [128, D], BF16, "junk")
            stt = [self.sb(ph, [128, 4], F32, "nst") for _ in range(2)]
            Rhb = [Region("hb0"), Region("hb1")]
            Rhn = [Region("hn0"), Region("hn1")]
            Rst = [Region("st0"), Region("st1")]
            Rj = Region("junk")
            for b in range(nblk):
                i = b % 2
                h_, n_, s_ = hb[i], hn[i], stt[i]
                P.dma("sp", h_[0:np_, :], src[b * np_:(b + 1) * np_, :], reads=[Rsrc], writes=[Rhb[i]])
                P.op("act", lambda h, h_=h_, s_=s_: h.activation(junk[0:np_, :], h_[0:np_, :], AF.Square, accum_out=s_[0:np_, 0:1]),
                     reads=[Rhb[i]], writes=[Rj, Rst[i]])
                P.op("act", lambda h, s_=s_: h.activation(s_[0:np_, 1:2], s_[0:np_, 0:1], AF.Sqrt, bias=EPS, scale=1.0 / D),
                     reads=[Rst[i]], writes=[Rst[i]])
                P.op("dve", lambda h, s_=s_: h.reciprocal(s_[0:np_, 2:3], s_[0:np_, 1:2]), reads=[Rst[i]], writes=[Rst[i]])
                P.op("dve", lambda h, h_=h_, n_=n_, s_=s_: h.tensor_scalar(n_[0:np_, :], h_[0:np_, :], s_[0:np_, 2:3], None, op0=ALU.mult),
                     reads=[Rhb[i], Rst[i]], writes=[Rhn[i]])
                for q in range(4):
                    fns = [(lambda h, kc=kc, n_=n_: h.transpose(self.tb[:, (kc % 8) * 128:(kc % 8) * 128 + np_],
                                                                n_[0:np_, kc * 128:(kc + 1) * 128], self.identb[0:np_, 0:np_]))
                           for kc in range(q * 8, q * 8 + 8)]
                    P.group("pe", fns, reads=[Rhn[i], self.Ridb], writes=[self.Rtb])
                    P.op("dve", lambda h, q=q, b=b: h.tensor_tensor(
                        self.hnT[:, q * 8:(q + 1) * 8, b * np_:(b + 1) * np_],
                        self.tb[:, :].rearrange("p (a c) -> p a c", a=8)[:, :, 0:np_],
                        self.gainT[:, q * 8:(q + 1) * 8].unsqueeze(2).to_broadcast([128, 8, np_]), ALU.mult),
                        reads=[self.Rtb, self.RgainT], writes=[self.RhnT])
            self.barrier()

    def phase_A(self):
        P, l, t = self.P, self.l, self.t
        first_tile = (t == 0)
        last_tile = (t == NTILE - 1)
        with contextlib.ExitStack() as ph:
            qT = self.sb(ph, [128, 12, TT], BF16, "qT")
            kT = self.sb(ph, [128, 4, TT + 128], BF16, "kT")
            vt = self.sb(ph, [128, NBLK + 1, 512], BF16, "vt")
            zT = self.sb(ph, [128, 12, TT], BF16, "zT")
            ost = self.sb(ph, [128, 2, 512], F32, "ost")
            RqT, RkT, Rvt, RzT, Rost = Region("qT"), Region("kT"), Region("vt"), Region("zT"), Region("ost")
            if not first_tile:
                P.op("act", lambda h: h.copy(kT[:, :, 0:128], self.kcar[:]), reads=[self.Rkcar], writes=[RkT])
                P.op("act", lambda h: h.copy(vt[:, 0, :], self.vcar[:]), reads=[self.Rvcar], writes=[Rvt])
            groups = list(range(G_AQ, G_AZ + 6))
            for g, wb, Rw in self.wstream("in", groups):
                if g < G_AK:
                    for j in range(2):
                        acc, Racc = self.mm_feat(wb, Rw, j, TT)
                        hd = (g - G_AQ) * 2 + j
                        P.op("act", lambda h, acc=acc, hd=hd: h.activation(qT[:, hd, :], acc[:, 0:TT], AF.Copy, scale=QSCALE),
                             reads=[Racc], writes=[RqT])
                elif g < G_AV:
                    for j in range(2):
                        acc, Racc = self.mm_feat(wb, Rw, j, TT)
                        kv = (g - G_AK) * 2 + j
                        P.op("dve", lambda h, acc=acc, kv=kv: h.tensor_copy(kT[:, kv, 128:128 + TT], acc[:, 0:TT]),
                             reads=[Racc], writes=[RkT])
                    if last_tile:
                        acc, Racc = self.mm_tok(wb, Rw, TT - 128, 128, 256)
                        c0 = (g - G_AK) * 256
                        P.op("dve", lambda h, acc=acc, c0=c0: h.tensor_copy(ost[:, 0, c0:c0 + 256], acc[:, 0:256]),
                             reads=[Racc], writes=[Rost])
                elif g < G_AZ:
                    c0 = (g - G_AV) * 256
                    for b in range(NBLK):
                        acc, Racc = self.mm_tok(wb, Rw, b * 128, 128, 256)
                        P.op("act", lambda h, acc=acc, b=b, c0=c0: h.copy(vt[:, b + 1, c0:c0 + 256], acc[:, 0:256]),
                             reads=[Racc], writes=[Rvt])
                        if last_tile and b == NBLK - 1:
                            P.op("dve", lambda h, acc=acc, c0=c0: h.tensor_copy(ost[:, 1, c0:c0 + 256], acc[:, 0:256]),
                                 reads=[Racc], writes=[Rost])
                else:
                    for j in range(2):
                        acc, Racc = self.mm_feat(wb, Rw, j, TT)
                        hd = (g - G_AZ) * 2 + j
                        P.op("act", lambda h, acc=acc, hd=hd: h.activation(zT[:, hd, :], acc[:, 0:TT], AF.Silu),
                             reads=[Racc], writes=[RzT])
            if last_tile:
                P.dma("sp", self.O["kp"][l], ost[:, 0, :], reads=[Rost], writes=[self.R["kp"]])
                P.dma("sp", self.O["vp"][l], ost[:, 1, :], reads=[Rost], writes=[self.R["vp"]])
            P.op("act", lambda h: h.copy(self.kcar[:], kT[:, :, TT:TT + 128]), reads=[RkT], writes=[self.Rkcar])
            P.op("act", lambda h: h.copy(self.vcar[:], vt[:, NBLK, :]), reads=[Rvt], writes=[self.Rvcar])
            L = [self.sb(ph, [128, 256], F32, "L") for _ in range(2)]
            Pn = [self.sb(ph, [128, 256], BF16, "Pn") for _ in range(2)]
            PT = [self.sb(ph, [128, 2, 128], BF16, "PT") for _ in range(2)]
            sm = [self.sb(ph, [128, 8], F32, "asm") for _ in range(2)]
            RL = [Region("L0"), Region("L1")]
            RPn = [Region("Pn0"), Region("Pn1")]
            RPT = [Region("PT0"), Region("PT1")]
            Rsm = [Region("sm0"), Region("sm1")]
            it = 0
            for b in range(NBLK):
                gfirst = first_tile and b == 0
                nk = 128 if gfirst else 256
                koff = b * 128 + (128 if gfirst else 0)
                dmo = 128 if gfirst else 0
                for hd in range(12):
                    kv = hd // 3
                    i = it % 2
                    it += 1
                    pS, RpS = self.pb[4 + i], self.Rpb[4 + i]
                    L_, Pn_, PT_, sm_ = L[i], Pn[i], PT[i], sm[i]
                    sk = self.sinkb[:, l * 12 + hd:l * 12 + hd + 1]
                    P.op("pe", lambda h, pS=pS, hd=hd, kv=kv, b=b, koff=koff, nk=nk: h.matmul(
                        pS[:, 0:nk], qT[:, hd, b * 128:(b + 1) * 128], kT[:, kv, koff:koff + nk], start=True, stop=True),
                        reads=[RqT, RkT], writes=[RpS])
                    P.op("dve", lambda h, pS=pS, L_=L_, hd=hd, nk=nk, dmo=dmo: h.scalar_tensor_tensor(
                        L_[:, 0:nk], self.dm[:, dmo:dmo + nk], -SLOPES[hd], pS[:, 0:nk], op0=ALU.mult, op1=ALU.add),
                        reads=[RpS, self.Rdm], writes=[RL[i]])
                    P.op("dve", lambda h, L_=L_, sm_=sm_, nk=nk: h.tensor_reduce(sm_[:, 0:1], L_[:, 0:nk], AX.X, ALU.max),
                         reads=[RL[i]], writes=[Rsm[i]])
                    P.op("dve", lambda h, sm_=sm_, sk=sk: h.tensor_scalar(sm_[:, 1:2], sm_[:, 0:1], sk, -1.0, op0=ALU.max, op1=ALU.mult),
                         reads=[Rsm[i], self.Rsinkb], writes=[Rsm[i]])
                    P.op("act", lambda h, L_=L_, sm_=sm_, nk=nk: h.activation(L_[:, 0:nk], L_[:, 0:nk], AF.Exp, bias=sm_[:, 1:2], scale=1.0,
                                                                              accum_out=sm_[:, 2:3]),
                         reads=[RL[i], Rsm[i]], writes=[RL[i], Rsm[i]])
                    P.op("act", lambda h, sm_=sm_, sk=sk: h.activation(sm_[:, 3:4], sk, AF.Exp, bias=sm_[:, 1:2], scale=1.0),
                         reads=[Rsm[i], self.Rsinkb], writes=[Rsm[i]])
                    P.op("dve", lambda h, sm_=sm_: h.tensor_tensor(sm_[:, 4:5], sm_[:, 2:3], sm_[:, 3:4], ALU.add), reads=[Rsm[i]], writes=[Rsm[i]])
                    P.op("dve", lambda h, sm_=sm_: h.reciprocal(sm_[:, 5:6], sm_[:, 4:5]), reads=[Rsm[i]], writes=[Rsm[i]])
                    P.op("dve", lambda h, L_=L_, Pn_=Pn_, sm_=sm_, nk=nk: h.tensor_scalar(Pn_[:, 0:nk], L_[:, 0:nk], sm_[:, 5:6], None, op0=ALU.mult),
                         reads=[RL[i], Rsm[i]], writes=[RPn[i]])
                    nkb = nk // 128
                    fns = [(lambda h, kb=kb, Pn_=Pn_: h.transpose(self.tb[:, kb * 128:(kb + 1) * 128], Pn_[:, kb * 128:(kb + 1) * 128], self.identb[:]))
                           for kb in range(nkb)]
                    P.group("pe", fns, reads=[RPn[i], self.Ridb], writes=[self.Rtb])
                    P.op("act", lambda h, PT_=PT_, nkb=nkb: h.copy(PT_[:, 0:nkb, :], self.tb[:, 0:nkb * 128].rearrange("p (a c) -> p a c", a=nkb)),
                         reads=[self.Rtb], writes=[RPT[i]])
                    pO, RpO = self.pb[i], self.Rpb[i]
                    vb0 = b + (1 if gfirst else 0)
                    fns = [(lambda h, kb=kb, pO=pO, PT_=PT_, kv=kv, vb0=vb0, nkb=nkb: h.matmul(
                        pO[:, 0:128], vt[:, vb0 + kb, kv * 128:(kv + 1) * 128], PT_[:, kb, :], start=(kb == 0), stop=(kb == nkb - 1)))
                        for kb in range(nkb)]
                    P.group("pe", fns, reads=[Rvt, RPT[i]], writes=[RpO])
                    P.op("dve", lambda h, pO=pO, hd=hd, b=b: h.tensor_tensor(self.yT[:, hd, b * 128:(b + 1) * 128], pO[:, 0:128],
                                                                              zT[:, hd, b * 128:(b + 1) * 128], ALU.mult),
                         reads=[RpO, RzT], writes=[self.RyT[0]])
            self.barrier()

    def phase_M(self):
        P, l, t = self.P, self.l, self.t
        with contextlib.ExitStack() as ph:
            qb = self.sb(ph, [128, NBLK, 768], BF16, "qb")
            kb_ = self.sb(ph, [128, NBLK, 768], BF16, "kb")
            va = self.sb(ph, [128, NBLK, 1536], BF16, "va")
            G = self.sb(ph, [128, NBLK, 1536], BF16, "G")
            gts = self.sb(ph, [128, NBLK, 12], F32, "gts")
            tmp = [self.sb(ph, [128, 256], F32, "mtmp") for _ in range(2)]
            Rqb, Rkb, Rva, RG, Rgts = Region("qb"), Region("kb"), Region("va"), Region("G"), Region("gts")
            Rtmp = [Region("mt0"), Region("mt1")]
            groups = list(range(G_MQ, G_CU)) + [G_GATE]
            ti = 0
            for g, wb, Rw in self.wstream("in", groups):
                ncols = 12 if g == G_GATE else 256
                for b in range(NBLK):
                    acc, Racc = self.mm_tok(wb, Rw, b * 128, 128, ncols)
                    if g == G_GATE:
                        P.op("dve", lambda h, acc=acc, b=b: h.tensor_tensor(gts[:, b, :], acc[:, 0:12], self.bifb[:, l * 12:(l + 1) * 12], ALU.add),
                             reads=[Racc, self.Rbifb], writes=[Rgts])
                    elif g < G_MK:
                        c0 = (g - G_MQ) * 256
                        P.op("act", lambda h, acc=acc, b=b, c0=c0: h.copy(qb[:, b, c0:c0 + 256], acc[:, 0:256]), reads=[Racc], writes=[Rqb])
                    elif g < G_MV:
                        c0 = (g - G_MK) * 256
                        P.op("act", lambda h, acc=acc, b=b, c0=c0: h.activation(kb_[:, b, c0:c0 + 256], acc[:, 0:256], AF.Copy, scale=QSCALE),
                             reads=[Racc], writes=[Rkb])
                    elif g < G_MO:
                        c0 = (g - G_MV) * 256
                        P.op("dve", lambda h, acc=acc, b=b, c0=c0: h.tensor_copy(va[:, b, c0:c0 + 256], acc[:, 0:256]), reads=[Racc], writes=[Rva])
                    elif g < G_MZ:
                        c0 = (g - G_MO) * 256
                        P.op("act", lambda h, acc=acc, b=b, c0=c0: h.activation(G[:, b, c0:c0 + 256], acc[:, 0:256], AF.Sigmoid), reads=[Racc], writes=[RG])
                    else:
                        c0 = (g - G_MZ) * 256
                        tm, Rtm = tmp[ti % 2], Rtmp[ti % 2]
                        ti += 1
                        P.op("act", lambda h, acc=acc, tm=tm: h.activation(tm[:], acc[:, 0:256], AF.Silu), reads=[Racc], writes=[Rtm])
                        P.op("dve", lambda h, tm=tm, b=b, c0=c0: h.tensor_tensor(G[:, b, c0:c0 + 256], G[:, b, c0:c0 + 256], tm[:], ALU.mult),
                             reads=[Rtm, RG], writes=[RG])
            sm = self.sb(ph, [128, 128], F32, "msm")
            Bd = self.sb(ph, [128, 6, 128], F32, "Bd")
            Dm = self.sb(ph, [128, 6, 128], F32, "Dm")
            w = self.sb(ph, [128, 768], BF16, "w")
            wT = self.sb(ph, [128, 768], BF16, "wT")
            qTs = self.sb(ph, [128, 768], BF16, "qTs")
            kTs = self.sb(ph, [128, 768], BF16, "kTs")
            qs = self.sb(ph, [128, 6, 128], BF16, "qs")
            qsT = self.sb(ph, [128, 6, 2, 128], BF16, "qsT")
            ksc = [self.sb(ph, [128, 6, 128], BF16, "ksc") for _ in range(2)]
            mo = self.sb(ph, [128, 1536], BF16, "mo")
            junk = self.sb(ph, [128, 256], BF16, "mjunk")
            Rsm, RBd, RDm, Rw_, RwT, RqTs, RkTs, Rqs, RqsT, Rmo, Rjunk = (Region(n) for n in
                ("msm", "Bd", "Dm", "w", "wT", "qTs", "kTs", "qs", "qsT", "mo", "mjunk"))
            Rksc = [Region("ksc0"), Region("ksc1")]
            P.op("dve", lambda h: h.memset(qsT[:], 0.0), writes=[RqsT])
            psm, Rpsm = self.pb[0], self.Rpb[0]
            pB = [self.pb[1], self.pb[2]]
            RpB = [self.Rpb[1], self.Rpb[2]]
            pS = [self.pb[3], self.pb[4]]
            RpS = [self.Rpb[3], self.Rpb[4]]
            pC = [self.pb[1], self.pb[2], self.pb[3]]
            RpC = [self.Rpb[1], self.Rpb[2], self.Rpb[3]]
            pN = [self.pb[4], self.pb[5], self.pb[6]]
            RpN = [self.Rpb[4], self.Rpb[5], self.Rpb[6]]
            c_e1, c_sp, c_an, c_al0, c_al1, c_bv = 0, 6, 12, 20, 28, 36
            c_mxB, c_rmD, c_m1, c_m2, c_msel, c_mns, c_als = 42, 54, 60, 66, 72, 78, 84
            c_int, c_mt, c_wi, c_emt, c_ws, c_wsc0, c_wsc1 = 90, 96, 102, 108, 114, 120, 0
            sm2 = self.sb(ph, [128, 64], F32, "msm2")
            Rsm2 = Region("msm2")
            d_wc0, d_wc1, d_den, d_dn, d_t, d_rd, d_ssq, d_f, d_dnn = 0, 6, 12, 18, 24, 30, 36, 42, 48
            S_ = lambda c, n=6: sm[:, c:c + n]
            S2 = lambda c, n=6: sm2[:, c:c + n]
            bc3 = lambda ap: ap.unsqueeze(2).to_broadcast([128, 6, 128])
            def do_block(b):
                ip = gts[:, b, 0:6]
                fp = gts[:, b, 6:12]
                P.op("act", lambda h, fp=fp: h.activation(S_(c_e1), fp, AF.Exp, scale=-1.0), reads=[Rgts], writes=[Rsm])
                P.op("act", lambda h: h.activation(S_(c_sp), S_(c_e1), AF.Ln, bias=1.0), reads=[Rsm], writes=[Rsm])
                fns = [lambda h: h.matmul(psm[:, 0:6], self.tri2[:], S_(c_sp), start=True, stop=True),
                       lambda h: h.matmul(psm[:, 8:14], self.onesc[:, 0:128], S_(c_sp), start=True, stop=True),
                       lambda h: h.matmul(psm[:, 16:22], self.onesc[:, 128:256], S_(c_sp), start=True, stop=True)]
                P.group("pe", fns, reads=[Rsm, self.Rtri2, self.Ronesc], writes=[Rpsm])
                P.op("dve", lambda h: h.tensor_copy(sm[:, c_an:c_an + 24], psm[:, 0:24]), reads=[Rpsm], writes=[Rsm])
                P.op("dve", lambda h, ip=ip: h.tensor_tensor(S_(c_bv), ip, S_(c_an), ALU.add), reads=[Rgts, Rsm], writes=[Rsm])
                P.op("dve", lambda h: h.tensor_tensor(Bd[:], self.identf[:].unsqueeze(1).to_broadcast([128, 6, 128]), bc3(S_(c_bv)), ALU.mult),
                     reads=[Rsm, self.Ridf], writes=[RBd])
                for j in range(2):
                    P.op("pe", lambda h, j=j: h.matmul(pB[j][:, 0:384], self.onesf[:], Bd[:, 3 * j:3 * j + 3, :].rearrange("p a c -> p (a c)"),
                                                       start=True, stop=True), reads=[RBd, self.Ronesf], writes=[RpB[j]])
                    P.op("dve", lambda h, j=j: h.tensor_reduce(sm[:, c_mxB + 6 * j:c_mxB + 6 * j + 6].rearrange("p (a c) -> p a c", a=3),
                                                               pB[j][:, 0:384].rearrange("p (a c s) -> p a c s", a=3, c=2), AX.X, ALU.max),
                         reads=[RpB[j]], writes=[Rsm])
                    P.op("dve", lambda h, j=j: h.tensor_tensor(Dm[:, 3 * j:3 * j + 3, :].rearrange("p a c -> p (a c)"), pB[j][:, 0:384],
                                                               self.mask6[:, 384 * j:384 * j + 384], ALU.add),
                         reads=[RpB[j], self.Rmask6], writes=[RDm])
                P.op("dve", lambda h: h.tensor_tensor(Dm[:], Dm[:], bc3(S_(c_an)), ALU.subtract), reads=[RDm, Rsm], writes=[RDm])
                P.op("dve", lambda h: h.tensor_reduce(S_(c_rmD), Dm[:], AX.X, ALU.max), reads=[RDm], writes=[Rsm])
                mxB = sm[:, c_mxB:c_mxB + 12].rearrange("p (a c) -> p a c", c=2)
                P.op("dve", lambda h: h.tensor_tensor(S_(c_m1), self.mst[:, 0:6], mxB[:, :, 0], ALU.max), reads=[Rsm, self.Rm], writes=[Rsm])
                P.op("dve", lambda h: h.tensor_tensor(S_(c_m1), S_(c_m1), S_(c_al0), ALU.subtract), reads=[Rsm], writes=[Rsm])
                P.op("dve", lambda h: h.tensor_tensor(S_(c_m2), S_(c_m1), mxB[:, :, 1], ALU.max), reads=[Rsm], writes=[Rsm])
                P.op("dve", lambda h: h.tensor_tensor(S_(c_m2), S_(c_m2), S_(c_al1), ALU.subtract), reads=[Rsm], writes=[Rsm])
                P.op("dve", lambda h: h.tensor_tensor(S2(d_wc0), self.mst[:, 0:6], S_(c_al0), ALU.subtract), reads=[Rsm, self.Rm], writes=[Rsm2])
                P.op("dve", lambda h: h.tensor_tensor(S2(d_wc0), S2(d_wc0), S_(c_m1), ALU.subtract), reads=[Rsm, Rsm2], writes=[Rsm2])
                P.op("dve", lambda h: h.tensor_tensor(S2(d_wc1), S_(c_m1), S_(c_al1), ALU.subtract), reads=[Rsm, Rsm2], writes=[Rsm2])
                P.op("dve", lambda h: h.tensor_tensor(S2(d_wc1), S2(d_wc1), S_(c_m2), ALU.subtract), reads=[Rsm, Rsm2], writes=[Rsm2])
                P.op("act", lambda h: h.activation(S2(d_wc0, 12), S2(d_wc0, 12), AF.Exp), reads=[Rsm2], writes=[Rsm2])
                P.op("dve", lambda h: h.tensor_copy(sm[0:64, c_msel:c_msel + 6], self.mst[0:64, 0:6]), reads=[self.Rm, Rsm], writes=[Rsm])
                P.op("dve", lambda h: h.tensor_copy(sm[64:128, c_msel:c_msel + 6], sm[64:128, c_m1:c_m1 + 6]), reads=[Rsm], writes=[Rsm])
                P.op("dve", lambda h: h.tensor_copy(sm[0:64, c_mns:c_mns + 6], sm[0:64, c_m1:c_m1 + 6]), reads=[Rsm], writes=[Rsm])
                P.op("dve", lambda h: h.tensor_copy(sm[64:128, c_mns:c_mns + 6], sm[64:128, c_m2:c_m2 + 6]), reads=[Rsm], writes=[Rsm])
                P.op("dve", lambda h: h.tensor_copy(sm[0:64, c_als:c_als + 6], sm[0:64, c_al0:c_al0 + 6]), reads=[Rsm], writes=[Rsm])
                P.op("dve", lambda h: h.tensor_copy(sm[64:128, c_als:c_als + 6], sm[64:128, c_al1:c_al1 + 6]), reads=[Rsm], writes=[Rsm])
                P.op("dve", lambda h: h.tensor_copy(self.mst[:, 0:6], S_(c_m2)), reads=[Rsm], writes=[self.Rm])
                P.op("dve", lambda h: h.tensor_tensor(S_(c_int), S_(c_msel), S_(c_an), ALU.subtract), reads=[Rsm], writes=[Rsm])
                P.op("dve", lambda h: h.tensor_tensor(S_(c_mt), S_(c_int), S_(c_rmD), ALU.max), reads=[Rsm], writes=[Rsm])
                P.op("dve", lambda h: h.tensor_tensor(S_(c_wi), S_(c_int), S_(c_mt), ALU.subtract), reads=[Rsm], writes=[Rsm])
                P.op("act", lambda h: h.activation(S_(c_wi), S_(c_wi), AF.Exp), reads=[Rsm], writes=[Rsm])
                P.op("act", lambda h: h.activation(S_(c_emt), S_(c_mt), AF.Exp, scale=-1.0), reads=[Rsm], writes=[Rsm])
                P.op("dve", lambda h: h.tensor_tensor(S_(c_ws), S_(c_bv), S_(c_als), ALU.subtract), reads=[Rsm], writes=[Rsm])
                P.op("dve", lambda h: h.tensor_tensor(S_(c_ws), S_(c_ws), S_(c_mns), ALU.subtract), reads=[Rsm], writes=[Rsm])
                P.op("act", lambda h: h.activation(S_(c_ws), S_(c_ws), AF.Exp), reads=[Rsm], writes=[Rsm])
                P.op("dve", lambda h: h.tensor_scalar(S_(c_wsc0), S_(c_ws), self.cmask[:, 0:1], None, op0=ALU.mult), reads=[Rsm, self.Rcmask], writes=[Rsm])
                P.op("dve", lambda h: h.tensor_scalar(S_(c_wsc1), S_(c_ws), self.cmask[:, 1:2], None, op0=ALU.mult), reads=[Rsm, self.Rcmask], writes=[Rsm])
                P.op("dve", lambda h: h.tensor_tensor(Dm[:], Dm[:], bc3(S_(c_mt)), ALU.subtract), reads=[RDm, Rsm], writes=[RDm])
                P.op("act", lambda h: h.activation(Dm[:], Dm[:], AF.Exp), reads=[RDm], writes=[RDm])
                for (srcb, Rsrcb, dstT, RdstT) in ((qb, Rqb, qTs, RqTs), (kb_, Rkb, kTs, RkTs)):
                    fns = [(lambda h, hh=hh, srcb=srcb: h.transpose(self.tb[:, hh * 128:(hh + 1) * 128], srcb[:, b, hh * 128:(hh + 1) * 128], self.identb[:]))
                           for hh in range(6)]
                    P.group("pe", fns, reads=[Rsrcb, self.Ridb], writes=[self.Rtb])
                    P.op("act", lambda h, dstT=dstT: h.copy(dstT[:], self.tb[:, 0:768]), reads=[self.Rtb], writes=[RdstT])
                fns = [(lambda h, hh=hh: h.matmul(pS[hh // 4][:, (hh % 4) * 128:(hh % 4) * 128 + 128], qTs[:, hh * 128:(hh + 1) * 128],
                                                  kTs[:, hh * 128:(hh + 1) * 128], start=True, stop=True)) for hh in range(6)]
                P.group("pe", fns, reads=[RqTs, RkTs], writes=[RpS[0], RpS[1]])
                P.op("dve", lambda h: h.tensor_tensor(w[:, 0:512], Dm[:, 0:4, :].rearrange("p a c -> p (a c)"), pS[0][:, 0:512], ALU.mult),
                     reads=[RDm, RpS[0]], writes=[Rw_])
                P.op("dve", lambda h: h.tensor_tensor(w[:, 512:768], Dm[:, 4:6, :].rearrange("p a c -> p (a c)"), pS[1][:, 0:256], ALU.mult),
                     reads=[RDm, RpS[1]], writes=[Rw_])
                fns = [(lambda h, hh=hh: h.transpose(self.tb[:, hh * 128:(hh + 1) * 128], w[:, hh * 128:(hh + 1) * 128], self.identb[:])) for hh in range(6)]
                P.group("pe", fns, reads=[Rw_, self.Ridb], writes=[self.Rtb])
                P.op("act", lambda h: h.copy(wT[:], self.tb[:, 0:768]), reads=[self.Rtb], writes=[RwT])
                P.op("dve", lambda h: h.tensor_tensor(qs[:], qb[:, b, :].rearrange("p (a c) -> p a c", a=6), bc3(S_(c_wi)), ALU.mult),
                     reads=[Rqb, Rsm], writes=[Rqs])
                fns = [(lambda h, hh=hh: h.transpose(self.tb[:, hh * 128:(hh + 1) * 128], qs[:, hh, :], self.identb[:])) for hh in range(6)]
                P.group("pe", fns, reads=[Rqs, self.Ridb], writes=[self.Rtb])
                tbv = self.tb[:, 0:768].rearrange("p (a c) -> p a c", a=6)
                P.op("act", lambda h: h.copy(qsT[:, :, 0, 0:64], tbv[:, :, 0:64]), reads=[self.Rtb], writes=[RqsT])
                P.op("act", lambda h: h.copy(qsT[:, :, 1, 64:128], tbv[:, :, 64:128]), reads=[self.Rtb], writes=[RqsT])
                kb3 = kb_[:, b, :].rearrange("p (a c) -> p a c", a=6)
                P.op("dve", lambda h: h.tensor_tensor(ksc[0][:], kb3, bc3(S_(c_wsc0)), ALU.mult), reads=[Rkb, Rsm], writes=[Rksc[0]])
                P.op("dve", lambda h: h.tensor_tensor(ksc[1][:], kb3, bc3(S_(c_wsc1)), ALU.mult), reads=[Rkb, Rsm], writes=[Rksc[1]])
                va3 = va[:, b, :].rearrange("p (a c) -> p a c", a=6)

                def state_update(c, wc_col, Cb_dst, nb_dst, RCb_dst, Rnb_dst):
                    fns = []
                    for hh in range(6):
                        fns.append(lambda h, hh=hh: h.matmul(pC[hh // 2][:, (hh % 2) * 256:(hh % 2) * 256 + 256], ksc[c][:, hh, :], va3[:, hh, :],
                                                             start=True, stop=True))
                    P.group("pe", fns, reads=[Rksc[c], Rva], writes=RpC)
                    fns = [(lambda h, hh=hh: h.matmul(psm[:, 32 + hh:33 + hh], ksc[c][:, hh, :], self.onesb[:, 0:1], start=True, stop=True)) for hh in range(6)]
                    P.group("pe", fns, reads=[Rksc[c], self.Ronesb], writes=[Rpsm])
                    for hh in range(6):
                        P.op("dve", lambda h, hh=hh: h.scalar_tensor_tensor(self.Cst[:, hh, :], self.Cst[:, hh, :], sm2[:, wc_col + hh:wc_col + hh + 1],
                                                                            pC[hh // 2][:, (hh % 2) * 256:(hh % 2) * 256 + 256], op0=ALU.mult, op1=ALU.add),
                             reads=[Rsm2, RpC[hh // 2], self.RC], writes=[self.RC])
                    P.op("dve", lambda h: h.tensor_tensor(S2(d_dnn), self.nst[:, 0:6], S2(wc_col), ALU.mult), reads=[self.Rn, Rsm2], writes=[Rsm2])
                    P.op("dve", lambda h: h.tensor_tensor(self.nst[:, 0:6], S2(d_dnn), psm[:, 32:38], ALU.add), reads=[Rsm2, Rpsm], writes=[self.Rn])
                    P.op("act", lambda h: h.copy(Cb_dst[:], self.Cst[:]), reads=[self.RC], writes=[RCb_dst])
                    P.op("act", lambda h: h.copy(nb_dst[:, 0:6], self.nst[:, 0:6]), reads=[self.Rn], writes=[Rnb_dst])

                state_update(0, d_wc0, self.Cb[1], self.nb[1], self.RCb[1], self.Rnb[1])
                for hh in range(6):
                    o_ = pN[hh // 2][:, (hh % 2) * 256:(hh % 2) * 256 + 256]
                    fns = [lambda h, hh=hh, o_=o_: h.matmul(o_, wT[:, hh * 128:(hh + 1) * 128], va3[:, hh, :], start=True, stop=False),
                           lambda h, hh=hh, o_=o_: h.matmul(o_, qsT[:, hh, 0, :], self.Cb[0][:, hh, :], start=False, stop=False),
                           lambda h, hh=hh, o_=o_: h.matmul(o_, qsT[:, hh, 1, :], self.Cb[1][:, hh, :], start=False, stop=True)]
                    P.group("pe", fns, reads=[RwT, Rva, RqsT, self.RCb[0], self.RCb[1]], writes=[RpN[hh // 2]])
                for hh in range(6):
                    o_ = psm[:, 40 + hh:41 + hh]
                    fns = [lambda h, hh=hh, o_=o_: h.matmul(o_, wT[:, hh * 128:(hh + 1) * 128], self.onesb[:, 0:1], start=True, stop=False),
                           lambda h, hh=hh, o_=o_: h.matmul(o_, qsT[:, hh, 0, :], self.nb[0][:, hh:hh + 1], start=False, stop=False),
                           lambda h, hh=hh, o_=o_: h.matmul(o_, qsT[:, hh, 1, :], self.nb[1][:, hh:hh + 1], start=False, stop=True)]
                    P.group("pe", fns, reads=[RwT, self.Ronesb, RqsT, self.Rnb[0], self.Rnb[1]], writes=[Rpsm])
                P.op("dve", lambda h: h.tensor_copy(S2(d_den), psm[:, 40:46]), reads=[Rpsm], writes=[Rsm2])
                P.op("dve", lambda h: h.scalar_tensor_tensor(S2(d_t), S2(d_den), -1.0, S2(d_den), op0=ALU.mult, op1=ALU.max), reads=[Rsm2], writes=[Rsm2])
                P.op("dve", lambda h: h.tensor_tensor(S2(d_t), S2(d_t), S_(c_emt), ALU.max), reads=[Rsm2, Rsm], writes=[Rsm2])
                P.op("dve", lambda h: h.reciprocal(S2(d_rd), S2(d_t)), reads=[Rsm2], writes=[Rsm2])
                for hh in range(6):
                    P.op("act", lambda h, hh=hh: h.activation(junk[:], pN[hh // 2][:, (hh % 2) * 256:(hh % 2) * 256 + 256], AF.Square,
                                                              accum_out=sm2[:, d_ssq + hh:d_ssq + hh + 1]),
                         reads=[RpN[hh // 2], Rsm2], writes=[Rjunk, Rsm2])
                P.op("dve", lambda h: h.tensor_tensor(S2(d_t), S2(d_rd), S2(d_rd), ALU.mult), reads=[Rsm2], writes=[Rsm2])
                P.op("dve", lambda h: h.tensor_tensor(S2(d_t), S2(d_t), S2(d_ssq), ALU.mult), reads=[Rsm2], writes=[Rsm2])
                P.op("act", lambda h: h.activation(S2(d_t), S2(d_t), AF.Sqrt, bias=EPS, scale=1.0 / 256.0), reads=[Rsm2], writes=[Rsm2])
                P.op("dve", lambda h: h.reciprocal(S2(d_f), S2(d_t)), reads=[Rsm2], writes=[Rsm2])
                P.op("dve", lambda h: h.tensor_tensor(S2(d_f), S2(d_f), S2(d_rd), ALU.mult), reads=[Rsm2], writes=[Rsm2])
                for hh in range(6):
                    P.op("dve", lambda h, hh=hh: h.scalar_tensor_tensor(mo[:, hh * 256:(hh + 1) * 256], pN[hh // 2][:, (hh % 2) * 256:(hh % 2) * 256 + 256],
                                                                        sm2[:, d_f + hh:d_f + hh + 1], G[:, b, hh * 256:(hh + 1) * 256],
                                                                        op0=ALU.mult, op1=ALU.mult),
                         reads=[RpN[hh // 2], Rsm2, RG], writes=[Rmo])
                for half in range(2):
                    fns = [(lambda h, j=j: h.transpose(self.tb[:, (j % 6) * 128:(j % 6) * 128 + 128], mo[:, j * 128:(j + 1) * 128], self.identb[:]))
                           for j in range(half * 6, half * 6 + 6)]
                    P.group("pe", fns, reads=[Rmo, self.Ridb], writes=[self.Rtb])
                    P.op("dve", lambda h, half=half: h.tensor_tensor(
                        self.yT[:, 12 + half * 6:12 + half * 6 + 6, b * 128:(b + 1) * 128],
                        self.tb[:, 0:768].rearrange("p (a c) -> p a c", a=6),
                        self.mhgT[:, half * 6:half * 6 + 6].unsqueeze(2).to_broadcast([128, 6, 128]), ALU.mult),
                        reads=[self.Rtb, self.RmhgT], writes=[self.RyT[1]])
                state_update(1, d_wc1, self.Cb[0], self.nb[0], self.RCb[0], self.Rnb[0])
            for b in range(NBLK):
                do_block(b)
            self.barrier()

    def phase_C(self):
        P, l, t = self.P, self.l, self.t
        last_tile = (t == NTILE - 1)
        with contextlib.ExitStack() as ph:
            uT = self.sb(ph, [128, 8, TT], BF16, "uT")
            vvb = self.sb(ph, [128, NBLK, 1024], BF16, "vvb")
            gv = [self.sb(ph, [128, 1024], F32, "gv") for _ in range(NBLK)]
            tmp = [self.sb(ph, [128, TT], F32, "ctmp") for _ in range(2)]
            junk = self.sb(ph, [128, 1024], BF16, "cjunk")
            sm = self.sb(ph, [128, 16], F32, "csm")
            t1 = self.sb(ph, [128, 4, 128], F32, "ct1")
            RuT, Rvvb, Rjunk, Rsm, Rt1 = Region("uT"), Region("vvb"), Region("cjunk"), Region("csm"), Region("ct1")
            Rgv = [Region("gv%d" % i) for i in range(NBLK)]
            Rtmp = [Region("ct0"), Region("ct1")]
            groups = list(range(G_CU, G_GATE))
            ti = 0
            for g, wb, Rw in self.wstream("in", groups):
                if g < G_CV:
                    for j in range(2):
                        acc, Racc = self.mm_feat(wb, Rw, j, TT)
                        gi = (g - G_CU) * 2 + j
                        P.op("act", lambda h, acc=acc, gi=gi: h.activation(uT[:, gi, :], acc[:, 0:TT], AF.Gelu), reads=[Racc], writes=[RuT])
                elif g < G_CZ:
                    c0 = (g - G_CV) * 256
                    for b in range(NBLK):
                        acc, Racc = self.mm_tok(wb, Rw, b * 128, 128, 256)
                        P.op("act", lambda h, acc=acc, b=b, c0=c0: h.activation(gv[b][:, c0:c0 + 256], acc[:, 0:256], AF.Gelu), reads=[Racc], writes=[Rgv[b]])
                else:
                    for j in range(2):
                        acc, Racc = self.mm_feat(wb, Rw, j, TT)
                        gi = (g - G_CZ) * 2 + j
                        tm, Rtm = tmp[ti % 2], Rtmp[ti % 2]
                        ti += 1
                        P.op("act", lambda h, acc=acc, tm=tm: h.activation(tm[:], acc[:, 0:TT], AF.Silu), reads=[Racc], writes=[Rtm])
                        P.op("dve", lambda h, tm=tm, gi=gi: h.tensor_tensor(uT[:, gi, :], uT[:, gi, :], tm[:], ALU.mult), reads=[Rtm, RuT], writes=[RuT])
            def do_blockc(b):
                g_ = gv[b]
                self.layernorm(g_, Rgv[b], 128, junk, Rjunk, sm, Rsm)
                P.op("act", lambda h, g_=g_, b=b: h.copy(vvb[:, b, :], g_[:]), reads=[Rgv[b]], writes=[Rvvb])
                if last_tile and b == NBLK - 1:
                    P.dma("sp", self.O["cvp"][l], g_[:], reads=[Rgv[b]], writes=[self.R["cvp"]])
                for half in range(2):
                    pS_, RpS_ = self.pb[4 + half], self.Rpb[4 + half]
                    fns = [(lambda h, gi=gi, pS_=pS_: h.matmul(pS_[:, (gi % 4) * 128:(gi % 4) * 128 + 128], vvb[:, b, gi * 128:(gi + 1) * 128],
                                                               self.wmT[:, gi, :], start=True, stop=True)) for gi in range(half * 4, half * 4 + 4)]
                    P.group("pe", fns, reads=[Rvvb, self.RwmT], writes=[RpS_])
                    P.op("dve", lambda h, pS_=pS_, half=half: h.tensor_tensor(t1[:].rearrange("p a c -> p (a c)"), pS_[:, 0:512],
                                                                              self.bsb[:, half * 512:(half + 1) * 512], ALU.add),
                         reads=[RpS_, self.Rbsb], writes=[Rt1])
                    P.op("dve", lambda h, half=half, b=b: h.tensor_tensor(self.yT[:, 24 + half * 4:28 + half * 4, b * 128:(b + 1) * 128], t1[:],
                                                                          uT[:, half * 4:half * 4 + 4, b * 128:(b + 1) * 128], ALU.mult),
                         reads=[Rt1, RuT], writes=[self.RyT[2]])
            for b in range(NBLK):
                do_blockc(b)
            self.barrier()

    def layernorm(self, g_, Rg, np_, junk, Rjunk, sm, Rsm):
        P = self.P
        P.op("act", lambda h: h.activation(junk[0:np_, :], g_[0:np_, :], AF.Copy, accum_out=sm[0:np_, 0:1]), reads=[Rg], writes=[Rjunk, Rsm])
        P.op("act", lambda h: h.activation(junk[0:np_, :], g_[0:np_, :], AF.Square, accum_out=sm[0:np_, 1:2]), reads=[Rg], writes=[Rjunk, Rsm])
        P.op("dve", lambda h: h.tensor_scalar(sm[0:np_, 2:4], sm[0:np_, 0:2], 1.0 / 1024.0, None, op0=ALU.mult), reads=[Rsm], writes=[Rsm])
        P.op("dve", lambda h: h.tensor_tensor(sm[0:np_, 4:5], sm[0:np_, 2:3], sm[0:np_, 2:3], ALU.mult), reads=[Rsm], writes=[Rsm])
        P.op("dve", lambda h: h.tensor_tensor(sm[0:np_, 5:6], sm[0:np_, 3:4], sm[0:np_, 4:5], ALU.subtract), reads=[Rsm], writes=[Rsm])
        P.op("act", lambda h: h.activation(sm[0:np_, 6:7], sm[0:np_, 5:6], AF.Sqrt, bias=EPS, scale=1.0), reads=[Rsm], writes=[Rsm])
        P.op("dve", lambda h: h.reciprocal(sm[0:np_, 7:8], sm[0:np_, 6:7]), reads=[Rsm], writes=[Rsm])
        P.op("dve", lambda h: h.tensor_scalar(g_[0:np_, :], g_[0:np_, :], sm[0:np_, 2:3], sm[0:np_, 7:8], op0=ALU.subtract, op1=ALU.mult),
             reads=[Rg, Rsm], writes=[Rg])
        P.op("dve", lambda h: h.tensor_tensor(g_[0:np_, :], g_[0:np_, :], self.lngb[0:np_, :], ALU.mult), reads=[Rg, self.Rln], writes=[Rg])
        P.op("dve", lambda h: h.tensor_tensor(g_[0:np_, :], g_[0:np_, :], self.lnbb[0:np_, :], ALU.add), reads=[Rg, self.Rln], writes=[Rg])

    def phase_O(self, src, Rsrc, dst, Rdst, np_, nblk):
        P = self.P
        with contextlib.ExitStack() as ph:
            hs = [self.sb(ph, [128, nblk, 256], F32, "hs") for _ in range(2)]
            Rhs = [Region("hs0"), Region("hs1")]
            i = 0
            for g, wb, Rw in self.wstream("out", list(range(NG_OUT))):
                h_, Rh_ = hs[i % 2], Rhs[i % 2]
                i += 1
                P.dma("sp", h_[0:np_, :, :], src[:, g * 256:(g + 1) * 256].rearrange("(b p) c -> p b c", p=np_), reads=[Rsrc], writes=[Rh_])
                for b in range(nblk):
                    acc, Racc = self.next_acc()
                    yT = self.yT
                    fns = [(lambda h, kc=kc, acc=acc, b=b, wb=wb: h.matmul(acc[0:np_, 0:256], yT[:, kc, b * np_:(b + 1) * np_], wb[:, kc, 0:256],
                                                                    start=(kc == 0), stop=(kc == 31))) for kc in range(32)]
                    P.group("pe", fns, reads=[Rw] + self.RyT, writes=[Racc])
                    P.op("dve", lambda h, acc=acc, h_=h_, b=b: h.tensor_tensor(h_[0:np_, b, :], h_[0:np_, b, :], acc[0:np_, 0:256], ALU.add),
                         reads=[Racc, Rh_], writes=[Rh_])
                P.dma("sp", dst[:, g * 256:(g + 1) * 256].rearrange("(b p) c -> p b c", p=np_), h_[0:np_, :, :], reads=[Rh_], writes=[Rdst])
            self.barrier()

    def phase_final(self, src, Rsrc, out, Rout, np_, nblk):
        P = self.P
        with contextlib.ExitStack() as ph:
            hb = [self.sb(ph, [128, D], F32, "fhb") for _ in range(2)]
            fg = self.sb(ph, [128, D], F32, "fg")
            junk = self.sb(ph, [128, D], BF16, "fjunk")
            stt = [self.sb(ph, [128, 4], F32, "fst") for _ in range(2)]
            Rhb = [Region("fhb0"), Region("fhb1")]
            Rst = [Region("fst0"), Region("fst1")]
            Rfg, Rj = Region("fg"), Region("fjunk")
            P.dma("sp", fg[:], self.I["fgain"].partition_broadcast(128), writes=[Rfg])
            for b in range(nblk):
                i = b % 2
                h_, s_ = hb[i], stt[i]
                P.dma("sp", h_[0:np_, :], src[b * np_:(b + 1) * np_, :], reads=[Rsrc], writes=[Rhb[i]])
                P.op("act", lambda h, h_=h_, s_=s_: h.activation(junk[0:np_, :], h_[0:np_, :], AF.Square, accum_out=s_[0:np_, 0:1]),
                     reads=[Rhb[i]], writes=[Rj, Rst[i]])
                P.op("act", lambda h, s_=s_: h.activation(s_[0:np_, 1:2], s_[0:np_, 0:1], AF.Sqrt, bias=EPS, scale=1.0 / D), reads=[Rst[i]], writes=[Rst[i]])
                P.op("dve", lambda h, s_=s_: h.reciprocal(s_[0:np_, 2:3], s_[0:np_, 1:2]), reads=[Rst[i]], writes=[Rst[i]])
                P.op("dve", lambda h, h_=h_, s_=s_: h.scalar_tensor_tensor(h_[0:np_, :], h_[0:np_, :], s_[0:np_, 2:3], fg[0:np_, :], op0=ALU.mult, op1=ALU.mult),
                     reads=[Rhb[i], Rst[i], Rfg], writes=[Rhb[i]])
                P.dma("sp", out[b * np_:(b + 1) * np_, :], h_[0:np_, :], reads=[Rhb[i]], writes=[Rout])
            self.barrier()

    def dbg_dump_y(self):
        P = self.P
        with contextlib.ExitStack() as ph:
            yf = self.sb(ph, [128, 32 * TT], F32, "dbgy")
            Ry = Region("dbgy")
            P.op("dve", lambda h: h.tensor_copy(yf[:], self.yT[:].rearrange("p a b -> p (a b)")), reads=self.RyT, writes=[Ry])
            P.dma("sp", self.O["dbg_y"], yf[:], reads=[Ry], writes=[self.R["dbg_y"]])
            self.barrier()

    def sample_layer(self):
        l, I, S, O = self.l, self.I, self.S, self.O
        src = I["xs"] if l == 0 else S["hs1"]
        Rsrc = self.R["xin"] if l == 0 else self.R["hs1"]
        dst = S["hs1"] if l == 0 else S["hs2"]
        Rdst = self.R["hs1"] if l == 0 else self.R["hs2"]
        self.phase_norm(src, Rsrc, 1, NS)
        self.s_phase_A()
        self.s_phase_M()
        self.s_phase_C()
        self.phase_O(src, Rsrc, dst, Rdst, NS, 1)
        if l == 1:
            self.phase_final(dst, Rdst, O["ys"], self.R["ys"], NS, 1)

    def s_proj(self, wb, Rw, ncols, evac):
        acc, Racc = self.mm_tok(wb, Rw, 0, NS, ncols)
        evac(acc, Racc)

    def s_to_yT(self, srcb, Rsrcb, kc0, n):
        P = self.P
        fns = [(lambda h, j=j: h.transpose(self.tb[:, j * NS:(j + 1) * NS], srcb[0:NS, j * 128:(j + 1) * 128], self.identb[0:NS, 0:NS])) for j in range(n)]
        P.group("pe", fns, reads=[Rsrcb, self.Ridb], writes=[self.Rtb])
        reg = self.RyT[0] if kc0 == 0 else (self.RyT[1] if kc0 == 12 else self.RyT[2])
        P.op("act", lambda h: h.copy(self.yT[:, kc0:kc0 + n, 0:NS], self.tb[:, 0:n * NS].rearrange("p (a c) -> p a c", a=n)), reads=[self.Rtb], writes=[reg])

    def s_phase_A(self):
        P, l, I, S, O = self.P, self.l, self.I, self.S, self.O
        with contextlib.ExitStack() as ph:
            qs_ = self.sb(ph, [NS, 1536], F32, "sq")
            kn = self.sb(ph, [NS, 512], F32, "skn")
            vn = self.sb(ph, [NS, 512], F32, "svn")
            za = self.sb(ph, [NS, 1536], F32, "sza")
            Rq, Rkn, Rvn, Rza = Region("sq"), Region("skn"), Region("svn"), Region("sza")
            for g, wb, Rw in self.wstream("in", list(range(G_AQ, G_AZ + 6))):
                def evac(acc, Racc, g=g):
                    if g < G_AK:
                        c0 = (g - G_AQ) * 256
                        P.op("act", lambda h: h.activation(qs_[:, c0:c0 + 256], acc[0:NS, 0:256], AF.Copy, scale=QSCALE), reads=[Racc], writes=[Rq])
                    elif g < G_AV:
                        c0 = (g - G_AK) * 256
                        P.op("dve", lambda h: h.tensor_copy(kn[:, c0:c0 + 256], acc[0:NS, 0:256]), reads=[Racc], writes=[Rkn])
                    elif g < G_AZ:
                        c0 = (g - G_AV) * 256
                        P.op("dve", lambda h: h.tensor_copy(vn[:, c0:c0 + 256], acc[0:NS, 0:256]), reads=[Racc], writes=[Rvn])
                    else:
                        c0 = (g - G_AZ) * 256
                        P.op("act", lambda h: h.activation(za[:, c0:c0 + 256], acc[0:NS, 0:256], AF.Silu), reads=[Racc], writes=[Rza])
                self.s_proj(wb, Rw, 256, evac)
            Rb = self.R["bnc"]
            P.dma("sp", O["ks"][l], kn[:], reads=[Rkn], writes=[self.R["ks"]])
            P.dma("sp", O["vs"][l], vn[:], reads=[Rvn], writes=[self.R["vs"]])
            P.dma("sp", S["bq"], qs_[:], reads=[Rq], writes=[Rb])
            P.dma("sp", S["bk"], kn[:], reads=[Rkn], writes=[Rb])
            P.dma("sp", S["bv"], vn[:], reads=[Rvn], writes=[Rb])
            q4 = self.sb(ph, [128, 3, 128], F32, "q4")
            k4 = self.sb(ph, [128, 128], F32, "k4")
            v4 = self.sb(ph, [128, 128], F32, "v4")
            R4 = Region("qkv4")
            P.dma("sp", q4[:].rearrange("p a b -> p (a b)"), S["bq"].rearrange("b (kv x) -> (b kv) x", kv=4), reads=[Rb], writes=[R4])
            P.dma("sp", k4[:], S["bk"].rearrange("b (kv x) -> (b kv) x", kv=4), reads=[Rb], writes=[R4])
            P.dma("sp", v4[:], S["bv"].rearrange("b (kv x) -> (b kv) x", kv=4), reads=[Rb], writes=[R4])
            slp = self.sb(ph, [128, 3], F32, "slp")
            sk4 = self.sb(ph, [128, 3], F32, "sk4")
            dist = self.sb(ph, [128, 129], F32, "dist")
            Rc = Region("sAc")
            P.dma("sp", slp[:], I["c_slp"], writes=[Rc])
            P.dma("sp", sk4[:], I["sk4"][l], writes=[Rc])
            P.dma("sp", dist[:], I["c_dist"].partition_broadcast(128), writes=[Rc])
            lg = self.sb(ph, [128, 3, 129], F32, "lg")
            ab = self.sb(ph, [128, 3, 129], F32, "ab")
            Rlg, Rab = Region("lg"), Region("ab")
            P.op("dve", lambda h: h.tensor_tensor(ab[:], dist[:].unsqueeze(1).to_broadcast([128, 3, 129]), slp[:].unsqueeze(2).to_broadcast([128, 3, 129]), ALU.mult),
                 reads=[Rc], writes=[Rab])
            KC = 16
            kvb = self.sb(ph, [128, KC, 128], F32, "kvb")
            tmp = self.sb(ph, [128, KC * 128], F32, "stmp")
            Rkvb, Rtmp = Region("kvb"), Region("stmp")
            for c in range(128 // KC):
                P.dma("sp", kvb[:].rearrange("p a b -> p (a b)"), I["ck"][l][:, c * KC * 128:(c + 1) * KC * 128], writes=[Rkvb])
                for g3 in range(3):
                    P.op("dve", lambda h, g3=g3: h.tensor_tensor(tmp[:].rearrange("p (a b) -> p a b", a=KC), kvb[:], q4[:, g3, :].unsqueeze(1).to_broadcast([128, KC, 128]), ALU.mult),
                         reads=[Rkvb, R4], writes=[Rtmp])
                    P.op("dve", lambda h, g3=g3, c=c: h.tensor_reduce(lg[:, g3, c * KC:(c + 1) * KC], tmp[:].rearrange("p (a b) -> p a b", a=KC), AX.X, ALU.add),
                         reads=[Rtmp], writes=[Rlg])
            P.op("dve", lambda h: h.tensor_tensor(tmp[:, 0:384].rearrange("p (a b) -> p a b", a=3), q4[:], k4[:].unsqueeze(1).to_broadcast([128, 3, 128]), ALU.mult),
                 reads=[R4], writes=[Rtmp])
            P.op("dve", lambda h: h.tensor_reduce(lg[:, :, 128], tmp[:, 0:384].rearrange("p (a b) -> p a b", a=3), AX.X, ALU.add), reads=[Rtmp], writes=[Rlg])
            P.op("dve", lambda h: h.tensor_tensor(lg[:], lg[:], ab[:], ALU.subtract), reads=[Rlg, Rab], writes=[Rlg])
            sm = self.sb(ph, [128, 24], F32, "sAsm")
            Rsm = Region("sAsm")
            P.op("dve", lambda h: h.tensor_reduce(sm[:, 0:3], lg[:], AX.X, ALU.max), reads=[Rlg], writes=[Rsm])
            P.op("dve", lambda h: h.tensor_tensor(sm[:, 0:3], sm[:, 0:3], sk4[:], ALU.max), reads=[Rsm, Rc], writes=[Rsm])
            P.op("dve", lambda h: h.tensor_tensor(lg[:], lg[:], sm[:, 0:3].unsqueeze(2).to_broadcast([128, 3, 129]), ALU.subtract), reads=[Rlg, Rsm], writes=[Rlg])
            P.op("act", lambda h: h.activation(lg[:], lg[:], AF.Exp), reads=[Rlg], writes=[Rlg])
            P.op("dve", lambda h: h.tensor_reduce(sm[:, 3:6], lg[:], AX.X, ALU.add), reads=[Rlg], writes=[Rsm])
            P.op("dve", lambda h: h.tensor_tensor(sm[:, 6:9], sk4[:], sm[:, 0:3], ALU.subtract), reads=[Rsm, Rc], writes=[Rsm])
            P.op("act", lambda h: h.activation(sm[:, 6:9], sm[:, 6:9], AF.Exp), reads=[Rsm], writes=[Rsm])
            P.op("dve", lambda h: h.tensor_tensor(sm[:, 3:6], sm[:, 3:6], sm[:, 6:9], ALU.add), reads=[Rsm], writes=[Rsm])
            P.op("dve", lambda h: h.reciprocal(sm[:, 9:12], sm[:, 3:6]), reads=[Rsm], writes=[Rsm])
            P.op("dve", lambda h: h.tensor_tensor(lg[:], lg[:], sm[:, 9:12].unsqueeze(2).to_broadcast([128, 3, 129]), ALU.mult), reads=[Rlg, Rsm], writes=[Rlg])
            o4 = self.sb(ph, [128, 3, 128], F32, "o4")
            o4t = self.sb(ph, [128, 3, 128], F32, "o4t")
            Ro4, Ro4t = Region("o4"), Region("o4t")
            for g3 in range(3):
                P.op("dve", lambda h, g3=g3: h.tensor_scalar(o4[:, g3, :], v4[:], lg[:, g3, 128:129], None, op0=ALU.mult), reads=[R4, Rlg], writes=[Ro4])
            for c in range(128 // KC):
                P.dma("sp", kvb[:].rearrange("p a b -> p (a b)"), I["cv"][l][:, c * KC * 128:(c + 1) * KC * 128], writes=[Rkvb])
                for g3 in range(3):
                    P.op("dve", lambda h, g3=g3, c=c: h.tensor_tensor(tmp[:].rearrange("p (d s) -> p d s", s=KC), kvb[:].rearrange("p s d -> p d s"),
                                                                      lg[:, g3, c * KC:(c + 1) * KC].unsqueeze(1).to_broadcast([128, 128, KC]), ALU.mult),
                         reads=[Rkvb, Rlg], writes=[Rtmp])
                    P.op("dve", lambda h, g3=g3: h.tensor_reduce(o4t[:, g3, :], tmp[:].rearrange("p (d s) -> p d s", s=KC), AX.X, ALU.add), reads=[Rtmp], writes=[Ro4t])
                    P.op("dve", lambda h, g3=g3: h.tensor_tensor(o4[:, g3, :], o4[:, g3, :], o4t[:, g3, :], ALU.add), reads=[Ro4, Ro4t], writes=[Ro4])
            P.dma("sp", S["bo"].rearrange("b (kv x) -> (b kv) x", kv=4), o4[:].rearrange("p a b -> p (a b)"), reads=[Ro4], writes=[Rb])
            ao = self.sb(ph, [NS, 1536], F32, "ao")
            yb = self.sb(ph, [NS, 1536], BF16, "syb")
            Rao, Ryb = Region("ao"), Region("syb")
            P.dma("sp", ao[:], S["bo"], reads=[Rb], writes=[Rao])
            P.op("dve", lambda h: h.tensor_tensor(yb[:], ao[:], za[:], ALU.mult), reads=[Rao, Rza], writes=[Ryb])
            self.s_to_yT(yb, Ryb, 0, 12)
            self.barrier()

    def s_phase_M(self):
        P, l, I, S, O = self.P, self.l, self.I, self.S, self.O
        with contextlib.ExitStack() as ph:
            q = self.sb(ph, [NS, 768], F32, "smq")
            k = self.sb(ph, [NS, 768], F32, "smk")
            v = self.sb(ph, [NS, 1536], BF16, "smv")
            G = self.sb(ph, [NS, 1536], F32, "smG")
            gts = self.sb(ph, [NS, 12], F32, "smg")
            tm = self.sb(ph, [NS, 256], F32, "smt")
            Rq, Rk, Rv, RG, Rg, Rtm = (Region(n) for n in ("smq", "smk", "smv", "smG", "smg", "smt"))
            for g, wb, Rw in self.wstream("in", list(range(G_MQ, G_CU)) + [G_GATE]):
                def evac(acc, Racc, g=g):
                    if g == G_GATE:
                        P.op("dve", lambda h: h.tensor_tensor(gts[:], acc[0:NS, 0:12], self.bifb[0:NS, l * 12:(l + 1) * 12], ALU.add), reads=[Racc, self.Rbifb], writes=[Rg])
                    elif g < G_MK:
                        c0 = (g - G_MQ) * 256
                        P.op("act", lambda h: h.copy(q[:, c0:c0 + 256], acc[0:NS, 0:256]), reads=[Racc], writes=[Rq])
                    elif g < G_MV:
                        c0 = (g - G_MK) * 256
                        P.op("act", lambda h: h.activation(k[:, c0:c0 + 256], acc[0:NS, 0:256], AF.Copy, scale=QSCALE), reads=[Racc], writes=[Rk])
                    elif g < G_MO:
                        c0 = (g - G_MV) * 256
                        P.op("dve", lambda h: h.tensor_copy(v[:, c0:c0 + 256], acc[0:NS, 0:256]), reads=[Racc], writes=[Rv])
                    elif g < G_MZ:
                        c0 = (g - G_MO) * 256
                        P.op("act", lambda h: h.activation(G[:, c0:c0 + 256], acc[0:NS, 0:256], AF.Sigmoid), reads=[Racc], writes=[RG])
                    else:
                        c0 = (g - G_MZ) * 256
                        P.op("act", lambda h: h.activation(tm[:], acc[0:NS, 0:256], AF.Silu), reads=[Racc], writes=[Rtm])
                        P.op("dve", lambda h: h.tensor_tensor(G[:, c0:c0 + 256], G[:, c0:c0 + 256], tm[:], ALU.mult), reads=[Rtm, RG], writes=[RG])
                self.s_proj(wb, Rw, 12 if g == G_GATE else 256, evac)
            sm = self.sb(ph, [NS, 96], F32, "smsm")
            Rsm = Region("smsm")
            n0 = self.sb(ph, [NS, 768], F32, "smn")
            Rn0 = Region("smn")
            P.dma("sp", n0[:], I["sn"][l], writes=[Rn0])
            P.dma("sp", sm[:, 0:6], I["sm"][l], writes=[Rsm])
            with contextlib.ExitStack() as ph2:
                mhgb = self.sb(ph2, [NS, 1536], F32, "mhgb")
                Rmh = Region("mhgb")
                P.dma("sp", mhgb[:], I["mhg"][l].partition_broadcast(NS), writes=[Rmh])
                P.op("dve", lambda h, mhgb=mhgb: h.tensor_tensor(G[:], G[:], mhgb[:], ALU.mult), reads=[RG, Rmh], writes=[RG])
                self.barrier()
            c_m0, c_sp, c_int, c_mt, c_wq, c_wi, c_qk, c_w, c_qn, c_den, c_t, c_rd, c_ssq, c_f, c_emt = [6 * i for i in range(15)]
            s_ = lambda c: sm[:, c:c + 6]
            ip, fp = gts[:, 0:6], gts[:, 6:12]
            P.op("act", lambda h: h.activation(s_(c_sp), fp, AF.Exp, scale=-1.0), reads=[Rg], writes=[Rsm])
            P.op("act", lambda h: h.activation(s_(c_sp), s_(c_sp), AF.Ln, bias=1.0), reads=[Rsm], writes=[Rsm])
            P.op("dve", lambda h: h.tensor_tensor(s_(c_int), s_(c_m0), s_(c_sp), ALU.subtract), reads=[Rsm], writes=[Rsm])
            P.op("dve", lambda h: h.tensor_tensor(s_(c_mt), s_(c_int), ip, ALU.max), reads=[Rsm, Rg], writes=[Rsm])
            P.op("dve", lambda h: h.tensor_tensor(s_(c_wq), ip, s_(c_mt), ALU.subtract), reads=[Rsm, Rg], writes=[Rsm])
            P.op("dve", lambda h: h.tensor_tensor(s_(c_wi), s_(c_int), s_(c_mt), ALU.subtract), reads=[Rsm], writes=[Rsm])
            P.op("act", lambda h: h.activation(sm[:, c_wq:c_wq + 12], sm[:, c_wq:c_wq + 12], AF.Exp), reads=[Rsm], writes=[Rsm])
            P.op("act", lambda h: h.activation(s_(c_emt), s_(c_mt), AF.Exp, scale=-1.0), reads=[Rsm], writes=[Rsm])
            P.dma("sp", O["ms"][l], s_(c_mt), reads=[Rsm], writes=[self.R["ms"]])
            big = self.sb(ph, [NS, 1536], F32, "smbig")
            Rbig = Region("smbig")
            bc6 = lambda ap, n: ap.unsqueeze(2).to_broadcast([NS, 6, n])
            q3 = q[:].rearrange("p (a b) -> p a b", a=6)
            k3 = k[:].rearrange("p (a b) -> p a b", a=6)
            n3 = n0[:].rearrange("p (a b) -> p a b", a=6)
            b3 = big[:, 0:768].rearrange("p (a b) -> p a b", a=6)
            P.op("dve", lambda h: h.tensor_tensor(b3, q3, k3, ALU.mult), reads=[Rq, Rk], writes=[Rbig])
            P.op("dve", lambda h: h.tensor_reduce(s_(c_qk), b3, AX.X, ALU.add), reads=[Rbig], writes=[Rsm])
            P.op("dve", lambda h: h.tensor_tensor(b3, q3, n3, ALU.mult), reads=[Rq, Rn0], writes=[Rbig])
            P.op("dve", lambda h: h.tensor_reduce(s_(c_qn), b3, AX.X, ALU.add), reads=[Rbig], writes=[Rsm])
            P.op("dve", lambda h: h.tensor_tensor(s_(c_w), s_(c_wq), s_(c_qk), ALU.mult), reads=[Rsm], writes=[Rsm])
            P.op("dve", lambda h: h.tensor_tensor(s_(c_den), s_(c_wi), s_(c_qn), ALU.mult), reads=[Rsm], writes=[Rsm])
            P.op("dve", lambda h: h.tensor_tensor(s_(c_den), s_(c_den), s_(c_w), ALU.add), reads=[Rsm], writes=[Rsm])
            P.op("dve", lambda h: h.scalar_tensor_tensor(s_(c_t), s_(c_den), -1.0, s_(c_den), op0=ALU.mult, op1=ALU.max), reads=[Rsm], writes=[Rsm])
            P.op("dve", lambda h: h.tensor_tensor(s_(c_t), s_(c_t), s_(c_emt), ALU.max), reads=[Rsm], writes=[Rsm])
            P.op("dve", lambda h: h.reciprocal(s_(c_rd), s_(c_t)), reads=[Rsm], writes=[Rsm])
            ksc = self.sb(ph, [NS, 768], F32, "smksc")
            Rksc = Region("smksc")
            ksc3 = ksc[:].rearrange("p (a b) -> p a b", a=6)
            P.op("dve", lambda h: h.tensor_tensor(ksc3, k3, bc6(s_(c_wq), 128), ALU.mult), reads=[Rk, Rsm], writes=[Rksc])
            P.op("dve", lambda h: h.tensor_tensor(n3, n3, bc6(s_(c_wi), 128), ALU.mult), reads=[Rn0, Rsm], writes=[Rn0])
            P.op("dve", lambda h: h.tensor_tensor(n0[:], n0[:], ksc[:], ALU.add), reads=[Rn0, Rksc], writes=[Rn0])
            P.dma("sp", O["ns"][l], n0[:], reads=[Rn0], writes=[self.R["ns"]])
            qT = self.sb(ph, [128, 6, NS], F32, "smqT")
            RqT = Region("smqT")
            pq, Rpq = self.pb[0], self.Rpb[0]
            fns = [(lambda h, hh=hh: h.transpose(pq[:, hh * NS:(hh + 1) * NS], q[0:NS, hh * 128:(hh + 1) * 128], self.identf[0:NS, 0:NS])) for hh in range(6)]
            P.group("pe", fns, reads=[Rq, self.Ridf], writes=[Rpq])
            P.op("act", lambda h: h.copy(qT[:].rearrange("p a b -> p (a b)"), pq[:, 0:6 * NS]), reads=[Rpq], writes=[RqT])
            i32 = self.sb(ph, [128, NS, NS], F32, "i32")
            Ri32 = Region("i32")
            P.dma("sp", i32[:].rearrange("p a b -> p (a b)"), I["c_i32"], writes=[Ri32])
            Wd = self.sb(ph, [NS, NS, 6], F32, "smWd")
            RWd = Region("smWd")
            P.op("dve", lambda h: h.tensor_tensor(Wd[:], self.identf[0:NS, 0:NS].unsqueeze(2).to_broadcast([NS, NS, 6]), s_(c_wi).unsqueeze(1).to_broadcast([NS, NS, 6]), ALU.mult),
                 reads=[Rsm, self.Ridf], writes=[RWd])
            pw, Rpw = self.pb[1], self.Rpb[1]
            P.op("pe", lambda h: h.matmul(pw[:, 0:NS * 6], self.onesf[0:NS, :], Wd[:].rearrange("p a b -> p (a b)"), start=True, stop=True), reads=[RWd, self.Ronesf], writes=[Rpw])
            wcb = self.sb(ph, [128, NS, 6], F32, "smwcb")
            Rwcb = Region("smwcb")
            P.op("act", lambda h: h.copy(wcb[:].rearrange("p a b -> p (a b)"), pw[:, 0:NS * 6]), reads=[Rpw], writes=[Rwcb])
            vb, Rvb = v, Rv
            Qm = self.sb(ph, [128, NS, NS], F32, "smQm")
            Km = self.sb(ph, [NS, NS, 128], BF16, "smKm")
            RQm, RKm = Region("smQm"), Region("smKm")
            CB = 4
            Cc = self.sb(ph, [128, CB, 256], F32, "smCc")
            RCc = Region("smCc")
            pR = [self.pb[4], self.pb[5], self.pb[6]]
            RpR = [self.Rpb[4], self.Rpb[5], self.Rpb[6]]
            pD = [self.pb[2], self.pb[3]]
            RpD = [self.Rpb[2], self.Rpb[3]]

            def do_head(hh):
                P.op("dve", lambda h: h.tensor_tensor(Qm[:], qT[:, hh, :].unsqueeze(2).to_broadcast([128, NS, NS]), i32[:], ALU.mult), reads=[RqT, Ri32], writes=[RQm])
                P.op("dve", lambda h: h.tensor_tensor(Km[:], self.identf[0:NS, 0:NS].unsqueeze(2).to_broadcast([NS, NS, 128]),
                                                      ksc[:, hh * 128:(hh + 1) * 128].unsqueeze(1).to_broadcast([NS, NS, 128]), ALU.mult),
                     reads=[Rksc, self.Ridf], writes=[RKm])
                oR = pR[hh // 2][0:NS, (hh % 2) * 256:(hh % 2) * 256 + 256]
                for cb in range(NS // CB):
                    P.dma("sp", Cc[:], I["sC"][l, cb * CB:(cb + 1) * CB, hh].rearrange("b d v -> d b v"), writes=[RCc])
                    for j in range(CB):
                        bp = cb * CB + j
                        P.op("pe", lambda h, j=j, bp=bp: h.matmul(oR, Qm[:, bp, :], Cc[:, j, :], start=(bp == 0), stop=(bp == NS - 1)),
                             reads=[RQm, RCc], writes=[RpR[hh // 2]])
                    for j in range(CB):
                        bp = cb * CB + j
                        pd, Rpd = pD[j % 2], RpD[j % 2]
                        P.op("pe", lambda h, j=j, bp=bp, pd=pd: h.matmul(pd[:, 0:256], Km[:, bp, :], vb[:, hh * 256:(hh + 1) * 256], start=True, stop=True),
                             reads=[RKm, Rvb], writes=[Rpd])
                        P.op("dve", lambda h, j=j, bp=bp, pd=pd: h.scalar_tensor_tensor(Cc[:, j, :], Cc[:, j, :], wcb[:, bp, hh:hh + 1], pd[:, 0:256], op0=ALU.mult, op1=ALU.add),
                             reads=[Rpd, Rwcb, RCc], writes=[RCc])
                    P.dma("sp", O["Cs"][l, cb * CB:(cb + 1) * CB, hh].rearrange("b d v -> d b v"), Cc[:], reads=[RCc], writes=[self.R["Cs"]])
            for hh in range(6):
                do_head(hh)
            num = big
            junk = self.sb(ph, [NS, 256], F32, "smjunk")
            Rjunk = Region("smjunk")
            for hh in range(6):
                sl = slice(hh * 256, (hh + 1) * 256)
                P.op("dve", lambda h, hh=hh, sl=sl: h.tensor_scalar(num[:, sl], v[:, sl], sm[:, c_w + hh:c_w + hh + 1], None, op0=ALU.mult), reads=[Rv, Rsm], writes=[Rbig])
                P.op("dve", lambda h, hh=hh, sl=sl: h.scalar_tensor_tensor(num[:, sl], pR[hh // 2][0:NS, (hh % 2) * 256:(hh % 2) * 256 + 256], sm[:, c_wi + hh:c_wi + hh + 1],
                                                                           num[:, sl], op0=ALU.mult, op1=ALU.add),
                     reads=[RpR[hh // 2], Rsm, Rbig], writes=[Rbig])
                P.op("act", lambda h, hh=hh, sl=sl: h.activation(junk[:], num[:, sl], AF.Square, accum_out=sm[:, c_ssq + hh:c_ssq + hh + 1]), reads=[Rbig, Rsm], writes=[Rjunk, Rsm])
            P.op("dve", lambda h: h.tensor_tensor(s_(c_t), s_(c_rd), s_(c_rd), ALU.mult), reads=[Rsm], writes=[Rsm])
            P.op("dve", lambda h: h.tensor_tensor(s_(c_t), s_(c_t), s_(c_ssq), ALU.mult), reads=[Rsm], writes=[Rsm])
            P.op("act", lambda h: h.activation(s_(c_t), s_(c_t), AF.Sqrt, bias=EPS, scale=1.0 / 256.0), reads=[Rsm], writes=[Rsm])
            P.op("dve", lambda h: h.reciprocal(s_(c_f), s_(c_t)), reads=[Rsm], writes=[Rsm])
            P.op("dve", lambda h: h.tensor_tensor(s_(c_f), s_(c_f), s_(c_rd), ALU.mult), reads=[Rsm], writes=[Rsm])
            num3 = num[:].rearrange("p (a b) -> p a b", a=6)
            P.op("dve", lambda h: h.tensor_tensor(num3, num3, bc6(s_(c_f), 256), ALU.mult), reads=[Rbig, Rsm], writes=[Rbig])
            yb = self.sb(ph, [NS, 1536], BF16, "smyb")
            Ryb = Region("smyb")
            P.op("dve", lambda h: h.tensor_tensor(yb[:], num[:], G[:], ALU.mult), reads=[Rbig, RG], writes=[Ryb])
            self.s_to_yT(yb, Ryb, 12, 12)
            self.barrier()

    def s_phase_C(self):
        P, l, I, S, O = self.P, self.l, self.I, self.S, self.O
        with contextlib.ExitStack() as ph:
            u = self.sb(ph, [NS, 1024], F32, "scu")
            vv = self.sb(ph, [NS, 1024], F32, "scv")
            tm = self.sb(ph, [NS, 256], F32, "sct")
            Ru, Rvv, Rtm = Region("scu"), Region("scv"), Region("sct")
            for g, wb, Rw in self.wstream("in", list(range(G_CU, G_GATE))):
                def evac(acc, Racc, g=g):
                    if g < G_CV:
                        c0 = (g - G_CU) * 256
                        P.op("act", lambda h: h.activation(u[:, c0:c0 + 256], acc[0:NS, 0:256], AF.Gelu), reads=[Racc], writes=[Ru])
                    elif g < G_CZ:
                        c0 = (g - G_CV) * 256
                        P.op("act", lambda h: h.activation(vv[:, c0:c0 + 256], acc[0:NS, 0:256], AF.Gelu), reads=[Racc], writes=[Rvv])
                    else:
                        c0 = (g - G_CZ) * 256
                        P.op("act", lambda h: h.activation(tm[:], acc[0:NS, 0:256], AF.Silu), reads=[Racc], writes=[Rtm])
                        P.op("dve", lambda h: h.tensor_tensor(u[:, c0:c0 + 256], u[:, c0:c0 + 256], tm[:], ALU.mult), reads=[Rtm, Ru], writes=[Ru])
                self.s_proj(wb, Rw, 256, evac)
            junk = self.sb(ph, [NS, 1024], BF16, "scj")
            sm = self.sb(ph, [NS, 16], F32, "scsm")
            Rj, Rsm = Region("scj"), Region("scsm")
            self.layernorm(vv, Rvv, NS, junk, Rj, sm, Rsm)
            P.dma("sp", O["cvs"][l], vv[:], reads=[Rvv], writes=[self.R["cvs"]])
            wb8 = self.sb(ph, [NS, 16], F32, "scw8")
            Rw8 = Region("scw8")
            P.dma("sp", wb8[:, 0:8], I["ws00"][l * 8:(l + 1) * 8].partition_broadcast(NS), writes=[Rw8])
            P.dma("sp", wb8[:, 8:16], I["bs0"][l * 8:(l + 1) * 8].partition_broadcast(NS), writes=[Rw8])
            v3 = vv[:].rearrange("p (a b) -> p a b", a=8)
            P.op("dve", lambda h: h.tensor_tensor(v3, v3, wb8[:, 0:8].unsqueeze(2).to_broadcast([NS, 8, 128]), ALU.mult), reads=[Rvv, Rw8], writes=[Rvv])
            P.op("dve", lambda h: h.tensor_tensor(v3, v3, wb8[:, 8:16].unsqueeze(2).to_broadcast([NS, 8, 128]), ALU.add), reads=[Rvv, Rw8], writes=[Rvv])
            yb = self.sb(ph, [NS, 1024], BF16, "scyb")
            Ryb = Region("scyb")
            P.op("dve", lambda h: h.tensor_tensor(yb[:], vv[:], u[:], ALU.mult), reads=[Rvv, Ru], writes=[Ryb])
            self.s_to_yT(yb, Ryb, 24, 8)
            self.barrier()


def _consts():
    c = {}
    c["c_ident"] = np.eye(128, dtype=np.float32)
    q = np.arange(128)[:, None]
    s = np.arange(256)[None, :]
    dist = q + 128 - s
    valid = (dist >= 0) & (dist <= 128)
    c["c_dm"] = np.where(valid, dist, 1.0e9).astype(np.float32)
    t = np.arange(128)[:, None]
    s2 = np.arange(128)[None, :]
    ok = ((t // 64) == (s2 // 64)) & (s2 <= t)
    m1 = np.where(ok, 0.0, -30000.0).astype(np.float32)
    c["c_mask6"] = np.tile(m1, (1, 6))
    c["c_tri2"] = ok.T.astype(np.float32).copy()
    onesc = np.zeros((128, 2, 128), np.float32)
    onesc[0:64, 0, :] = 1.0
    onesc[64:128, 1, :] = 1.0
    c["c_onesc"] = onesc.reshape(128, 256)
    cm = np.zeros((128, 2), np.float32)
    cm[0:64, 0] = 1.0
    cm[64:128, 1] = 1.0
    c["c_cmask"] = cm
    c["c_tril"] = (np.arange(128)[None, :] >= np.arange(128)[:, None]).astype(np.float32)
    c["c_dist"] = np.concatenate([np.arange(128, 0, -1), [0]]).astype(np.float32)
    c["c_slp"] = np.tile(np.array(SLOPES, np.float32).reshape(4, 3), (32, 1))
    c["c_i32"] = np.tile(np.eye(NS, dtype=np.float32).reshape(1, NS * NS), (128, 1))
    return c


_NC_CACHE = {}


def kernel(x_prompt, x_sample, cache_win_k, cache_win_v, state_mlstm_C, state_mlstm_n, state_mlstm_m,
           norm_gain, w_in, b_if, attn_sinks, m_head_gain, c_ln_gain, c_ln_bias, c_w_s, c_b_s, w_out, final_gain, _ncores=8, _debug=False):
    f = lambda a: np.ascontiguousarray(np.asarray(a, dtype=np.float32))
    w_in = f(w_in)
    w_out = f(w_out)
    wp = np.zeros((2, D, NG_IN * 256), np.float32)
    wp[:, :, 0:7168] = w_in[:, :, 0:7168]
    wp[:, :, 7168:13312] = w_in[:, :, 7180:13324]
    wp[:, :, 13312:13324] = w_in[:, :, 7168:7180]
    wp = np.ascontiguousarray(wp.reshape(2, 32, 128, NG_IN, 256).transpose(0, 3, 2, 1, 4)).reshape(2, NG_IN, 128, 32 * 256)
    wo = np.ascontiguousarray(w_out.reshape(2, 32, 128, NG_OUT, 256).transpose(0, 3, 2, 1, 4)).reshape(2, NG_OUT, 128, 32 * 256)
    common = dict(_consts())
    common["w_in"] = wp
    common["w_out"] = wo
    common["xs"] = f(x_sample).reshape(NS, D)
    common["gainT"] = np.ascontiguousarray(f(norm_gain).reshape(2, 32, 128).transpose(0, 2, 1))
    common["fgain"] = f(final_gain)
    common["bif"] = f(b_if).reshape(24)
    common["sinks"] = f(attn_sinks).reshape(24)
    common["mhgT"] = np.ascontiguousarray(f(m_head_gain).reshape(2, 12, 128).transpose(0, 2, 1))
    common["lng"] = f(c_ln_gain)
    common["sk4"] = np.ascontiguousarray(np.tile(f(attn_sinks).reshape(2, 1, 4, 3), (1, 32, 1, 1)).reshape(2, 128, 3))
    common["mhg"] = f(m_head_gain)
    common["lnb"] = f(c_ln_bias)
    common["wsT"] = np.ascontiguousarray(f(c_w_s).transpose(0, 3, 1, 2)).reshape(2, 128, 1024)
    common["bs"] = f(c_b_s).reshape(2, 1024)
    common["ws00"] = np.ascontiguousarray(f(c_w_s)[:, :, 0, 0]).reshape(16)
    common["bs0"] = np.ascontiguousarray(f(c_b_s)[:, :, 0]).reshape(16)
    common["ck"] = np.ascontiguousarray(f(cache_win_k).transpose(0, 1, 3, 2, 4)).reshape(2, 128, 128 * 128)
    common["cv"] = np.ascontiguousarray(f(cache_win_v).transpose(0, 1, 3, 2, 4)).reshape(2, 128, 128 * 128)
    common["sC"] = f(state_mlstm_C)
    common["sn"] = f(state_mlstm_n).reshape(2, NS, 768)
    common["sm"] = f(state_mlstm_m)
    xp = f(x_prompt)
    in_maps = []
    for c in range(_ncores):
        m = dict(common)
        m["xp"] = xp[c % 4]
        in_maps.append(m)
    if _debug:
        nc = KB(do_sample=False, debug=True).build()
        res = run_bass_kernel_spmd(nc, in_maps, core_ids=list(range(_ncores)))
        return res.results[0]
    if "nc" not in _NC_CACHE:
        _NC_CACHE["nc"] = KB().build()
    nc = _NC_CACHE["nc"]
    res = run_bass_kernel_spmd(nc, in_maps, core_ids=list(range(_ncores)))
    r = list(res.results)
    while len(r) < 4:
        r.append(r[0])
    y_prompt = np.stack([r[b]["yp"] for b in range(4)]).reshape(4, SEQ, D)
    r0 = r[0]
    y_sample = r0["ys"].reshape(NS, 1, D)
    kp = np.stack([r[b]["kp"] for b in range(4)], axis=1).reshape(2, 4, 128, 4, 128)
    vp = np.stack([r[b]["vp"] for b in range(4)], axis=1).reshape(2, 4, 128, 4, 128)
    ks = r0["ks"].reshape(2, NS, 1, 4, 128)
    vs = r0["vs"].reshape(2, NS, 1, 4, 128)
    Cp = np.stack([r[b]["Cp"].reshape(2, 128, 6, 256).transpose(0, 2, 1, 3) for b in range(4)], axis=1)
    npp = np.stack([r[b]["np"].transpose(0, 2, 1) for b in range(4)], axis=1)
    mp = np.stack([r[b]["mp"].reshape(2, 6) for b in range(4)], axis=1)
    Cs = r0["Cs"]
    ns = r0["ns"].reshape(2, NS, 6, 128)
    ms = r0["ms"]
    cvp = np.stack([r[b]["cvp"] for b in range(4)], axis=1)
    cvs = r0["cvs"].reshape(2, NS, 1, 1024)
    outs = (y_prompt, y_sample, kp, vp, ks, vs, Cp, npp, mp, Cs, ns, ms, cvp, cvs)
    return tuple(np.ascontiguousarray(o, dtype=np.float32) for o in outs)
```

```python
import contextlib
import numpy as np
import concourse.bass as bass
import concourse.mybir as mybir
from concourse.bass_utils import run_bass_kernel_spmd

F32 = mybir.dt.float32
BF16 = mybir.dt.bfloat16
ALU = mybir.AluOpType
AF = mybir.ActivationFunctionType
AX = mybir.AxisListType

D = 4096
SEQ = 2048
NS = 32
DIN = 13324
NG_IN = 29
NG_OUT = 8
TT = 256
NBLK = TT // 128
NTILE = SEQ // TT
EPS = 1e-6
SLOPES = [2.0 ** (-8.0 * (h + 1) / 12.0) for h in range(12)]
QSCALE = 128.0 ** -0.5
G_AQ, G_AK, G_AV, G_AZ = 0, 3, 4, 5
G_MQK, G_MV, G_MO, G_MZ = 8, 11, 14, 17
G_CU, G_CV, G_CZ, G_GATE = 20, 22, 26, 28
HA, KV, HM, GC = 6, 2, 3, 4
KY = 16
PAIRS = [[0, 1], [2, 3], [4, 5], [6, 7]]
SBUF_LIMIT = 182 * 1024


class Region:
    __slots__ = ("name", "last_w", "readers", "excl")

    def __init__(self, name, excl=False):
        self.name = name
        self.last_w = None
        self.readers = []
        self.excl = excl


class EngineCtx:
    def __init__(self, name, sem):
        self.name = name
        self.sem = sem
        self.count = 0
        self.known = {}
        self.instrs = []


class Prog:
    def __init__(self, nc, stack, n_dma_sems=24):
        self.nc = nc
        self.eng = {}
        self.sems = {}
        for name in ("pe", "act", "dve", "pool", "sp"):
            sem = stack.enter_context(nc.semaphore("s_" + name))
            self.eng[name] = EngineCtx(name, sem)
            self.sems["e_" + name] = sem
        self.dma_pool = {}
        for q in ("sp", "pool"):
            lst = []
            for i in range(n_dma_sems):
                key = "d_%s_%d" % (q, i)
                self.sems[key] = stack.enter_context(nc.semaphore(key))
                lst.append([key, 0])
            self.dma_pool[q] = [lst, 0]
        self.n_instr = 0

    def _need(self, e, tok, waits):
        if tok is None:
            return
        key, val, src = tok
        if src == "pe" and e.name == "pe":
            return
        if e.known.get(key, 0) >= val:
            return
        if waits.get(key, 0) < val:
            waits[key] = val

    def _deps(self, e, reads, writes):
        waits = {}
        for r in reads:
            if r.excl:
                writes = list(writes) + [r]
                continue
            self._need(e, r.last_w, waits)
        for r in writes:
            self._need(e, r.last_w, waits)
            for t in r.readers:
                self._need(e, t, waits)
        return waits

    def _commit(self, tok, reads, writes):
        for r in reads:
            if r.excl:
                r.last_w = tok
                r.readers = []
                continue
            r.readers.append(tok)
            if len(r.readers) > 48:
                best = {}
                for t in r.readers:
                    if best.get(t[0], (None, -1))[1] < t[1]:
                        best[t[0]] = t
                r.readers = list(best.values())
        for r in writes:
            r.last_w = tok
            r.readers = []

    def _emit_waits(self, e, waits):
        for key, val in waits.items():
            e.instrs.append(("wait", self.sems[key], val))
            e.known[key] = val

    def op(self, engname, fn, reads=(), writes=()):
        return self.group(engname, [fn], reads, writes)

    def group(self, engname, fns, reads=(), writes=()):
        e = self.eng[engname]
        self._emit_waits(e, self._deps(e, reads, writes))
        e.count += 1
        tok = ("e_" + engname, e.count, engname)
        for fn in fns[:-1]:
            e.instrs.append(("op", fn, None, 0))
        e.instrs.append(("op", fns[-1], e.sem, 1))
        self._commit(tok, reads, writes)
        self.n_instr += len(fns)
        return tok

    def dma(self, qname, out, in_, reads=(), writes=()):
        e = self.eng[qname]
        waits = self._deps(e, reads, writes)
        pool, idx = self.dma_pool[qname]
        ent = pool[idx % len(pool)]
        self.dma_pool[qname][1] = idx + 1
        key, cum = ent
        if cum > 0 and e.known.get(key, 0) < cum and waits.get(key, 0) < cum:
            waits[key] = cum
        self._emit_waits(e, waits)
        ent[1] = cum + 16
        tok = (key, cum + 16, "dma")

        def fn(h, out=out, in_=in_):
            return h.dma_start(out=out, in_=in_)
        e.instrs.append(("op", fn, self.sems[key], 16))
        self._commit(tok, reads, writes)
        self.n_instr += 1
        return tok

    def coll(self, stack, ins_ap, outs_ap, reads=(), writes=()):
        e = self.eng["pool"]
        self._emit_waits(e, self._deps(e, reads, writes))
        key = "cc_%d" % len([k for k in self.sems if k.startswith("cc_")])
        sem = stack.enter_context(self.nc.semaphore(key))
        self.sems[key] = sem
        tok = (key, 1, "cc")

        def fn(h, ins_ap=ins_ap, outs_ap=outs_ap):
            return h.collective_compute("AllReduce", ALU.add, replica_groups=PAIRS, ins=[ins_ap], outs=[outs_ap])
        e.instrs.append(("op", fn, sem, 1))
        e.instrs.append(("wait", sem, 1))
        e.known[key] = 1
        self._commit(tok, reads, writes)
        return tok

    def wait_tok(self, engname, tok):
        e = self.eng[engname]
        waits = {}
        self._need(e, tok, waits)
        self._emit_waits(e, waits)

    def barrier(self, bar_out, bar_in, bar_region):
        sp = self.eng["sp"]
        waits = {}
        snap = {}
        for name in ("pe", "act", "dve"):
            c = self.eng[name].count
            snap["e_" + name] = c
            if c > 0:
                self._need(sp, ("e_" + name, c, name), waits)
        for key, cum in self.dma_pool["sp"][0]:
            snap[key] = cum
            if cum > 0:
                self._need(sp, (key, cum, "dma"), waits)
        self._emit_waits(sp, waits)
        tok = self.dma("sp", bar_out, bar_in, writes=[bar_region])
        for name in ("pe", "act", "dve"):
            self.wait_tok(name, tok)
            e = self.eng[name]
            for k, v in snap.items():
                if e.known.get(k, 0) < v:
                    e.known[k] = v

    def final_wait(self, engname, regions):
        e = self.eng[engname]
        waits = {}
        for r in regions:
            self._need(e, r.last_w, waits)
            for t in r.readers:
                self._need(e, t, waits)
        self._emit_waits(e, waits)

    def replay(self):
        nc = self.nc
        with nc.Block() as block:
            def mk(e):
                def body(h):
                    for ins in e.instrs:
                        if ins[0] == "wait":
                            h.wait_ge(ins[1], ins[2])
                        elif ins[2] is None:
                            ins[1](h)
                        else:
                            ins[1](h).then_inc(ins[2], ins[3])
                return body
            block.tensor(mk(self.eng["pe"]))
            block.scalar(mk(self.eng["act"]))
            block.vector(mk(self.eng["dve"]))
            block.gpsimd(mk(self.eng["pool"]))
            block.sync(mk(self.eng["sp"]))


class KB:
    def __init__(self, do_sample=True):
        self.do_sample = do_sample
        self.nc = bass.Bass("TRN2", target_bir_lowering=False)
        self.uid = 0
        self.sb_bytes = 0
        self.sb_peak = 0

    def sb(self, stack, shape, dt, name="t"):
        self.uid += 1
        n = 1
        for s in shape[1:]:
            n *= s
        nbytes = n * (4 if dt == F32 else 2)
        self.sb_bytes += nbytes
        self.sb_peak = max(self.sb_peak, self.sb_bytes)
        assert self.sb_bytes <= SBUF_LIMIT, ("SBUF overflow", name, self.sb_bytes)
        t = stack.enter_context(self.nc.sbuf_tensor("%s_%d" % (name, self.uid), list(shape), dt))

        def rel():
            self.sb_bytes -= nbytes
        stack.callback(rel)
        return t

    def dram_in(self, name, shape, dt=F32):
        return self.nc.dram_tensor(name, list(shape), dt, kind="ExternalInput").ap()

    def dram_out(self, name, shape, dt=F32):
        return self.nc.dram_tensor(name, list(shape), dt, kind="ExternalOutput").ap()

    def dram_scr(self, name, shape, dt=F32):
        return self.nc.dram_tensor(name, list(shape), dt, kind="Internal").ap()

    def build(self):
        nc = self.nc
        I = {}
        I["xp"] = self.dram_in("xp", [SEQ, D])
        I["xs"] = self.dram_in("xs", [NS, D])
        I["w_in"] = self.dram_in("w_in", [2, NG_IN, 128, 32 * 256])
        I["w_out"] = self.dram_in("w_out", [2, NG_OUT, 128, KY * 512])
        I["gainT"] = self.dram_in("gainT", [2, 128, 32])
        I["fgain"] = self.dram_in("fgain", [D])
        I["bif"] = self.dram_in("bif", [12])
        I["sinks"] = self.dram_in("sinks", [12])
        I["nslp"] = self.dram_in("nslp", [HA])
        I["mhgT"] = self.dram_in("mhgT", [2, 128, 6])
        I["mhg"] = self.dram_in("mhg", [2, 768])
        I["lng"] = self.dram_in("lng", [2, 1024])
        I["lnb"] = self.dram_in("lnb", [2, 1024])
        I["wsT"] = self.dram_in("wsT", [2, 128, GC * 128])
        I["bs"] = self.dram_in("bs", [2, GC * 128])
        I["ws00"] = self.dram_in("ws00", [2 * GC])
        I["bs0"] = self.dram_in("bs0", [2 * GC])
        I["ck"] = self.dram_in("ck", [2, 64, 128 * 128])
        I["cv"] = self.dram_in("cv", [2, 64, 128 * 128])
        I["sk4"] = self.dram_in("sk4", [2, 64, 3])
        I["c_slp"] = self.dram_in("c_slp", [64, 3])
        I["sC"] = self.dram_in("sC", [2, NS, HM, 128, 256])
        I["sn"] = self.dram_in("sn", [2, NS, HM * 128])
        I["sm"] = self.dram_in("sm", [2, NS, HM])
        I["c_ident"] = self.dram_in("c_ident", [128, 128])
        I["c_dm"] = self.dram_in("c_dm", [128, 256])
        I["c_dm0"] = self.dram_in("c_dm0", [128, 256])
        I["c_mask6"] = self.dram_in("c_mask6", [128, 384])
        I["c_tri2"] = self.dram_in("c_tri2", [128, 128])
        I["c_onesc"] = self.dram_in("c_onesc", [128, 256])
        I["c_cmask"] = self.dram_in("c_cmask", [128, 2])
        I["c_tril"] = self.dram_in("c_tril", [128, 128])
        I["c_dist"] = self.dram_in("c_dist", [129])
        I["c_i32"] = self.dram_in("c_i32", [128, NS * NS])
        O = {}
        O["yp"] = self.dram_out("yp", [SEQ, D])
        O["ys"] = self.dram_out("ys", [NS, D])
        O["kp"] = self.dram_out("kp", [2, 128, 256])
        O["vp"] = self.dram_out("vp", [2, 128, 256])
        O["ks"] = self.dram_out("ks", [2, NS, 256])
        O["vs"] = self.dram_out("vs", [2, NS, 256])
        O["Cp"] = self.dram_out("Cp", [2, 128, HM * 256])
        O["np"] = self.dram_out("np", [2, 128, HM])
        O["mp"] = self.dram_out("mp", [2, 1, HM])
        O["Cs"] = self.dram_out("Cs", [2, NS, HM, 128, 256])
        O["ns"] = self.dram_out("ns", [2, NS, HM * 128])
        O["ms"] = self.dram_out("ms", [2, NS, HM])
        O["cvp"] = self.dram_out("cvp", [2, 128, 1024])
        O["cvs"] = self.dram_out("cvs", [2, NS, 1024])
        self.I, self.O = I, O
        S = {}
        S["wb_in"] = self.dram_scr("wb_in", [2, NG_IN, 128, 32 * 256], BF16)
        S["wb_out"] = self.dram_scr("wb_out", [2, NG_OUT, 128, KY * 512], BF16)
        S["h1"] = self.dram_scr("h1", [SEQ, D])
        S["hs1"] = self.dram_scr("hs1", [NS, D])
        for l in range(2):
            S["po%d" % l] = self.dram_scr("po%d" % l, [SEQ, D])
            S["red%d" % l] = self.dram_scr("red%d" % l, [SEQ, D])
            S["pos%d" % l] = self.dram_scr("pos%d" % l, [NS, D])
            S["reds%d" % l] = self.dram_scr("reds%d" % l, [NS, D])
        S["bar"] = self.dram_scr("bar", [2, 16])
        S["bq"] = self.dram_scr("bq", [NS, 768])
        S["bk"] = self.dram_scr("bk", [NS, 256])
        S["bv"] = self.dram_scr("bv", [NS, 256])
        S["bo"] = self.dram_scr("bo", [NS, 768])
        self.S = S

        with contextlib.ExitStack() as st:
            self.st = st
            P = self.P = Prog(nc, st)
            self.R = {}
            for k in list(O.keys()) + ["h1", "hs1", "bar", "xin", "bnc"]:
                self.R[k] = Region(k)
            self.Rw = {}
            self.pb = [st.enter_context(nc.psum_tensor("pb%d" % i, [128, 512], F32)) for i in range(7)]
            self.Rpb = [Region("pb%d" % i, excl=True) for i in range(7)]
            self.tb = st.enter_context(nc.psum_tensor("tb0", [128, 1024], BF16))
            self.Rtb = Region("tb0", excl=True)
            self.acc_i = 0
            self.emit_all()
            P.replay()
        return nc

    def const(self, name, shape, dt, src_ap):
        t = self.sb(self.st, shape, dt, name)
        r = Region(name)
        self.P.dma("sp", t[:], src_ap, writes=[r])
        return t, r

    def barrier(self):
        self.P.barrier(self.S["bar"][0:1, :], self.S["bar"][1:2, :], self.R["bar"])

    def next_acc(self):
        i = self.acc_i % 4
        self.acc_i += 1
        return self.pb[i], self.Rpb[i]

    def convert_weights(self, l):
        P, S, I = self.P, self.S, self.I
        for g in range(NG_IN):
            r = Region("wbin%d_%d" % (l, g))
            self.Rw[("in", l, g)] = r
            P.dma("pool", S["wb_in"][l, g], I["w_in"][l, g], writes=[r])
        for g in range(NG_OUT):
            r = Region("wbout%d_%d" % (l, g))
            self.Rw[("out", l, g)] = r
            P.dma("pool", S["wb_out"][l, g], I["w_out"][l, g], writes=[r])

    def emit_all(self):
        P, nc, I, O, S, st = self.P, self.nc, self.I, self.O, self.S, self.st
        sbp = lambda shape, dt, name: self.sb(st, shape, dt, name)
        self.convert_weights(0)
        self.convert_weights(1)
        identf, Ridf = self.const("identf", [128, 128], F32, I["c_ident"])
        self.identf, self.Ridf = identf, Ridf
        identb = sbp([128, 128], BF16, "identb")
        Ridb = Region("identb")
        P.op("dve", lambda h: h.tensor_copy(identb[:], identf[:]), reads=[Ridf], writes=[Ridb])
        self.identb, self.Ridb = identb, Ridb
        self.dm, self.Rdm = self.const("dm", [128, 256], F32, I["c_dm"])
        self.dm0, self.Rdm0 = self.const("dm0", [128, 256], F32, I["c_dm0"])
        self.mask6, self.Rmask6 = self.const("mask6", [128, 384], F32, I["c_mask6"])
        self.tri2, self.Rtri2 = self.const("tri2", [128, 128], F32, I["c_tri2"])
        self.onesc, self.Ronesc = self.const("onesc", [128, 256], F32, I["c_onesc"])
        self.cmask, self.Rcmask = self.const("cmask", [128, 2], F32, I["c_cmask"])
        self.bifb, self.Rbifb = self.const("bifb", [128, 12], F32, I["bif"].partition_broadcast(128))
        self.sinkb, self.Rsinkb = self.const("sinkb", [128, 12], F32, I["sinks"].partition_broadcast(128))
        self.nslp, self.Rnslp = self.const("nslp", [128, HA], F32, I["nslp"].partition_broadcast(128))
        onesf = sbp([128, 128], F32, "onesf")
        self.Ronesf = Region("onesf")
        P.op("dve", lambda h: h.memset(onesf[:], 1.0), writes=[self.Ronesf])
        self.onesf = onesf
        onesb = sbp([128, 2], BF16, "onesb")
        self.Ronesb = Region("onesb")
        P.op("dve", lambda h: h.memset(onesb[:], 1.0), writes=[self.Ronesb])
        self.onesb = onesb
        tril, Rtril = self.const("tril", [128, 128], F32, I["c_tril"])
        self.hnT = sbp([128, 32, TT], BF16, "hnT")
        self.RhnT = Region("hnT")
        self.yT = sbp([128, KY, TT], BF16, "yT")
        self.RyT = [Region("yT_A"), Region("yT_M"), Region("yT_C")]
        self.wbuf = [sbp([128, 32 * 256], BF16, "wbuf") for _ in range(3)]
        self.Rwbuf = [Region("wbuf%d" % i) for i in range(3)]
        self.w_i = 0
        self.Cst = sbp([128, HM, 256], F32, "Cst")
        self.Cb = [sbp([128, HM, 256], BF16, "Cb") for _ in range(2)]
        self.nst = sbp([128, 8], F32, "nst")
        self.nb = [sbp([128, 8], BF16, "nb") for _ in range(2)]
        self.mst = sbp([128, 8], F32, "mst")
        self.RC = Region("Cst")
        self.RCb = [Region("Cb0"), Region("Cb1")]
        self.Rn = Region("nst")
        self.Rnb = [Region("nb0"), Region("nb1")]
        self.Rm = Region("mst")
        self.kcar = sbp([128, KV, 128], BF16, "kcar")
        self.vcar = sbp([128, KV * 128], BF16, "vcar")
        self.Rkcar, self.Rvcar = Region("kcar"), Region("vcar")
        self.gainT = sbp([128, 32], F32, "gainT")
        self.RgainT = Region("gainT")
        self.mhgT = sbp([128, 6], F32, "mhgT")
        self.RmhgT = Region("mhgT")
        self.wmT = sbp([128, GC, 128], BF16, "wmT")
        self.RwmT = Region("wmT")
        self.bsb = sbp([128, GC * 128], F32, "bsb")
        self.Rbsb = Region("bsb")
        self.lngb = sbp([128, 1024], F32, "lngb")
        self.lnbb = sbp([128, 1024], F32, "lnbb")
        self.Rln = Region("ln")
        self.Rpo = {}
        self.Rred = {}

        for l in range(2):
            self.l = l
            P.dma("sp", self.gainT[:], I["gainT"][l], writes=[self.RgainT])
            P.dma("sp", self.mhgT[:], I["mhgT"][l], writes=[self.RmhgT])
            P.dma("sp", self.bsb[:], I["bs"][l].partition_broadcast(128), writes=[self.Rbsb])
            P.dma("sp", self.lngb[:], I["lng"][l].partition_broadcast(128), writes=[self.Rln])
            P.dma("sp", self.lnbb[:], I["lnb"][l].partition_broadcast(128), writes=[self.Rln])
            with contextlib.ExitStack() as ph:
                wtmp = self.sb(ph, [128, GC, 128], F32, "wtmp")
                Rwtmp = Region("wtmp")
                P.dma("sp", wtmp[:].rearrange("p a b -> p (a b)"), I["wsT"][l], writes=[Rwtmp])
                P.op("dve", lambda h, wtmp=wtmp: h.tensor_tensor(self.wmT[:], wtmp[:], tril[:].unsqueeze(1).to_broadcast([128, GC, 128]), ALU.mult),
                     reads=[Rwtmp, Rtril], writes=[self.RwmT])
                self.barrier()
            P.op("dve", lambda h: h.memset(self.Cst[:], 0.0), writes=[self.RC])
            P.op("dve", lambda h: h.memset(self.Cb[0][:], 0.0), writes=[self.RCb[0]])
            P.op("dve", lambda h: h.memset(self.nst[:], 0.0), writes=[self.Rn])
            P.op("dve", lambda h: h.memset(self.nb[0][:], 0.0), writes=[self.Rnb[0]])
            P.op("dve", lambda h: h.memset(self.mst[:], 0.0), writes=[self.Rm])
            for t in range(NTILE):
                self.t = t
                rows = slice(t * TT, (t + 1) * TT)
                if l == 0:
                    self.phase_norm(I["xp"][rows, :], self.R["xin"], NBLK, 128)
                else:
                    self.phase_norm(I["xp"][rows, :], self.R["xin"], NBLK, 128,
                                    add=(S["red0"][rows, :], self.Rred[(0, t)]), store=(S["h1"][rows, :], self.R["h1"]))
                self.phase_A()
                self.phase_M()
                self.phase_C()
                rp = Region("po%d_%d" % (l, t))
                rr = Region("red%d_%d" % (l, t))
                self.Rpo[(l, t)], self.Rred[(l, t)] = rp, rr
                self.phase_O(S["po%d" % l][rows, :], rp, 128, NBLK)
                P.coll(st, S["po%d" % l][rows, :], S["red%d" % l][rows, :], reads=[rp], writes=[rr])
            P.dma("sp", O["Cp"][l], self.Cst[:].rearrange("p a b -> p (a b)"), reads=[self.RC], writes=[self.R["Cp"]])
            P.dma("sp", O["np"][l], self.nst[:, 0:HM], reads=[self.Rn], writes=[self.R["np"]])
            P.dma("sp", O["mp"][l], self.mst[0:1, 0:HM], reads=[self.Rm], writes=[self.R["mp"]])
            if self.do_sample:
                self.sample_layer()
        for t in range(NTILE):
            rows = slice(t * TT, (t + 1) * TT)
            self.phase_final(S["h1"][rows, :], self.R["h1"], S["red1"][rows, :], self.Rred[(1, t)], O["yp"][rows, :], self.R["yp"], 128, NBLK)
        if self.do_sample:
            self.phase_final(S["hs1"], self.R["hs1"], S["reds1"], self.Rred[(1, "s")], O["ys"], self.R["ys"], NS, 1)
        outs = [self.R[k] for k in O.keys()]
        P.final_wait("sp", outs)

    def wnext(self, kind, g):
        i = self.w_i % 3
        self.w_i += 1
        src = self.S["wb_in"] if kind == "in" else self.S["wb_out"]
        self.P.dma("sp", self.wbuf[i][:], src[self.l, g], reads=[self.Rw[(kind, self.l, g)]], writes=[self.Rwbuf[i]])
        if kind == "in":
            return self.wbuf[i][:].rearrange("p (k c) -> p k c", k=32), self.Rwbuf[i]
        return self.wbuf[i][:].rearrange("p (k c) -> p k c", k=KY), self.Rwbuf[i]

    def wstream(self, kind, groups):
        pend = []
        for idx, g in enumerate(groups):
            if idx == 0:
                pend.append(self.wnext(kind, g))
            if idx + 1 < len(groups):
                pend.append(self.wnext(kind, groups[idx + 1]))
            buf, reg = pend.pop(0)
            yield g, buf, reg

    def mm_feat(self, wbuf, Rw, j, ntok):
        acc, Racc = self.next_acc()
        hnT = self.hnT
        fns = [(lambda h, kc=kc: h.matmul(acc[:, 0:ntok], wbuf[:, kc, j * 128:(j + 1) * 128], hnT[:, kc, 0:ntok],
                                          start=(kc == 0), stop=(kc == 31))) for kc in range(32)]
        self.P.group("pe", fns, reads=[Rw, self.RhnT], writes=[Racc])
        return acc, Racc

    def mm_tok(self, wbuf, Rw, t0, nt, ncols):
        acc, Racc = self.next_acc()
        hnT = self.hnT
        fns = [(lambda h, kc=kc: h.matmul(acc[0:nt, 0:ncols], hnT[:, kc, t0:t0 + nt], wbuf[:, kc, 0:ncols],
                                          start=(kc == 0), stop=(kc == 31))) for kc in range(32)]
        self.P.group("pe", fns, reads=[Rw, self.RhnT], writes=[Racc])
        return acc, Racc

    def phase_norm(self, src, Rsrc, nblk, np_, add=None, store=None):
        P = self.P
        with contextlib.ExitStack() as ph:
            hb = [self.sb(ph, [128, D], F32, "hb") for _ in range(2)]
            hb2 = self.sb(ph, [128, D], F32, "hb2") if add is not None else None
            hn = self.sb(ph, [128, D], BF16, "hn")
            junk = self.sb(ph, [128, D], BF16, "junk")
            stt = [self.sb(ph, [128, 4], F32, "nst") for _ in range(2)]
            Rhb = [Region("hb0"), Region("hb1")]
            Rhb2 = Region("hb2")
            Rhn = Region("hn")
            Rst = [Region("st0"), Region("st1")]
            Rj = Region("junk")
            for b in range(nblk):
                i = b % 2
                h_, n_, s_ = hb[i], hn, stt[i]
                rs = slice(b * np_, (b + 1) * np_)
                P.dma("sp", h_[0:np_, :], src[rs, :], reads=[Rsrc], writes=[Rhb[i]])
                if add is not None:
                    P.dma("sp", hb2[0:np_, :], add[0][rs, :], reads=[add[1]], writes=[Rhb2])
                    P.op("dve", lambda h, h_=h_: h.tensor_tensor(h_[0:np_, :], h_[0:np_, :], hb2[0:np_, :], ALU.add), reads=[Rhb[i], Rhb2], writes=[Rhb[i]])
                    if store is not None:
                        P.dma("sp", store[0][rs, :], h_[0:np_, :], reads=[Rhb[i]], writes=[store[1]])
                P.op("act", lambda h, h_=h_, s_=s_: h.activation(junk[0:np_, :], h_[0:np_, :], AF.Square, accum_out=s_[0:np_, 0:1]),
                     reads=[Rhb[i]], writes=[Rj, Rst[i]])
                P.op("act", lambda h, s_=s_: h.activation(s_[0:np_, 1:2], s_[0:np_, 0:1], AF.Sqrt, bias=EPS, scale=1.0 / D),
                     reads=[Rst[i]], writes=[Rst[i]])
                P.op("dve", lambda h, s_=s_: h.reciprocal(s_[0:np_, 2:3], s_[0:np_, 1:2]), reads=[Rst[i]], writes=[Rst[i]])
                P.op("dve", lambda h, h_=h_, n_=n_, s_=s_: h.tensor_scalar(n_[0:np_, :], h_[0:np_, :], s_[0:np_, 2:3], None, op0=ALU.mult),
                     reads=[Rhb[i], Rst[i]], writes=[Rhn])
                for q in range(4):
                    fns = [(lambda h, kc=kc, n_=n_: h.transpose(self.tb[:, (kc % 8) * 128:(kc % 8) * 128 + np_],
                                                                n_[0:np_, kc * 128:(kc + 1) * 128], self.identb[0:np_, 0:np_]))
                           for kc in range(q * 8, q * 8 + 8)]
                    P.group("pe", fns, reads=[Rhn, self.Ridb], writes=[self.Rtb])
                    P.op("dve", lambda h, q=q, b=b: h.tensor_tensor(
                        self.hnT[:, q * 8:(q + 1) * 8, b * np_:(b + 1) * np_],
                        self.tb[:, :].rearrange("p (a c) -> p a c", a=8)[:, :, 0:np_],
                        self.gainT[:, q * 8:(q + 1) * 8].unsqueeze(2).to_broadcast([128, 8, np_]), ALU.mult),
                        reads=[self.Rtb, self.RgainT], writes=[self.RhnT])
            self.barrier()

    def phase_A(self):
        P, l, t = self.P, self.l, self.t
        first_tile = (t == 0)
        last_tile = (t == NTILE - 1)
        KW = KV * 128
        with contextlib.ExitStack() as ph:
            qT = self.sb(ph, [128, HA, TT], BF16, "qT")
            kT = self.sb(ph, [128, KV, TT + 128], BF16, "kT")
            vt = self.sb(ph, [128, NBLK + 1, KW], BF16, "vt")
            zT = self.sb(ph, [128, HA, TT], BF16, "zT")
            ost = self.sb(ph, [128, 2, KW], F32, "ost")
            RqT, RkT, Rvt, RzT, Rost = Region("qT"), Region("kT"), Region("vt"), Region("zT"), Region("ost")
            if not first_tile:
                P.op("act", lambda h: h.copy(kT[:, :, 0:128], self.kcar[:]), reads=[self.Rkcar], writes=[RkT])
                P.op("act", lambda h: h.copy(vt[:, 0, :], self.vcar[:]), reads=[self.Rvcar], writes=[Rvt])
            for g, wb, Rw in self.wstream("in", list(range(G_AQ, G_MQK))):
                if g < G_AK:
                    for j in range(2):
                        acc, Racc = self.mm_feat(wb, Rw, j, TT)
                        hd = (g - G_AQ) * 2 + j
                        P.op("act", lambda h, acc=acc, hd=hd: h.activation(qT[:, hd, :], acc[:, 0:TT], AF.Copy, scale=QSCALE),
                             reads=[Racc], writes=[RqT])
                elif g < G_AV:
                    for j in range(2):
                        acc, Racc = self.mm_feat(wb, Rw, j, TT)
                        P.op("dve", lambda h, acc=acc, j=j: h.tensor_copy(kT[:, j, 128:128 + TT], acc[:, 0:TT]), reads=[Racc], writes=[RkT])
                    if last_tile:
                        acc, Racc = self.mm_tok(wb, Rw, TT - 128, 128, 256)
                        P.op("dve", lambda h, acc=acc: h.tensor_copy(ost[:, 0, :], acc[:, 0:256]), reads=[Racc], writes=[Rost])
                elif g < G_AZ:
                    for b in range(NBLK):
                        acc, Racc = self.mm_tok(wb, Rw, b * 128, 128, 256)
                        P.op("act", lambda h, acc=acc, b=b: h.copy(vt[:, b + 1, :], acc[:, 0:256]), reads=[Racc], writes=[Rvt])
                        if last_tile and b == NBLK - 1:
                            P.op("dve", lambda h, acc=acc: h.tensor_copy(ost[:, 1, :], acc[:, 0:256]), reads=[Racc], writes=[Rost])
                else:
                    for j in range(2):
                        acc, Racc = self.mm_feat(wb, Rw, j, TT)
                        hd = (g - G_AZ) * 2 + j
                        P.op("act", lambda h, acc=acc, hd=hd: h.activation(zT[:, hd, :], acc[:, 0:TT], AF.Silu), reads=[Racc], writes=[RzT])
            if last_tile:
                P.dma("sp", self.O["kp"][l], ost[:, 0, :], reads=[Rost], writes=[self.R["kp"]])
                P.dma("sp", self.O["vp"][l], ost[:, 1, :], reads=[Rost], writes=[self.R["vp"]])
            P.op("act", lambda h: h.copy(self.kcar[:], kT[:, :, TT:TT + 128]), reads=[RkT], writes=[self.Rkcar])
            P.op("act", lambda h: h.copy(self.vcar[:], vt[:, NBLK, :]), reads=[Rvt], writes=[self.Rvcar])
            L = [self.sb(ph, [128, 256], F32, "L") for _ in range(2)]
            Pn = [self.sb(ph, [128, 256], BF16, "Pn") for _ in range(2)]
            PT = [self.sb(ph, [128, 2, 128], BF16, "PT") for _ in range(2)]
            sm = [self.sb(ph, [128, 8], F32, "asm") for _ in range(2)]
            RL = [Region("L0"), Region("L1")]
            RPn = [Region("Pn0"), Region("Pn1")]
            RPT = [Region("PT0"), Region("PT1")]
            Rsm = [Region("sm0"), Region("sm1")]
            it = 0
            for b in range(NBLK):
                gfirst = first_tile and b == 0
                nk = 128 if gfirst else 256
                koff = b * 128 + (128 if gfirst else 0)
                dmo = 128 if gfirst else 0
                for hd in range(HA):
                    kv = hd // 3
                    i = it % 2
                    it += 1
                    pS, RpS = self.pb[4 + i], self.Rpb[4 + i]
                    L_, Pn_, PT_, sm_ = L[i], Pn[i], PT[i], sm[i]
                    sk = self.sinkb[:, l * HA + hd:l * HA + hd + 1]
                    ns_ = self.nslp[:, hd:hd + 1]
                    P.op("pe", lambda h, pS=pS, hd=hd, kv=kv, b=b, koff=koff, nk=nk: h.matmul(
                        pS[:, 0:nk], qT[:, hd, b * 128:(b + 1) * 128], kT[:, kv, koff:koff + nk], start=True, stop=True),
                        reads=[RqT, RkT], writes=[RpS])
                    P.op("dve", lambda h, pS=pS, L_=L_, ns_=ns_, nk=nk, dmo=dmo: h.scalar_tensor_tensor(
                        L_[:, 0:nk], self.dm[:, dmo:dmo + nk], ns_, pS[:, 0:nk], op0=ALU.mult, op1=ALU.add),
                        reads=[RpS, self.Rdm, self.Rnslp], writes=[RL[i]])
                    P.op("dve", lambda h, L_=L_, sm_=sm_, nk=nk: h.tensor_reduce(sm_[:, 0:1], L_[:, 0:nk], AX.X, ALU.max),
                         reads=[RL[i]], writes=[Rsm[i]])
                    P.op("dve", lambda h, sm_=sm_, sk=sk: h.tensor_scalar(sm_[:, 1:2], sm_[:, 0:1], sk, -1.0, op0=ALU.max, op1=ALU.mult),
                         reads=[Rsm[i], self.Rsinkb], writes=[Rsm[i]])
                    P.op("act", lambda h, L_=L_, sm_=sm_, nk=nk: h.activation(L_[:, 0:nk], L_[:, 0:nk], AF.Exp, bias=sm_[:, 1:2], scale=1.0,
                                                                              accum_out=sm_[:, 2:3]),
                         reads=[RL[i], Rsm[i]], writes=[RL[i], Rsm[i]])
                    P.op("act", lambda h, sm_=sm_, sk=sk: h.activation(sm_[:, 3:4], sk, AF.Exp, bias=sm_[:, 1:2], scale=1.0),
                         reads=[Rsm[i], self.Rsinkb], writes=[Rsm[i]])
                    P.op("dve", lambda h, sm_=sm_: h.tensor_tensor(sm_[:, 4:5], sm_[:, 2:3], sm_[:, 3:4], ALU.add), reads=[Rsm[i]], writes=[Rsm[i]])
                    P.op("dve", lambda h, sm_=sm_: h.reciprocal(sm_[:, 5:6], sm_[:, 4:5]), reads=[Rsm[i]], writes=[Rsm[i]])
                    P.op("dve", lambda h, L_=L_, Pn_=Pn_, sm_=sm_, nk=nk: h.tensor_scalar(Pn_[:, 0:nk], L_[:, 0:nk], sm_[:, 5:6], None, op0=ALU.mult),
                         reads=[RL[i], Rsm[i]], writes=[RPn[i]])
                    nkb = nk // 128
                    fns = [(lambda h, kb=kb, Pn_=Pn_: h.transpose(self.tb[:, kb * 128:(kb + 1) * 128], Pn_[:, kb * 128:(kb + 1) * 128], self.identb[:]))
                           for kb in range(nkb)]
                    P.group("pe", fns, reads=[RPn[i], self.Ridb], writes=[self.Rtb])
                    P.op("act", lambda h, PT_=PT_, nkb=nkb: h.copy(PT_[:, 0:nkb, :], self.tb[:, 0:nkb * 128].rearrange("p (a c) -> p a c", a=nkb)),
                         reads=[self.Rtb], writes=[RPT[i]])
                    pO, RpO = self.pb[i], self.Rpb[i]
                    vb0 = b + (1 if gfirst else 0)
                    fns = [(lambda h, kb=kb, pO=pO, PT_=PT_, kv=kv, vb0=vb0, nkb=nkb: h.matmul(
                        pO[:, 0:128], vt[:, vb0 + kb, kv * 128:(kv + 1) * 128], PT_[:, kb, :], start=(kb == 0), stop=(kb == nkb - 1)))
                        for kb in range(nkb)]
                    P.group("pe", fns, reads=[Rvt, RPT[i]], writes=[RpO])
                    P.op("dve", lambda h, pO=pO, hd=hd, b=b: h.tensor_tensor(self.yT[:, hd, b * 128:(b + 1) * 128], pO[:, 0:128],
                                                                              zT[:, hd, b * 128:(b + 1) * 128], ALU.mult),
                         reads=[RpO, RzT], writes=[self.RyT[0]])
            self.barrier()

    def phase_M(self):
        P, l, t = self.P, self.l, self.t
        NH = HM
        QW = NH * 128
        VW = NH * 256
        with contextlib.ExitStack() as ph:
            qb = self.sb(ph, [128, NBLK, QW], BF16, "qb")
            kb_ = self.sb(ph, [128, NBLK, QW], BF16, "kb")
            va = self.sb(ph, [128, NBLK, VW], BF16, "va")
            G = self.sb(ph, [128, NBLK, VW], BF16, "G")
            gts = self.sb(ph, [128, NBLK, 2 * NH], F32, "gts")
            tmp = [self.sb(ph, [128, 256], F32, "mtmp") for _ in range(2)]
            Rqb, Rkb, Rva, RG, Rgts = Region("qb"), Region("kb"), Region("va"), Region("G"), Region("gts")
            Rtmp = [Region("mt0"), Region("mt1")]
            groups = list(range(G_MQK, G_CU)) + [G_GATE]
            ti = 0
            for g, wb, Rw in self.wstream("in", groups):
                ncols = 2 * NH if g == G_GATE else 256
                for b in range(NBLK):
                    acc, Racc = self.mm_tok(wb, Rw, b * 128, 128, ncols)
                    if g == G_GATE:
                        P.op("dve", lambda h, acc=acc, b=b: h.tensor_tensor(gts[:, b, :], acc[:, 0:2 * NH], self.bifb[:, l * 2 * NH:(l + 1) * 2 * NH], ALU.add),
                             reads=[Racc, self.Rbifb], writes=[Rgts])
                    elif g < G_MV:
                        for j in range(2):
                            ch = (g - G_MQK) * 2 + j
                            if ch < NH:
                                P.op("act", lambda h, acc=acc, b=b, ch=ch, j=j: h.copy(qb[:, b, ch * 128:(ch + 1) * 128], acc[:, j * 128:(j + 1) * 128]),
                                     reads=[Racc], writes=[Rqb])
                            else:
                                P.op("act", lambda h, acc=acc, b=b, ch=ch, j=j: h.activation(kb_[:, b, (ch - NH) * 128:(ch - NH + 1) * 128],
                                                                                              acc[:, j * 128:(j + 1) * 128], AF.Copy, scale=QSCALE),
                                     reads=[Racc], writes=[Rkb])
                    elif g < G_MO:
                        c0 = (g - G_MV) * 256
                        P.op("dve", lambda h, acc=acc, b=b, c0=c0: h.tensor_copy(va[:, b, c0:c0 + 256], acc[:, 0:256]), reads=[Racc], writes=[Rva])
                    elif g < G_MZ:
                        c0 = (g - G_MO) * 256
                        P.op("act", lambda h, acc=acc, b=b, c0=c0: h.activation(G[:, b, c0:c0 + 256], acc[:, 0:256], AF.Sigmoid), reads=[Racc], writes=[RG])
                    else:
                        c0 = (g - G_MZ) * 256
                        tm, Rtm = tmp[ti % 2], Rtmp[ti % 2]
                        ti += 1
                        P.op("act", lambda h, acc=acc, tm=tm: h.activation(tm[:], acc[:, 0:256], AF.Silu), reads=[Racc], writes=[Rtm])
                        P.op("dve", lambda h, tm=tm, b=b, c0=c0: h.tensor_tensor(G[:, b, c0:c0 + 256], G[:, b, c0:c0 + 256], tm[:], ALU.mult),
                             reads=[Rtm, RG], writes=[RG])
            sm = self.sb(ph, [128, 128], F32, "msm")
            Bd = self.sb(ph, [128, NH, 128], F32, "Bd")
            Dm = self.sb(ph, [128, NH, 128], F32, "Dm")
            w = self.sb(ph, [128, QW], BF16, "w")
            wT = self.sb(ph, [128, QW], BF16, "wT")
            qTs = self.sb(ph, [128, QW], BF16, "qTs")
            kTs = self.sb(ph, [128, QW], BF16, "kTs")
            qs = self.sb(ph, [128, NH, 128], BF16, "qs")
            qsT = self.sb(ph, [128, NH, 2, 128], BF16, "qsT")
            ksc = [self.sb(ph, [128, NH, 128], BF16, "ksc") for _ in range(2)]
            mo = self.sb(ph, [128, VW], BF16, "mo")
            junk = self.sb(ph, [128, 256], BF16, "mjunk")
            Rsm, RBd, RDm, Rw_, RwT, RqTs, RkTs, Rqs, RqsT, Rmo, Rjunk = (Region(n) for n in
                ("msm", "Bd", "Dm", "w", "wT", "qTs", "kTs", "qs", "qsT", "mo", "mjunk"))
            Rksc = [Region("ksc0"), Region("ksc1")]
            P.op("dve", lambda h: h.memset(qsT[:], 0.0), writes=[RqsT])
            psm, Rpsm = self.pb[0], self.Rpb[0]
            pB, RpB = self.pb[1], self.Rpb[1]
            pS, RpS = self.pb[2], self.Rpb[2]
            pC = [self.pb[1], self.pb[2]]
            RpC = [self.Rpb[1], self.Rpb[2]]
            pN = [self.pb[3], self.pb[4]]
            RpN = [self.Rpb[3], self.Rpb[4]]
            c_e1, c_sp, c_an, c_al0, c_al1, c_bv = 0, 6, 12, 20, 28, 36
            c_mxB, c_rmD, c_m1, c_m2, c_msel, c_mns, c_als = 42, 54, 60, 66, 72, 78, 84
            c_int, c_mt, c_wi, c_emt, c_ws, c_wsc0, c_wsc1 = 90, 96, 102, 108, 114, 120, 0
            sm2 = self.sb(ph, [128, 64], F32, "msm2")
            Rsm2 = Region("msm2")
            d_wc0, d_wc1, d_den, d_dn, d_t, d_rd, d_ssq, d_f, d_dnn = 0, 6, 12, 18, 24, 30, 36, 42, 48
            S_ = lambda c, n=NH: sm[:, c:c + n]
            S2 = lambda c, n=NH: sm2[:, c:c + n]
            bc3 = lambda ap: ap.unsqueeze(2).to_broadcast([128, NH, 128])

            def do_block(b):
                ip = gts[:, b, 0:NH]
                fp = gts[:, b, NH:2 * NH]
                P.op("act", lambda h: h.activation(S_(c_e1), fp, AF.Exp, scale=-1.0), reads=[Rgts], writes=[Rsm])
                P.op("act", lambda h: h.activation(S_(c_sp), S_(c_e1), AF.Ln, bias=1.0), reads=[Rsm], writes=[Rsm])
                fns = [lambda h: h.matmul(psm[:, 0:NH], self.tri2[:], S_(c_sp), start=True, stop=True),
                       lambda h: h.matmul(psm[:, 8:8 + NH], self.onesc[:, 0:128], S_(c_sp), start=True, stop=True),
                       lambda h: h.matmul(psm[:, 16:16 + NH], self.onesc[:, 128:256], S_(c_sp), start=True, stop=True)]
                P.group("pe", fns, reads=[Rsm, self.Rtri2, self.Ronesc], writes=[Rpsm])
                for (dst, srcc) in ((c_an, 0), (c_al0, 8), (c_al1, 16)):
                    P.op("dve", lambda h, dst=dst, srcc=srcc: h.tensor_copy(S_(dst), psm[:, srcc:srcc + NH]), reads=[Rpsm], writes=[Rsm])
                P.op("dve", lambda h: h.tensor_tensor(S_(c_bv), ip, S_(c_an), ALU.add), reads=[Rgts, Rsm], writes=[Rsm])
                P.op("dve", lambda h: h.tensor_tensor(Bd[:], self.identf[:].unsqueeze(1).to_broadcast([128, NH, 128]), bc3(S_(c_bv)), ALU.mult),
                     reads=[Rsm, self.Ridf], writes=[RBd])
                P.op("pe", lambda h: h.matmul(pB[:, 0:QW], self.onesf[:], Bd[:].rearrange("p a c -> p (a c)"), start=True, stop=True),
                     reads=[RBd, self.Ronesf], writes=[RpB])
                P.op("dve", lambda h: h.tensor_reduce(sm[:, c_mxB:c_mxB + 2 * NH].rearrange("p (a c) -> p a c", a=NH),
                                                      pB[:, 0:QW].rearrange("p (a c s) -> p a c s", a=NH, c=2), AX.X, ALU.max),
                     reads=[RpB], writes=[Rsm])
                P.op("dve", lambda h: h.tensor_tensor(Dm[:].rearrange("p a c -> p (a c)"), pB[:, 0:QW], self.mask6[:, 0:QW], ALU.add),
                     reads=[RpB, self.Rmask6], writes=[RDm])
                P.op("dve", lambda h: h.tensor_tensor(Dm[:], Dm[:], bc3(S_(c_an)), ALU.subtract), reads=[RDm, Rsm], writes=[RDm])
                P.op("dve", lambda h: h.tensor_reduce(S_(c_rmD), Dm[:], AX.X, ALU.max), reads=[RDm], writes=[Rsm])
                mxB = sm[:, c_mxB:c_mxB + 2 * NH].rearrange("p (a c) -> p a c", c=2)
                m0 = self.mst[:, 0:NH]
                P.op("dve", lambda h: h.tensor_tensor(S_(c_m1), m0, mxB[:, :, 0], ALU.max), reads=[Rsm, self.Rm], writes=[Rsm])
                P.op("dve", lambda h: h.tensor_tensor(S_(c_m1), S_(c_m1), S_(c_al0), ALU.subtract), reads=[Rsm], writes=[Rsm])
                P.op("dve", lambda h: h.tensor_tensor(S_(c_m2), S_(c_m1), mxB[:, :, 1], ALU.max), reads=[Rsm], writes=[Rsm])
                P.op("dve", lambda h: h.tensor_tensor(S_(c_m2), S_(c_m2), S_(c_al1), ALU.subtract), reads=[Rsm], writes=[Rsm])
                P.op("dve", lambda h: h.tensor_tensor(S2(d_wc0), m0, S_(c_al0), ALU.subtract), reads=[Rsm, self.Rm], writes=[Rsm2])
                P.op("dve", lambda h: h.tensor_tensor(S2(d_wc0), S2(d_wc0), S_(c_m1), ALU.subtract), reads=[Rsm, Rsm2], writes=[Rsm2])
                P.op("dve", lambda h: h.tensor_tensor(S2(d_wc1), S_(c_m1), S_(c_al1), ALU.subtract), reads=[Rsm, Rsm2], writes=[Rsm2])
                P.op("dve", lambda h: h.tensor_tensor(S2(d_wc1), S2(d_wc1), S_(c_m2), ALU.subtract), reads=[Rsm, Rsm2], writes=[Rsm2])
                P.op("act", lambda h: h.activation(S2(d_wc0), S2(d_wc0), AF.Exp), reads=[Rsm2], writes=[Rsm2])
                P.op("act", lambda h: h.activation(S2(d_wc1), S2(d_wc1), AF.Exp), reads=[Rsm2], writes=[Rsm2])
                P.op("dve", lambda h: h.tensor_copy(sm[0:64, c_msel:c_msel + NH], self.mst[0:64, 0:NH]), reads=[self.Rm, Rsm], writes=[Rsm])
                P.op("dve", lambda h: h.tensor_copy(sm[64:128, c_msel:c_msel + NH], sm[64:128, c_m1:c_m1 + NH]), reads=[Rsm], writes=[Rsm])
                P.op("dve", lambda h: h.tensor_copy(sm[0:64, c_mns:c_mns + NH], sm[0:64, c_m1:c_m1 + NH]), reads=[Rsm], writes=[Rsm])
                P.op("dve", lambda h: h.tensor_copy(sm[64:128, c_mns:c_mns + NH], sm[64:128, c_m2:c_m2 + NH]), reads=[Rsm], writes=[Rsm])
                P.op("dve", lambda h: h.tensor_copy(sm[0:64, c_als:c_als + NH], sm[0:64, c_al0:c_al0 + NH]), reads=[Rsm], writes=[Rsm])
                P.op("dve", lambda h: h.tensor_copy(sm[64:128, c_als:c_als + NH], sm[64:128, c_al1:c_al1 + NH]), reads=[Rsm], writes=[Rsm])
                P.op("dve", lambda h: h.tensor_copy(self.mst[:, 0:NH], S_(c_m2)), reads=[Rsm], writes=[self.Rm])
                P.op("dve", lambda h: h.tensor_tensor(S_(c_int), S_(c_msel), S_(c_an), ALU.subtract), reads=[Rsm], writes=[Rsm])
                P.op("dve", lambda h: h.tensor_tensor(S_(c_mt), S_(c_int), S_(c_rmD), ALU.max), reads=[Rsm], writes=[Rsm])
                P.op("dve", lambda h: h.tensor_tensor(S_(c_wi), S_(c_int), S_(c_mt), ALU.subtract), reads=[Rsm], writes=[Rsm])
                P.op("act", lambda h: h.activation(S_(c_wi), S_(c_wi), AF.Exp), reads=[Rsm], writes=[Rsm])
                P.op("act", lambda h: h.activation(S_(c_emt), S_(c_mt), AF.Exp, scale=-1.0), reads=[Rsm], writes=[Rsm])
                P.op("dve", lambda h: h.tensor_tensor(S_(c_ws), S_(c_bv), S_(c_als), ALU.subtract), reads=[Rsm], writes=[Rsm])
                P.op("dve", lambda h: h.tensor_tensor(S_(c_ws), S_(c_ws), S_(c_mns), ALU.subtract), reads=[Rsm], writes=[Rsm])
                P.op("act", lambda h: h.activation(S_(c_ws), S_(c_ws), AF.Exp), reads=[Rsm], writes=[Rsm])
                P.op("dve", lambda h: h.tensor_scalar(S_(c_wsc0), S_(c_ws), self.cmask[:, 0:1], None, op0=ALU.mult), reads=[Rsm, self.Rcmask], writes=[Rsm])
                P.op("dve", lambda h: h.tensor_scalar(S_(c_wsc1), S_(c_ws), self.cmask[:, 1:2], None, op0=ALU.mult), reads=[Rsm, self.Rcmask], writes=[Rsm])
                P.op("dve", lambda h: h.tensor_tensor(Dm[:], Dm[:], bc3(S_(c_mt)), ALU.subtract), reads=[RDm, Rsm], writes=[RDm])
                P.op("act", lambda h: h.activation(Dm[:], Dm[:], AF.Exp), reads=[RDm], writes=[RDm])
                for (srcb, Rsrcb, dstT, RdstT) in ((qb, Rqb, qTs, RqTs), (kb_, Rkb, kTs, RkTs)):
                    fns = [(lambda h, hh=hh, srcb=srcb: h.transpose(self.tb[:, hh * 128:(hh + 1) * 128], srcb[:, b, hh * 128:(hh + 1) * 128], self.identb[:]))
                           for hh in range(NH)]
                    P.group("pe", fns, reads=[Rsrcb, self.Ridb], writes=[self.Rtb])
                    P.op("act", lambda h, dstT=dstT: h.copy(dstT[:], self.tb[:, 0:QW]), reads=[self.Rtb], writes=[RdstT])
                fns = [(lambda h, hh=hh: h.matmul(pS[:, hh * 128:(hh + 1) * 128], qTs[:, hh * 128:(hh + 1) * 128],
                                                  kTs[:, hh * 128:(hh + 1) * 128], start=True, stop=True)) for hh in range(NH)]
                P.group("pe", fns, reads=[RqTs, RkTs], writes=[RpS])
                P.op("dve", lambda h: h.tensor_tensor(w[:], Dm[:].rearrange("p a c -> p (a c)"), pS[:, 0:QW], ALU.mult), reads=[RDm, RpS], writes=[Rw_])
                fns = [(lambda h, hh=hh: h.transpose(self.tb[:, hh * 128:(hh + 1) * 128], w[:, hh * 128:(hh + 1) * 128], self.identb[:])) for hh in range(NH)]
                P.group("pe", fns, reads=[Rw_, self.Ridb], writes=[self.Rtb])
                P.op("act", lambda h: h.copy(wT[:], self.tb[:, 0:QW]), reads=[self.Rtb], writes=[RwT])
                P.op("dve", lambda h: h.tensor_tensor(qs[:], qb[:, b, :].rearrange("p (a c) -> p a c", a=NH), bc3(S_(c_wi)), ALU.mult),
                     reads=[Rqb, Rsm], writes=[Rqs])
                fns = [(lambda h, hh=hh: h.transpose(self.tb[:, hh * 128:(hh + 1) * 128], qs[:, hh, :], self.identb[:])) for hh in range(NH)]
                P.group("pe", fns, reads=[Rqs, self.Ridb], writes=[self.Rtb])
                tbv = self.tb[:, 0:QW].rearrange("p (a c) -> p a c", a=NH)
                P.op("act", lambda h: h.copy(qsT[:, :, 0, 0:64], tbv[:, :, 0:64]), reads=[self.Rtb], writes=[RqsT])
                P.op("act", lambda h: h.copy(qsT[:, :, 1, 64:128], tbv[:, :, 64:128]), reads=[self.Rtb], writes=[RqsT])
                kb3 = kb_[:, b, :].rearrange("p (a c) -> p a c", a=NH)
                P.op("dve", lambda h: h.tensor_tensor(ksc[0][:], kb3, bc3(S_(c_wsc0)), ALU.mult), reads=[Rkb, Rsm], writes=[Rksc[0]])
                P.op("dve", lambda h: h.tensor_tensor(ksc[1][:], kb3, bc3(S_(c_wsc1)), ALU.mult), reads=[Rkb, Rsm], writes=[Rksc[1]])
                va3 = va[:, b, :].rearrange("p (a c) -> p a c", a=NH)

                def state_update(c, wc_col, Cb_dst, nb_dst, RCb_dst, Rnb_dst):
                    fns = []
                    for hh in range(NH):
                        fns.append(lambda h, hh=hh: h.matmul(pC[hh // 2][:, (hh % 2) * 256:(hh % 2) * 256 + 256], ksc[c][:, hh, :], va3[:, hh, :],
                                                             start=True, stop=True))
                    P.group("pe", fns, reads=[Rksc[c], Rva], writes=RpC)
                    fns = [(lambda h, hh=hh: h.matmul(psm[:, 32 + hh:33 + hh], ksc[c][:, hh, :], self.onesb[:, 0:1], start=True, stop=True)) for hh in range(NH)]
                    P.group("pe", fns, reads=[Rksc[c], self.Ronesb], writes=[Rpsm])
                    for hh in range(NH):
                        P.op("dve", lambda h, hh=hh: h.scalar_tensor_tensor(self.Cst[:, hh, :], self.Cst[:, hh, :], sm2[:, wc_col + hh:wc_col + hh + 1],
                                                                            pC[hh // 2][:, (hh % 2) * 256:(hh % 2) * 256 + 256], op0=ALU.mult, op1=ALU.add),
                             reads=[Rsm2, RpC[hh // 2], self.RC], writes=[self.RC])
                    P.op("dve", lambda h: h.tensor_tensor(S2(d_dnn), self.nst[:, 0:NH], S2(wc_col), ALU.mult), reads=[self.Rn, Rsm2], writes=[Rsm2])
                    P.op("dve", lambda h: h.tensor_tensor(self.nst[:, 0:NH], S2(d_dnn), psm[:, 32:32 + NH], ALU.add), reads=[Rsm2, Rpsm], writes=[self.Rn])
                    P.op("act", lambda h: h.copy(Cb_dst[:], self.Cst[:]), reads=[self.RC], writes=[RCb_dst])
                    P.op("act", lambda h: h.copy(nb_dst[:, 0:NH], self.nst[:, 0:NH]), reads=[self.Rn], writes=[Rnb_dst])

                state_update(0, d_wc0, self.Cb[1], self.nb[1], self.RCb[1], self.Rnb[1])
                for hh in range(NH):
                    o_ = pN[hh // 2][:, (hh % 2) * 256:(hh % 2) * 256 + 256]
                    fns = [lambda h, hh=hh, o_=o_: h.matmul(o_, wT[:, hh * 128:(hh + 1) * 128], va3[:, hh, :], start=True, stop=False),
                           lambda h, hh=hh, o_=o_: h.matmul(o_, qsT[:, hh, 0, :], self.Cb[0][:, hh, :], start=False, stop=False),
                           lambda h, hh=hh, o_=o_: h.matmul(o_, qsT[:, hh, 1, :], self.Cb[1][:, hh, :], start=False, stop=True)]
                    P.group("pe", fns, reads=[RwT, Rva, RqsT, self.RCb[0], self.RCb[1]], writes=[RpN[hh // 2]])
                for hh in range(NH):
                    o_ = psm[:, 40 + hh:41 + hh]
                    fns = [lambda h, hh=hh, o_=o_: h.matmul(o_, wT[:, hh * 128:(hh + 1) * 128], self.onesb[:, 0:1], start=True, stop=False),
                           lambda h, hh=hh, o_=o_: h.matmul(o_, qsT[:, hh, 0, :], self.nb[0][:, hh:hh + 1], start=False, stop=False),
                           lambda h, hh=hh, o_=o_: h.matmul(o_, qsT[:, hh, 1, :], self.nb[1][:, hh:hh + 1], start=False, stop=True)]
                    P.group("pe", fns, reads=[RwT, self.Ronesb, RqsT, self.Rnb[0], self.Rnb[1]], writes=[Rpsm])
                P.op("dve", lambda h: h.tensor_copy(S2(d_den), psm[:, 40:40 + NH]), reads=[Rpsm], writes=[Rsm2])
                P.op("dve", lambda h: h.scalar_tensor_tensor(S2(d_t), S2(d_den), -1.0, S2(d_den), op0=ALU.mult, op1=ALU.max), reads=[Rsm2], writes=[Rsm2])
                P.op("dve", lambda h: h.tensor_tensor(S2(d_t), S2(d_t), S_(c_emt), ALU.max), reads=[Rsm2, Rsm], writes=[Rsm2])
                P.op("dve", lambda h: h.reciprocal(S2(d_rd), S2(d_t)), reads=[Rsm2], writes=[Rsm2])
                for hh in range(NH):
                    P.op("act", lambda h, hh=hh: h.activation(junk[:], pN[hh // 2][:, (hh % 2) * 256:(hh % 2) * 256 + 256], AF.Square,
                                                              accum_out=sm2[:, d_ssq + hh:d_ssq + hh + 1]),
                         reads=[RpN[hh // 2], Rsm2], writes=[Rjunk, Rsm2])
                P.op("dve", lambda h: h.tensor_tensor(S2(d_t), S2(d_rd), S2(d_rd), ALU.mult), reads=[Rsm2], writes=[Rsm2])
                P.op("dve", lambda h: h.tensor_tensor(S2(d_t), S2(d_t), S2(d_ssq), ALU.mult), reads=[Rsm2], writes=[Rsm2])
                P.op("act", lambda h: h.activation(S2(d_t), S2(d_t), AF.Sqrt, bias=EPS, scale=1.0 / 256.0), reads=[Rsm2], writes=[Rsm2])
                P.op("dve", lambda h: h.reciprocal(S2(d_f), S2(d_t)), reads=[Rsm2], writes=[Rsm2])
                P.op("dve", lambda h: h.tensor_tensor(S2(d_f), S2(d_f), S2(d_rd), ALU.mult), reads=[Rsm2], writes=[Rsm2])
                for hh in range(NH):
                    P.op("dve", lambda h, hh=hh: h.scalar_tensor_tensor(mo[:, hh * 256:(hh + 1) * 256], pN[hh // 2][:, (hh % 2) * 256:(hh % 2) * 256 + 256],
                                                                        sm2[:, d_f + hh:d_f + hh + 1], G[:, b, hh * 256:(hh + 1) * 256],
                                                                        op0=ALU.mult, op1=ALU.mult),
                         reads=[RpN[hh // 2], Rsm2, RG], writes=[Rmo])
                fns = [(lambda h, j=j: h.transpose(self.tb[:, j * 128:(j + 1) * 128], mo[:, j * 128:(j + 1) * 128], self.identb[:])) for j in range(6)]
                P.group("pe", fns, reads=[Rmo, self.Ridb], writes=[self.Rtb])
                P.op("dve", lambda h: h.tensor_tensor(
                    self.yT[:, 6:12, b * 128:(b + 1) * 128],
                    self.tb[:, 0:768].rearrange("p (a c) -> p a c", a=6),
                    self.mhgT[:, 0:6].unsqueeze(2).to_broadcast([128, 6, 128]), ALU.mult),
                    reads=[self.Rtb, self.RmhgT], writes=[self.RyT[1]])
                state_update(1, d_wc1, self.Cb[0], self.nb[0], self.RCb[0], self.Rnb[0])
            for b in range(NBLK):
                do_block(b)
            self.barrier()

    def phase_C(self):
        P, l, t = self.P, self.l, self.t
        last_tile = (t == NTILE - 1)
        with contextlib.ExitStack() as ph:
            uT = self.sb(ph, [128, GC, TT], BF16, "uT")
            vvb = self.sb(ph, [128, NBLK, 1024], BF16, "vvb")
            gv = [self.sb(ph, [128, 1024], F32, "gv") for _ in range(NBLK)]
            tmp = [self.sb(ph, [128, TT], F32, "ctmp") for _ in range(2)]
            junk = self.sb(ph, [128, 1024], BF16, "cjunk")
            sm = self.sb(ph, [128, 16], F32, "csm")
            t1 = self.sb(ph, [128, GC, 128], F32, "ct1")
            RuT, Rvvb, Rjunk, Rsm, Rt1 = Region("uT"), Region("vvb"), Region("cjunk"), Region("csm"), Region("ct1")
            Rgv = [Region("gv%d" % i) for i in range(NBLK)]
            Rtmp = [Region("ct0"), Region("ct1")]
            ti = 0
            for g, wb, Rw in self.wstream("in", list(range(G_CU, G_GATE))):
                if g < G_CV:
                    for j in range(2):
                        acc, Racc = self.mm_feat(wb, Rw, j, TT)
                        gi = (g - G_CU) * 2 + j
                        P.op("act", lambda h, acc=acc, gi=gi: h.activation(uT[:, gi, :], acc[:, 0:TT], AF.Gelu), reads=[Racc], writes=[RuT])
                elif g < G_CZ:
                    c0 = (g - G_CV) * 256
                    for b in range(NBLK):
                        acc, Racc = self.mm_tok(wb, Rw, b * 128, 128, 256)
                        P.op("act", lambda h, acc=acc, b=b, c0=c0: h.activation(gv[b][:, c0:c0 + 256], acc[:, 0:256], AF.Gelu), reads=[Racc], writes=[Rgv[b]])
                else:
                    for j in range(2):
                        acc, Racc = self.mm_feat(wb, Rw, j, TT)
                        gi = (g - G_CZ) * 2 + j
                        tm, Rtm = tmp[ti % 2], Rtmp[ti % 2]
                        ti += 1
                        P.op("act", lambda h, acc=acc, tm=tm: h.activation(tm[:], acc[:, 0:TT], AF.Silu), reads=[Racc], writes=[Rtm])
                        P.op("dve", lambda h, tm=tm, gi=gi: h.tensor_tensor(uT[:, gi, :], uT[:, gi, :], tm[:], ALU.mult), reads=[Rtm, RuT], writes=[RuT])

            def do_blockc(b):
                g_ = gv[b]
                self.layernorm(g_, Rgv[b], 128, junk, Rjunk, sm, Rsm)
                P.op("act", lambda h: h.copy(vvb[:, b, :], g_[:]), reads=[Rgv[b]], writes=[Rvvb])
                if last_tile and b == NBLK - 1:
                    P.dma("sp", self.O["cvp"][l], g_[:], reads=[Rgv[b]], writes=[self.R["cvp"]])
                pS_, RpS_ = self.pb[4 + (b % 2)], self.Rpb[4 + (b % 2)]
                fns = [(lambda h, gi=gi: h.matmul(pS_[:, gi * 128:(gi + 1) * 128], vvb[:, b, gi * 128:(gi + 1) * 128],
                                                  self.wmT[:, gi, :], start=True, stop=True)) for gi in range(GC)]
                P.group("pe", fns, reads=[Rvvb, self.RwmT], writes=[RpS_])
                P.op("dve", lambda h: h.tensor_tensor(t1[:].rearrange("p a c -> p (a c)"), pS_[:, 0:GC * 128], self.bsb[:, 0:GC * 128], ALU.add),
                     reads=[RpS_, self.Rbsb], writes=[Rt1])
                P.op("dve", lambda h: h.tensor_tensor(self.yT[:, 12:12 + GC, b * 128:(b + 1) * 128], t1[:], uT[:, :, b * 128:(b + 1) * 128], ALU.mult),
                     reads=[Rt1, RuT], writes=[self.RyT[2]])
            for b in range(NBLK):
                do_blockc(b)
            self.barrier()

    def layernorm(self, g_, Rg, np_, junk, Rjunk, sm, Rsm):
        P = self.P
        P.op("act", lambda h: h.activation(junk[0:np_, :], g_[0:np_, :], AF.Copy, accum_out=sm[0:np_, 0:1]), reads=[Rg], writes=[Rjunk, Rsm])
        P.op("act", lambda h: h.activation(junk[0:np_, :], g_[0:np_, :], AF.Square, accum_out=sm[0:np_, 1:2]), reads=[Rg], writes=[Rjunk, Rsm])
        P.op("dve", lambda h: h.tensor_scalar(sm[0:np_, 2:4], sm[0:np_, 0:2], 1.0 / 1024.0, None, op0=ALU.mult), reads=[Rsm], writes=[Rsm])
        P.op("dve", lambda h: h.tensor_tensor(sm[0:np_, 4:5], sm[0:np_, 2:3], sm[0:np_, 2:3], ALU.mult), reads=[Rsm], writes=[Rsm])
        P.op("dve", lambda h: h.tensor_tensor(sm[0:np_, 5:6], sm[0:np_, 3:4], sm[0:np_, 4:5], ALU.subtract), reads=[Rsm], writes=[Rsm])
        P.op("act", lambda h: h.activation(sm[0:np_, 6:7], sm[0:np_, 5:6], AF.Sqrt, bias=EPS, scale=1.0), reads=[Rsm], writes=[Rsm])
        P.op("dve", lambda h: h.reciprocal(sm[0:np_, 7:8], sm[0:np_, 6:7]), reads=[Rsm], writes=[Rsm])
        P.op("dve", lambda h: h.tensor_scalar(g_[0:np_, :], g_[0:np_, :], sm[0:np_, 2:3], sm[0:np_, 7:8], op0=ALU.subtract, op1=ALU.mult),
             reads=[Rg, Rsm], writes=[Rg])
        P.op("dve", lambda h: h.tensor_tensor(g_[0:np_, :], g_[0:np_, :], self.lngb[0:np_, :], ALU.mult), reads=[Rg, self.Rln], writes=[Rg])
        P.op("dve", lambda h: h.tensor_tensor(g_[0:np_, :], g_[0:np_, :], self.lnbb[0:np_, :], ALU.add), reads=[Rg, self.Rln], writes=[Rg])

    def phase_O(self, dst, Rdst, np_, nblk):
        P = self.P
        with contextlib.ExitStack() as ph:
            hs = [self.sb(ph, [128, nblk, 512], F32, "hs") for _ in range(2)]
            Rhs = [Region("hs0"), Region("hs1")]
            i = 0
            for g, wb, Rw in self.wstream("out", list(range(NG_OUT))):
                h_, Rh_ = hs[i % 2], Rhs[i % 2]
                i += 1
                for b in range(nblk):
                    acc, Racc = self.next_acc()
                    yT = self.yT
                    fns = [(lambda h, kc=kc, acc=acc, b=b, wb=wb: h.matmul(acc[0:np_, 0:512], yT[:, kc, b * np_:(b + 1) * np_], wb[:, kc, 0:512],
                                                                           start=(kc == 0), stop=(kc == KY - 1))) for kc in range(KY)]
                    P.group("pe", fns, reads=[Rw] + self.RyT, writes=[Racc])
                    if b % 2 == 0:
                        P.op("act", lambda h, acc=acc, h_=h_, b=b: h.copy(h_[0:np_, b, :], acc[0:np_, 0:512]), reads=[Racc], writes=[Rh_])
                    else:
                        P.op("dve", lambda h, acc=acc, h_=h_, b=b: h.tensor_copy(h_[0:np_, b, :], acc[0:np_, 0:512]), reads=[Racc], writes=[Rh_])
                P.dma("sp", dst[:, g * 512:(g + 1) * 512].rearrange("(b p) c -> p b c", p=np_), h_[0:np_, :, :], reads=[Rh_], writes=[Rdst])
            self.barrier()

    def phase_final(self, srcA, RsrcA, srcB, RsrcB, out, Rout, np_, nblk):
        P = self.P
        with contextlib.ExitStack() as ph:
            hb = [self.sb(ph, [128, D], F32, "fhb") for _ in range(2)]
            hb2 = self.sb(ph, [128, D], F32, "fhb2")
            fg = self.sb(ph, [128, D], F32, "fg")
            junk = self.sb(ph, [128, D], BF16, "fjunk")
            stt = [self.sb(ph, [128, 4], F32, "fst") for _ in range(2)]
            Rhb = [Region("fhb0"), Region("fhb1")]
            Rhb2 = Region("fhb2")
            Rst = [Region("fst0"), Region("fst1")]
            Rfg, Rj = Region("fg"), Region("fjunk")
            P.dma("sp", fg[:], self.I["fgain"].partition_broadcast(128), writes=[Rfg])
            for b in range(nblk):
                i = b % 2
                h_, s_ = hb[i], stt[i]
                rs = slice(b * np_, (b + 1) * np_)
                P.dma("sp", h_[0:np_, :], srcA[rs, :], reads=[RsrcA], writes=[Rhb[i]])
                P.dma("sp", hb2[0:np_, :], srcB[rs, :], reads=[RsrcB], writes=[Rhb2])
                P.op("dve", lambda h, h_=h_: h.tensor_tensor(h_[0:np_, :], h_[0:np_, :], hb2[0:np_, :], ALU.add), reads=[Rhb[i], Rhb2], writes=[Rhb[i]])
                P.op("act", lambda h, h_=h_, s_=s_: h.activation(junk[0:np_, :], h_[0:np_, :], AF.Square, accum_out=s_[0:np_, 0:1]),
                     reads=[Rhb[i]], writes=[Rj, Rst[i]])
                P.op("act", lambda h, s_=s_: h.activation(s_[0:np_, 1:2], s_[0:np_, 0:1], AF.Sqrt, bias=EPS, scale=1.0 / D), reads=[Rst[i]], writes=[Rst[i]])
                P.op("dve", lambda h, s_=s_: h.reciprocal(s_[0:np_, 2:3], s_[0:np_, 1:2]), reads=[Rst[i]], writes=[Rst[i]])
                P.op("dve", lambda h, h_=h_, s_=s_: h.scalar_tensor_tensor(h_[0:np_, :], h_[0:np_, :], s_[0:np_, 2:3], fg[0:np_, :], op0=ALU.mult, op1=ALU.mult),
                     reads=[Rhb[i], Rst[i], Rfg], writes=[Rhb[i]])
                P.dma("sp", out[rs, :], h_[0:np_, :], reads=[Rhb[i]], writes=[Rout])
            self.barrier()

    def sample_layer(self):
        l, I, S, O, P = self.l, self.I, self.S, self.O, self.P
        if l == 0:
            self.phase_norm(I["xs"], self.R["xin"], 1, NS)
        else:
            self.phase_norm(I["xs"], self.R["xin"], 1, NS, add=(S["reds0"], self.Rred[(0, "s")]), store=(S["hs1"], self.R["hs1"]))
        self.s_phase_A()
        self.s_phase_M()
        self.s_phase_C()
        rp = Region("pos%d" % l)
        rr = Region("reds%d" % l)
        self.Rred[(l, "s")] = rr
        self.phase_O(S["pos%d" % l], rp, NS, 1)
        P.coll(self.st, S["pos%d" % l], S["reds%d" % l], reads=[rp], writes=[rr])

    def s_proj(self, wb, Rw, ncols, evac):
        acc, Racc = self.mm_tok(wb, Rw, 0, NS, ncols)
        evac(acc, Racc)

    def s_to_yT(self, srcb, Rsrcb, kc0, n):
        P = self.P
        fns = [(lambda h, j=j: h.transpose(self.tb[:, j * NS:(j + 1) * NS], srcb[0:NS, j * 128:(j + 1) * 128], self.identb[0:NS, 0:NS])) for j in range(n)]
        P.group("pe", fns, reads=[Rsrcb, self.Ridb], writes=[self.Rtb])
        reg = self.RyT[0] if kc0 == 0 else (self.RyT[1] if kc0 == 6 else self.RyT[2])
        P.op("act", lambda h: h.copy(self.yT[:, kc0:kc0 + n, 0:NS], self.tb[:, 0:n * NS].rearrange("p (a c) -> p a c", a=n)), reads=[self.Rtb], writes=[reg])

    def s_phase_A(self):
        P, l, I, S, O = self.P, self.l, self.I, self.S, self.O
        NP = NS * KV
        with contextlib.ExitStack() as ph:
            qs_ = self.sb(ph, [NS, 768], F32, "sq")
            kn = self.sb(ph, [NS, 256], F32, "skn")
            vn = self.sb(ph, [NS, 256], F32, "svn")
            za = self.sb(ph, [NS, 768], F32, "sza")
            Rq, Rkn, Rvn, Rza = Region("sq"), Region("skn"), Region("svn"), Region("sza")
            for g, wb, Rw in self.wstream("in", list(range(G_AQ, G_MQK))):
                def evac(acc, Racc, g=g):
                    if g < G_AK:
                        c0 = (g - G_AQ) * 256
                        P.op("act", lambda h: h.activation(qs_[:, c0:c0 + 256], acc[0:NS, 0:256], AF.Copy, scale=QSCALE), reads=[Racc], writes=[Rq])
                    elif g < G_AV:
                        P.op("dve", lambda h: h.tensor_copy(kn[:], acc[0:NS, 0:256]), reads=[Racc], writes=[Rkn])
                    elif g < G_AZ:
                        P.op("dve", lambda h: h.tensor_copy(vn[:], acc[0:NS, 0:256]), reads=[Racc], writes=[Rvn])
                    else:
                        c0 = (g - G_AZ) * 256
                        P.op("act", lambda h: h.activation(za[:, c0:c0 + 256], acc[0:NS, 0:256], AF.Silu), reads=[Racc], writes=[Rza])
                self.s_proj(wb, Rw, 256, evac)
            Rb = self.R["bnc"]
            P.dma("sp", O["ks"][l], kn[:], reads=[Rkn], writes=[self.R["ks"]])
            P.dma("sp", O["vs"][l], vn[:], reads=[Rvn], writes=[self.R["vs"]])
            P.dma("sp", S["bq"], qs_[:], reads=[Rq], writes=[Rb])
            P.dma("sp", S["bk"], kn[:], reads=[Rkn], writes=[Rb])
            P.dma("sp", S["bv"], vn[:], reads=[Rvn], writes=[Rb])
            q4 = self.sb(ph, [NP, 3, 128], F32, "q4")
            k4 = self.sb(ph, [NP, 128], F32, "k4")
            v4 = self.sb(ph, [NP, 128], F32, "v4")
            R4 = Region("qkv4")
            P.dma("sp", q4[:].rearrange("p a b -> p (a b)"), S["bq"].rearrange("b (kv x) -> (b kv) x", kv=KV), reads=[Rb], writes=[R4])
            P.dma("sp", k4[:], S["bk"].rearrange("b (kv x) -> (b kv) x", kv=KV), reads=[Rb], writes=[R4])
            P.dma("sp", v4[:], S["bv"].rearrange("b (kv x) -> (b kv) x", kv=KV), reads=[Rb], writes=[R4])
            slp = self.sb(ph, [NP, 3], F32, "slp")
            sk4 = self.sb(ph, [NP, 3], F32, "sk4")
            dist = self.sb(ph, [NP, 129], F32, "dist")
            Rc = Region("sAc")
            P.dma("sp", slp[:], I["c_slp"], writes=[Rc])
            P.dma("sp", sk4[:], I["sk4"][l], writes=[Rc])
            P.dma("sp", dist[:], I["c_dist"].partition_broadcast(NP), writes=[Rc])
            lg = self.sb(ph, [NP, 3, 129], F32, "lg")
            ab = self.sb(ph, [NP, 3, 129], F32, "ab")
            Rlg, Rab = Region("lg"), Region("ab")
            P.op("dve", lambda h: h.tensor_tensor(ab[:], dist[:].unsqueeze(1).to_broadcast([NP, 3, 129]), slp[:].unsqueeze(2).to_broadcast([NP, 3, 129]), ALU.mult),
                 reads=[Rc], writes=[Rab])
            KC = 16
            kvb = self.sb(ph, [NP, KC, 128], F32, "kvb")
            tmp = self.sb(ph, [NP, KC * 128], F32, "stmp")
            Rkvb, Rtmp = Region("kvb"), Region("stmp")
            for c in range(128 // KC):
                P.dma("sp", kvb[:].rearrange("p a b -> p (a b)"), I["ck"][l][:, c * KC * 128:(c + 1) * KC * 128], writes=[Rkvb])
                for g3 in range(3):
                    P.op("dve", lambda h, g3=g3: h.tensor_tensor(tmp[:].rearrange("p (a b) -> p a b", a=KC), kvb[:], q4[:, g3, :].unsqueeze(1).to_broadcast([NP, KC, 128]), ALU.mult),
                         reads=[Rkvb, R4], writes=[Rtmp])
                    P.op("dve", lambda h, g3=g3, c=c: h.tensor_reduce(lg[:, g3, c * KC:(c + 1) * KC], tmp[:].rearrange("p (a b) -> p a b", a=KC), AX.X, ALU.add),
                         reads=[Rtmp], writes=[Rlg])
            P.op("dve", lambda h: h.tensor_tensor(tmp[:, 0:384].rearrange("p (a b) -> p a b", a=3), q4[:], k4[:].unsqueeze(1).to_broadcast([NP, 3, 128]), ALU.mult),
                 reads=[R4], writes=[Rtmp])
            P.op("dve", lambda h: h.tensor_reduce(lg[:, :, 128], tmp[:, 0:384].rearrange("p (a b) -> p a b", a=3), AX.X, ALU.add), reads=[Rtmp], writes=[Rlg])
            P.op("dve", lambda h: h.tensor_tensor(lg[:], lg[:], ab[:], ALU.subtract), reads=[Rlg, Rab], writes=[Rlg])
            sm = self.sb(ph, [NP, 24], F32, "sAsm")
            Rsm = Region("sAsm")
            P.op("dve", lambda h: h.tensor_reduce(sm[:, 0:3], lg[:], AX.X, ALU.max), reads=[Rlg], writes=[Rsm])
            P.op("dve", lambda h: h.tensor_tensor(sm[:, 0:3], sm[:, 0:3], sk4[:], ALU.max), reads=[Rsm, Rc], writes=[Rsm])
            P.op("dve", lambda h: h.tensor_tensor(lg[:], lg[:], sm[:, 0:3].unsqueeze(2).to_broadcast([NP, 3, 129]), ALU.subtract), reads=[Rlg, Rsm], writes=[Rlg])
            P.op("act", lambda h: h.activation(lg[:], lg[:], AF.Exp), reads=[Rlg], writes=[Rlg])
            P.op("dve", lambda h: h.tensor_reduce(sm[:, 3:6], lg[:], AX.X, ALU.add), reads=[Rlg], writes=[Rsm])
            P.op("dve", lambda h: h.tensor_tensor(sm[:, 6:9], sk4[:], sm[:, 0:3], ALU.subtract), reads=[Rsm, Rc], writes=[Rsm])
            P.op("act", lambda h: h.activation(sm[:, 6:9], sm[:, 6:9], AF.Exp), reads=[Rsm], writes=[Rsm])
            P.op("dve", lambda h: h.tensor_tensor(sm[:, 3:6], sm[:, 3:6], sm[:, 6:9], ALU.add), reads=[Rsm], writes=[Rsm])
            P.op("dve", lambda h: h.reciprocal(sm[:, 9:12], sm[:, 3:6]), reads=[Rsm], writes=[Rsm])
            P.op("dve", lambda h: h.tensor_tensor(lg[:], lg[:], sm[:, 9:12].unsqueeze(2).to_broadcast([NP, 3, 129]), ALU.mult), reads=[Rlg, Rsm], writes=[Rlg])
            o4 = self.sb(ph, [NP, 3, 128], F32, "o4")
            o4t = self.sb(ph, [NP, 3, 128], F32, "o4t")
            Ro4, Ro4t = Region("o4"), Region("o4t")
            for g3 in range(3):
                P.op("dve", lambda h, g3=g3: h.tensor_scalar(o4[:, g3, :], v4[:], lg[:, g3, 128:129], None, op0=ALU.mult), reads=[R4, Rlg], writes=[Ro4])
            for c in range(128 // KC):
                P.dma("sp", kvb[:].rearrange("p a b -> p (a b)"), I["cv"][l][:, c * KC * 128:(c + 1) * KC * 128], writes=[Rkvb])
                for g3 in range(3):
                    P.op("dve", lambda h, g3=g3, c=c: h.tensor_tensor(tmp[:].rearrange("p (d s) -> p d s", s=KC), kvb[:].rearrange("p s d -> p d s"),
                                                                      lg[:, g3, c * KC:(c + 1) * KC].unsqueeze(1).to_broadcast([NP, 128, KC]), ALU.mult),
                         reads=[Rkvb, Rlg], writes=[Rtmp])
                    P.op("dve", lambda h, g3=g3: h.tensor_reduce(o4t[:, g3, :], tmp[:].rearrange("p (d s) -> p d s", s=KC), AX.X, ALU.add), reads=[Rtmp], writes=[Ro4t])
                    P.op("dve", lambda h, g3=g3: h.tensor_tensor(o4[:, g3, :], o4[:, g3, :], o4t[:, g3, :], ALU.add), reads=[Ro4, Ro4t], writes=[Ro4])
            P.dma("sp", S["bo"].rearrange("b (kv x) -> (b kv) x", kv=KV), o4[:].rearrange("p a b -> p (a b)"), reads=[Ro4], writes=[Rb])
            ao = self.sb(ph, [NS, 768], F32, "ao")
            yb = self.sb(ph, [NS, 768], BF16, "syb")
            Rao, Ryb = Region("ao"), Region("syb")
            P.dma("sp", ao[:], S["bo"], reads=[Rb], writes=[Rao])
            P.op("dve", lambda h: h.tensor_tensor(yb[:], ao[:], za[:], ALU.mult), reads=[Rao, Rza], writes=[Ryb])
            self.s_to_yT(yb, Ryb, 0, 6)
            self.barrier()

    def s_phase_M(self):
        P, l, I, S, O = self.P, self.l, self.I, self.S, self.O
        NH = HM
        QW, VW = NH * 128, NH * 256
        with contextlib.ExitStack() as ph:
            q = self.sb(ph, [NS, QW], F32, "smq")
            k = self.sb(ph, [NS, QW], F32, "smk")
            v = self.sb(ph, [NS, VW], BF16, "smv")
            G = self.sb(ph, [NS, VW], F32, "smG")
            gts = self.sb(ph, [NS, 2 * NH], F32, "smg")
            tm = self.sb(ph, [NS, 256], F32, "smt")
            Rq, Rk, Rv, RG, Rg, Rtm = (Region(n) for n in ("smq", "smk", "smv", "smG", "smg", "smt"))
            for g, wb, Rw in self.wstream("in", list(range(G_MQK, G_CU)) + [G_GATE]):
                def evac(acc, Racc, g=g):
                    if g == G_GATE:
                        P.op("dve", lambda h: h.tensor_tensor(gts[:], acc[0:NS, 0:2 * NH], self.bifb[0:NS, l * 2 * NH:(l + 1) * 2 * NH], ALU.add), reads=[Racc, self.Rbifb], writes=[Rg])
                    elif g < G_MV:
                        for j in range(2):
                            ch = (g - G_MQK) * 2 + j
                            if ch < NH:
                                P.op("act", lambda h, ch=ch, j=j: h.copy(q[:, ch * 128:(ch + 1) * 128], acc[0:NS, j * 128:(j + 1) * 128]), reads=[Racc], writes=[Rq])
                            else:
                                P.op("act", lambda h, ch=ch, j=j: h.activation(k[:, (ch - NH) * 128:(ch - NH + 1) * 128], acc[0:NS, j * 128:(j + 1) * 128], AF.Copy, scale=QSCALE),
                                     reads=[Racc], writes=[Rk])
                    elif g < G_MO:
                        c0 = (g - G_MV) * 256
                        P.op("dve", lambda h: h.tensor_copy(v[:, c0:c0 + 256], acc[0:NS, 0:256]), reads=[Racc], writes=[Rv])
                    elif g < G_MZ:
                        c0 = (g - G_MO) * 256
                        P.op("act", lambda h: h.activation(G[:, c0:c0 + 256], acc[0:NS, 0:256], AF.Sigmoid), reads=[Racc], writes=[RG])
                    else:
                        c0 = (g - G_MZ) * 256
                        P.op("act", lambda h: h.activation(tm[:], acc[0:NS, 0:256], AF.Silu), reads=[Racc], writes=[Rtm])
                        P.op("dve", lambda h: h.tensor_tensor(G[:, c0:c0 + 256], G[:, c0:c0 + 256], tm[:], ALU.mult), reads=[Rtm, RG], writes=[RG])
                self.s_proj(wb, Rw, 2 * NH if g == G_GATE else 256, evac)
            sm = self.sb(ph, [NS, 96], F32, "smsm")
            Rsm = Region("smsm")
            n0 = self.sb(ph, [NS, QW], F32, "smn")
            Rn0 = Region("smn")
            P.dma("sp", n0[:], I["sn"][l], writes=[Rn0])
            P.dma("sp", sm[:, 0:NH], I["sm"][l], writes=[Rsm])
            with contextlib.ExitStack() as ph2:
                mhgb = self.sb(ph2, [NS, VW], F32, "mhgb")
                Rmh = Region("mhgb")
                P.dma("sp", mhgb[:], I["mhg"][l].partition_broadcast(NS), writes=[Rmh])
                P.op("dve", lambda h, mhgb=mhgb: h.tensor_tensor(G[:], G[:], mhgb[:], ALU.mult), reads=[RG, Rmh], writes=[RG])
                self.barrier()
            c_m0, c_sp, c_int, c_mt, c_wq, c_wi, c_qk, c_w, c_qn, c_den, c_t, c_rd, c_ssq, c_f, c_emt = [6 * i for i in range(15)]
            s_ = lambda c: sm[:, c:c + NH]
            ip, fp = gts[:, 0:NH], gts[:, NH:2 * NH]
            P.op("act", lambda h: h.activation(s_(c_sp), fp, AF.Exp, scale=-1.0), reads=[Rg], writes=[Rsm])
            P.op("act", lambda h: h.activation(s_(c_sp), s_(c_sp), AF.Ln, bias=1.0), reads=[Rsm], writes=[Rsm])
            P.op("dve", lambda h: h.tensor_tensor(s_(c_int), s_(c_m0), s_(c_sp), ALU.subtract), reads=[Rsm], writes=[Rsm])
            P.op("dve", lambda h: h.tensor_tensor(s_(c_mt), s_(c_int), ip, ALU.max), reads=[Rsm, Rg], writes=[Rsm])
            P.op("dve", lambda h: h.tensor_tensor(s_(c_wq), ip, s_(c_mt), ALU.subtract), reads=[Rsm, Rg], writes=[Rsm])
            P.op("dve", lambda h: h.tensor_tensor(s_(c_wi), s_(c_int), s_(c_mt), ALU.subtract), reads=[Rsm], writes=[Rsm])
            P.op("act", lambda h: h.activation(s_(c_wq), s_(c_wq), AF.Exp), reads=[Rsm], writes=[Rsm])
            P.op("act", lambda h: h.activation(s_(c_wi), s_(c_wi), AF.Exp), reads=[Rsm], writes=[Rsm])
            P.op("act", lambda h: h.activation(s_(c_emt), s_(c_mt), AF.Exp, scale=-1.0), reads=[Rsm], writes=[Rsm])
            P.dma("sp", O["ms"][l], s_(c_mt), reads=[Rsm], writes=[self.R["ms"]])
            big = self.sb(ph, [NS, VW], F32, "smbig")
            Rbig = Region("smbig")
            bc6 = lambda ap, n: ap.unsqueeze(2).to_broadcast([NS, NH, n])
            q3 = q[:].rearrange("p (a b) -> p a b", a=NH)
            k3 = k[:].rearrange("p (a b) -> p a b", a=NH)
            n3 = n0[:].rearrange("p (a b) -> p a b", a=NH)
            b3 = big[:, 0:QW].rearrange("p (a b) -> p a b", a=NH)
            P.op("dve", lambda h: h.tensor_tensor(b3, q3, k3, ALU.mult), reads=[Rq, Rk], writes=[Rbig])
            P.op("dve", lambda h: h.tensor_reduce(s_(c_qk), b3, AX.X, ALU.add), reads=[Rbig], writes=[Rsm])
            P.op("dve", lambda h: h.tensor_tensor(b3, q3, n3, ALU.mult), reads=[Rq, Rn0], writes=[Rbig])
            P.op("dve", lambda h: h.tensor_reduce(s_(c_qn), b3, AX.X, ALU.add), reads=[Rbig], writes=[Rsm])
            P.op("dve", lambda h: h.tensor_tensor(s_(c_w), s_(c_wq), s_(c_qk), ALU.mult), reads=[Rsm], writes=[Rsm])
            P.op("dve", lambda h: h.tensor_tensor(s_(c_den), s_(c_wi), s_(c_qn), ALU.mult), reads=[Rsm], writes=[Rsm])
            P.op("dve", lambda h: h.tensor_tensor(s_(c_den), s_(c_den), s_(c_w), ALU.add), reads=[Rsm], writes=[Rsm])
            P.op("dve", lambda h: h.scalar_tensor_tensor(s_(c_t), s_(c_den), -1.0, s_(c_den), op0=ALU.mult, op1=ALU.max), reads=[Rsm], writes=[Rsm])
            P.op("dve", lambda h: h.tensor_tensor(s_(c_t), s_(c_t), s_(c_emt), ALU.max), reads=[Rsm], writes=[Rsm])
            P.op("dve", lambda h: h.reciprocal(s_(c_rd), s_(c_t)), reads=[Rsm], writes=[Rsm])
            ksc = self.sb(ph, [NS, QW], F32, "smksc")
            Rksc = Region("smksc")
            ksc3 = ksc[:].rearrange("p (a b) -> p a b", a=NH)
            P.op("dve", lambda h: h.tensor_tensor(ksc3, k3, bc6(s_(c_wq), 128), ALU.mult), reads=[Rk, Rsm], writes=[Rksc])
            P.op("dve", lambda h: h.tensor_tensor(n3, n3, bc6(s_(c_wi), 128), ALU.mult), reads=[Rn0, Rsm], writes=[Rn0])
            P.op("dve", lambda h: h.tensor_tensor(n0[:], n0[:], ksc[:], ALU.add), reads=[Rn0, Rksc], writes=[Rn0])
            P.dma("sp", O["ns"][l], n0[:], reads=[Rn0], writes=[self.R["ns"]])
            qT = self.sb(ph, [128, NH, NS], F32, "smqT")
            RqT = Region("smqT")
            pq, Rpq = self.pb[0], self.Rpb[0]
            fns = [(lambda h, hh=hh: h.transpose(pq[:, hh * NS:(hh + 1) * NS], q[0:NS, hh * 128:(hh + 1) * 128], self.identf[0:NS, 0:NS])) for hh in range(NH)]
            P.group("pe", fns, reads=[Rq, self.Ridf], writes=[Rpq])
            P.op("act", lambda h: h.copy(qT[:].rearrange("p a b -> p (a b)"), pq[:, 0:NH * NS]), reads=[Rpq], writes=[RqT])
            i32 = self.sb(ph, [128, NS, NS], F32, "i32")
            Ri32 = Region("i32")
            P.dma("sp", i32[:].rearrange("p a b -> p (a b)"), I["c_i32"], writes=[Ri32])
            Wd = self.sb(ph, [NS, NS, NH], F32, "smWd")
            RWd = Region("smWd")
            P.op("dve", lambda h: h.tensor_tensor(Wd[:], self.identf[0:NS, 0:NS].unsqueeze(2).to_broadcast([NS, NS, NH]), s_(c_wi).unsqueeze(1).to_broadcast([NS, NS, NH]), ALU.mult),
                 reads=[Rsm, self.Ridf], writes=[RWd])
            pw, Rpw = self.pb[1], self.Rpb[1]
            P.op("pe", lambda h: h.matmul(pw[:, 0:NS * NH], self.onesf[0:NS, :], Wd[:].rearrange("p a b -> p (a b)"), start=True, stop=True), reads=[RWd, self.Ronesf], writes=[Rpw])
            wcb = self.sb(ph, [128, NS, NH], F32, "smwcb")
            Rwcb = Region("smwcb")
            P.op("act", lambda h: h.copy(wcb[:].rearrange("p a b -> p (a b)"), pw[:, 0:NS * NH]), reads=[Rpw], writes=[Rwcb])
            vb, Rvb = v, Rv
            Qm = self.sb(ph, [128, NS, NS], F32, "smQm")
            Km = self.sb(ph, [NS, NS, 128], BF16, "smKm")
            RQm, RKm = Region("smQm"), Region("smKm")
            CB = 8
            Cc = self.sb(ph, [128, CB, 256], F32, "smCc")
            RCc = Region("smCc")
            pR = [self.pb[4], self.pb[5]]
            RpR = [self.Rpb[4], self.Rpb[5]]
            pD = [self.pb[2], self.pb[3]]
            RpD = [self.Rpb[2], self.Rpb[3]]

            def do_head(hh):
                P.op("dve", lambda h: h.tensor_tensor(Qm[:], qT[:, hh, :].unsqueeze(2).to_broadcast([128, NS, NS]), i32[:], ALU.mult), reads=[RqT, Ri32], writes=[RQm])
                P.op("dve", lambda h: h.tensor_tensor(Km[:], self.identf[0:NS, 0:NS].unsqueeze(2).to_broadcast([NS, NS, 128]),
                                                      ksc[:, hh * 128:(hh + 1) * 128].unsqueeze(1).to_broadcast([NS, NS, 128]), ALU.mult),
                     reads=[Rksc, self.Ridf], writes=[RKm])
                oR = pR[hh // 2][0:NS, (hh % 2) * 256:(hh % 2) * 256 + 256]
                for cb in range(NS // CB):
                    P.dma("sp", Cc[:], I["sC"][l, cb * CB:(cb + 1) * CB, hh].rearrange("b d v -> d b v"), writes=[RCc])
                    for j in range(CB):
                        bp = cb * CB + j
                        P.op("pe", lambda h, j=j, bp=bp: h.matmul(oR, Qm[:, bp, :], Cc[:, j, :], start=(bp == 0), stop=(bp == NS - 1)),
                             reads=[RQm, RCc], writes=[RpR[hh // 2]])
                    for j in range(CB):
                        bp = cb * CB + j
                        pd, Rpd = pD[j % 2], RpD[j % 2]
                        P.op("pe", lambda h, j=j, bp=bp, pd=pd: h.matmul(pd[:, 0:256], Km[:, bp, :], vb[:, hh * 256:(hh + 1) * 256], start=True, stop=True),
                             reads=[RKm, Rvb], writes=[Rpd])
                        P.op("dve", lambda h, j=j, bp=bp, pd=pd: h.scalar_tensor_tensor(Cc[:, j, :], Cc[:, j, :], wcb[:, bp, hh:hh + 1], pd[:, 0:256], op0=ALU.mult, op1=ALU.add),
                             reads=[Rpd, Rwcb, RCc], writes=[RCc])
                    P.dma("sp", O["Cs"][l, cb * CB:(cb + 1) * CB, hh].rearrange("b d v -> d b v"), Cc[:], reads=[RCc], writes=[self.R["Cs"]])
            for hh in range(NH):
                do_head(hh)
            num = big
            junk = self.sb(ph, [NS, 256], F32, "smjunk")
            Rjunk = Region("smjunk")
            for hh in range(NH):
                sl = slice(hh * 256, (hh + 1) * 256)
                P.op("dve", lambda h, hh=hh, sl=sl: h.tensor_scalar(num[:, sl], v[:, sl], sm[:, c_w + hh:c_w + hh + 1], None, op0=ALU.mult), reads=[Rv, Rsm], writes=[Rbig])
                P.op("dve", lambda h, hh=hh, sl=sl: h.scalar_tensor_tensor(num[:, sl], pR[hh // 2][0:NS, (hh % 2) * 256:(hh % 2) * 256 + 256], sm[:, c_wi + hh:c_wi + hh + 1],
                                                                           num[:, sl], op0=ALU.mult, op1=ALU.add),
                     reads=[RpR[hh // 2], Rsm, Rbig], writes=[Rbig])
                P.op("act", lambda h, hh=hh, sl=sl: h.activation(junk[:], num[:, sl], AF.Square, accum_out=sm[:, c_ssq + hh:c_ssq + hh + 1]), reads=[Rbig, Rsm], writes=[Rjunk, Rsm])
            P.op("dve", lambda h: h.tensor_tensor(s_(c_t), s_(c_rd), s_(c_rd), ALU.mult), reads=[Rsm], writes=[Rsm])
            P.op("dve", lambda h: h.tensor_tensor(s_(c_t), s_(c_t), s_(c_ssq), ALU.mult), reads=[Rsm], writes=[Rsm])
            P.op("act", lambda h: h.activation(s_(c_t), s_(c_t), AF.Sqrt, bias=EPS, scale=1.0 / 256.0), reads=[Rsm], writes=[Rsm])
            P.op("dve", lambda h: h.reciprocal(s_(c_f), s_(c_t)), reads=[Rsm], writes=[Rsm])
            P.op("dve", lambda h: h.tensor_tensor(s_(c_f), s_(c_f), s_(c_rd), ALU.mult), reads=[Rsm], writes=[Rsm])
            num3 = num[:].rearrange("p (a b) -> p a b", a=NH)
            P.op("dve", lambda h: h.tensor_tensor(num3, num3, bc6(s_(c_f), 256), ALU.mult), reads=[Rbig, Rsm], writes=[Rbig])
            yb = self.sb(ph, [NS, VW], BF16, "smyb")
            Ryb = Region("smyb")
            P.op("dve", lambda h: h.tensor_tensor(yb[:], num[:], G[:], ALU.mult), reads=[Rbig, RG], writes=[Ryb])
            self.s_to_yT(yb, Ryb, 6, 6)
            self.barrier()

    def s_phase_C(self):
        P, l, I, S, O = self.P, self.l, self.I, self.S, self.O
        CW = GC * 128
        with contextlib.ExitStack() as ph:
            u = self.sb(ph, [NS, CW], F32, "scu")
            vv = self.sb(ph, [NS, 1024], F32, "scv")
            tm = self.sb(ph, [NS, 256], F32, "sct")
            Ru, Rvv, Rtm = Region("scu"), Region("scv"), Region("sct")
            for g, wb, Rw in self.wstream("in", list(range(G_CU, G_GATE))):
                def evac(acc, Racc, g=g):
                    if g < G_CV:
                        c0 = (g - G_CU) * 256
                        P.op("act", lambda h: h.activation(u[:, c0:c0 + 256], acc[0:NS, 0:256], AF.Gelu), reads=[Racc], writes=[Ru])
                    elif g < G_CZ:
                        c0 = (g - G_CV) * 256
                        P.op("act", lambda h: h.activation(vv[:, c0:c0 + 256], acc[0:NS, 0:256], AF.Gelu), reads=[Racc], writes=[Rvv])
                    else:
                        c0 = (g - G_CZ) * 256
                        P.op("act", lambda h: h.activation(tm[:], acc[0:NS, 0:256], AF.Silu), reads=[Racc], writes=[Rtm])
                        P.op("dve", lambda h: h.tensor_tensor(u[:, c0:c0 + 256], u[:, c0:c0 + 256], tm[:], ALU.mult), reads=[Rtm, Ru], writes=[Ru])
                self.s_proj(wb, Rw, 256, evac)
            junk = self.sb(ph, [NS, 1024], BF16, "scj")
            sm = self.sb(ph, [NS, 16], F32, "scsm")
            Rj, Rsm = Region("scj"), Region("scsm")
            self.layernorm(vv, Rvv, NS, junk, Rj, sm, Rsm)
            P.dma("sp", O["cvs"][l], vv[:], reads=[Rvv], writes=[self.R["cvs"]])
            wb8 = self.sb(ph, [NS, 16], F32, "scw8")
            Rw8 = Region("scw8")
            P.dma("sp", wb8[:, 0:GC], I["ws00"][l * GC:(l + 1) * GC].partition_broadcast(NS), writes=[Rw8])
            P.dma("sp", wb8[:, 8:8 + GC], I["bs0"][l * GC:(l + 1) * GC].partition_broadcast(NS), writes=[Rw8])
            v3 = vv[:, 0:CW].rearrange("p (a b) -> p a b", a=GC)
            P.op("dve", lambda h: h.tensor_tensor(v3, v3, wb8[:, 0:GC].unsqueeze(2).to_broadcast([NS, GC, 128]), ALU.mult), reads=[Rvv, Rw8], writes=[Rvv])
            P.op("dve", lambda h: h.tensor_tensor(v3, v3, wb8[:, 8:8 + GC].unsqueeze(2).to_broadcast([NS, GC, 128]), ALU.add), reads=[Rvv, Rw8], writes=[Rvv])
            yb = self.sb(ph, [NS, CW], BF16, "scyb")
            Ryb = Region("scyb")
            P.op("dve", lambda h: h.tensor_tensor(yb[:], vv[:, 0:CW], u[:], ALU.mult), reads=[Rvv, Ru], writes=[Ryb])
            self.s_to_yT(yb, Ryb, 12, GC)
            self.barrier()


def _consts():
    c = {}
    c["c_ident"] = np.eye(128, dtype=np.float32)
    q = np.arange(128)[:, None]
    s = np.arange(256)[None, :]
    dist = q + 128 - s
    valid = (dist >= 0) & (dist <= 128)
    c["c_dm"] = np.where(valid, dist, 1.0e9).astype(np.float32)
    c["c_dm0"] = c["c_dm"].copy()
    t = np.arange(128)[:, None]
    s2 = np.arange(128)[None, :]
    ok = ((t // 64) == (s2 // 64)) & (s2 <= t)
    m1 = np.where(ok, 0.0, -30000.0).astype(np.float32)
    c["c_mask6"] = np.tile(m1, (1, 3))
    c["c_tri2"] = ok.T.astype(np.float32).copy()
    onesc = np.zeros((128, 2, 128), np.float32)
    onesc[0:64, 0, :] = 1.0
    onesc[64:128, 1, :] = 1.0
    c["c_onesc"] = onesc.reshape(128, 256)
    cm = np.zeros((128, 2), np.float32)
    cm[0:64, 0] = 1.0
    cm[64:128, 1] = 1.0
    c["c_cmask"] = cm
    c["c_tril"] = (np.arange(128)[None, :] >= np.arange(128)[:, None]).astype(np.float32)
    c["c_dist"] = np.concatenate([np.arange(128, 0, -1), [0]]).astype(np.float32)
    c["c_i32"] = np.tile(np.eye(NS, dtype=np.float32).reshape(1, NS * NS), (128, 1))
    return c


_NC_CACHE = {}
_OFF = dict(aq=0, ak=1536, av=2048, az=2560, mq=4096, mk=4864, mv=5632, mi=7168, mf=7174, mo=7180, mz=8716, cu=10252, cv=11276, cz=12300)


def _cols(c):
    r = lambda base, n, w: list(range(base + c * w, base + c * w + w)) if n is None else None
    idx = []
    idx += list(range(_OFF["aq"] + c * 768, _OFF["aq"] + (c + 1) * 768))
    idx += list(range(_OFF["ak"] + c * 256, _OFF["ak"] + (c + 1) * 256))
    idx += list(range(_OFF["av"] + c * 256, _OFF["av"] + (c + 1) * 256))
    idx += list(range(_OFF["az"] + c * 768, _OFF["az"] + (c + 1) * 768))
    idx += list(range(_OFF["mq"] + c * 384, _OFF["mq"] + (c + 1) * 384))
    idx += list(range(_OFF["mk"] + c * 384, _OFF["mk"] + (c + 1) * 384))
    idx += list(range(_OFF["mv"] + c * 768, _OFF["mv"] + (c + 1) * 768))
    idx += list(range(_OFF["mo"] + c * 768, _OFF["mo"] + (c + 1) * 768))
    idx += list(range(_OFF["mz"] + c * 768, _OFF["mz"] + (c + 1) * 768))
    idx += list(range(_OFF["cu"] + c * 512, _OFF["cu"] + (c + 1) * 512))
    idx += list(range(_OFF["cv"] + c * 512, _OFF["cv"] + (c + 1) * 512))
    idx += list(range(_OFF["cv"] + (1 - c) * 512, _OFF["cv"] + (2 - c) * 512))
    idx += list(range(_OFF["cz"] + c * 512, _OFF["cz"] + (c + 1) * 512))
    idx += list(range(_OFF["mi"] + c * 3, _OFF["mi"] + (c + 1) * 3))
    idx += list(range(_OFF["mf"] + c * 3, _OFF["mf"] + (c + 1) * 3))
    return np.array(idx)


def kernel(x_prompt, x_sample, cache_win_k, cache_win_v, state_mlstm_C, state_mlstm_n, state_mlstm_m,
           norm_gain, w_in, b_if, attn_sinks, m_head_gain, c_ln_gain, c_ln_bias, c_w_s, c_b_s, w_out, final_gain, _ncores=8):
    f = lambda a: np.ascontiguousarray(np.asarray(a, dtype=np.float32))
    w_in, w_out = f(w_in), f(w_out)
    xp = f(x_prompt)
    base = dict(_consts())
    base["xs"] = f(x_sample).reshape(NS, D)
    base["gainT"] = np.ascontiguousarray(f(norm_gain).reshape(2, 32, 128).transpose(0, 2, 1))
    base["fgain"] = f(final_gain)
    half = []
    for c in range(2):
        m = {}
        idx = _cols(c)
        wp = np.zeros((2, D, NG_IN * 256), np.float32)
        wp[:, :, 0:idx.size] = w_in[:, :, idx]
        m["w_in"] = np.ascontiguousarray(wp.reshape(2, 32, 128, NG_IN, 256).transpose(0, 3, 2, 1, 4)).reshape(2, NG_IN, 128, 32 * 256)
        rows = np.concatenate([np.arange(c * 768, (c + 1) * 768), 1536 + np.arange(c * 768, (c + 1) * 768), 3072 + np.arange(c * 512, (c + 1) * 512)])
        wo = w_out[:, rows, :]
        m["w_out"] = np.ascontiguousarray(wo.reshape(2, KY, 128, NG_OUT, 512).transpose(0, 3, 2, 1, 4)).reshape(2, NG_OUT, 128, KY * 512)
        bi = f(b_if)
        m["bif"] = np.ascontiguousarray(bi[:, :, c * 3:(c + 1) * 3]).reshape(12)
        sk = f(attn_sinks)[:, c * 6:(c + 1) * 6]
        m["sinks"] = np.ascontiguousarray(sk).reshape(12)
        m["nslp"] = -np.array(SLOPES[c * 6:(c + 1) * 6], np.float32)
        mh = f(m_head_gain)[:, c * 768:(c + 1) * 768]
        m["mhg"] = np.ascontiguousarray(mh)
        m["mhgT"] = np.ascontiguousarray(mh.reshape(2, 6, 128).transpose(0, 2, 1))
        perm = np.concatenate([np.arange(c * 512, (c + 1) * 512), np.arange((1 - c) * 512, (2 - c) * 512)])
        m["lng"] = np.ascontiguousarray(f(c_ln_gain)[:, perm])
        m["lnb"] = np.ascontiguousarray(f(c_ln_bias)[:, perm])
        ws = f(c_w_s)[:, c * 4:(c + 1) * 4]
        m["wsT"] = np.ascontiguousarray(ws.transpose(0, 3, 1, 2)).reshape(2, 128, GC * 128)
        bsv = f(c_b_s)[:, c * 4:(c + 1) * 4]
        m["bs"] = np.ascontiguousarray(bsv).reshape(2, GC * 128)
        m["ws00"] = np.ascontiguousarray(ws[:, :, 0, 0]).reshape(2 * GC)
        m["bs0"] = np.ascontiguousarray(bsv[:, :, 0]).reshape(2 * GC)
        ck = f(cache_win_k)[:, :, :, c * 2:(c + 1) * 2]
        cvv = f(cache_win_v)[:, :, :, c * 2:(c + 1) * 2]
        m["ck"] = np.ascontiguousarray(ck.transpose(0, 1, 3, 2, 4)).reshape(2, 64, 128 * 128)
        m["cv"] = np.ascontiguousarray(cvv.transpose(0, 1, 3, 2, 4)).reshape(2, 64, 128 * 128)
        m["sk4"] = np.ascontiguousarray(np.tile(sk.reshape(2, 1, 2, 3), (1, 32, 1, 1)).reshape(2, 64, 3))
        m["c_slp"] = np.tile(np.array(SLOPES[c * 6:(c + 1) * 6], np.float32).reshape(2, 3), (32, 1))
        m["sC"] = np.ascontiguousarray(f(state_mlstm_C)[:, :, c * 3:(c + 1) * 3])
        m["sn"] = np.ascontiguousarray(f(state_mlstm_n)[:, :, c * 3:(c + 1) * 3]).reshape(2, NS, HM * 128)
        m["sm"] = np.ascontiguousarray(f(state_mlstm_m)[:, :, c * 3:(c + 1) * 3])
        half.append(m)
    in_maps = []
    for core in range(_ncores):
        m = dict(base)
        m.update(half[core % 2])
        m["xp"] = xp[(core // 2) % 4]
        in_maps.append(m)
    if "nc" not in _NC_CACHE:
        _NC_CACHE["nc"] = KB().build()
    nc = _NC_CACHE["nc"]
    res = run_bass_kernel_spmd(nc, in_maps, core_ids=list(range(_ncores)))
    r = list(res.results)
    while len(r) < 8:
        r = r + r[0:2]
    pr = lambda b: (r[2 * b], r[2 * b + 1])
    y_prompt = np.stack([r[2 * b]["yp"] for b in range(4)]).reshape(4, SEQ, D)
    y_sample = r[0]["ys"].reshape(NS, 1, D)
    cat = lambda key, b, shp, ax: np.concatenate([pr(b)[0][key].reshape(shp), pr(b)[1][key].reshape(shp)], axis=ax)
    kp = np.stack([cat("kp", b, (2, 128, 2, 128), 2) for b in range(4)], axis=1)
    vp = np.stack([cat("vp", b, (2, 128, 2, 128), 2) for b in range(4)], axis=1)
    ks = cat("ks", 0, (2, NS, 1, 2, 128), 3)
    vs = cat("vs", 0, (2, NS, 1, 2, 128), 3)
    Cp = np.stack([np.concatenate([x["Cp"].reshape(2, 128, 3, 256).transpose(0, 2, 1, 3) for x in pr(b)], axis=1) for b in range(4)], axis=1)
    npp = np.stack([np.concatenate([x["np"].transpose(0, 2, 1) for x in pr(b)], axis=1) for b in range(4)], axis=1)
    mp = np.stack([np.concatenate([x["mp"].reshape(2, 3) for x in pr(b)], axis=1) for b in range(4)], axis=1)
    Cs = np.concatenate([r[0]["Cs"], r[1]["Cs"]], axis=2)
    ns = np.concatenate([r[0]["ns"].reshape(2, NS, 3, 128), r[1]["ns"].reshape(2, NS, 3, 128)], axis=2)
    ms = np.concatenate([r[0]["ms"], r[1]["ms"]], axis=2)
    cvp = np.stack([r[2 * b]["cvp"] for b in range(4)], axis=1)
    cvs = r[0]["cvs"].reshape(2, NS, 1, 1024)
    outs = (y_prompt, y_sample, kp, vp, ks, vs, Cp, npp, mp, Cs, ns, ms, cvp, cvs)
    return tuple(np.ascontiguousarray(o, dtype=np.float32) for o in outs)
```

```python
import contextlib
import numpy as np
import concourse.bass as bass
import concourse.mybir as mybir
from concourse.bass_utils import run_bass_kernel_spmd

F32 = mybir.dt.float32
BF16 = mybir.dt.bfloat16
ALU = mybir.AluOpType
AF = mybir.ActivationFunctionType
AX = mybir.AxisListType

D = 4096
SEQ = 2048
NS = 32
DIN = 13324
NG_IN = 29
NG_OUT = 8
TT = 512
NBLK = TT // 128
NTILE = SEQ // TT
EPS = 1e-6
SLOPES = [2.0 ** (-8.0 * (h + 1) / 12.0) for h in range(12)]
QSCALE = 128.0 ** -0.5
G_AQ, G_AK, G_AV, G_AZ = 0, 3, 4, 5
G_MQK, G_MV, G_MO, G_MZ = 8, 11, 14, 17
G_CU, G_CV, G_CZ, G_GATE = 20, 22, 26, 28
HA, KV, HM, GC = 6, 2, 3, 4
KY = 16
PAIRS = [[0, 1], [2, 3], [4, 5], [6, 7]]
SBUF_LIMIT = 182 * 1024


class Region:
    __slots__ = ("name", "last_w", "readers", "excl")

    def __init__(self, name, excl=False):
        self.name = name
        self.last_w = None
        self.readers = []
        self.excl = excl


class EngineCtx:
    def __init__(self, name, sem):
        self.name = name
        self.sem = sem
        self.count = 0
        self.known = {}
        self.instrs = []


class Prog:
    def __init__(self, nc, stack, n_dma_sems=24):
        self.nc = nc
        self.eng = {}
        self.sems = {}
        for name in ("pe", "act", "dve", "pool", "sp"):
            sem = stack.enter_context(nc.semaphore("s_" + name))
            self.eng[name] = EngineCtx(name, sem)
            self.sems["e_" + name] = sem
        self.dma_pool = {}
        for q in ("sp", "pool"):
            lst = []
            for i in range(n_dma_sems if q == "sp" else 3):
                key = "d_%s_%d" % (q, i)
                self.sems[key] = stack.enter_context(nc.semaphore(key))
                lst.append([key, 0])
            self.dma_pool[q] = [lst, 0]
        self.n_instr = 0

    def _need(self, e, tok, waits):
        if tok is None:
            return
        key, val, src = tok
        if src == "pe" and e.name == "pe":
            return
        if e.known.get(key, 0) >= val:
            return
        if waits.get(key, 0) < val:
            waits[key] = val

    def _deps(self, e, reads, writes):
        waits = {}
        for r in reads:
            if r.excl:
                writes = list(writes) + [r]
                continue
            self._need(e, r.last_w, waits)
        for r in writes:
            self._need(e, r.last_w, waits)
            for t in r.readers:
                self._need(e, t, waits)
        return waits

    def _commit(self, tok, reads, writes):
        for r in reads:
            if r.excl:
                r.last_w = tok
                r.readers = []
                continue
            r.readers.append(tok)
            if len(r.readers) > 48:
                best = {}
                for t in r.readers:
                    if best.get(t[0], (None, -1))[1] < t[1]:
                        best[t[0]] = t
                r.readers = list(best.values())
        for r in writes:
            r.last_w = tok
            r.readers = []

    def _emit_waits(self, e, waits):
        for key, val in waits.items():
            e.instrs.append(("wait", self.sems[key], val))
            e.known[key] = val

    def op(self, engname, fn, reads=(), writes=()):
        return self.group(engname, [fn], reads, writes)

    def group(self, engname, fns, reads=(), writes=()):
        e = self.eng[engname]
        self._emit_waits(e, self._deps(e, reads, writes))
        e.count += 1
        tok = ("e_" + engname, e.count, engname)
        for fn in fns[:-1]:
            e.instrs.append(("op", fn, None, 0))
        e.instrs.append(("op", fns[-1], e.sem, 1))
        self._commit(tok, reads, writes)
        self.n_instr += len(fns)
        return tok

    def dma(self, qname, out, in_, reads=(), writes=()):
        e = self.eng[qname]
        waits = self._deps(e, reads, writes)
        pool, idx = self.dma_pool[qname]
        ent = pool[idx % len(pool)]
        self.dma_pool[qname][1] = idx + 1
        key, cum = ent
        if cum > 0 and e.known.get(key, 0) < cum and waits.get(key, 0) < cum:
            waits[key] = cum
        self._emit_waits(e, waits)
        ent[1] = cum + 16
        tok = (key, cum + 16, "dma")

        def fn(h, out=out, in_=in_):
            return h.dma_start(out=out, in_=in_)
        e.instrs.append(("op", fn, self.sems[key], 16))
        self._commit(tok, reads, writes)
        self.n_instr += 1
        return tok

    def coll(self, stack, ins_ap, outs_ap, reads=(), writes=()):
        e = self.eng["pool"]
        self._emit_waits(e, self._deps(e, reads, writes))
        key = "cc_%d" % len([k for k in self.sems if k.startswith("cc_")])
        sem = stack.enter_context(self.nc.semaphore(key))
        self.sems[key] = sem
        tok = (key, 1, "cc")

        def fn(h, ins_ap=ins_ap, outs_ap=outs_ap):
            return h.collective_compute("AllReduce", ALU.add, replica_groups=PAIRS, ins=[ins_ap], outs=[outs_ap])
        e.instrs.append(("op", fn, sem, 1))
        e.instrs.append(("wait", sem, 1))
        e.known[key] = 1
        self._commit(tok, reads, writes)
        return tok

    def wait_tok(self, engname, tok):
        e = self.eng[engname]
        waits = {}
        self._need(e, tok, waits)
        self._emit_waits(e, waits)

    def barrier(self, bar_out, bar_in, bar_region):
        sp = self.eng["sp"]
        waits = {}
        snap = {}
        for name in ("pe", "act", "dve"):
            c = self.eng[name].count
            snap["e_" + name] = c
            if c > 0:
                self._need(sp, ("e_" + name, c, name), waits)
        for key, cum in self.dma_pool["sp"][0]:
            snap[key] = cum
            if cum > 0:
                self._need(sp, (key, cum, "dma"), waits)
        self._emit_waits(sp, waits)
        tok = self.dma("sp", bar_out, bar_in, writes=[bar_region])
        for name in ("pe", "act", "dve"):
            self.wait_tok(name, tok)
            e = self.eng[name]
            for k, v in snap.items():
                if e.known.get(k, 0) < v:
                    e.known[k] = v

    def final_wait(self, engname, regions):
        e = self.eng[engname]
        waits = {}
        for r in regions:
            self._need(e, r.last_w, waits)
            for t in r.readers:
                self._need(e, t, waits)
        self._emit_waits(e, waits)

    def replay(self):
        nc = self.nc
        with nc.Block() as block:
            def mk(e):
                def body(h):
                    for ins in e.instrs:
                        if ins[0] == "wait":
                            h.wait_ge(ins[1], ins[2])
                        elif ins[2] is None:
                            ins[1](h)
                        else:
                            ins[1](h).then_inc(ins[2], ins[3])
                return body
            block.tensor(mk(self.eng["pe"]))
            block.scalar(mk(self.eng["act"]))
            block.vector(mk(self.eng["dve"]))
            block.gpsimd(mk(self.eng["pool"]))
            block.sync(mk(self.eng["sp"]))


class KB:
    def __init__(self, do_sample=True):
        self.do_sample = do_sample
        self.nc = bass.Bass("TRN2", target_bir_lowering=False)
        self.uid = 0
        self.sb_bytes = 0
        self.sb_peak = 0

    def sb(self, stack, shape, dt, name="t"):
        self.uid += 1
        n = 1
        for s in shape[1:]:
            n *= s
        nbytes = n * (4 if dt == F32 else 2)
        self.sb_bytes += nbytes
        self.sb_peak = max(self.sb_peak, self.sb_bytes)
        assert self.sb_bytes <= SBUF_LIMIT, ("SBUF overflow", name, self.sb_bytes)
        t = stack.enter_context(self.nc.sbuf_tensor("%s_%d" % (name, self.uid), list(shape), dt))

        def rel():
            self.sb_bytes -= nbytes
        stack.callback(rel)
        return t

    def dram_in(self, name, shape, dt=F32):
        return self.nc.dram_tensor(name, list(shape), dt, kind="ExternalInput").ap()

    def dram_out(self, name, shape, dt=F32):
        return self.nc.dram_tensor(name, list(shape), dt, kind="ExternalOutput").ap()

    def dram_scr(self, name, shape, dt=F32):
        return self.nc.dram_tensor(name, list(shape), dt, kind="Internal").ap()

    def build(self):
        nc = self.nc
        I = {}
        I["xp"] = self.dram_in("xp", [SEQ, D])
        I["xs"] = self.dram_in("xs", [NS, D])
        I["w_in"] = self.dram_in("w_in", [2, NG_IN, 128, 32 * 256])
        I["w_out"] = self.dram_in("w_out", [2, NG_OUT, 128, KY * 512])
        I["gainT"] = self.dram_in("gainT", [2, 128, 32])
        I["fgain"] = self.dram_in("fgain", [D])
        I["bif"] = self.dram_in("bif", [12])
        I["sinks"] = self.dram_in("sinks", [12])
        I["nslp"] = self.dram_in("nslp", [HA])
        I["mhgT"] = self.dram_in("mhgT", [2, 128, 6])
        I["mhg"] = self.dram_in("mhg", [2, 768])
        I["lng"] = self.dram_in("lng", [2, 1024])
        I["lnb"] = self.dram_in("lnb", [2, 1024])
        I["wsT"] = self.dram_in("wsT", [2, 128, GC * 128])
        I["bs"] = self.dram_in("bs", [2, GC * 128])
        I["ws00"] = self.dram_in("ws00", [2 * GC])
        I["bs0"] = self.dram_in("bs0", [2 * GC])
        I["ck"] = self.dram_in("ck", [2, 64, 128 * 128])
        I["cv"] = self.dram_in("cv", [2, 64, 128 * 128])
        I["sk4"] = self.dram_in("sk4", [2, 64, 3])
        I["c_slp"] = self.dram_in("c_slp", [64, 3])
        I["sC"] = self.dram_in("sC", [2, NS, HM, 128, 256])
        I["sn"] = self.dram_in("sn", [2, NS, HM * 128])
        I["sm"] = self.dram_in("sm", [2, NS, HM])
        I["c_ident"] = self.dram_in("c_ident", [128, 128])
        I["c_dm"] = self.dram_in("c_dm", [128, 256])
        I["c_dm0"] = self.dram_in("c_dm0", [128, 256])
        I["c_mask6"] = self.dram_in("c_mask6", [128, 384])
        I["c_tri2"] = self.dram_in("c_tri2", [128, 128])
        I["c_onesc"] = self.dram_in("c_onesc", [128, 256])
        I["c_cmask"] = self.dram_in("c_cmask", [128, 2])
        I["c_tril"] = self.dram_in("c_tril", [128, 128])
        I["c_dist"] = self.dram_in("c_dist", [129])
        I["c_i32"] = self.dram_in("c_i32", [128, NS * NS])
        O = {}
        O["yp"] = self.dram_out("yp", [SEQ, D])
        O["ys"] = self.dram_out("ys", [NS, D])
        O["kp"] = self.dram_out("kp", [2, 128, 256])
        O["vp"] = self.dram_out("vp", [2, 128, 256])
        O["ks"] = self.dram_out("ks", [2, NS, 256])
        O["vs"] = self.dram_out("vs", [2, NS, 256])
        O["Cp"] = self.dram_out("Cp", [2, 128, HM * 256])
        O["np"] = self.dram_out("np", [2, 128, HM])
        O["mp"] = self.dram_out("mp", [2, 1, HM])
        O["Cs"] = self.dram_out("Cs", [2, NS, HM, 128, 256])
        O["ns"] = self.dram_out("ns", [2, NS, HM * 128])
        O["ms"] = self.dram_out("ms", [2, NS, HM])
        O["cvp"] = self.dram_out("cvp", [2, 128, 1024])
        O["cvs"] = self.dram_out("cvs", [2, NS, 1024])
        self.I, self.O = I, O
        S = {}
        S["wb_in"] = self.dram_scr("wb_in", [2, NG_IN, 128, 32 * 256], BF16)
        S["wb_out"] = self.dram_scr("wb_out", [2, NG_OUT, 128, KY * 512], BF16)
        S["h1"] = self.dram_scr("h1", [SEQ, D])
        S["hs1"] = self.dram_scr("hs1", [NS, D])
        for l in range(2):
            S["po%d" % l] = self.dram_scr("po%d" % l, [SEQ, D])
            S["red%d" % l] = self.dram_scr("red%d" % l, [SEQ, D])
            S["pos%d" % l] = self.dram_scr("pos%d" % l, [NS, D])
            S["reds%d" % l] = self.dram_scr("reds%d" % l, [NS, D])
        S["bar"] = self.dram_scr("bar", [2, 16])
        S["bq"] = self.dram_scr("bq", [NS, 768])
        S["bk"] = self.dram_scr("bk", [NS, 256])
        S["bv"] = self.dram_scr("bv", [NS, 256])
        S["bo"] = self.dram_scr("bo", [NS, 768])
        self.S = S

        with contextlib.ExitStack() as st:
            self.st = st
            P = self.P = Prog(nc, st)
            self.R = {}
            for k in list(O.keys()) + ["h1", "hs1", "bar", "xin", "bnc"]:
                self.R[k] = Region(k)
            self.Rw = {}
            self.pb = [st.enter_context(nc.psum_tensor("pb%d" % i, [128, 512], F32)) for i in range(7)]
            self.Rpb = [Region("pb%d" % i, excl=True) for i in range(7)]
            self.tb = st.enter_context(nc.psum_tensor("tb0", [128, 1024], BF16))
            self.Rtb = Region("tb0", excl=True)
            self.acc_i = 0
            self.emit_all()
            P.replay()
        return nc

    def const(self, name, shape, dt, src_ap):
        t = self.sb(self.st, shape, dt, name)
        r = Region(name)
        self.P.dma("sp", t[:], src_ap, writes=[r])
        return t, r

    def barrier(self):
        self.P.barrier(self.S["bar"][0:1, :], self.S["bar"][1:2, :], self.R["bar"])

    def next_acc(self):
        i = self.acc_i % 4
        self.acc_i += 1
        return self.pb[i], self.Rpb[i]

    def convert_weights(self, l):
        P, S, I = self.P, self.S, self.I
        for g in range(NG_IN):
            r = Region("wbin%d_%d" % (l, g))
            self.Rw[("in", l, g)] = r
            P.dma("pool", S["wb_in"][l, g], I["w_in"][l, g], writes=[r])
        for g in range(NG_OUT):
            r = Region("wbout%d_%d" % (l, g))
            self.Rw[("out", l, g)] = r
            P.dma("pool", S["wb_out"][l, g], I["w_out"][l, g], writes=[r])

    def emit_all(self):
        P, nc, I, O, S, st = self.P, self.nc, self.I, self.O, self.S, self.st
        sbp = lambda shape, dt, name: self.sb(st, shape, dt, name)
        self.convert_weights(0)
        self.convert_weights(1)
        self.build_wseq()
        identf, Ridf = self.const("identf", [128, 128], F32, I["c_ident"])
        self.identf, self.Ridf = identf, Ridf
        identb = sbp([128, 128], BF16, "identb")
        Ridb = Region("identb")
        P.op("dve", lambda h: h.tensor_copy(identb[:], identf[:]), reads=[Ridf], writes=[Ridb])
        self.identb, self.Ridb = identb, Ridb
        self.dm, self.Rdm = self.const("dm", [128, 256], F32, I["c_dm"])
        self.dm0, self.Rdm0 = self.const("dm0", [128, 256], F32, I["c_dm0"])
        self.mask6, self.Rmask6 = self.const("mask6", [128, 384], F32, I["c_mask6"])
        self.tri2, self.Rtri2 = self.const("tri2", [128, 128], F32, I["c_tri2"])
        self.onesc, self.Ronesc = self.const("onesc", [128, 256], F32, I["c_onesc"])
        self.cmask, self.Rcmask = self.const("cmask", [128, 2], F32, I["c_cmask"])
        self.bifb, self.Rbifb = self.const("bifb", [128, 12], F32, I["bif"].partition_broadcast(128))
        self.sinkb, self.Rsinkb = self.const("sinkb", [128, 12], F32, I["sinks"].partition_broadcast(128))
        self.nslp, self.Rnslp = self.const("nslp", [128, HA], F32, I["nslp"].partition_broadcast(128))
        onesf = sbp([128, 128], F32, "onesf")
        self.Ronesf = Region("onesf")
        P.op("dve", lambda h: h.memset(onesf[:], 1.0), writes=[self.Ronesf])
        self.onesf = onesf
        onesb = sbp([128, 2], BF16, "onesb")
        self.Ronesb = Region("onesb")
        P.op("dve", lambda h: h.memset(onesb[:], 1.0), writes=[self.Ronesb])
        self.onesb = onesb
        tril, Rtril = self.const("tril", [128, 128], F32, I["c_tril"])
        self.hnT = sbp([128, 32, TT], BF16, "hnT")
        self.RhnT = Region("hnT")
        self.yT = sbp([128, KY, TT], BF16, "yT")
        self.RyT = [Region("yT_A"), Region("yT_M"), Region("yT_C")]
        self.wbuf = [sbp([128, 32 * 256], BF16, "wbuf") for _ in range(3)]
        self.Rwbuf = [Region("wbuf%d" % i) for i in range(3)]
        self.w_i = 0
        self.Cst = sbp([128, HM, 256], F32, "Cst")
        self.Cb = [sbp([128, HM, 256], BF16, "Cb") for _ in range(2)]
        self.nst = sbp([128, 8], F32, "nst")
        self.nb = [sbp([128, 8], BF16, "nb") for _ in range(2)]
        self.mst = sbp([128, 8], F32, "mst")
        self.RC = Region("Cst")
        self.RCb = [Region("Cb0"), Region("Cb1")]
        self.Rn = Region("nst")
        self.Rnb = [Region("nb0"), Region("nb1")]
        self.Rm = Region("mst")
        self.kcar = sbp([128, KV, 128], BF16, "kcar")
        self.vcar = sbp([128, KV * 128], BF16, "vcar")
        self.Rkcar, self.Rvcar = Region("kcar"), Region("vcar")
        self.gainT = sbp([128, 32], F32, "gainT")
        self.RgainT = Region("gainT")
        self.mhgT = sbp([128, 6], F32, "mhgT")
        self.RmhgT = Region("mhgT")
        self.wmT = sbp([128, GC, 128], BF16, "wmT")
        self.RwmT = Region("wmT")
        self.bsb = sbp([128, GC * 128], F32, "bsb")
        self.Rbsb = Region("bsb")
        self.lngb = sbp([128, 1024], F32, "lngb")
        self.lnbb = sbp([128, 1024], F32, "lnbb")
        self.Rln = Region("ln")
        self.Rpo = {}
        self.Rred = {}

        for l in range(2):
            self.l = l
            P.dma("sp", self.gainT[:], I["gainT"][l], writes=[self.RgainT])
            P.dma("sp", self.mhgT[:], I["mhgT"][l], writes=[self.RmhgT])
            P.dma("sp", self.bsb[:], I["bs"][l].partition_broadcast(128), writes=[self.Rbsb])
            P.dma("sp", self.lngb[:], I["lng"][l].partition_broadcast(128), writes=[self.Rln])
            P.dma("sp", self.lnbb[:], I["lnb"][l].partition_broadcast(128), writes=[self.Rln])
            with contextlib.ExitStack() as ph:
                wtmp = self.sb(ph, [128, GC, 128], F32, "wtmp")
                Rwtmp = Region("wtmp")
                P.dma("sp", wtmp[:].rearrange("p a b -> p (a b)"), I["wsT"][l], writes=[Rwtmp])
                P.op("dve", lambda h, wtmp=wtmp: h.tensor_tensor(self.wmT[:], wtmp[:], tril[:].unsqueeze(1).to_broadcast([128, GC, 128]), ALU.mult),
                     reads=[Rwtmp, Rtril], writes=[self.RwmT])
                self.barrier()
            P.op("dve", lambda h: h.memset(self.Cst[:], 0.0), writes=[self.RC])
            P.op("dve", lambda h: h.memset(self.Cb[0][:], 0.0), writes=[self.RCb[0]])
            P.op("dve", lambda h: h.memset(self.nst[:], 0.0), writes=[self.Rn])
            P.op("dve", lambda h: h.memset(self.nb[0][:], 0.0), writes=[self.Rnb[0]])
            P.op("dve", lambda h: h.memset(self.mst[:], 0.0), writes=[self.Rm])
            for t in range(NTILE):
                self.t = t
                rows = slice(t * TT, (t + 1) * TT)
                if l == 0:
                    self.phase_norm(I["xp"][rows, :], self.R["xin"], NBLK, 128)
                else:
                    self.phase_norm(I["xp"][rows, :], self.R["xin"], NBLK, 128,
                                    add=(S["red0"][rows, :], self.Rred[(0, t)]), store=(S["h1"][rows, :], self.R["h1"]))
                self.phase_A()
                self.phase_M()
                self.phase_C()
                rp = Region("po%d_%d" % (l, t))
                rr = Region("red%d_%d" % (l, t))
                self.Rpo[(l, t)], self.Rred[(l, t)] = rp, rr
                self.phase_O(S["po%d" % l][rows, :], rp, 128, NBLK)
                for cpart in range(TT // 256):
                    crow = slice(t * TT + cpart * 256, t * TT + (cpart + 1) * 256)
                    P.coll(st, S["po%d" % l][crow, :], S["red%d" % l][crow, :], reads=[rp], writes=[rr])
            P.dma("sp", O["Cp"][l], self.Cst[:].rearrange("p a b -> p (a b)"), reads=[self.RC], writes=[self.R["Cp"]])
            P.dma("sp", O["np"][l], self.nst[:, 0:HM], reads=[self.Rn], writes=[self.R["np"]])
            P.dma("sp", O["mp"][l], self.mst[0:1, 0:HM], reads=[self.Rm], writes=[self.R["mp"]])
            if self.do_sample:
                self.sample_layer()
        for t in range(NTILE):
            rows = slice(t * TT, (t + 1) * TT)
            self.phase_final(S["h1"][rows, :], self.R["h1"], S["red1"][rows, :], self.Rred[(1, t)], O["yp"][rows, :], self.R["yp"], 128, NBLK)
        if self.do_sample:
            self.phase_final(S["hs1"], self.R["hs1"], S["reds1"], self.Rred[(1, "s")], O["ys"], self.R["ys"], NS, 1)
        outs = [self.R[k] for k in O.keys()]
        P.final_wait("sp", outs)

    def build_wseq(self):
        per = ([("in", g) for g in range(G_AQ, G_MQK)] + [("in", g) for g in list(range(G_MQK, G_CU)) + [G_GATE]]
               + [("in", g) for g in range(G_CU, G_GATE)] + [("out", g) for g in range(NG_OUT)])
        seq = []
        for l in range(2):
            for _ in range(NTILE + (1 if self.do_sample else 0)):
                seq += [(k, l, g) for (k, g) in per]
        self.wseq = seq
        self.w_issued = 0
        self.w_cons = 0

    def wstream(self, kind, groups):
        for g in groups:
            idx = self.w_cons
            assert self.wseq[idx] == (kind, self.l, g), (self.wseq[idx], kind, self.l, g)
            while self.w_issued < min(len(self.wseq), idx + 3):
                k2, l2, g2 = self.wseq[self.w_issued]
                i2 = self.w_issued % 3
                src = self.S["wb_in"] if k2 == "in" else self.S["wb_out"]
                self.P.dma("sp", self.wbuf[i2][:], src[l2, g2], reads=[self.Rw[(k2, l2, g2)]], writes=[self.Rwbuf[i2]])
                self.w_issued += 1
            self.w_cons += 1
            i = idx % 3
            if kind == "in":
                yield g, self.wbuf[i][:].rearrange("p (k c) -> p k c", k=32), self.Rwbuf[i]
            else:
                yield g, self.wbuf[i][:].rearrange("p (k c) -> p k c", k=KY), self.Rwbuf[i]

    def mm_feat(self, wbuf, Rw, j, ntok):
        acc, Racc = self.next_acc()
        hnT = self.hnT
        fns = [(lambda h, kc=kc: h.matmul(acc[:, 0:ntok], wbuf[:, kc, j * 128:(j + 1) * 128], hnT[:, kc, 0:ntok],
                                          start=(kc == 0), stop=(kc == 31))) for kc in range(32)]
        self.P.group("pe", fns, reads=[Rw, self.RhnT], writes=[Racc])
        return acc, Racc

    def mm_tok(self, wbuf, Rw, t0, nt, ncols):
        acc, Racc = self.next_acc()
        hnT = self.hnT
        fns = [(lambda h, kc=kc: h.matmul(acc[0:nt, 0:ncols], hnT[:, kc, t0:t0 + nt], wbuf[:, kc, 0:ncols],
                                          start=(kc == 0), stop=(kc == 31))) for kc in range(32)]
        self.P.group("pe", fns, reads=[Rw, self.RhnT], writes=[Racc])
        return acc, Racc

    def phase_norm(self, src, Rsrc, nblk, np_, add=None, store=None):
        P = self.P
        with contextlib.ExitStack() as ph:
            hb0 = self.sb(ph, [128, D], F32, "hb")
            hb = [hb0, hb0]
            hb2 = self.sb(ph, [128, D], F32, "hb2") if add is not None else None
            hn = self.sb(ph, [128, D], BF16, "hn")
            junk = hn
            stt = [self.sb(ph, [128, 4], F32, "nst") for _ in range(2)]
            Rhb0 = Region("hb0")
            Rhb = [Rhb0, Rhb0]
            Rhb2 = Region("hb2")
            Rhn = Region("hn")
            Rst = [Region("st0"), Region("st1")]
            Rj = Rhn
            for b in range(nblk):
                i = b % 2
                h_, n_, s_ = hb[i], hn, stt[i]
                rs = slice(b * np_, (b + 1) * np_)
                P.dma("sp", h_[0:np_, :], src[rs, :], reads=[Rsrc], writes=[Rhb[i]])
                if add is not None:
                    P.dma("sp", hb2[0:np_, :], add[0][rs, :], reads=[add[1]], writes=[Rhb2])
                    P.op("dve", lambda h, h_=h_: h.tensor_tensor(h_[0:np_, :], h_[0:np_, :], hb2[0:np_, :], ALU.add), reads=[Rhb[i], Rhb2], writes=[Rhb[i]])
                    if store is not None:
                        P.dma("sp", store[0][rs, :], h_[0:np_, :], reads=[Rhb[i]], writes=[store[1]])
                P.op("act", lambda h, h_=h_, s_=s_: h.activation(junk[0:np_, :], h_[0:np_, :], AF.Square, accum_out=s_[0:np_, 0:1]),
                     reads=[Rhb[i]], writes=[Rj, Rst[i]])
                P.op("act", lambda h, s_=s_: h.activation(s_[0:np_, 1:2], s_[0:np_, 0:1], AF.Sqrt, bias=EPS, scale=1.0 / D),
                     reads=[Rst[i]], writes=[Rst[i]])
                P.op("dve", lambda h, s_=s_: h.reciprocal(s_[0:np_, 2:3], s_[0:np_, 1:2]), reads=[Rst[i]], writes=[Rst[i]])
                P.op("dve", lambda h, h_=h_, n_=n_, s_=s_: h.tensor_scalar(n_[0:np_, :], h_[0:np_, :], s_[0:np_, 2:3], None, op0=ALU.mult),
                     reads=[Rhb[i], Rst[i]], writes=[Rhn])
                for q in range(4):
                    fns = [(lambda h, kc=kc, n_=n_: h.transpose(self.tb[:, (kc % 8) * 128:(kc % 8) * 128 + np_],
                                                                n_[0:np_, kc * 128:(kc + 1) * 128], self.identb[0:np_, 0:np_]))
                           for kc in range(q * 8, q * 8 + 8)]
                    P.group("pe", fns, reads=[Rhn, self.Ridb], writes=[self.Rtb])
                    P.op("dve", lambda h, q=q, b=b: h.tensor_tensor(
                        self.hnT[:, q * 8:(q + 1) * 8, b * np_:(b + 1) * np_],
                        self.tb[:, :].rearrange("p (a c) -> p a c", a=8)[:, :, 0:np_],
                        self.gainT[:, q * 8:(q + 1) * 8].unsqueeze(2).to_broadcast([128, 8, np_]), ALU.mult),
                        reads=[self.Rtb, self.RgainT], writes=[self.RhnT])
            self.barrier()

    def phase_A(self):
        P, l, t = self.P, self.l, self.t
        first_tile = (t == 0)
        last_tile = (t == NTILE - 1)
        KW = KV * 128
        with contextlib.ExitStack() as ph:
            qT = self.sb(ph, [128, HA, TT], BF16, "qT")
            kT = self.sb(ph, [128, KV, TT + 128], BF16, "kT")
            vt = self.sb(ph, [128, NBLK + 1, KW], BF16, "vt")
            zT = self.sb(ph, [128, HA, TT], BF16, "zT")
            ost = self.sb(ph, [128, 2, KW], F32, "ost")
            RqT, RkT, Rvt, RzT, Rost = Region("qT"), Region("kT"), Region("vt"), Region("zT"), Region("ost")
            if not first_tile:
                P.op("act", lambda h: h.copy(kT[:, :, 0:128], self.kcar[:]), reads=[self.Rkcar], writes=[RkT])
                P.op("act", lambda h: h.copy(vt[:, 0, :], self.vcar[:]), reads=[self.Rvcar], writes=[Rvt])
            for g, wb, Rw in self.wstream("in", list(range(G_AQ, G_MQK))):
                if g < G_AK:
                    for j in range(2):
                        acc, Racc = self.mm_feat(wb, Rw, j, TT)
                        hd = (g - G_AQ) * 2 + j
                        P.op("act", lambda h, acc=acc, hd=hd: h.activation(qT[:, hd, :], acc[:, 0:TT], AF.Copy, scale=QSCALE),
                             reads=[Racc], writes=[RqT])
                elif g < G_AV:
                    for j in range(2):
                        acc, Racc = self.mm_feat(wb, Rw, j, TT)
                        P.op("dve", lambda h, acc=acc, j=j: h.tensor_copy(kT[:, j, 128:128 + TT], acc[:, 0:TT]), reads=[Racc], writes=[RkT])
                    if last_tile:
                        acc, Racc = self.mm_tok(wb, Rw, TT - 128, 128, 256)
                        P.op("dve", lambda h, acc=acc: h.tensor_copy(ost[:, 0, :], acc[:, 0:256]), reads=[Racc], writes=[Rost])
                elif g < G_AZ:
                    for b in range(NBLK):
                        acc, Racc = self.mm_tok(wb, Rw, b * 128, 128, 256)
                        P.op("act", lambda h, acc=acc, b=b: h.copy(vt[:, b + 1, :], acc[:, 0:256]), reads=[Racc], writes=[Rvt])
                        if last_tile and b == NBLK - 1:
                            P.op("dve", lambda h, acc=acc: h.tensor_copy(ost[:, 1, :], acc[:, 0:256]), reads=[Racc], writes=[Rost])
                else:
                    for j in range(2):
                        acc, Racc = self.mm_feat(wb, Rw, j, TT)
                        hd = (g - G_AZ) * 2 + j
                        P.op("act", lambda h, acc=acc, hd=hd: h.activation(zT[:, hd, :], acc[:, 0:TT], AF.Silu), reads=[Racc], writes=[RzT])
            if last_tile:
                P.dma("sp", self.O["kp"][l], ost[:, 0, :], reads=[Rost], writes=[self.R["kp"]])
                P.dma("sp", self.O["vp"][l], ost[:, 1, :], reads=[Rost], writes=[self.R["vp"]])
            P.op("act", lambda h: h.copy(self.kcar[:], kT[:, :, TT:TT + 128]), reads=[RkT], writes=[self.Rkcar])
            P.op("act", lambda h: h.copy(self.vcar[:], vt[:, NBLK, :]), reads=[Rvt], writes=[self.Rvcar])
            L = [self.sb(ph, [128, 256], F32, "L") for _ in range(2)]
            Pn = [self.sb(ph, [128, 256], BF16, "Pn") for _ in range(2)]
            PT = [self.sb(ph, [128, 2, 128], BF16, "PT") for _ in range(2)]
            sm = [self.sb(ph, [128, 8], F32, "asm") for _ in range(2)]
            RL = [Region("L0"), Region("L1")]
            RPn = [Region("Pn0"), Region("Pn1")]
            RPT = [Region("PT0"), Region("PT1")]
            Rsm = [Region("sm0"), Region("sm1")]
            it = 0
            for b in range(NBLK):
                gfirst = first_tile and b == 0
                nk = 128 if gfirst else 256
                koff = b * 128 + (128 if gfirst else 0)
                dmo = 128 if gfirst else 0
                for hd in range(HA):
                    kv = hd // 3
                    i = it % 2
                    it += 1
                    pS, RpS = self.pb[4 + i], self.Rpb[4 + i]
                    L_, Pn_, PT_, sm_ = L[i], Pn[i], PT[i], sm[i]
                    sk = self.sinkb[:, l * HA + hd:l * HA + hd + 1]
                    ns_ = self.nslp[:, hd:hd + 1]
                    P.op("pe", lambda h, pS=pS, hd=hd, kv=kv, b=b, koff=koff, nk=nk: h.matmul(
                        pS[:, 0:nk], qT[:, hd, b * 128:(b + 1) * 128], kT[:, kv, koff:koff + nk], start=True, stop=True),
                        reads=[RqT, RkT], writes=[RpS])
                    P.op("dve", lambda h, pS=pS, L_=L_, ns_=ns_, nk=nk, dmo=dmo: h.scalar_tensor_tensor(
                        L_[:, 0:nk], self.dm[:, dmo:dmo + nk], ns_, pS[:, 0:nk], op0=ALU.mult, op1=ALU.add),
                        reads=[RpS, self.Rdm, self.Rnslp], writes=[RL[i]])
                    P.op("dve", lambda h, L_=L_, sm_=sm_, nk=nk: h.tensor_reduce(sm_[:, 0:1], L_[:, 0:nk], AX.X, ALU.max),
                         reads=[RL[i]], writes=[Rsm[i]])
                    P.op("dve", lambda h, sm_=sm_, sk=sk: h.tensor_scalar(sm_[:, 1:2], sm_[:, 0:1], sk, -1.0, op0=ALU.max, op1=ALU.mult),
                         reads=[Rsm[i], self.Rsinkb], writes=[Rsm[i]])
                    P.op("act", lambda h, L_=L_, sm_=sm_, nk=nk: h.activation(L_[:, 0:nk], L_[:, 0:nk], AF.Exp, bias=sm_[:, 1:2], scale=1.0,
                                                                              accum_out=sm_[:, 2:3]),
                         reads=[RL[i], Rsm[i]], writes=[RL[i], Rsm[i]])
                    P.op("act", lambda h, sm_=sm_, sk=sk: h.activation(sm_[:, 3:4], sk, AF.Exp, bias=sm_[:, 1:2], scale=1.0),
                         reads=[Rsm[i], self.Rsinkb], writes=[Rsm[i]])
                    P.op("dve", lambda h, sm_=sm_: h.tensor_tensor(sm_[:, 4:5], sm_[:, 2:3], sm_[:, 3:4], ALU.add), reads=[Rsm[i]], writes=[Rsm[i]])
                    P.op("dve", lambda h, sm_=sm_: h.reciprocal(sm_[:, 5:6], sm_[:, 4:5]), reads=[Rsm[i]], writes=[Rsm[i]])
                    P.op("dve", lambda h, L_=L_, Pn_=Pn_, sm_=sm_, nk=nk: h.tensor_scalar(Pn_[:, 0:nk], L_[:, 0:nk], sm_[:, 5:6], None, op0=ALU.mult),
                         reads=[RL[i], Rsm[i]], writes=[RPn[i]])
                    nkb = nk // 128
                    fns = [(lambda h, kb=kb, Pn_=Pn_: h.transpose(self.tb[:, kb * 128:(kb + 1) * 128], Pn_[:, kb * 128:(kb + 1) * 128], self.identb[:]))
                           for kb in range(nkb)]
                    P.group("pe", fns, reads=[RPn[i], self.Ridb], writes=[self.Rtb])
                    P.op("act", lambda h, PT_=PT_, nkb=nkb: h.copy(PT_[:, 0:nkb, :], self.tb[:, 0:nkb * 128].rearrange("p (a c) -> p a c", a=nkb)),
                         reads=[self.Rtb], writes=[RPT[i]])
                    pO, RpO = self.pb[i], self.Rpb[i]
                    vb0 = b + (1 if gfirst else 0)
                    fns = [(lambda h, kb=kb, pO=pO, PT_=PT_, kv=kv, vb0=vb0, nkb=nkb: h.matmul(
                        pO[:, 0:128], vt[:, vb0 + kb, kv * 128:(kv + 1) * 128], PT_[:, kb, :], start=(kb == 0), stop=(kb == nkb - 1)))
                        for kb in range(nkb)]
                    P.group("pe", fns, reads=[Rvt, RPT[i]], writes=[RpO])
                    P.op("dve", lambda h, pO=pO, hd=hd, b=b: h.tensor_tensor(self.yT[:, hd, b * 128:(b + 1) * 128], pO[:, 0:128],
                                                                              zT[:, hd, b * 128:(b + 1) * 128], ALU.mult),
                         reads=[RpO, RzT], writes=[self.RyT[0]])
            self.barrier()

    def phase_M(self):
        P, l, t = self.P, self.l, self.t
        NH = HM
        QW = NH * 128
        VW = NH * 256
        with contextlib.ExitStack() as ph:
            qb = self.sb(ph, [128, NBLK, QW], BF16, "qb")
            kb_ = self.sb(ph, [128, NBLK, QW], BF16, "kb")
            va = self.sb(ph, [128, NBLK, VW], BF16, "va")
            G = self.sb(ph, [128, NBLK, VW], BF16, "G")
            gts = self.sb(ph, [128, NBLK, 2 * NH], F32, "gts")
            tmp = [self.sb(ph, [128, 256], F32, "mtmp") for _ in range(2)]
            Rqb, Rkb, Rva, RG, Rgts = Region("qb"), Region("kb"), Region("va"), Region("G"), Region("gts")
            Rtmp = [Region("mt0"), Region("mt1")]
            groups = list(range(G_MQK, G_CU)) + [G_GATE]
            ti = 0
            for g, wb, Rw in self.wstream("in", groups):
                ncols = 2 * NH if g == G_GATE else 256
                for b in range(NBLK):
                    acc, Racc = self.mm_tok(wb, Rw, b * 128, 128, ncols)
                    if g == G_GATE:
                        P.op("dve", lambda h, acc=acc, b=b: h.tensor_tensor(gts[:, b, :], acc[:, 0:2 * NH], self.bifb[:, l * 2 * NH:(l + 1) * 2 * NH], ALU.add),
                             reads=[Racc, self.Rbifb], writes=[Rgts])
                    elif g < G_MV:
                        for j in range(2):
                            ch = (g - G_MQK) * 2 + j
                            if ch < NH:
                                P.op("act", lambda h, acc=acc, b=b, ch=ch, j=j: h.copy(qb[:, b, ch * 128:(ch + 1) * 128], acc[:, j * 128:(j + 1) * 128]),
                                     reads=[Racc], writes=[Rqb])
                            else:
                                P.op("act", lambda h, acc=acc, b=b, ch=ch, j=j: h.activation(kb_[:, b, (ch - NH) * 128:(ch - NH + 1) * 128],
                                                                                              acc[:, j * 128:(j + 1) * 128], AF.Copy, scale=QSCALE),
                                     reads=[Racc], writes=[Rkb])
                    elif g < G_MO:
                        c0 = (g - G_MV) * 256
                        P.op("dve", lambda h, acc=acc, b=b, c0=c0: h.tensor_copy(va[:, b, c0:c0 + 256], acc[:, 0:256]), reads=[Racc], writes=[Rva])
                    elif g < G_MZ:
                        c0 = (g - G_MO) * 256
                        P.op("act", lambda h, acc=acc, b=b, c0=c0: h.activation(G[:, b, c0:c0 + 256], acc[:, 0:256], AF.Sigmoid), reads=[Racc], writes=[RG])
                    else:
                        c0 = (g - G_MZ) * 256
                        tm, Rtm = tmp[ti % 2], Rtmp[ti % 2]
                        ti += 1
                        P.op("act", lambda h, acc=acc, tm=tm: h.activation(tm[:], acc[:, 0:256], AF.Silu), reads=[Racc], writes=[Rtm])
                        P.op("dve", lambda h, tm=tm, b=b, c0=c0: h.tensor_tensor(G[:, b, c0:c0 + 256], G[:, b, c0:c0 + 256], tm[:], ALU.mult),
                             reads=[Rtm, RG], writes=[RG])
            sm = self.sb(ph, [128, 128], F32, "msm")
            Bd = self.sb(ph, [128, NH, 128], F32, "Bd")
            Dm = self.sb(ph, [128, NH, 128], F32, "Dm")
            w = self.sb(ph, [128, QW], BF16, "w")
            wT = self.sb(ph, [128, QW], BF16, "wT")
            qTs = self.sb(ph, [128, QW], BF16, "qTs")
            kTs = self.sb(ph, [128, QW], BF16, "kTs")
            qs = self.sb(ph, [128, NH, 128], BF16, "qs")
            qsT = self.sb(ph, [128, NH, 2, 128], BF16, "qsT")
            ksc = [self.sb(ph, [128, NH, 128], BF16, "ksc") for _ in range(2)]
            mo = self.sb(ph, [128, VW], BF16, "mo")
            junk = self.sb(ph, [128, 256], BF16, "mjunk")
            Rsm, RBd, RDm, Rw_, RwT, RqTs, RkTs, Rqs, RqsT, Rmo, Rjunk = (Region(n) for n in
                ("msm", "Bd", "Dm", "w", "wT", "qTs", "kTs", "qs", "qsT", "mo", "mjunk"))
            Rksc = [Region("ksc0"), Region("ksc1")]
            P.op("dve", lambda h: h.memset(qsT[:], 0.0), writes=[RqsT])
            psm, Rpsm = self.pb[0], self.Rpb[0]
            pB, RpB = self.pb[1], self.Rpb[1]
            pS, RpS = self.pb[2], self.Rpb[2]
            pC = [self.pb[1], self.pb[2]]
            RpC = [self.Rpb[1], self.Rpb[2]]
            pN = [self.pb[3], self.pb[4]]
            RpN = [self.Rpb[3], self.Rpb[4]]
            c_e1, c_sp, c_an, c_al0, c_al1, c_bv = 0, 6, 12, 20, 28, 36
            c_mxB, c_rmD, c_m1, c_m2, c_msel, c_mns, c_als = 42, 54, 60, 66, 72, 78, 84
            c_int, c_mt, c_wi, c_emt, c_ws, c_wsc0, c_wsc1 = 90, 96, 102, 108, 114, 120, 0
            sm2 = self.sb(ph, [128, 64], F32, "msm2")
            Rsm2 = Region("msm2")
            d_wc0, d_wc1, d_den, d_dn, d_t, d_rd, d_ssq, d_f, d_dnn = 0, 6, 12, 18, 24, 30, 36, 42, 48
            S_ = lambda c, n=NH: sm[:, c:c + n]
            S2 = lambda c, n=NH: sm2[:, c:c + n]
            bc3 = lambda ap: ap.unsqueeze(2).to_broadcast([128, NH, 128])

            def do_block(b):
                ip = gts[:, b, 0:NH]
                fp = gts[:, b, NH:2 * NH]
                P.op("act", lambda h: h.activation(S_(c_e1), fp, AF.Exp, scale=-1.0), reads=[Rgts], writes=[Rsm])
                P.op("act", lambda h: h.activation(S_(c_sp), S_(c_e1), AF.Ln, bias=1.0), reads=[Rsm], writes=[Rsm])
                fns = [lambda h: h.matmul(psm[:, 0:NH], self.tri2[:], S_(c_sp), start=True, stop=True),
                       lambda h: h.matmul(psm[:, 8:8 + NH], self.onesc[:, 0:128], S_(c_sp), start=True, stop=True),
                       lambda h: h.matmul(psm[:, 16:16 + NH], self.onesc[:, 128:256], S_(c_sp), start=True, stop=True)]
                P.group("pe", fns, reads=[Rsm, self.Rtri2, self.Ronesc], writes=[Rpsm])
                for (dst, srcc) in ((c_an, 0), (c_al0, 8), (c_al1, 16)):
                    P.op("dve", lambda h, dst=dst, srcc=srcc: h.tensor_copy(S_(dst), psm[:, srcc:srcc + NH]), reads=[Rpsm], writes=[Rsm])
                P.op("dve", lambda h: h.tensor_tensor(S_(c_bv), ip, S_(c_an), ALU.add), reads=[Rgts, Rsm], writes=[Rsm])
                P.op("dve", lambda h: h.tensor_tensor(Bd[:], self.identf[:].unsqueeze(1).to_broadcast([128, NH, 128]), bc3(S_(c_bv)), ALU.mult),
                     reads=[Rsm, self.Ridf], writes=[RBd])
                P.op("pe", lambda h: h.matmul(pB[:, 0:QW], self.onesf[:], Bd[:].rearrange("p a c -> p (a c)"), start=True, stop=True),
                     reads=[RBd, self.Ronesf], writes=[RpB])
                P.op("dve", lambda h: h.tensor_reduce(sm[:, c_mxB:c_mxB + 2 * NH].rearrange("p (a c) -> p a c", a=NH),
                                                      pB[:, 0:QW].rearrange("p (a c s) -> p a c s", a=NH, c=2), AX.X, ALU.max),
                     reads=[RpB], writes=[Rsm])
                P.op("dve", lambda h: h.tensor_tensor(Dm[:].rearrange("p a c -> p (a c)"), pB[:, 0:QW], self.mask6[:, 0:QW], ALU.add),
                     reads=[RpB, self.Rmask6], writes=[RDm])
                P.op("dve", lambda h: h.tensor_tensor(Dm[:], Dm[:], bc3(S_(c_an)), ALU.subtract), reads=[RDm, Rsm], writes=[RDm])
                P.op("dve", lambda h: h.tensor_reduce(S_(c_rmD), Dm[:], AX.X, ALU.max), reads=[RDm], writes=[Rsm])
                mxB = sm[:, c_mxB:c_mxB + 2 * NH].rearrange("p (a c) -> p a c", c=2)
                m0 = self.mst[:, 0:NH]
                P.op("dve", lambda h: h.tensor_tensor(S_(c_m1), m0, mxB[:, :, 0], ALU.max), reads=[Rsm, self.Rm], writes=[Rsm])
                P.op("dve", lambda h: h.tensor_tensor(S_(c_m1), S_(c_m1), S_(c_al0), ALU.subtract), reads=[Rsm], writes=[Rsm])
                P.op("dve", lambda h: h.tensor_tensor(S_(c_m2), S_(c_m1), mxB[:, :, 1], ALU.max), reads=[Rsm], writes=[Rsm])
                P.op("dve", lambda h: h.tensor_tensor(S_(c_m2), S_(c_m2), S_(c_al1), ALU.subtract), reads=[Rsm], writes=[Rsm])
                P.op("dve", lambda h: h.tensor_tensor(S2(d_wc0), m0, S_(c_al0), ALU.subtract), reads=[Rsm, self.Rm], writes=[Rsm2])
                P.op("dve", lambda h: h.tensor_tensor(S2(d_wc0), S2(d_wc0), S_(c_m1), ALU.subtract), reads=[Rsm, Rsm2], writes=[Rsm2])
                P.op("dve", lambda h: h.tensor_tensor(S2(d_wc1), S_(c_m1), S_(c_al1), ALU.subtract), reads=[Rsm, Rsm2], writes=[Rsm2])
                P.op("dve", lambda h: h.tensor_tensor(S2(d_wc1), S2(d_wc1), S_(c_m2), ALU.subtract), reads=[Rsm, Rsm2], writes=[Rsm2])
                P.op("act", lambda h: h.activation(S2(d_wc0), S2(d_wc0), AF.Exp), reads=[Rsm2], writes=[Rsm2])
                P.op("act", lambda h: h.activation(S2(d_wc1), S2(d_wc1), AF.Exp), reads=[Rsm2], writes=[Rsm2])
                P.op("dve", lambda h: h.tensor_copy(sm[0:64, c_msel:c_msel + NH], self.mst[0:64, 0:NH]), reads=[self.Rm, Rsm], writes=[Rsm])
                P.op("dve", lambda h: h.tensor_copy(sm[64:128, c_msel:c_msel + NH], sm[64:128, c_m1:c_m1 + NH]), reads=[Rsm], writes=[Rsm])
                P.op("dve", lambda h: h.tensor_copy(sm[0:64, c_mns:c_mns + NH], sm[0:64, c_m1:c_m1 + NH]), reads=[Rsm], writes=[Rsm])
                P.op("dve", lambda h: h.tensor_copy(sm[64:128, c_mns:c_mns + NH], sm[64:128, c_m2:c_m2 + NH]), reads=[Rsm], writes=[Rsm])
                P.op("dve", lambda h: h.tensor_copy(sm[0:64, c_als:c_als + NH], sm[0:64, c_al0:c_al0 + NH]), reads=[Rsm], writes=[Rsm])
                P.op("dve", lambda h: h.tensor_copy(sm[64:128, c_als:c_als + NH], sm[64:128, c_al1:c_al1 + NH]), reads=[Rsm], writes=[Rsm])
                P.op("dve", lambda h: h.tensor_copy(self.mst[:, 0:NH], S_(c_m2)), reads=[Rsm], writes=[self.Rm])
                P.op("dve", lambda h: h.tensor_tensor(S_(c_int), S_(c_msel), S_(c_an), ALU.subtract), reads=[Rsm], writes=[Rsm])
                P.op("dve", lambda h: h.tensor_tensor(S_(c_mt), S_(c_int), S_(c_rmD), ALU.max), reads=[Rsm], writes=[Rsm])
                P.op("dve", lambda h: h.tensor_tensor(S_(c_wi), S_(c_int), S_(c_mt), ALU.subtract), reads=[Rsm], writes=[Rsm])
                P.op("act", lambda h: h.activation(S_(c_wi), S_(c_wi), AF.Exp), reads=[Rsm], writes=[Rsm])
                P.op("act", lambda h: h.activation(S_(c_emt), S_(c_mt), AF.Exp, scale=-1.0), reads=[Rsm], writes=[Rsm])
                P.op("dve", lambda h: h.tensor_tensor(S_(c_ws), S_(c_bv), S_(c_als), ALU.subtract), reads=[Rsm], writes=[Rsm])
                P.op("dve", lambda h: h.tensor_tensor(S_(c_ws), S_(c_ws), S_(c_mns), ALU.subtract), reads=[Rsm], writes=[Rsm])
                P.op("act", lambda h: h.activation(S_(c_ws), S_(c_ws), AF.Exp), reads=[Rsm], writes=[Rsm])
                P.op("dve", lambda h: h.tensor_scalar(S_(c_wsc0), S_(c_ws), self.cmask[:, 0:1], None, op0=ALU.mult), reads=[Rsm, self.Rcmask], writes=[Rsm])
                P.op("dve", lambda h: h.tensor_scalar(S_(c_wsc1), S_(c_ws), self.cmask[:, 1:2], None, op0=ALU.mult), reads=[Rsm, self.Rcmask], writes=[Rsm])
                P.op("dve", lambda h: h.tensor_tensor(Dm[:], Dm[:], bc3(S_(c_mt)), ALU.subtract), reads=[RDm, Rsm], writes=[RDm])
                P.op("act", lambda h: h.activation(Dm[:], Dm[:], AF.Exp), reads=[RDm], writes=[RDm])
                for (srcb, Rsrcb, dstT, RdstT) in ((qb, Rqb, qTs, RqTs), (kb_, Rkb, kTs, RkTs)):
                    fns = [(lambda h, hh=hh, srcb=srcb: h.transpose(self.tb[:, hh * 128:(hh + 1) * 128], srcb[:, b, hh * 128:(hh + 1) * 128], self.identb[:]))
                           for hh in range(NH)]
                    P.group("pe", fns, reads=[Rsrcb, self.Ridb], writes=[self.Rtb])
                    P.op("act", lambda h, dstT=dstT: h.copy(dstT[:], self.tb[:, 0:QW]), reads=[self.Rtb], writes=[RdstT])
                fns = [(lambda h, hh=hh: h.matmul(pS[:, hh * 128:(hh + 1) * 128], qTs[:, hh * 128:(hh + 1) * 128],
                                                  kTs[:, hh * 128:(hh + 1) * 128], start=True, stop=True)) for hh in range(NH)]
                P.group("pe", fns, reads=[RqTs, RkTs], writes=[RpS])
                P.op("dve", lambda h: h.tensor_tensor(w[:], Dm[:].rearrange("p a c -> p (a c)"), pS[:, 0:QW], ALU.mult), reads=[RDm, RpS], writes=[Rw_])
                fns = [(lambda h, hh=hh: h.transpose(self.tb[:, hh * 128:(hh + 1) * 128], w[:, hh * 128:(hh + 1) * 128], self.identb[:])) for hh in range(NH)]
                P.group("pe", fns, reads=[Rw_, self.Ridb], writes=[self.Rtb])
                P.op("act", lambda h: h.copy(wT[:], self.tb[:, 0:QW]), reads=[self.Rtb], writes=[RwT])
                P.op("dve", lambda h: h.tensor_tensor(qs[:], qb[:, b, :].rearrange("p (a c) -> p a c", a=NH), bc3(S_(c_wi)), ALU.mult),
                     reads=[Rqb, Rsm], writes=[Rqs])
                fns = [(lambda h, hh=hh: h.transpose(self.tb[:, hh * 128:(hh + 1) * 128], qs[:, hh, :], self.identb[:])) for hh in range(NH)]
                P.group("pe", fns, reads=[Rqs, self.Ridb], writes=[self.Rtb])
                tbv = self.tb[:, 0:QW].rearrange("p (a c) -> p a c", a=NH)
                P.op("act", lambda h: h.copy(qsT[:, :, 0, 0:64], tbv[:, :, 0:64]), reads=[self.Rtb], writes=[RqsT])
                P.op("act", lambda h: h.copy(qsT[:, :, 1, 64:128], tbv[:, :, 64:128]), reads=[self.Rtb], writes=[RqsT])
                kb3 = kb_[:, b, :].rearrange("p (a c) -> p a c", a=NH)
                P.op("dve", lambda h: h.tensor_tensor(ksc[0][:], kb3, bc3(S_(c_wsc0)), ALU.mult), reads=[Rkb, Rsm], writes=[Rksc[0]])
                P.op("dve", lambda h: h.tensor_tensor(ksc[1][:], kb3, bc3(S_(c_wsc1)), ALU.mult), reads=[Rkb, Rsm], writes=[Rksc[1]])
                va3 = va[:, b, :].rearrange("p (a c) -> p a c", a=NH)

                def state_update(c, wc_col, Cb_dst, nb_dst, RCb_dst, Rnb_dst):
                    fns = []
                    for hh in range(NH):
                        fns.append(lambda h, hh=hh: h.matmul(pC[hh // 2][:, (hh % 2) * 256:(hh % 2) * 256 + 256], ksc[c][:, hh, :], va3[:, hh, :],
                                                             start=True, stop=True))
                    P.group("pe", fns, reads=[Rksc[c], Rva], writes=RpC)
                    fns = [(lambda h, hh=hh: h.matmul(psm[:, 32 + hh:33 + hh], ksc[c][:, hh, :], self.onesb[:, 0:1], start=True, stop=True)) for hh in range(NH)]
                    P.group("pe", fns, reads=[Rksc[c], self.Ronesb], writes=[Rpsm])
                    for hh in range(NH):
                        P.op("dve", lambda h, hh=hh: h.scalar_tensor_tensor(self.Cst[:, hh, :], self.Cst[:, hh, :], sm2[:, wc_col + hh:wc_col + hh + 1],
                                                                            pC[hh // 2][:, (hh % 2) * 256:(hh % 2) * 256 + 256], op0=ALU.mult, op1=ALU.add),
                             reads=[Rsm2, RpC[hh // 2], self.RC], writes=[self.RC])
                    P.op("dve", lambda h: h.tensor_tensor(S2(d_dnn), self.nst[:, 0:NH], S2(wc_col), ALU.mult), reads=[self.Rn, Rsm2], writes=[Rsm2])
                    P.op("dve", lambda h: h.tensor_tensor(self.nst[:, 0:NH], S2(d_dnn), psm[:, 32:32 + NH], ALU.add), reads=[Rsm2, Rpsm], writes=[self.Rn])
                    P.op("act", lambda h: h.copy(Cb_dst[:], self.Cst[:]), reads=[self.RC], writes=[RCb_dst])
                    P.op("act", lambda h: h.copy(nb_dst[:, 0:NH], self.nst[:, 0:NH]), reads=[self.Rn], writes=[Rnb_dst])

                state_update(0, d_wc0, self.Cb[1], self.nb[1], self.RCb[1], self.Rnb[1])
                for hh in range(NH):
                    o_ = pN[hh // 2][:, (hh % 2) * 256:(hh % 2) * 256 + 256]
                    fns = [lambda h, hh=hh, o_=o_: h.matmul(o_, wT[:, hh * 128:(hh + 1) * 128], va3[:, hh, :], start=True, stop=False),
                           lambda h, hh=hh, o_=o_: h.matmul(o_, qsT[:, hh, 0, :], self.Cb[0][:, hh, :], start=False, stop=False),
                           lambda h, hh=hh, o_=o_: h.matmul(o_, qsT[:, hh, 1, :], self.Cb[1][:, hh, :], start=False, stop=True)]
                    P.group("pe", fns, reads=[RwT, Rva, RqsT, self.RCb[0], self.RCb[1]], writes=[RpN[hh // 2]])
                for hh in range(NH):
                    o_ = psm[:, 40 + hh:41 + hh]
                    fns = [lambda h, hh=hh, o_=o_: h.matmul(o_, wT[:, hh * 128:(hh + 1) * 128], self.onesb[:, 0:1], start=True, stop=False),
                           lambda h, hh=hh, o_=o_: h.matmul(o_, qsT[:, hh, 0, :], self.nb[0][:, hh:hh + 1], start=False, stop=False),
                           lambda h, hh=hh, o_=o_: h.matmul(o_, qsT[:, hh, 1, :], self.nb[1][:, hh:hh + 1], start=False, stop=True)]
                    P.group("pe", fns, reads=[RwT, self.Ronesb, RqsT, self.Rnb[0], self.Rnb[1]], writes=[Rpsm])
                P.op("dve", lambda h: h.tensor_copy(S2(d_den), psm[:, 40:40 + NH]), reads=[Rpsm], writes=[Rsm2])
                P.op("dve", lambda h: h.scalar_tensor_tensor(S2(d_t), S2(d_den), -1.0, S2(d_den), op0=ALU.mult, op1=ALU.max), reads=[Rsm2], writes=[Rsm2])
                P.op("dve", lambda h: h.tensor_tensor(S2(d_t), S2(d_t), S_(c_emt), ALU.max), reads=[Rsm2, Rsm], writes=[Rsm2])
                P.op("dve", lambda h: h.reciprocal(S2(d_rd), S2(d_t)), reads=[Rsm2], writes=[Rsm2])
                for hh in range(NH):
                    P.op("act", lambda h, hh=hh: h.activation(junk[:], pN[hh // 2][:, (hh % 2) * 256:(hh % 2) * 256 + 256], AF.Square,
                                                              accum_out=sm2[:, d_ssq + hh:d_ssq + hh + 1]),
                         reads=[RpN[hh // 2], Rsm2], writes=[Rjunk, Rsm2])
                P.op("dve", lambda h: h.tensor_tensor(S2(d_t), S2(d_rd), S2(d_rd), ALU.mult), reads=[Rsm2], writes=[Rsm2])
                P.op("dve", lambda h: h.tensor_tensor(S2(d_t), S2(d_t), S2(d_ssq), ALU.mult), reads=[Rsm2], writes=[Rsm2])
                P.op("act", lambda h: h.activation(S2(d_t), S2(d_t), AF.Sqrt, bias=EPS, scale=1.0 / 256.0), reads=[Rsm2], writes=[Rsm2])
                P.op("dve", lambda h: h.reciprocal(S2(d_f), S2(d_t)), reads=[Rsm2], writes=[Rsm2])
                P.op("dve", lambda h: h.tensor_tensor(S2(d_f), S2(d_f), S2(d_rd), ALU.mult), reads=[Rsm2], writes=[Rsm2])
                for hh in range(NH):
                    P.op("dve", lambda h, hh=hh: h.scalar_tensor_tensor(mo[:, hh * 256:(hh + 1) * 256], pN[hh // 2][:, (hh % 2) * 256:(hh % 2) * 256 + 256],
                                                                        sm2[:, d_f + hh:d_f + hh + 1], G[:, b, hh * 256:(hh + 1) * 256],
                                                                        op0=ALU.mult, op1=ALU.mult),
                         reads=[RpN[hh // 2], Rsm2, RG], writes=[Rmo])
                fns = [(lambda h, j=j: h.transpose(self.tb[:, j * 128:(j + 1) * 128], mo[:, j * 128:(j + 1) * 128], self.identb[:])) for j in range(6)]
                P.group("pe", fns, reads=[Rmo, self.Ridb], writes=[self.Rtb])
                P.op("dve", lambda h: h.tensor_tensor(
                    self.yT[:, 6:12, b * 128:(b + 1) * 128],
                    self.tb[:, 0:768].rearrange("p (a c) -> p a c", a=6),
                    self.mhgT[:, 0:6].unsqueeze(2).to_broadcast([128, 6, 128]), ALU.mult),
                    reads=[self.Rtb, self.RmhgT], writes=[self.RyT[1]])
                state_update(1, d_wc1, self.Cb[0], self.nb[0], self.RCb[0], self.Rnb[0])
            for b in range(NBLK):
                do_block(b)
            self.barrier()

    def phase_C(self):
        P, l, t = self.P, self.l, self.t
        last_tile = (t == NTILE - 1)
        with contextlib.ExitStack() as ph:
            uT = self.sb(ph, [128, GC, TT], BF16, "uT")
            vvb = self.sb(ph, [128, NBLK, 1024], BF16, "vvb")
            gv = [self.sb(ph, [128, 1024], F32, "gv") for _ in range(NBLK)]
            tmp = [self.sb(ph, [128, TT], F32, "ctmp") for _ in range(2)]
            junk = self.sb(ph, [128, 1024], BF16, "cjunk")
            sm = self.sb(ph, [128, 16], F32, "csm")
            t1 = self.sb(ph, [128, GC, 128], F32, "ct1")
            RuT, Rvvb, Rjunk, Rsm, Rt1 = Region("uT"), Region("vvb"), Region("cjunk"), Region("csm"), Region("ct1")
            Rgv = [Region("gv%d" % i) for i in range(NBLK)]
            Rtmp = [Region("ct0"), Region("ct1")]
            ti = 0
            for g, wb, Rw in self.wstream("in", list(range(G_CU, G_GATE))):
                if g < G_CV:
                    for j in range(2):
                        acc, Racc = self.mm_feat(wb, Rw, j, TT)
                        gi = (g - G_CU) * 2 + j
                        P.op("act", lambda h, acc=acc, gi=gi: h.activation(uT[:, gi, :], acc[:, 0:TT], AF.Gelu), reads=[Racc], writes=[RuT])
                elif g < G_CZ:
                    c0 = (g - G_CV) * 256
                    for b in range(NBLK):
                        acc, Racc = self.mm_tok(wb, Rw, b * 128, 128, 256)
                        P.op("act", lambda h, acc=acc, b=b, c0=c0: h.activation(gv[b][:, c0:c0 + 256], acc[:, 0:256], AF.Gelu), reads=[Racc], writes=[Rgv[b]])
                else:
                    for j in range(2):
                        acc, Racc = self.mm_feat(wb, Rw, j, TT)
                        gi = (g - G_CZ) * 2 + j
                        tm, Rtm = tmp[ti % 2], Rtmp[ti % 2]
                        ti += 1
                        P.op("act", lambda h, acc=acc, tm=tm: h.activation(tm[:], acc[:, 0:TT], AF.Silu), reads=[Racc], writes=[Rtm])
                        P.op("dve", lambda h, tm=tm, gi=gi: h.tensor_tensor(uT[:, gi, :], uT[:, gi, :], tm[:], ALU.mult), reads=[Rtm, RuT], writes=[RuT])

            def do_blockc(b):
                g_ = gv[b]
                self.layernorm(g_, Rgv[b], 128, junk, Rjunk, sm, Rsm)
                P.op("act", lambda h: h.copy(vvb[:, b, :], g_[:]), reads=[Rgv[b]], writes=[Rvvb])
                if last_tile and b == NBLK - 1:
                    P.dma("sp", self.O["cvp"][l], g_[:], reads=[Rgv[b]], writes=[self.R["cvp"]])
                pS_, RpS_ = self.pb[4 + (b % 2)], self.Rpb[4 + (b % 2)]
                fns = [(lambda h, gi=gi: h.matmul(pS_[:, gi * 128:(gi + 1) * 128], vvb[:, b, gi * 128:(gi + 1) * 128],
                                                  self.wmT[:, gi, :], start=True, stop=True)) for gi in range(GC)]
                P.group("pe", fns, reads=[Rvvb, self.RwmT], writes=[RpS_])
                P.op("dve", lambda h: h.tensor_tensor(t1[:].rearrange("p a c -> p (a c)"), pS_[:, 0:GC * 128], self.bsb[:, 0:GC * 128], ALU.add),
                     reads=[RpS_, self.Rbsb], writes=[Rt1])
                P.op("dve", lambda h: h.tensor_tensor(self.yT[:, 12:12 + GC, b * 128:(b + 1) * 128], t1[:], uT[:, :, b * 128:(b + 1) * 128], ALU.mult),
                     reads=[Rt1, RuT], writes=[self.RyT[2]])
            for b in range(NBLK):
                do_blockc(b)
            self.barrier()

    def layernorm(self, g_, Rg, np_, junk, Rjunk, sm, Rsm):
        P = self.P
        P.op("act", lambda h: h.activation(junk[0:np_, :], g_[0:np_, :], AF.Copy, accum_out=sm[0:np_, 0:1]), reads=[Rg], writes=[Rjunk, Rsm])
        P.op("act", lambda h: h.activation(junk[0:np_, :], g_[0:np_, :], AF.Square, accum_out=sm[0:np_, 1:2]), reads=[Rg], writes=[Rjunk, Rsm])
        P.op("dve", lambda h: h.tensor_scalar(sm[0:np_, 2:4], sm[0:np_, 0:2], 1.0 / 1024.0, None, op0=ALU.mult), reads=[Rsm], writes=[Rsm])
        P.op("dve", lambda h: h.tensor_tensor(sm[0:np_, 4:5], sm[0:np_, 2:3], sm[0:np_, 2:3], ALU.mult), reads=[Rsm], writes=[Rsm])
        P.op("dve", lambda h: h.tensor_tensor(sm[0:np_, 5:6], sm[0:np_, 3:4], sm[0:np_, 4:5], ALU.subtract), reads=[Rsm], writes=[Rsm])
        P.op("act", lambda h: h.activation(sm[0:np_, 6:7], sm[0:np_, 5:6], AF.Sqrt, bias=EPS, scale=1.0), reads=[Rsm], writes=[Rsm])
        P.op("dve", lambda h: h.reciprocal(sm[0:np_, 7:8], sm[0:np_, 6:7]), reads=[Rsm], writes=[Rsm])
        P.op("dve", lambda h: h.tensor_scalar(g_[0:np_, :], g_[0:np_, :], sm[0:np_, 2:3], sm[0:np_, 7:8], op0=ALU.subtract, op1=ALU.mult),
             reads=[Rg, Rsm], writes=[Rg])
        P.op("dve", lambda h: h.tensor_tensor(g_[0:np_, :], g_[0:np_, :], self.lngb[0:np_, :], ALU.mult), reads=[Rg, self.Rln], writes=[Rg])
        P.op("dve", lambda h: h.tensor_tensor(g_[0:np_, :], g_[0:np_, :], self.lnbb[0:np_, :], ALU.add), reads=[Rg, self.Rln], writes=[Rg])

    def phase_O(self, dst, Rdst, np_, nblk):
        P = self.P
        with contextlib.ExitStack() as ph:
            hs = [self.sb(ph, [128, nblk, 512], F32, "hs") for _ in range(2)]
            Rhs = [Region("hs0"), Region("hs1")]
            i = 0
            for g, wb, Rw in self.wstream("out", list(range(NG_OUT))):
                h_, Rh_ = hs[i % 2], Rhs[i % 2]
                i += 1
                for b in range(nblk):
                    acc, Racc = self.next_acc()
                    yT = self.yT
                    fns = [(lambda h, kc=kc, acc=acc, b=b, wb=wb: h.matmul(acc[0:np_, 0:512], yT[:, kc, b * np_:(b + 1) * np_], wb[:, kc, 0:512],
                                                                           start=(kc == 0), stop=(kc == KY - 1))) for kc in range(KY)]
                    P.group("pe", fns, reads=[Rw] + self.RyT, writes=[Racc])
                    if b % 2 == 0:
                        P.op("act", lambda h, acc=acc, h_=h_, b=b: h.copy(h_[0:np_, b, :], acc[0:np_, 0:512]), reads=[Racc], writes=[Rh_])
                    else:
                        P.op("dve", lambda h, acc=acc, h_=h_, b=b: h.tensor_copy(h_[0:np_, b, :], acc[0:np_, 0:512]), reads=[Racc], writes=[Rh_])
                P.dma("sp", dst[:, g * 512:(g + 1) * 512].rearrange("(b p) c -> p b c", p=np_), h_[0:np_, :, :], reads=[Rh_], writes=[Rdst])
            self.barrier()

    def phase_final(self, srcA, RsrcA, srcB, RsrcB, out, Rout, np_, nblk):
        P = self.P
        with contextlib.ExitStack() as ph:
            hb0 = self.sb(ph, [128, D], F32, "fhb")
            hb = [hb0, hb0]
            hb2 = self.sb(ph, [128, D], F32, "fhb2")
            fg = self.sb(ph, [128, D], F32, "fg")
            junk = hb2
            stt = [self.sb(ph, [128, 4], F32, "fst") for _ in range(2)]
            Rhb0 = Region("fhb0")
            Rhb = [Rhb0, Rhb0]
            Rhb2 = Region("fhb2")
            Rst = [Region("fst0"), Region("fst1")]
            Rfg, Rj = Region("fg"), Rhb2
            P.dma("sp", fg[:], self.I["fgain"].partition_broadcast(128), writes=[Rfg])
            for b in range(nblk):
                i = b % 2
                h_, s_ = hb[i], stt[i]
                rs = slice(b * np_, (b + 1) * np_)
                P.dma("sp", h_[0:np_, :], srcA[rs, :], reads=[RsrcA], writes=[Rhb[i]])
                P.dma("sp", hb2[0:np_, :], srcB[rs, :], reads=[RsrcB], writes=[Rhb2])
                P.op("dve", lambda h, h_=h_: h.tensor_tensor(h_[0:np_, :], h_[0:np_, :], hb2[0:np_, :], ALU.add), reads=[Rhb[i], Rhb2], writes=[Rhb[i]])
                P.op("act", lambda h, h_=h_, s_=s_: h.activation(junk[0:np_, :], h_[0:np_, :], AF.Square, accum_out=s_[0:np_, 0:1]),
                     reads=[Rhb[i]], writes=[Rj, Rst[i]])
                P.op("act", lambda h, s_=s_: h.activation(s_[0:np_, 1:2], s_[0:np_, 0:1], AF.Sqrt, bias=EPS, scale=1.0 / D), reads=[Rst[i]], writes=[Rst[i]])
                P.op("dve", lambda h, s_=s_: h.reciprocal(s_[0:np_, 2:3], s_[0:np_, 1:2]), reads=[Rst[i]], writes=[Rst[i]])
                P.op("dve", lambda h, h_=h_, s_=s_: h.scalar_tensor_tensor(h_[0:np_, :], h_[0:np_, :], s_[0:np_, 2:3], fg[0:np_, :], op0=ALU.mult, op1=ALU.mult),
                     reads=[Rhb[i], Rst[i], Rfg], writes=[Rhb[i]])
                P.dma("sp", out[rs, :], h_[0:np_, :], reads=[Rhb[i]], writes=[Rout])
            self.barrier()

    def sample_layer(self):
        l, I, S, O, P = self.l, self.I, self.S, self.O, self.P
        if l == 0:
            self.phase_norm(I["xs"], self.R["xin"], 1, NS)
        else:
            self.phase_norm(I["xs"], self.R["xin"], 1, NS, add=(S["reds0"], self.Rred[(0, "s")]), store=(S["hs1"], self.R["hs1"]))
        self.s_phase_A()
        self.s_phase_M()
        self.s_phase_C()
        rp = Region("pos%d" % l)
        rr = Region("reds%d" % l)
        self.Rred[(l, "s")] = rr
        self.phase_O(S["pos%d" % l], rp, NS, 1)
        P.coll(self.st, S["pos%d" % l], S["reds%d" % l], reads=[rp], writes=[rr])

    def s_proj(self, wb, Rw, ncols, evac):
        acc, Racc = self.mm_tok(wb, Rw, 0, NS, ncols)
        evac(acc, Racc)

    def s_to_yT(self, srcb, Rsrcb, kc0, n):
        P = self.P
        fns = [(lambda h, j=j: h.transpose(self.tb[:, j * NS:(j + 1) * NS], srcb[0:NS, j * 128:(j + 1) * 128], self.identb[0:NS, 0:NS])) for j in range(n)]
        P.group("pe", fns, reads=[Rsrcb, self.Ridb], writes=[self.Rtb])
        reg = self.RyT[0] if kc0 == 0 else (self.RyT[1] if kc0 == 6 else self.RyT[2])
        P.op("act", lambda h: h.copy(self.yT[:, kc0:kc0 + n, 0:NS], self.tb[:, 0:n * NS].rearrange("p (a c) -> p a c", a=n)), reads=[self.Rtb], writes=[reg])

    def s_phase_A(self):
        P, l, I, S, O = self.P, self.l, self.I, self.S, self.O
        NP = NS * KV
        with contextlib.ExitStack() as ph:
            qs_ = self.sb(ph, [NS, 768], F32, "sq")
            kn = self.sb(ph, [NS, 256], F32, "skn")
            vn = self.sb(ph, [NS, 256], F32, "svn")
            za = self.sb(ph, [NS, 768], F32, "sza")
            Rq, Rkn, Rvn, Rza = Region("sq"), Region("skn"), Region("svn"), Region("sza")
            for g, wb, Rw in self.wstream("in", list(range(G_AQ, G_MQK))):
                def evac(acc, Racc, g=g):
                    if g < G_AK:
                        c0 = (g - G_AQ) * 256
                        P.op("act", lambda h: h.activation(qs_[:, c0:c0 + 256], acc[0:NS, 0:256], AF.Copy, scale=QSCALE), reads=[Racc], writes=[Rq])
                    elif g < G_AV:
                        P.op("dve", lambda h: h.tensor_copy(kn[:], acc[0:NS, 0:256]), reads=[Racc], writes=[Rkn])
                    elif g < G_AZ:
                        P.op("dve", lambda h: h.tensor_copy(vn[:], acc[0:NS, 0:256]), reads=[Racc], writes=[Rvn])
                    else:
                        c0 = (g - G_AZ) * 256
                        P.op("act", lambda h: h.activation(za[:, c0:c0 + 256], acc[0:NS, 0:256], AF.Silu), reads=[Racc], writes=[Rza])
                self.s_proj(wb, Rw, 256, evac)
            Rb = self.R["bnc"]
            P.dma("sp", O["ks"][l], kn[:], reads=[Rkn], writes=[self.R["ks"]])
            P.dma("sp", O["vs"][l], vn[:], reads=[Rvn], writes=[self.R["vs"]])
            P.dma("sp", S["bq"], qs_[:], reads=[Rq], writes=[Rb])
            P.dma("sp", S["bk"], kn[:], reads=[Rkn], writes=[Rb])
            P.dma("sp", S["bv"], vn[:], reads=[Rvn], writes=[Rb])
            q4 = self.sb(ph, [NP, 3, 128], F32, "q4")
            k4 = self.sb(ph, [NP, 128], F32, "k4")
            v4 = self.sb(ph, [NP, 128], F32, "v4")
            R4 = Region("qkv4")
            P.dma("sp", q4[:].rearrange("p a b -> p (a b)"), S["bq"].rearrange("b (kv x) -> (b kv) x", kv=KV), reads=[Rb], writes=[R4])
            P.dma("sp", k4[:], S["bk"].rearrange("b (kv x) -> (b kv) x", kv=KV), reads=[Rb], writes=[R4])
            P.dma("sp", v4[:], S["bv"].rearrange("b (kv x) -> (b kv) x", kv=KV), reads=[Rb], writes=[R4])
            slp = self.sb(ph, [NP, 3], F32, "slp")
            sk4 = self.sb(ph, [NP, 3], F32, "sk4")
            dist = self.sb(ph, [NP, 129], F32, "dist")
            Rc = Region("sAc")
            P.dma("sp", slp[:], I["c_slp"], writes=[Rc])
            P.dma("sp", sk4[:], I["sk4"][l], writes=[Rc])
            P.dma("sp", dist[:], I["c_dist"].partition_broadcast(NP), writes=[Rc])
            lg = self.sb(ph, [NP, 3, 129], F32, "lg")
            ab = self.sb(ph, [NP, 3, 129], F32, "ab")
            Rlg, Rab = Region("lg"), Region("ab")
            P.op("dve", lambda h: h.tensor_tensor(ab[:], dist[:].unsqueeze(1).to_broadcast([NP, 3, 129]), slp[:].unsqueeze(2).to_broadcast([NP, 3, 129]), ALU.mult),
                 reads=[Rc], writes=[Rab])
            KC = 16
            kvb = self.sb(ph, [NP, KC, 128], F32, "kvb")
            tmp = self.sb(ph, [NP, KC * 128], F32, "stmp")
            Rkvb, Rtmp = Region("kvb"), Region("stmp")
            for c in range(128 // KC):
                P.dma("sp", kvb[:].rearrange("p a b -> p (a b)"), I["ck"][l][:, c * KC * 128:(c + 1) * KC * 128], writes=[Rkvb])
                for g3 in range(3):
                    P.op("dve", lambda h, g3=g3: h.tensor_tensor(tmp[:].rearrange("p (a b) -> p a b", a=KC), kvb[:], q4[:, g3, :].unsqueeze(1).to_broadcast([NP, KC, 128]), ALU.mult),
                         reads=[Rkvb, R4], writes=[Rtmp])
                    P.op("dve", lambda h, g3=g3, c=c: h.tensor_reduce(lg[:, g3, c * KC:(c + 1) * KC], tmp[:].rearrange("p (a b) -> p a b", a=KC), AX.X, ALU.add),
                         reads=[Rtmp], writes=[Rlg])
            P.op("dve", lambda h: h.tensor_tensor(tmp[:, 0:384].rearrange("p (a b) -> p a b", a=3), q4[:], k4[:].unsqueeze(1).to_broadcast([NP, 3, 128]), ALU.mult),
                 reads=[R4], writes=[Rtmp])
            P.op("dve", lambda h: h.tensor_reduce(lg[:, :, 128], tmp[:, 0:384].rearrange("p (a b) -> p a b", a=3), AX.X, ALU.add), reads=[Rtmp], writes=[Rlg])
            P.op("dve", lambda h: h.tensor_tensor(lg[:], lg[:], ab[:], ALU.subtract), reads=[Rlg, Rab], writes=[Rlg])
            sm = self.sb(ph, [NP, 24], F32, "sAsm")
            Rsm = Region("sAsm")
            P.op("dve", lambda h: h.tensor_reduce(sm[:, 0:3], lg[:], AX.X, ALU.max), reads=[Rlg], writes=[Rsm])
            P.op("dve", lambda h: h.tensor_tensor(sm[:, 0:3], sm[:, 0:3], sk4[:], ALU.max), reads=[Rsm, Rc], writes=[Rsm])
            P.op("dve", lambda h: h.tensor_tensor(lg[:], lg[:], sm[:, 0:3].unsqueeze(2).to_broadcast([NP, 3, 129]), ALU.subtract), reads=[Rlg, Rsm], writes=[Rlg])
            P.op("act", lambda h: h.activation(lg[:], lg[:], AF.Exp), reads=[Rlg], writes=[Rlg])
            P.op("dve", lambda h: h.tensor_reduce(sm[:, 3:6], lg[:], AX.X, ALU.add), reads=[Rlg], writes=[Rsm])
            P.op("dve", lambda h: h.tensor_tensor(sm[:, 6:9], sk4[:], sm[:, 0:3], ALU.subtract), reads=[Rsm, Rc], writes=[Rsm])
            P.op("act", lambda h: h.activation(sm[:, 6:9], sm[:, 6:9], AF.Exp), reads=[Rsm], writes=[Rsm])
            P.op("dve", lambda h: h.tensor_tensor(sm[:, 3:6], sm[:, 3:6], sm[:, 6:9], ALU.add), reads=[Rsm], writes=[Rsm])
            P.op("dve", lambda h: h.reciprocal(sm[:, 9:12], sm[:, 3:6]), reads=[Rsm], writes=[Rsm])
            P.op("dve", lambda h: h.tensor_tensor(lg[:], lg[:], sm[:, 9:12].unsqueeze(2).to_broadcast([NP, 3, 129]), ALU.mult), reads=[Rlg, Rsm], writes=[Rlg])
            o4 = self.sb(ph, [NP, 3, 128], F32, "o4")
            o4t = self.sb(ph, [NP, 3, 128], F32, "o4t")
            Ro4, Ro4t = Region("o4"), Region("o4t")
            for g3 in range(3):
                P.op("dve", lambda h, g3=g3: h.tensor_scalar(o4[:, g3, :], v4[:], lg[:, g3, 128:129], None, op0=ALU.mult), reads=[R4, Rlg], writes=[Ro4])
            for c in range(128 // KC):
                P.dma("sp", kvb[:].rearrange("p a b -> p (a b)"), I["cv"][l][:, c * KC * 128:(c + 1) * KC * 128], writes=[Rkvb])
                for g3 in range(3):
                    P.op("dve", lambda h, g3=g3, c=c: h.tensor_tensor(tmp[:].rearrange("p (d s) -> p d s", s=KC), kvb[:].rearrange("p s d -> p d s"),
                                                                      lg[:, g3, c * KC:(c + 1) * KC].unsqueeze(1).to_broadcast([NP, 128, KC]), ALU.mult),
                         reads=[Rkvb, Rlg], writes=[Rtmp])
                    P.op("dve", lambda h, g3=g3: h.tensor_reduce(o4t[:, g3, :], tmp[:].rearrange("p (d s) -> p d s", s=KC), AX.X, ALU.add), reads=[Rtmp], writes=[Ro4t])
                    P.op("dve", lambda h, g3=g3: h.tensor_tensor(o4[:, g3, :], o4[:, g3, :], o4t[:, g3, :], ALU.add), reads=[Ro4, Ro4t], writes=[Ro4])
            P.dma("sp", S["bo"].rearrange("b (kv x) -> (b kv) x", kv=KV), o4[:].rearrange("p a b -> p (a b)"), reads=[Ro4], writes=[Rb])
            ao = self.sb(ph, [NS, 768], F32, "ao")
            yb = self.sb(ph, [NS, 768], BF16, "syb")
            Rao, Ryb = Region("ao"), Region("syb")
            P.dma("sp", ao[:], S["bo"], reads=[Rb], writes=[Rao])
            P.op("dve", lambda h: h.tensor_tensor(yb[:], ao[:], za[:], ALU.mult), reads=[Rao, Rza], writes=[Ryb])
            self.s_to_yT(yb, Ryb, 0, 6)
            self.barrier()

    def s_phase_M(self):
        P, l, I, S, O = self.P, self.l, self.I, self.S, self.O
        NH = HM
        QW, VW = NH * 128, NH * 256
        with contextlib.ExitStack() as ph:
            q = self.sb(ph, [NS, QW], F32, "smq")
            k = self.sb(ph, [NS, QW], F32, "smk")
            v = self.sb(ph, [NS, VW], BF16, "smv")
            G = self.sb(ph, [NS, VW], F32, "smG")
            gts = self.sb(ph, [NS, 2 * NH], F32, "smg")
            tm = self.sb(ph, [NS, 256], F32, "smt")
            Rq, Rk, Rv, RG, Rg, Rtm = (Region(n) for n in ("smq", "smk", "smv", "smG", "smg", "smt"))
            for g, wb, Rw in self.wstream("in", list(range(G_MQK, G_CU)) + [G_GATE]):
                def evac(acc, Racc, g=g):
                    if g == G_GATE:
                        P.op("dve", lambda h: h.tensor_tensor(gts[:], acc[0:NS, 0:2 * NH], self.bifb[0:NS, l * 2 * NH:(l + 1) * 2 * NH], ALU.add), reads=[Racc, self.Rbifb], writes=[Rg])
                    elif g < G_MV:
                        for j in range(2):
                            ch = (g - G_MQK) * 2 + j
                            if ch < NH:
                                P.op("act", lambda h, ch=ch, j=j: h.copy(q[:, ch * 128:(ch + 1) * 128], acc[0:NS, j * 128:(j + 1) * 128]), reads=[Racc], writes=[Rq])
                            else:
                                P.op("act", lambda h, ch=ch, j=j: h.activation(k[:, (ch - NH) * 128:(ch - NH + 1) * 128], acc[0:NS, j * 128:(j + 1) * 128], AF.Copy, scale=QSCALE),
                                     reads=[Racc], writes=[Rk])
                    elif g < G_MO:
                        c0 = (g - G_MV) * 256
                        P.op("dve", lambda h: h.tensor_copy(v[:, c0:c0 + 256], acc[0:NS, 0:256]), reads=[Racc], writes=[Rv])
                    elif g < G_MZ:
                        c0 = (g - G_MO) * 256
                        P.op("act", lambda h: h.activation(G[:, c0:c0 + 256], acc[0:NS, 0:256], AF.Sigmoid), reads=[Racc], writes=[RG])
                    else:
                        c0 = (g - G_MZ) * 256
                        P.op("act", lambda h: h.activation(tm[:], acc[0:NS, 0:256], AF.Silu), reads=[Racc], writes=[Rtm])
                        P.op("dve", lambda h: h.tensor_tensor(G[:, c0:c0 + 256], G[:, c0:c0 + 256], tm[:], ALU.mult), reads=[Rtm, RG], writes=[RG])
                self.s_proj(wb, Rw, 2 * NH if g == G_GATE else 256, evac)
            sm = self.sb(ph, [NS, 96], F32, "smsm")
            Rsm = Region("smsm")
            n0 = self.sb(ph, [NS, QW], F32, "smn")
            Rn0 = Region("smn")
            P.dma("sp", n0[:], I["sn"][l], writes=[Rn0])
            P.dma("sp", sm[:, 0:NH], I["sm"][l], writes=[Rsm])
            with contextlib.ExitStack() as ph2:
                mhgb = self.sb(ph2, [NS, VW], F32, "mhgb")
                Rmh = Region("mhgb")
                P.dma("sp", mhgb[:], I["mhg"][l].partition_broadcast(NS), writes=[Rmh])
                P.op("dve", lambda h, mhgb=mhgb: h.tensor_tensor(G[:], G[:], mhgb[:], ALU.mult), reads=[RG, Rmh], writes=[RG])
                self.barrier()
            c_m0, c_sp, c_int, c_mt, c_wq, c_wi, c_qk, c_w, c_qn, c_den, c_t, c_rd, c_ssq, c_f, c_emt = [6 * i for i in range(15)]
            s_ = lambda c: sm[:, c:c + NH]
            ip, fp = gts[:, 0:NH], gts[:, NH:2 * NH]
            P.op("act", lambda h: h.activation(s_(c_sp), fp, AF.Exp, scale=-1.0), reads=[Rg], writes=[Rsm])
            P.op("act", lambda h: h.activation(s_(c_sp), s_(c_sp), AF.Ln, bias=1.0), reads=[Rsm], writes=[Rsm])
            P.op("dve", lambda h: h.tensor_tensor(s_(c_int), s_(c_m0), s_(c_sp), ALU.subtract), reads=[Rsm], writes=[Rsm])
            P.op("dve", lambda h: h.tensor_tensor(s_(c_mt), s_(c_int), ip, ALU.max), reads=[Rsm, Rg], writes=[Rsm])
            P.op("dve", lambda h: h.tensor_tensor(s_(c_wq), ip, s_(c_mt), ALU.subtract), reads=[Rsm, Rg], writes=[Rsm])
            P.op("dve", lambda h: h.tensor_tensor(s_(c_wi), s_(c_int), s_(c_mt), ALU.subtract), reads=[Rsm], writes=[Rsm])
            P.op("act", lambda h: h.activation(s_(c_wq), s_(c_wq), AF.Exp), reads=[Rsm], writes=[Rsm])
            P.op("act", lambda h: h.activation(s_(c_wi), s_(c_wi), AF.Exp), reads=[Rsm], writes=[Rsm])
            P.op("act", lambda h: h.activation(s_(c_emt), s_(c_mt), AF.Exp, scale=-1.0), reads=[Rsm], writes=[Rsm])
            P.dma("sp", O["ms"][l], s_(c_mt), reads=[Rsm], writes=[self.R["ms"]])
            big = self.sb(ph, [NS, VW], F32, "smbig")
            Rbig = Region("smbig")
            bc6 = lambda ap, n: ap.unsqueeze(2).to_broadcast([NS, NH, n])
            q3 = q[:].rearrange("p (a b) -> p a b", a=NH)
            k3 = k[:].rearrange("p (a b) -> p a b", a=NH)
            n3 = n0[:].rearrange("p (a b) -> p a b", a=NH)
            b3 = big[:, 0:QW].rearrange("p (a b) -> p a b", a=NH)
            P.op("dve", lambda h: h.tensor_tensor(b3, q3, k3, ALU.mult), reads=[Rq, Rk], writes=[Rbig])
            P.op("dve", lambda h: h.tensor_reduce(s_(c_qk), b3, AX.X, ALU.add), reads=[Rbig], writes=[Rsm])
            P.op("dve", lambda h: h.tensor_tensor(b3, q3, n3, ALU.mult), reads=[Rq, Rn0], writes=[Rbig])
            P.op("dve", lambda h: h.tensor_reduce(s_(c_qn), b3, AX.X, ALU.add), reads=[Rbig], writes=[Rsm])
            P.op("dve", lambda h: h.tensor_tensor(s_(c_w), s_(c_wq), s_(c_qk), ALU.mult), reads=[Rsm], writes=[Rsm])
            P.op("dve", lambda h: h.tensor_tensor(s_(c_den), s_(c_wi), s_(c_qn), ALU.mult), reads=[Rsm], writes=[Rsm])
            P.op("dve", lambda h: h.tensor_tensor(s_(c_den), s_(c_den), s_(c_w), ALU.add), reads=[Rsm], writes=[Rsm])
            P.op("dve", lambda h: h.scalar_tensor_tensor(s_(c_t), s_(c_den), -1.0, s_(c_den), op0=ALU.mult, op1=ALU.max), reads=[Rsm], writes=[Rsm])
            P.op("dve", lambda h: h.tensor_tensor(s_(c_t), s_(c_t), s_(c_emt), ALU.max), reads=[Rsm], writes=[Rsm])
            P.op("dve", lambda h: h.reciprocal(s_(c_rd), s_(c_t)), reads=[Rsm], writes=[Rsm])
            ksc = self.sb(ph, [NS, QW], F32, "smksc")
            Rksc = Region("smksc")
            ksc3 = ksc[:].rearrange("p (a b) -> p a b", a=NH)
            P.op("dve", lambda h: h.tensor_tensor(ksc3, k3, bc6(s_(c_wq), 128), ALU.mult), reads=[Rk, Rsm], writes=[Rksc])
            P.op("dve", lambda h: h.tensor_tensor(n3, n3, bc6(s_(c_wi), 128), ALU.mult), reads=[Rn0, Rsm], writes=[Rn0])
            P.op("dve", lambda h: h.tensor_tensor(n0[:], n0[:], ksc[:], ALU.add), reads=[Rn0, Rksc], writes=[Rn0])
            P.dma("sp", O["ns"][l], n0[:], reads=[Rn0], writes=[self.R["ns"]])
            qT = self.sb(ph, [128, NH, NS], F32, "smqT")
            RqT = Region("smqT")
            pq, Rpq = self.pb[0], self.Rpb[0]
            fns = [(lambda h, hh=hh: h.transpose(pq[:, hh * NS:(hh + 1) * NS], q[0:NS, hh * 128:(hh + 1) * 128], self.identf[0:NS, 0:NS])) for hh in range(NH)]
            P.group("pe", fns, reads=[Rq, self.Ridf], writes=[Rpq])
            P.op("act", lambda h: h.copy(qT[:].rearrange("p a b -> p (a b)"), pq[:, 0:NH * NS]), reads=[Rpq], writes=[RqT])
            i32 = self.sb(ph, [128, NS, NS], F32, "i32")
            Ri32 = Region("i32")
            P.dma("sp", i32[:].rearrange("p a b -> p (a b)"), I["c_i32"], writes=[Ri32])
            Wd = self.sb(ph, [NS, NS, NH], F32, "smWd")
            RWd = Region("smWd")
            P.op("dve", lambda h: h.tensor_tensor(Wd[:], self.identf[0:NS, 0:NS].unsqueeze(2).to_broadcast([NS, NS, NH]), s_(c_wi).unsqueeze(1).to_broadcast([NS, NS, NH]), ALU.mult),
                 reads=[Rsm, self.Ridf], writes=[RWd])
            pw, Rpw = self.pb[1], self.Rpb[1]
            P.op("pe", lambda h: h.matmul(pw[:, 0:NS * NH], self.onesf[0:NS, :], Wd[:].rearrange("p a b -> p (a b)"), start=True, stop=True), reads=[RWd, self.Ronesf], writes=[Rpw])
            wcb = self.sb(ph, [128, NS, NH], F32, "smwcb")
            Rwcb = Region("smwcb")
            P.op("act", lambda h: h.copy(wcb[:].rearrange("p a b -> p (a b)"), pw[:, 0:NS * NH]), reads=[Rpw], writes=[Rwcb])
            vb, Rvb = v, Rv
            Qm = self.sb(ph, [128, NS, NS], F32, "smQm")
            Km = self.sb(ph, [NS, NS, 128], BF16, "smKm")
            RQm, RKm = Region("smQm"), Region("smKm")
            CB = 8
            Cc = self.sb(ph, [128, CB, 256], F32, "smCc")
            RCc = Region("smCc")
            pR = [self.pb[4], self.pb[5]]
            RpR = [self.Rpb[4], self.Rpb[5]]
            pD = [self.pb[2], self.pb[3]]
            RpD = [self.Rpb[2], self.Rpb[3]]

            def do_head(hh):
                P.op("dve", lambda h: h.tensor_tensor(Qm[:], qT[:, hh, :].unsqueeze(2).to_broadcast([128, NS, NS]), i32[:], ALU.mult), reads=[RqT, Ri32], writes=[RQm])
                P.op("dve", lambda h: h.tensor_tensor(Km[:], self.identf[0:NS, 0:NS].unsqueeze(2).to_broadcast([NS, NS, 128]),
                                                      ksc[:, hh * 128:(hh + 1) * 128].unsqueeze(1).to_broadcast([NS, NS, 128]), ALU.mult),
                     reads=[Rksc, self.Ridf], writes=[RKm])
                oR = pR[hh // 2][0:NS, (hh % 2) * 256:(hh % 2) * 256 + 256]
                for cb in range(NS // CB):
                    P.dma("sp", Cc[:], I["sC"][l, cb * CB:(cb + 1) * CB, hh].rearrange("b d v -> d b v"), writes=[RCc])
                    for j in range(CB):
                        bp = cb * CB + j
                        P.op("pe", lambda h, j=j, bp=bp: h.matmul(oR, Qm[:, bp, :], Cc[:, j, :], start=(bp == 0), stop=(bp == NS - 1)),
                             reads=[RQm, RCc], writes=[RpR[hh // 2]])
                    for j in range(CB):
                        bp = cb * CB + j
                        pd, Rpd = pD[j % 2], RpD[j % 2]
                        P.op("pe", lambda h, j=j, bp=bp, pd=pd: h.matmul(pd[:, 0:256], Km[:, bp, :], vb[:, hh * 256:(hh + 1) * 256], start=True, stop=True),
                             reads=[RKm, Rvb], writes=[Rpd])
                        P.op("dve", lambda h, j=j, bp=bp, pd=pd: h.scalar_tensor_tensor(Cc[:, j, :], Cc[:, j, :], wcb[:, bp, hh:hh + 1], pd[:, 0:256], op0=ALU.mult, op1=ALU.add),
                             reads=[Rpd, Rwcb, RCc], writes=[RCc])
                    P.dma("sp", O["Cs"][l, cb * CB:(cb + 1) * CB, hh].rearrange("b d v -> d b v"), Cc[:], reads=[RCc], writes=[self.R["Cs"]])
            for hh in range(NH):
                do_head(hh)
            num = big
            junk = self.sb(ph, [NS, 256], F32, "smjunk")
            Rjunk = Region("smjunk")
            for hh in range(NH):
                sl = slice(hh * 256, (hh + 1) * 256)
                P.op("dve", lambda h, hh=hh, sl=sl: h.tensor_scalar(num[:, sl], v[:, sl], sm[:, c_w + hh:c_w + hh + 1], None, op0=ALU.mult), reads=[Rv, Rsm], writes=[Rbig])
                P.op("dve", lambda h, hh=hh, sl=sl: h.scalar_tensor_tensor(num[:, sl], pR[hh // 2][0:NS, (hh % 2) * 256:(hh % 2) * 256 + 256], sm[:, c_wi + hh:c_wi + hh + 1],
                                                                           num[:, sl], op0=ALU.mult, op1=ALU.add),
                     reads=[RpR[hh // 2], Rsm, Rbig], writes=[Rbig])
                P.op("act", lambda h, hh=hh, sl=sl: h.activation(junk[:], num[:, sl], AF.Square, accum_out=sm[:, c_ssq + hh:c_ssq + hh + 1]), reads=[Rbig, Rsm], writes=[Rjunk, Rsm])
            P.op("dve", lambda h: h.tensor_tensor(s_(c_t), s_(c_rd), s_(c_rd), ALU.mult), reads=[Rsm], writes=[Rsm])
            P.op("dve", lambda h: h.tensor_tensor(s_(c_t), s_(c_t), s_(c_ssq), ALU.mult), reads=[Rsm], writes=[Rsm])
            P.op("act", lambda h: h.activation(s_(c_t), s_(c_t), AF.Sqrt, bias=EPS, scale=1.0 / 256.0), reads=[Rsm], writes=[Rsm])
            P.op("dve", lambda h: h.reciprocal(s_(c_f), s_(c_t)), reads=[Rsm], writes=[Rsm])
            P.op("dve", lambda h: h.tensor_tensor(s_(c_f), s_(c_f), s_(c_rd), ALU.mult), reads=[Rsm], writes=[Rsm])
            num3 = num[:].rearrange("p (a b) -> p a b", a=NH)
            P.op("dve", lambda h: h.tensor_tensor(num3, num3, bc6(s_(c_f), 256), ALU.mult), reads=[Rbig, Rsm], writes=[Rbig])
            yb = self.sb(ph, [NS, VW], BF16, "smyb")
            Ryb = Region("smyb")
            P.op("dve", lambda h: h.tensor_tensor(yb[:], num[:], G[:], ALU.mult), reads=[Rbig, RG], writes=[Ryb])
            self.s_to_yT(yb, Ryb, 6, 6)
            self.barrier()

    def s_phase_C(self):
        P, l, I, S, O = self.P, self.l, self.I, self.S, self.O
        CW = GC * 128
        with contextlib.ExitStack() as ph:
            u = self.sb(ph, [NS, CW], F32, "scu")
            vv = self.sb(ph, [NS, 1024], F32, "scv")
            tm = self.sb(ph, [NS, 256], F32, "sct")
            Ru, Rvv, Rtm = Region("scu"), Region("scv"), Region("sct")
            for g, wb, Rw in self.wstream("in", list(range(G_CU, G_GATE))):
                def evac(acc, Racc, g=g):
                    if g < G_CV:
                        c0 = (g - G_CU) * 256
                        P.op("act", lambda h: h.activation(u[:, c0:c0 + 256], acc[0:NS, 0:256], AF.Gelu), reads=[Racc], writes=[Ru])
                    elif g < G_CZ:
                        c0 = (g - G_CV) * 256
                        P.op("act", lambda h: h.activation(vv[:, c0:c0 + 256], acc[0:NS, 0:256], AF.Gelu), reads=[Racc], writes=[Rvv])
                    else:
                        c0 = (g - G_CZ) * 256
                        P.op("act", lambda h: h.activation(tm[:], acc[0:NS, 0:256], AF.Silu), reads=[Racc], writes=[Rtm])
                        P.op("dve", lambda h: h.tensor_tensor(u[:, c0:c0 + 256], u[:, c0:c0 + 256], tm[:], ALU.mult), reads=[Rtm, Ru], writes=[Ru])
                self.s_proj(wb, Rw, 256, evac)
            junk = self.sb(ph, [NS, 1024], BF16, "scj")
            sm = self.sb(ph, [NS, 16], F32, "scsm")
            Rj, Rsm = Region("scj"), Region("scsm")
            self.layernorm(vv, Rvv, NS, junk, Rj, sm, Rsm)
            P.dma("sp", O["cvs"][l], vv[:], reads=[Rvv], writes=[self.R["cvs"]])
            wb8 = self.sb(ph, [NS, 16], F32, "scw8")
            Rw8 = Region("scw8")
            P.dma("sp", wb8[:, 0:GC], I["ws00"][l * GC:(l + 1) * GC].partition_broadcast(NS), writes=[Rw8])
            P.dma("sp", wb8[:, 8:8 + GC], I["bs0"][l * GC:(l + 1) * GC].partition_broadcast(NS), writes=[Rw8])
            v3 = vv[:, 0:CW].rearrange("p (a b) -> p a b", a=GC)
            P.op("dve", lambda h: h.tensor_tensor(v3, v3, wb8[:, 0:GC].unsqueeze(2).to_broadcast([NS, GC, 128]), ALU.mult), reads=[Rvv, Rw8], writes=[Rvv])
            P.op("dve", lambda h: h.tensor_tensor(v3, v3, wb8[:, 8:8 + GC].unsqueeze(2).to_broadcast([NS, GC, 128]), ALU.add), reads=[Rvv, Rw8], writes=[Rvv])
            yb = self.sb(ph, [NS, CW], BF16, "scyb")
            Ryb = Region("scyb")
            P.op("dve", lambda h: h.tensor_tensor(yb[:], vv[:, 0:CW], u[:], ALU.mult), reads=[Rvv, Ru], writes=[Ryb])
            self.s_to_yT(yb, Ryb, 12, GC)
            self.barrier()


def _consts():
    c = {}
    c["c_ident"] = np.eye(128, dtype=np.float32)
    q = np.arange(128)[:, None]
    s = np.arange(256)[None, :]
    dist = q + 128 - s
    valid = (dist >= 0) & (dist <= 128)
    c["c_dm"] = np.where(valid, dist, 1.0e9).astype(np.float32)
    c["c_dm0"] = c["c_dm"].copy()
    t = np.arange(128)[:, None]
    s2 = np.arange(128)[None, :]
    ok = ((t // 64) == (s2 // 64)) & (s2 <= t)
    m1 = np.where(ok, 0.0, -30000.0).astype(np.float32)
    c["c_mask6"] = np.tile(m1, (1, 3))
    c["c_tri2"] = ok.T.astype(np.float32).copy()
    onesc = np.zeros((128, 2, 128), np.float32)
    onesc[0:64, 0, :] = 1.0
    onesc[64:128, 1, :] = 1.0
    c["c_onesc"] = onesc.reshape(128, 256)
    cm = np.zeros((128, 2), np.float32)
    cm[0:64, 0] = 1.0
    cm[64:128, 1] = 1.0
    c["c_cmask"] = cm
    c["c_tril"] = (np.arange(128)[None, :] >= np.arange(128)[:, None]).astype(np.float32)
    c["c_dist"] = np.concatenate([np.arange(128, 0, -1), [0]]).astype(np.float32)
    c["c_i32"] = np.tile(np.eye(NS, dtype=np.float32).reshape(1, NS * NS), (128, 1))
    return c


_NC_CACHE = {}
_OFF = dict(aq=0, ak=1536, av=2048, az=2560, mq=4096, mk=4864, mv=5632, mi=7168, mf=7174, mo=7180, mz=8716, cu=10252, cv=11276, cz=12300)


def _cols(c):
    r = lambda base, n, w: list(range(base + c * w, base + c * w + w)) if n is None else None
    idx = []
    idx += list(range(_OFF["aq"] + c * 768, _OFF["aq"] + (c + 1) * 768))
    idx += list(range(_OFF["ak"] + c * 256, _OFF["ak"] + (c + 1) * 256))
    idx += list(range(_OFF["av"] + c * 256, _OFF["av"] + (c + 1) * 256))
    idx += list(range(_OFF["az"] + c * 768, _OFF["az"] + (c + 1) * 768))
    idx += list(range(_OFF["mq"] + c * 384, _OFF["mq"] + (c + 1) * 384))
    idx += list(range(_OFF["mk"] + c * 384, _OFF["mk"] + (c + 1) * 384))
    idx += list(range(_OFF["mv"] + c * 768, _OFF["mv"] + (c + 1) * 768))
    idx += list(range(_OFF["mo"] + c * 768, _OFF["mo"] + (c + 1) * 768))
    idx += list(range(_OFF["mz"] + c * 768, _OFF["mz"] + (c + 1) * 768))
    idx += list(range(_OFF["cu"] + c * 512, _OFF["cu"] + (c + 1) * 512))
    idx += list(range(_OFF["cv"] + c * 512, _OFF["cv"] + (c + 1) * 512))
    idx += list(range(_OFF["cv"] + (1 - c) * 512, _OFF["cv"] + (2 - c) * 512))
    idx += list(range(_OFF["cz"] + c * 512, _OFF["cz"] + (c + 1) * 512))
    idx += list(range(_OFF["mi"] + c * 3, _OFF["mi"] + (c + 1) * 3))
    idx += list(range(_OFF["mf"] + c * 3, _OFF["mf"] + (c + 1) * 3))
    return np.array(idx)


def kernel(x_prompt, x_sample, cache_win_k, cache_win_v, state_mlstm_C, state_mlstm_n, state_mlstm_m,
           norm_gain, w_in, b_if, attn_sinks, m_head_gain, c_ln_gain, c_ln_bias, c_w_s, c_b_s, w_out, final_gain, _ncores=8):
    f = lambda a: np.ascontiguousarray(np.asarray(a, dtype=np.float32))
    w_in, w_out = f(w_in), f(w_out)
    xp = f(x_prompt)
    base = dict(_consts())
    base["xs"] = f(x_sample).reshape(NS, D)
    base["gainT"] = np.ascontiguousarray(f(norm_gain).reshape(2, 32, 128).transpose(0, 2, 1))
    base["fgain"] = f(final_gain)
    half = []
    for c in range(2):
        m = {}
        idx = _cols(c)
        wp = np.zeros((2, D, NG_IN * 256), np.float32)
        wp[:, :, 0:idx.size] = w_in[:, :, idx]
        m["w_in"] = np.ascontiguousarray(wp.reshape(2, 32, 128, NG_IN, 256).transpose(0, 3, 2, 1, 4)).reshape(2, NG_IN, 128, 32 * 256)
        rows = np.concatenate([np.arange(c * 768, (c + 1) * 768), 1536 + np.arange(c * 768, (c + 1) * 768), 3072 + np.arange(c * 512, (c + 1) * 512)])
        wo = w_out[:, rows, :]
        m["w_out"] = np.ascontiguousarray(wo.reshape(2, KY, 128, NG_OUT, 512).transpose(0, 3, 2, 1, 4)).reshape(2, NG_OUT, 128, KY * 512)
        bi = f(b_if)
        m["bif"] = np.ascontiguousarray(bi[:, :, c * 3:(c + 1) * 3]).reshape(12)
        sk = f(attn_sinks)[:, c * 6:(c + 1) * 6]
        m["sinks"] = np.ascontiguousarray(sk).reshape(12)
        m["nslp"] = -np.array(SLOPES[c * 6:(c + 1) * 6], np.float32)
        mh = f(m_head_gain)[:, c * 768:(c + 1) * 768]
        m["mhg"] = np.ascontiguousarray(mh)
        m["mhgT"] = np.ascontiguousarray(mh.reshape(2, 6, 128).transpose(0, 2, 1))
        perm = np.concatenate([np.arange(c * 512, (c + 1) * 512), np.arange((1 - c) * 512, (2 - c) * 512)])
        m["lng"] = np.ascontiguousarray(f(c_ln_gain)[:, perm])
        m["lnb"] = np.ascontiguousarray(f(c_ln_bias)[:, perm])
        ws = f(c_w_s)[:, c * 4:(c + 1) * 4]
        m["wsT"] = np.ascontiguousarray(ws.transpose(0, 3, 1, 2)).reshape(2, 128, GC * 128)
        bsv = f(c_b_s)[:, c * 4:(c + 1) * 4]
        m["bs"] = np.ascontiguousarray(bsv).reshape(2, GC * 128)
        m["ws00"] = np.ascontiguousarray(ws[:, :, 0, 0]).reshape(2 * GC)
        m["bs0"] = np.ascontiguousarray(bsv[:, :, 0]).reshape(2 * GC)
        ck = f(cache_win_k)[:, :, :, c * 2:(c + 1) * 2]
        cvv = f(cache_win_v)[:, :, :, c * 2:(c + 1) * 2]
        m["ck"] = np.ascontiguousarray(ck.transpose(0, 1, 3, 2, 4)).reshape(2, 64, 128 * 128)
        m["cv"] = np.ascontiguousarray(cvv.transpose(0, 1, 3, 2, 4)).reshape(2, 64, 128 * 128)
        m["sk4"] = np.ascontiguousarray(np.tile(sk.reshape(2, 1, 2, 3), (1, 32, 1, 1)).reshape(2, 64, 3))
        m["c_slp"] = np.tile(np.array(SLOPES[c * 6:(c + 1) * 6], np.float32).reshape(2, 3), (32, 1))
        m["sC"] = np.ascontiguousarray(f(state_mlstm_C)[:, :, c * 3:(c + 1) * 3])
        m["sn"] = np.ascontiguousarray(f(state_mlstm_n)[:, :, c * 3:(c + 1) * 3]).reshape(2, NS, HM * 128)
        m["sm"] = np.ascontiguousarray(f(state_mlstm_m)[:, :, c * 3:(c + 1) * 3])
        half.append(m)
    in_maps = []
    for core in range(_ncores):
        m = dict(base)
        m.update(half[core % 2])
        m["xp"] = xp[(core // 2) % 4]
        in_maps.append(m)
    if "nc" not in _NC_CACHE:
        _NC_CACHE["nc"] = KB().build()
    nc = _NC_CACHE["nc"]
    res = run_bass_kernel_spmd(nc, in_maps, core_ids=list(range(_ncores)))
    r = list(res.results)
    while len(r) < 8:
        r = r + r[0:2]
    pr = lambda b: (r[2 * b], r[2 * b + 1])
    y_prompt = np.stack([r[2 * b]["yp"] for b in range(4)]).reshape(4, SEQ, D)
    y_sample = r[0]["ys"].reshape(NS, 1, D)
    cat = lambda key, b, shp, ax: np.concatenate([pr(b)[0][key].reshape(shp), pr(b)[1][key].reshape(shp)], axis=ax)
    kp = np.stack([cat("kp", b, (2, 128, 2, 128), 2) for b in range(4)], axis=1)
    vp = np.stack([cat("vp", b, (2, 128, 2, 128), 2) for b in range(4)], axis=1)
    ks = cat("ks", 0, (2, NS, 1, 2, 128), 3)
    vs = cat("vs", 0, (2, NS, 1, 2, 128), 3)
    Cp = np.stack([np.concatenate([x["Cp"].reshape(2, 128, 3, 256).transpose(0, 2, 1, 3) for x in pr(b)], axis=1) for b in range(4)], axis=1)
    npp = np.stack([np.concatenate([x["np"].transpose(0, 2, 1) for x in pr(b)], axis=1) for b in range(4)], axis=1)
    mp = np.stack([np.concatenate([x["mp"].reshape(2, 3) for x in pr(b)], axis=1) for b in range(4)], axis=1)
    Cs = np.concatenate([r[0]["Cs"], r[1]["Cs"]], axis=2)
    ns = np.concatenate([r[0]["ns"].reshape(2, NS, 3, 128), r[1]["ns"].reshape(2, NS, 3, 128)], axis=2)
    ms = np.concatenate([r[0]["ms"], r[1]["ms"]], axis=2)
    cvp = np.stack([r[2 * b]["cvp"] for b in range(4)], axis=1)
    cvs = r[0]["cvs"].reshape(2, NS, 1, 1024)
    outs = (y_prompt, y_sample, kp, vp, ks, vs, Cp, npp, mp, Cs, ns, ms, cvp, cvs)
    return tuple(np.ascontiguousarray(o, dtype=np.float32) for o in outs)
```

```python
import contextlib
import numpy as np
import concourse.bass as bass
import concourse.mybir as mybir
from concourse.bass_utils import run_bass_kernel_spmd

F32 = mybir.dt.float32
BF16 = mybir.dt.bfloat16
ALU = mybir.AluOpType
AF = mybir.ActivationFunctionType
AX = mybir.AxisListType

D = 4096
SEQ = 2048
NS = 32
DIN = 13324
NG_IN = 29
NG_OUT = 8
TT = 512
NBLK = TT // 128
NTILE = SEQ // TT
EPS = 1e-6
SLOPES = [2.0 ** (-8.0 * (h + 1) / 12.0) for h in range(12)]
QSCALE = 128.0 ** -0.5
G_AQ, G_AK, G_AV, G_AZ = 0, 3, 4, 5
G_MQK, G_MV, G_MO, G_MZ = 8, 11, 14, 17
G_CU, G_CV, G_CZ, G_GATE = 20, 22, 26, 28
HA, KV, HM, GC = 6, 2, 3, 4
KY = 16
PAIRS = [[0, 1], [2, 3], [4, 5], [6, 7]]
SBUF_LIMIT = 182 * 1024


class Region:
    __slots__ = ("name", "last_w", "readers", "excl")

    def __init__(self, name, excl=False):
        self.name = name
        self.last_w = None
        self.readers = []
        self.excl = excl


class EngineCtx:
    def __init__(self, name, sem):
        self.name = name
        self.sem = sem
        self.count = 0
        self.known = {}
        self.instrs = []


class Prog:
    def __init__(self, nc, stack, n_dma_sems=24):
        self.nc = nc
        self.eng = {}
        self.sems = {}
        for name in ("pe", "act", "dve", "pool", "sp"):
            sem = stack.enter_context(nc.semaphore("s_" + name))
            self.eng[name] = EngineCtx(name, sem)
            self.sems["e_" + name] = sem
        self.dma_pool = {}
        for q in ("sp", "pool"):
            lst = []
            for i in range(n_dma_sems if q == "sp" else 3):
                key = "d_%s_%d" % (q, i)
                self.sems[key] = stack.enter_context(nc.semaphore(key))
                lst.append([key, 0])
            self.dma_pool[q] = [lst, 0]
        self.n_instr = 0

    def _need(self, e, tok, waits):
        if tok is None:
            return
        key, val, src = tok
        if src == "pe" and e.name == "pe":
            return
        if e.known.get(key, 0) >= val:
            return
        if waits.get(key, 0) < val:
            waits[key] = val

    def _deps(self, e, reads, writes):
        waits = {}
        for r in reads:
            if r.excl:
                writes = list(writes) + [r]
                continue
            self._need(e, r.last_w, waits)
        for r in writes:
            self._need(e, r.last_w, waits)
            for t in r.readers:
                self._need(e, t, waits)
        return waits

    def _commit(self, tok, reads, writes):
        for r in reads:
            if r.excl:
                r.last_w = tok
                r.readers = []
                continue
            r.readers.append(tok)
            if len(r.readers) > 48:
                best = {}
                for t in r.readers:
                    if best.get(t[0], (None, -1))[1] < t[1]:
                        best[t[0]] = t
                r.readers = list(best.values())
        for r in writes:
            r.last_w = tok
            r.readers = []

    def _emit_waits(self, e, waits):
        for key, val in waits.items():
            e.instrs.append(("wait", self.sems[key], val))
            e.known[key] = val

    def op(self, engname, fn, reads=(), writes=()):
        return self.group(engname, [fn], reads, writes)

    def group(self, engname, fns, reads=(), writes=()):
        e = self.eng[engname]
        self._emit_waits(e, self._deps(e, reads, writes))
        e.count += 1
        tok = ("e_" + engname, e.count, engname)
        for fn in fns[:-1]:
            e.instrs.append(("op", fn, None, 0))
        e.instrs.append(("op", fns[-1], e.sem, 1))
        self._commit(tok, reads, writes)
        self.n_instr += len(fns)
        return tok

    def dma(self, qname, out, in_, reads=(), writes=()):
        e = self.eng[qname]
        waits = self._deps(e, reads, writes)
        pool, idx = self.dma_pool[qname]
        ent = pool[idx % len(pool)]
        self.dma_pool[qname][1] = idx + 1
        key, cum = ent
        if cum > 0 and e.known.get(key, 0) < cum and waits.get(key, 0) < cum:
            waits[key] = cum
        self._emit_waits(e, waits)
        ent[1] = cum + 16
        tok = (key, cum + 16, "dma")

        def fn(h, out=out, in_=in_):
            return h.dma_start(out=out, in_=in_)
        e.instrs.append(("op", fn, self.sems[key], 16))
        self._commit(tok, reads, writes)
        self.n_instr += 1
        return tok

    def coll(self, stack, ins_ap, outs_ap, reads=(), writes=()):
        e = self.eng["pool"]
        self._emit_waits(e, self._deps(e, reads, writes))
        key = "cc_%d" % len([k for k in self.sems if k.startswith("cc_")])
        sem = stack.enter_context(self.nc.semaphore(key))
        self.sems[key] = sem
        tok = (key, 1, "cc")

        def fn(h, ins_ap=ins_ap, outs_ap=outs_ap):
            return h.collective_compute("AllReduce", ALU.add, replica_groups=PAIRS, ins=[ins_ap], outs=[outs_ap])
        e.instrs.append(("op", fn, sem, 1))
        e.instrs.append(("wait", sem, 1))
        e.known[key] = 1
        self._commit(tok, reads, writes)
        return tok

    def wait_tok(self, engname, tok):
        e = self.eng[engname]
        waits = {}
        self._need(e, tok, waits)
        self._emit_waits(e, waits)

    def barrier(self, bar_out, bar_in, bar_region):
        sp = self.eng["sp"]
        waits = {}
        snap = {}
        for name in ("pe", "act", "dve"):
            c = self.eng[name].count
            snap["e_" + name] = c
            if c > 0:
                self._need(sp, ("e_" + name, c, name), waits)
        for key, cum in self.dma_pool["sp"][0]:
            snap[key] = cum
            if cum > 0:
                self._need(sp, (key, cum, "dma"), waits)
        self._emit_waits(sp, waits)
        tok = self.dma("sp", bar_out, bar_in, writes=[bar_region])
        for name in ("pe", "act", "dve"):
            self.wait_tok(name, tok)
            e = self.eng[name]
            for k, v in snap.items():
                if e.known.get(k, 0) < v:
                    e.known[k] = v

    def final_wait(self, engname, regions):
        e = self.eng[engname]
        waits = {}
        for r in regions:
            self._need(e, r.last_w, waits)
            for t in r.readers:
                self._need(e, t, waits)
        self._emit_waits(e, waits)

    def replay(self):
        nc = self.nc
        with nc.Block() as block:
            def mk(e):
                def body(h):
                    for ins in e.instrs:
                        if ins[0] == "wait":
                            h.wait_ge(ins[1], ins[2])
                        elif ins[2] is None:
                            ins[1](h)
                        else:
                            ins[1](h).then_inc(ins[2], ins[3])
                return body
            block.tensor(mk(self.eng["pe"]))
            block.scalar(mk(self.eng["act"]))
            block.vector(mk(self.eng["dve"]))
            block.gpsimd(mk(self.eng["pool"]))
            block.sync(mk(self.eng["sp"]))


class KB:
    def __init__(self, do_sample=True):
        self.do_sample = do_sample
        self.nc = bass.Bass("TRN2", target_bir_lowering=False)
        self.uid = 0
        self.sb_bytes = 0
        self.sb_peak = 0

    def sb(self, stack, shape, dt, name="t"):
        self.uid += 1
        n = 1
        for s in shape[1:]:
            n *= s
        nbytes = n * (4 if dt == F32 else 2)
        self.sb_bytes += nbytes
        self.sb_peak = max(self.sb_peak, self.sb_bytes)
        assert self.sb_bytes <= SBUF_LIMIT, ("SBUF overflow", name, self.sb_bytes)
        t = stack.enter_context(self.nc.sbuf_tensor("%s_%d" % (name, self.uid), list(shape), dt))

        def rel():
            self.sb_bytes -= nbytes
        stack.callback(rel)
        return t

    def dram_in(self, name, shape, dt=F32):
        return self.nc.dram_tensor(name, list(shape), dt, kind="ExternalInput").ap()

    def dram_out(self, name, shape, dt=F32):
        return self.nc.dram_tensor(name, list(shape), dt, kind="ExternalOutput").ap()

    def dram_scr(self, name, shape, dt=F32):
        return self.nc.dram_tensor(name, list(shape), dt, kind="Internal").ap()

    def build(self):
        nc = self.nc
        I = {}
        I["xp"] = self.dram_in("xp", [SEQ, D])
        I["xs"] = self.dram_in("xs", [NS, D])
        I["w_in"] = self.dram_in("w_in", [2, NG_IN, 128, 32 * 256])
        I["w_out"] = self.dram_in("w_out", [2, NG_OUT, 128, KY * 512])
        I["gainT"] = self.dram_in("gainT", [2, 128, 32])
        I["fgain"] = self.dram_in("fgain", [D])
        I["bif"] = self.dram_in("bif", [12])
        I["sinks"] = self.dram_in("sinks", [12])
        I["nslp"] = self.dram_in("nslp", [HA])
        I["mhgT"] = self.dram_in("mhgT", [2, 128, 6])
        I["mhg"] = self.dram_in("mhg", [2, 768])
        I["lng"] = self.dram_in("lng", [2, 1024])
        I["lnb"] = self.dram_in("lnb", [2, 1024])
        I["wsT"] = self.dram_in("wsT", [2, 128, GC * 128])
        I["bs"] = self.dram_in("bs", [2, GC * 128])
        I["ws00"] = self.dram_in("ws00", [2 * GC])
        I["bs0"] = self.dram_in("bs0", [2 * GC])
        I["ck"] = self.dram_in("ck", [2, 64, 128 * 128])
        I["cv"] = self.dram_in("cv", [2, 64, 128 * 128])
        I["sk4"] = self.dram_in("sk4", [2, 64, 3])
        I["c_slp"] = self.dram_in("c_slp", [64, 3])
        I["sC"] = self.dram_in("sC", [2, NS, HM, 128, 256])
        I["sn"] = self.dram_in("sn", [2, NS, HM * 128])
        I["sm"] = self.dram_in("sm", [2, NS, HM])
        I["c_ident"] = self.dram_in("c_ident", [128, 128])
        I["c_dm"] = self.dram_in("c_dm", [128, 256])
        I["c_dm0"] = self.dram_in("c_dm0", [128, 256])
        I["c_mask6"] = self.dram_in("c_mask6", [128, 384])
        I["c_tri2"] = self.dram_in("c_tri2", [128, 128])
        I["c_onesc"] = self.dram_in("c_onesc", [128, 256])
        I["c_cmask"] = self.dram_in("c_cmask", [128, 2])
        I["c_tril"] = self.dram_in("c_tril", [128, 128])
        I["c_dist"] = self.dram_in("c_dist", [129])
        I["c_i32"] = self.dram_in("c_i32", [128, NS * NS])
        O = {}
        O["yp"] = self.dram_out("yp", [SEQ, D])
        O["ys"] = self.dram_out("ys", [NS, D])
        O["kp"] = self.dram_out("kp", [2, 128, 256])
        O["vp"] = self.dram_out("vp", [2, 128, 256])
        O["ks"] = self.dram_out("ks", [2, NS, 256])
        O["vs"] = self.dram_out("vs", [2, NS, 256])
        O["Cp"] = self.dram_out("Cp", [2, 128, HM * 256])
        O["np"] = self.dram_out("np", [2, 128, HM])
        O["mp"] = self.dram_out("mp", [2, 1, HM])
        O["Cs"] = self.dram_out("Cs", [2, NS, HM, 128, 256])
        O["ns"] = self.dram_out("ns", [2, NS, HM * 128])
        O["ms"] = self.dram_out("ms", [2, NS, HM])
        O["cvp"] = self.dram_out("cvp", [2, 128, 1024])
        O["cvs"] = self.dram_out("cvs", [2, NS, 1024])
        self.I, self.O = I, O
        S = {}
        S["wb_in"] = self.dram_scr("wb_in", [2, NG_IN, 128, 32 * 256], BF16)
        S["wb_out"] = self.dram_scr("wb_out", [2, NG_OUT, 128, KY * 512], BF16)
        S["h1"] = self.dram_scr("h1", [SEQ, D])
        S["hs1"] = self.dram_scr("hs1", [NS, D])
        for l in range(2):
            S["po%d" % l] = self.dram_scr("po%d" % l, [SEQ, D])
            S["red%d" % l] = self.dram_scr("red%d" % l, [SEQ, D])
            S["pos%d" % l] = self.dram_scr("pos%d" % l, [NS, D])
            S["reds%d" % l] = self.dram_scr("reds%d" % l, [NS, D])
        S["bar"] = self.dram_scr("bar", [2, 16])
        S["bq"] = self.dram_scr("bq", [NS, 768])
        S["bk"] = self.dram_scr("bk", [NS, 256])
        S["bv"] = self.dram_scr("bv", [NS, 256])
        S["bo"] = self.dram_scr("bo", [NS, 768])
        self.S = S

        with contextlib.ExitStack() as st:
            self.st = st
            P = self.P = Prog(nc, st)
            self.R = {}
            for k in list(O.keys()) + ["h1", "hs1", "bar", "xin", "bnc"]:
                self.R[k] = Region(k)
            self.Rw = {}
            self.pb = [st.enter_context(nc.psum_tensor("pb%d" % i, [128, 512], F32)) for i in range(7)]
            self.Rpb = [Region("pb%d" % i, excl=True) for i in range(7)]
            self.tb = st.enter_context(nc.psum_tensor("tb0", [128, 1024], BF16))
            self.Rtb = Region("tb0", excl=True)
            self.acc_i = 0
            self.emit_all()
            P.replay()
        return nc

    def const(self, name, shape, dt, src_ap):
        t = self.sb(self.st, shape, dt, name)
        r = Region(name)
        self.P.dma("sp", t[:], src_ap, writes=[r])
        return t, r

    def barrier(self):
        self.P.barrier(self.S["bar"][0:1, :], self.S["bar"][1:2, :], self.R["bar"])

    def next_acc(self):
        i = self.acc_i % 4
        self.acc_i += 1
        return self.pb[i], self.Rpb[i]

    def convert_weights(self, l):
        P, S, I = self.P, self.S, self.I
        for g in range(NG_IN):
            r = Region("wbin%d_%d" % (l, g))
            self.Rw[("in", l, g)] = r
            P.dma("pool", S["wb_in"][l, g], I["w_in"][l, g], writes=[r])
        for g in range(NG_OUT):
            r = Region("wbout%d_%d" % (l, g))
            self.Rw[("out", l, g)] = r
            P.dma("pool", S["wb_out"][l, g], I["w_out"][l, g], writes=[r])

    def emit_all(self):
        P, nc, I, O, S, st = self.P, self.nc, self.I, self.O, self.S, self.st
        sbp = lambda shape, dt, name: self.sb(st, shape, dt, name)
        self.convert_weights(0)
        self.convert_weights(1)
        self.build_wseq()
        identf, Ridf = self.const("identf", [128, 128], F32, I["c_ident"])
        self.identf, self.Ridf = identf, Ridf
        identb = sbp([128, 128], BF16, "identb")
        Ridb = Region("identb")
        P.op("dve", lambda h: h.tensor_copy(identb[:], identf[:]), reads=[Ridf], writes=[Ridb])
        self.identb, self.Ridb = identb, Ridb
        self.dm, self.Rdm = self.const("dm", [128, 256], F32, I["c_dm"])
        self.dm0, self.Rdm0 = self.const("dm0", [128, 256], F32, I["c_dm0"])
        self.mask6, self.Rmask6 = self.const("mask6", [128, 384], F32, I["c_mask6"])
        self.tri2, self.Rtri2 = self.const("tri2", [128, 128], F32, I["c_tri2"])
        self.onesc, self.Ronesc = self.const("onesc", [128, 256], F32, I["c_onesc"])
        self.cmask, self.Rcmask = self.const("cmask", [128, 2], F32, I["c_cmask"])
        self.bifb, self.Rbifb = self.const("bifb", [128, 12], F32, I["bif"].partition_broadcast(128))
        self.sinkb, self.Rsinkb = self.const("sinkb", [128, 12], F32, I["sinks"].partition_broadcast(128))
        self.nslp, self.Rnslp = self.const("nslp", [128, HA], F32, I["nslp"].partition_broadcast(128))
        onesf = sbp([128, 128], F32, "onesf")
        self.Ronesf = Region("onesf")
        P.op("dve", lambda h: h.memset(onesf[:], 1.0), writes=[self.Ronesf])
        self.onesf = onesf
        onesb = sbp([128, 2], BF16, "onesb")
        self.Ronesb = Region("onesb")
        P.op("dve", lambda h: h.memset(onesb[:], 1.0), writes=[self.Ronesb])
        self.onesb = onesb
        tril, Rtril = self.const("tril", [128, 128], F32, I["c_tril"])
        self.hnT = sbp([128, 32, TT], BF16, "hnT")
        self.RhnT = Region("hnT")
        self.yT = sbp([128, KY, TT], BF16, "yT")
        self.RyT = [Region("yT_A"), Region("yT_M"), Region("yT_C")]
        self.wbuf = [sbp([128, 32 * 256], BF16, "wbuf") for _ in range(3)]
        self.Rwbuf = [Region("wbuf%d" % i) for i in range(3)]
        self.w_i = 0
        self.Cst = sbp([128, HM, 256], F32, "Cst")
        self.Cb = [sbp([128, HM, 256], BF16, "Cb") for _ in range(2)]
        self.nst = sbp([128, 8], F32, "nst")
        self.nb = [sbp([128, 8], BF16, "nb") for _ in range(2)]
        self.mst = sbp([128, 8], F32, "mst")
        self.RC = Region("Cst")
        self.RCb = [Region("Cb0"), Region("Cb1")]
        self.Rn = Region("nst")
        self.Rnb = [Region("nb0"), Region("nb1")]
        self.Rm = Region("mst")
        self.kcar = sbp([128, KV, 128], BF16, "kcar")
        self.vcar = sbp([128, KV * 128], BF16, "vcar")
        self.Rkcar, self.Rvcar = Region("kcar"), Region("vcar")
        self.gainT = sbp([128, 32], F32, "gainT")
        self.RgainT = Region("gainT")
        self.mhgT = sbp([128, 6], F32, "mhgT")
        self.RmhgT = Region("mhgT")
        self.wmT = sbp([128, GC, 128], BF16, "wmT")
        self.RwmT = Region("wmT")
        self.bsb = sbp([128, GC * 128], F32, "bsb")
        self.Rbsb = Region("bsb")
        self.lngb = sbp([128, 1024], F32, "lngb")
        self.lnbb = sbp([128, 1024], F32, "lnbb")
        self.Rln = Region("ln")
        self.Rpo = {}
        self.Rred = {}

        for l in range(2):
            self.l = l
            P.dma("sp", self.gainT[:], I["gainT"][l], writes=[self.RgainT])
            P.dma("sp", self.mhgT[:], I["mhgT"][l], writes=[self.RmhgT])
            P.dma("sp", self.bsb[:], I["bs"][l].partition_broadcast(128), writes=[self.Rbsb])
            P.dma("sp", self.lngb[:], I["lng"][l].partition_broadcast(128), writes=[self.Rln])
            P.dma("sp", self.lnbb[:], I["lnb"][l].partition_broadcast(128), writes=[self.Rln])
            with contextlib.ExitStack() as ph:
                wtmp = self.sb(ph, [128, GC, 128], F32, "wtmp")
                Rwtmp = Region("wtmp")
                P.dma("sp", wtmp[:].rearrange("p a b -> p (a b)"), I["wsT"][l], writes=[Rwtmp])
                P.op("dve", lambda h, wtmp=wtmp: h.tensor_tensor(self.wmT[:], wtmp[:], tril[:].unsqueeze(1).to_broadcast([128, GC, 128]), ALU.mult),
                     reads=[Rwtmp, Rtril], writes=[self.RwmT])
                self.barrier()
            P.op("dve", lambda h: h.memset(self.Cst[:], 0.0), writes=[self.RC])
            P.op("dve", lambda h: h.memset(self.Cb[0][:], 0.0), writes=[self.RCb[0]])
            P.op("dve", lambda h: h.memset(self.nst[:], 0.0), writes=[self.Rn])
            P.op("dve", lambda h: h.memset(self.nb[0][:], 0.0), writes=[self.Rnb[0]])
            P.op("dve", lambda h: h.memset(self.mst[:], 0.0), writes=[self.Rm])
            for t in range(NTILE):
                self.t = t
                rows = slice(t * TT, (t + 1) * TT)
                if l == 0:
                    self.phase_norm(I["xp"][rows, :], self.R["xin"], NBLK, 128)
                else:
                    self.phase_norm(I["xp"][rows, :], self.R["xin"], NBLK, 128,
                                    add=(S["red0"][rows, :], self.Rred[(0, t)]), store=(S["h1"][rows, :], self.R["h1"]))
                self.phase_A()
                self.phase_M()
                self.phase_C()
                rp = Region("po%d_%d" % (l, t))
                rr = Region("red%d_%d" % (l, t))
                self.Rpo[(l, t)], self.Rred[(l, t)] = rp, rr
                self.phase_O(S["po%d" % l][rows, :], rp, 128, NBLK)
                for cpart in range(TT // 256):
                    crow = slice(t * TT + cpart * 256, t * TT + (cpart + 1) * 256)
                    P.coll(st, S["po%d" % l][crow, :], S["red%d" % l][crow, :], reads=[rp], writes=[rr])
            P.dma("sp", O["Cp"][l], self.Cst[:].rearrange("p a b -> p (a b)"), reads=[self.RC], writes=[self.R["Cp"]])
            P.dma("sp", O["np"][l], self.nst[:, 0:HM], reads=[self.Rn], writes=[self.R["np"]])
            P.dma("sp", O["mp"][l], self.mst[0:1, 0:HM], reads=[self.Rm], writes=[self.R["mp"]])
            if self.do_sample:
                self.sample_layer()
        for t in range(NTILE):
            rows = slice(t * TT, (t + 1) * TT)
            self.phase_final(S["h1"][rows, :], self.R["h1"], S["red1"][rows, :], self.Rred[(1, t)], O["yp"][rows, :], self.R["yp"], 128, NBLK)
        if self.do_sample:
            self.phase_final(S["hs1"], self.R["hs1"], S["reds1"], self.Rred[(1, "s")], O["ys"], self.R["ys"], NS, 1)
        outs = [self.R[k] for k in O.keys()]
        P.final_wait("sp", outs)

    def build_wseq(self):
        per = ([("in", g) for g in range(G_AQ, G_MQK)] + [("in", g) for g in list(range(G_MQK, G_CU)) + [G_GATE]]
               + [("in", g) for g in range(G_CU, G_GATE)] + [("out", g) for g in range(NG_OUT)])
        seq = []
        for l in range(2):
            for _ in range(NTILE + (1 if self.do_sample else 0)):
                seq += [(k, l, g) for (k, g) in per]
        self.wseq = seq
        self.w_issued = 0
        self.w_cons = 0

    def wstream(self, kind, groups):
        for g in groups:
            idx = self.w_cons
            assert self.wseq[idx] == (kind, self.l, g), (self.wseq[idx], kind, self.l, g)
            while self.w_issued < min(len(self.wseq), idx + 3):
                k2, l2, g2 = self.wseq[self.w_issued]
                i2 = self.w_issued % 3
                src = self.S["wb_in"] if k2 == "in" else self.S["wb_out"]
                self.P.dma("sp", self.wbuf[i2][:], src[l2, g2], reads=[self.Rw[(k2, l2, g2)]], writes=[self.Rwbuf[i2]])
                self.w_issued += 1
            self.w_cons += 1
            i = idx % 3
            if kind == "in":
                yield g, self.wbuf[i][:].rearrange("p (k c) -> p k c", k=32), self.Rwbuf[i]
            else:
                yield g, self.wbuf[i][:].rearrange("p (k c) -> p k c", k=KY), self.Rwbuf[i]

    def mm_feat(self, wbuf, Rw, j, ntok):
        acc, Racc = self.next_acc()
        hnT = self.hnT
        fns = [(lambda h, kc=kc: h.matmul(acc[:, 0:ntok], wbuf[:, kc, j * 128:(j + 1) * 128], hnT[:, kc, 0:ntok],
                                          start=(kc == 0), stop=(kc == 31))) for kc in range(32)]
        self.P.group("pe", fns, reads=[Rw, self.RhnT], writes=[Racc])
        return acc, Racc

    def mm_tok(self, wbuf, Rw, t0, nt, ncols):
        acc, Racc = self.next_acc()
        hnT = self.hnT
        fns = [(lambda h, kc=kc: h.matmul(acc[0:nt, 0:ncols], hnT[:, kc, t0:t0 + nt], wbuf[:, kc, 0:ncols],
                                          start=(kc == 0), stop=(kc == 31))) for kc in range(32)]
        self.P.group("pe", fns, reads=[Rw, self.RhnT], writes=[Racc])
        return acc, Racc

    def phase_norm(self, src, Rsrc, nblk, np_, add=None, store=None):
        P = self.P
        with contextlib.ExitStack() as ph:
            hb0 = self.sb(ph, [128, D], F32, "hb")
            hb = [hb0, hb0]
            hb2 = self.sb(ph, [128, D], F32, "hb2") if add is not None else None
            hn = self.sb(ph, [128, D], BF16, "hn")
            junk = hn
            stt = [self.sb(ph, [128, 4], F32, "nst") for _ in range(2)]
            Rhb0 = Region("hb0")
            Rhb = [Rhb0, Rhb0]
            Rhb2 = Region("hb2")
            Rhn = Region("hn")
            Rst = [Region("st0"), Region("st1")]
            Rj = Rhn
            for b in range(nblk):
                i = b % 2
                h_, n_, s_ = hb[i], hn, stt[i]
                rs = slice(b * np_, (b + 1) * np_)
                P.dma("sp", h_[0:np_, :], src[rs, :], reads=[Rsrc], writes=[Rhb[i]])
                if add is not None:
                    P.dma("sp", hb2[0:np_, :], add[0][rs, :], reads=[add[1]], writes=[Rhb2])
                    P.op("dve", lambda h, h_=h_: h.tensor_tensor(h_[0:np_, :], h_[0:np_, :], hb2[0:np_, :], ALU.add), reads=[Rhb[i], Rhb2], writes=[Rhb[i]])
                    if store is not None:
                        P.dma("sp", store[0][rs, :], h_[0:np_, :], reads=[Rhb[i]], writes=[store[1]])
                P.op("act", lambda h, h_=h_, s_=s_: h.activation(junk[0:np_, :], h_[0:np_, :], AF.Square, accum_out=s_[0:np_, 0:1]),
                     reads=[Rhb[i]], writes=[Rj, Rst[i]])
                P.op("act", lambda h, s_=s_: h.activation(s_[0:np_, 1:2], s_[0:np_, 0:1], AF.Sqrt, bias=EPS, scale=1.0 / D),
                     reads=[Rst[i]], writes=[Rst[i]])
                P.op("dve", lambda h, s_=s_: h.reciprocal(s_[0:np_, 2:3], s_[0:np_, 1:2]), reads=[Rst[i]], writes=[Rst[i]])
                P.op("dve", lambda h, h_=h_, n_=n_, s_=s_: h.tensor_scalar(n_[0:np_, :], h_[0:np_, :], s_[0:np_, 2:3], None, op0=ALU.mult),
                     reads=[Rhb[i], Rst[i]], writes=[Rhn])
                for q in range(4):
                    fns = [(lambda h, kc=kc, n_=n_: h.transpose(self.tb[:, (kc % 8) * 128:(kc % 8) * 128 + np_],
                                                                n_[0:np_, kc * 128:(kc + 1) * 128], self.identb[0:np_, 0:np_]))
                           for kc in range(q * 8, q * 8 + 8)]
                    P.group("pe", fns, reads=[Rhn, self.Ridb], writes=[self.Rtb])
                    P.op("dve", lambda h, q=q, b=b: h.tensor_tensor(
                        self.hnT[:, q * 8:(q + 1) * 8, b * np_:(b + 1) * np_],
                        self.tb[:, :].rearrange("p (a c) -> p a c", a=8)[:, :, 0:np_],
                        self.gainT[:, q * 8:(q + 1) * 8].unsqueeze(2).to_broadcast([128, 8, np_]), ALU.mult),
                        reads=[self.Rtb, self.RgainT], writes=[self.RhnT])
            self.barrier()

    def phase_A(self):
        P, l, t = self.P, self.l, self.t
        first_tile = (t == 0)
        last_tile = (t == NTILE - 1)
        KW = KV * 128
        with contextlib.ExitStack() as ph:
            qT = self.sb(ph, [128, HA, TT], BF16, "qT")
            kT = self.sb(ph, [128, KV, TT + 128], BF16, "kT")
            vt = self.sb(ph, [128, NBLK + 1, KW], BF16, "vt")
            zT = self.sb(ph, [128, HA, TT], BF16, "zT")
            ost = self.sb(ph, [128, 2, KW], F32, "ost")
            RqT, RkT, Rvt, RzT, Rost = Region("qT"), Region("kT"), Region("vt"), Region("zT"), Region("ost")
            if not first_tile:
                P.op("act", lambda h: h.copy(kT[:, :, 0:128], self.kcar[:]), reads=[self.Rkcar], writes=[RkT])
                P.op("act", lambda h: h.copy(vt[:, 0, :], self.vcar[:]), reads=[self.Rvcar], writes=[Rvt])
            for g, wb, Rw in self.wstream("in", list(range(G_AQ, G_MQK))):
                if g < G_AK:
                    for j in range(2):
                        acc, Racc = self.mm_feat(wb, Rw, j, TT)
                        hd = (g - G_AQ) * 2 + j
                        P.op("act", lambda h, acc=acc, hd=hd: h.activation(qT[:, hd, :], acc[:, 0:TT], AF.Copy, scale=QSCALE),
                             reads=[Racc], writes=[RqT])
                elif g < G_AV:
                    for j in range(2):
                        acc, Racc = self.mm_feat(wb, Rw, j, TT)
                        P.op("dve", lambda h, acc=acc, j=j: h.tensor_copy(kT[:, j, 128:128 + TT], acc[:, 0:TT]), reads=[Racc], writes=[RkT])
                    if last_tile:
                        acc, Racc = self.mm_tok(wb, Rw, TT - 128, 128, 256)
                        P.op("dve", lambda h, acc=acc: h.tensor_copy(ost[:, 0, :], acc[:, 0:256]), reads=[Racc], writes=[Rost])
                elif g < G_AZ:
                    for b in range(NBLK):
                        acc, Racc = self.mm_tok(wb, Rw, b * 128, 128, 256)
                        P.op("act", lambda h, acc=acc, b=b: h.copy(vt[:, b + 1, :], acc[:, 0:256]), reads=[Racc], writes=[Rvt])
                        if last_tile and b == NBLK - 1:
                            P.op("dve", lambda h, acc=acc: h.tensor_copy(ost[:, 1, :], acc[:, 0:256]), reads=[Racc], writes=[Rost])
                else:
                    for j in range(2):
                        acc, Racc = self.mm_feat(wb, Rw, j, TT)
                        hd = (g - G_AZ) * 2 + j
                        P.op("act", lambda h, acc=acc, hd=hd: h.activation(zT[:, hd, :], acc[:, 0:TT], AF.Silu), reads=[Racc], writes=[RzT])
            if last_tile:
                P.dma("sp", self.O["kp"][l], ost[:, 0, :], reads=[Rost], writes=[self.R["kp"]])
                P.dma("sp", self.O["vp"][l], ost[:, 1, :], reads=[Rost], writes=[self.R["vp"]])
            P.op("act", lambda h: h.copy(self.kcar[:], kT[:, :, TT:TT + 128]), reads=[RkT], writes=[self.Rkcar])
            P.op("act", lambda h: h.copy(self.vcar[:], vt[:, NBLK, :]), reads=[Rvt], writes=[self.Rvcar])
            NB_ = HA
            L = [self.sb(ph, [128, 256], F32, "L") for _ in range(NB_)]
            Pn = [self.sb(ph, [128, 256], BF16, "Pn") for _ in range(NB_)]
            PT = [self.sb(ph, [128, 2, 128], BF16, "PT") for _ in range(NB_)]
            sm = [self.sb(ph, [128, 8], F32, "asm") for _ in range(NB_)]
            RL = [Region("L%d" % i) for i in range(NB_)]
            RPn = [Region("Pn%d" % i) for i in range(NB_)]
            RPT = [Region("PT%d" % i) for i in range(NB_)]
            Rsm = [Region("sm%d" % i) for i in range(NB_)]

            def attn_block(b):
                gfirst = first_tile and b == 0
                nk = 128 if gfirst else 256
                koff = b * 128 + (128 if gfirst else 0)
                dmo = 128 if gfirst else 0
                nkb = nk // 128
                vb0 = b + (1 if gfirst else 0)
                for hd in range(HA):
                    kv = hd // 3
                    pS, RpS = self.pb[4 + hd % 3], self.Rpb[4 + hd % 3]
                    L_, sm_ = L[hd], sm[hd]
                    sk = self.sinkb[:, l * HA + hd:l * HA + hd + 1]
                    ns_ = self.nslp[:, hd:hd + 1]
                    P.op("pe", lambda h, pS=pS, hd=hd, kv=kv: h.matmul(
                        pS[:, 0:nk], qT[:, hd, b * 128:(b + 1) * 128], kT[:, kv, koff:koff + nk], start=True, stop=True),
                        reads=[RqT, RkT], writes=[RpS])
                    P.op("dve", lambda h, pS=pS, L_=L_, ns_=ns_: h.scalar_tensor_tensor(
                        L_[:, 0:nk], self.dm[:, dmo:dmo + nk], ns_, pS[:, 0:nk], op0=ALU.mult, op1=ALU.add),
                        reads=[RpS, self.Rdm, self.Rnslp], writes=[RL[hd]])
                    P.op("dve", lambda h, L_=L_, sm_=sm_: h.tensor_reduce(sm_[:, 0:1], L_[:, 0:nk], AX.X, ALU.max),
                         reads=[RL[hd]], writes=[Rsm[hd]])
                    P.op("dve", lambda h, sm_=sm_, sk=sk: h.tensor_scalar(sm_[:, 1:2], sm_[:, 0:1], sk, -1.0, op0=ALU.max, op1=ALU.mult),
                         reads=[Rsm[hd], self.Rsinkb], writes=[Rsm[hd]])
                for hd in range(HA):
                    L_, sm_ = L[hd], sm[hd]
                    sk = self.sinkb[:, l * HA + hd:l * HA + hd + 1]
                    P.op("act", lambda h, L_=L_, sm_=sm_: h.activation(L_[:, 0:nk], L_[:, 0:nk], AF.Exp, bias=sm_[:, 1:2], scale=1.0,
                                                                       accum_out=sm_[:, 2:3]),
                         reads=[RL[hd], Rsm[hd]], writes=[RL[hd], Rsm[hd]])
                    P.op("act", lambda h, sm_=sm_, sk=sk: h.activation(sm_[:, 3:4], sk, AF.Exp, bias=sm_[:, 1:2], scale=1.0),
                         reads=[Rsm[hd], self.Rsinkb], writes=[Rsm[hd]])
                for hd in range(HA):
                    L_, Pn_, sm_ = L[hd], Pn[hd], sm[hd]
                    P.op("dve", lambda h, sm_=sm_: h.tensor_tensor(sm_[:, 4:5], sm_[:, 2:3], sm_[:, 3:4], ALU.add), reads=[Rsm[hd]], writes=[Rsm[hd]])
                    P.op("dve", lambda h, sm_=sm_: h.reciprocal(sm_[:, 5:6], sm_[:, 4:5]), reads=[Rsm[hd]], writes=[Rsm[hd]])
                    P.op("dve", lambda h, L_=L_, Pn_=Pn_, sm_=sm_: h.tensor_scalar(Pn_[:, 0:nk], L_[:, 0:nk], sm_[:, 5:6], None, op0=ALU.mult),
                         reads=[RL[hd], Rsm[hd]], writes=[RPn[hd]])
                for hd in range(HA):
                    Pn_, PT_ = Pn[hd], PT[hd]
                    fns = [(lambda h, kb=kb, Pn_=Pn_: h.transpose(self.tb[:, kb * 128:(kb + 1) * 128], Pn_[:, kb * 128:(kb + 1) * 128], self.identb[:]))
                           for kb in range(nkb)]
                    P.group("pe", fns, reads=[RPn[hd], self.Ridb], writes=[self.Rtb])
                    P.op("act", lambda h, PT_=PT_: h.copy(PT_[:, 0:nkb, :], self.tb[:, 0:nkb * 128].rearrange("p (a c) -> p a c", a=nkb)),
                         reads=[self.Rtb], writes=[RPT[hd]])
                for hd in range(HA):
                    kv = hd // 3
                    PT_ = PT[hd]
                    pO, RpO = self.pb[hd % 4], self.Rpb[hd % 4]
                    fns = [(lambda h, kb=kb, pO=pO, PT_=PT_, kv=kv: h.matmul(
                        pO[:, 0:128], vt[:, vb0 + kb, kv * 128:(kv + 1) * 128], PT_[:, kb, :], start=(kb == 0), stop=(kb == nkb - 1)))
                        for kb in range(nkb)]
                    P.group("pe", fns, reads=[Rvt, RPT[hd]], writes=[RpO])
                    P.op("dve", lambda h, pO=pO, hd=hd: h.tensor_tensor(self.yT[:, hd, b * 128:(b + 1) * 128], pO[:, 0:128],
                                                                         zT[:, hd, b * 128:(b + 1) * 128], ALU.mult),
                         reads=[RpO, RzT], writes=[self.RyT[0]])
            for b in range(NBLK):
                attn_block(b)
            self.barrier()

    def phase_M(self):
        P, l, t = self.P, self.l, self.t
        NH = HM
        QW = NH * 128
        VW = NH * 256
        with contextlib.ExitStack() as ph:
            qb = self.sb(ph, [128, NBLK, QW], BF16, "qb")
            kb_ = self.sb(ph, [128, NBLK, QW], BF16, "kb")
            va = self.sb(ph, [128, NBLK, VW], BF16, "va")
            G = self.sb(ph, [128, NBLK, VW], BF16, "G")
            gts = self.sb(ph, [128, NBLK, 2 * NH], F32, "gts")
            tmp = [self.sb(ph, [128, 256], F32, "mtmp") for _ in range(2)]
            Rqb, Rkb, Rva, RG, Rgts = Region("qb"), Region("kb"), Region("va"), Region("G"), Region("gts")
            Rtmp = [Region("mt0"), Region("mt1")]
            groups = list(range(G_MQK, G_CU)) + [G_GATE]
            ti = 0
            for g, wb, Rw in self.wstream("in", groups):
                ncols = 2 * NH if g == G_GATE else 256
                for b in range(NBLK):
                    acc, Racc = self.mm_tok(wb, Rw, b * 128, 128, ncols)
                    if g == G_GATE:
                        P.op("dve", lambda h, acc=acc, b=b: h.tensor_tensor(gts[:, b, :], acc[:, 0:2 * NH], self.bifb[:, l * 2 * NH:(l + 1) * 2 * NH], ALU.add),
                             reads=[Racc, self.Rbifb], writes=[Rgts])
                    elif g < G_MV:
                        for j in range(2):
                            ch = (g - G_MQK) * 2 + j
                            if ch < NH:
                                P.op("act", lambda h, acc=acc, b=b, ch=ch, j=j: h.copy(qb[:, b, ch * 128:(ch + 1) * 128], acc[:, j * 128:(j + 1) * 128]),
                                     reads=[Racc], writes=[Rqb])
                            else:
                                P.op("act", lambda h, acc=acc, b=b, ch=ch, j=j: h.activation(kb_[:, b, (ch - NH) * 128:(ch - NH + 1) * 128],
                                                                                              acc[:, j * 128:(j + 1) * 128], AF.Copy, scale=QSCALE),
                                     reads=[Racc], writes=[Rkb])
                    elif g < G_MO:
                        c0 = (g - G_MV) * 256
                        P.op("dve", lambda h, acc=acc, b=b, c0=c0: h.tensor_copy(va[:, b, c0:c0 + 256], acc[:, 0:256]), reads=[Racc], writes=[Rva])
                    elif g < G_MZ:
                        c0 = (g - G_MO) * 256
                        P.op("act", lambda h, acc=acc, b=b, c0=c0: h.activation(G[:, b, c0:c0 + 256], acc[:, 0:256], AF.Sigmoid), reads=[Racc], writes=[RG])
                    else:
                        c0 = (g - G_MZ) * 256
                        tm, Rtm = tmp[ti % 2], Rtmp[ti % 2]
                        ti += 1
                        P.op("act", lambda h, acc=acc, tm=tm: h.activation(tm[:], acc[:, 0:256], AF.Silu), reads=[Racc], writes=[Rtm])
                        P.op("dve", lambda h, tm=tm, b=b, c0=c0: h.tensor_tensor(G[:, b, c0:c0 + 256], G[:, b, c0:c0 + 256], tm[:], ALU.mult),
                             reads=[Rtm, RG], writes=[RG])
            sm = self.sb(ph, [128, 128], F32, "msm")
            Bd = self.sb(ph, [128, NH, 128], F32, "Bd")
            Dm = self.sb(ph, [128, NH, 128], F32, "Dm")
            w = self.sb(ph, [128, QW], BF16, "w")
            wT = self.sb(ph, [128, QW], BF16, "wT")
            qTs = self.sb(ph, [128, QW], BF16, "qTs")
            kTs = self.sb(ph, [128, QW], BF16, "kTs")
            qs = self.sb(ph, [128, NH, 128], BF16, "qs")
            qsT = self.sb(ph, [128, NH, 2, 128], BF16, "qsT")
            ksc = [self.sb(ph, [128, NH, 128], BF16, "ksc") for _ in range(2)]
            mo = self.sb(ph, [128, VW], BF16, "mo")
            junk = self.sb(ph, [128, 256], BF16, "mjunk")
            Rsm, RBd, RDm, Rw_, RwT, RqTs, RkTs, Rqs, RqsT, Rmo, Rjunk = (Region(n) for n in
                ("msm", "Bd", "Dm", "w", "wT", "qTs", "kTs", "qs", "qsT", "mo", "mjunk"))
            Rksc = [Region("ksc0"), Region("ksc1")]
            P.op("dve", lambda h: h.memset(qsT[:], 0.0), writes=[RqsT])
            psm, Rpsm = self.pb[0], self.Rpb[0]
            pB, RpB = self.pb[1], self.Rpb[1]
            pS, RpS = self.pb[2], self.Rpb[2]
            pC = [self.pb[1], self.pb[2]]
            RpC = [self.Rpb[1], self.Rpb[2]]
            pN = [self.pb[3], self.pb[4]]
            RpN = [self.Rpb[3], self.Rpb[4]]
            c_e1, c_sp, c_an, c_al0, c_al1, c_bv = 0, 6, 12, 20, 28, 36
            c_mxB, c_rmD, c_m1, c_m2, c_msel, c_mns, c_als = 42, 54, 60, 66, 72, 78, 84
            c_int, c_mt, c_wi, c_emt, c_ws, c_wsc0, c_wsc1 = 90, 96, 102, 108, 114, 120, 0
            sm2 = self.sb(ph, [128, 64], F32, "msm2")
            Rsm2 = Region("msm2")
            d_wc0, d_wc1, d_den, d_dn, d_t, d_rd, d_ssq, d_f, d_dnn = 0, 6, 12, 18, 24, 30, 36, 42, 48
            S_ = lambda c, n=NH: sm[:, c:c + n]
            S2 = lambda c, n=NH: sm2[:, c:c + n]
            bc3 = lambda ap: ap.unsqueeze(2).to_broadcast([128, NH, 128])

            def do_block(b):
                ip = gts[:, b, 0:NH]
                fp = gts[:, b, NH:2 * NH]
                P.op("act", lambda h: h.activation(S_(c_e1), fp, AF.Exp, scale=-1.0), reads=[Rgts], writes=[Rsm])
                P.op("act", lambda h: h.activation(S_(c_sp), S_(c_e1), AF.Ln, bias=1.0), reads=[Rsm], writes=[Rsm])
                fns = [lambda h: h.matmul(psm[:, 0:NH], self.tri2[:], S_(c_sp), start=True, stop=True),
                       lambda h: h.matmul(psm[:, 8:8 + NH], self.onesc[:, 0:128], S_(c_sp), start=True, stop=True),
                       lambda h: h.matmul(psm[:, 16:16 + NH], self.onesc[:, 128:256], S_(c_sp), start=True, stop=True)]
                P.group("pe", fns, reads=[Rsm, self.Rtri2, self.Ronesc], writes=[Rpsm])
                for (dst, srcc) in ((c_an, 0), (c_al0, 8), (c_al1, 16)):
                    P.op("dve", lambda h, dst=dst, srcc=srcc: h.tensor_copy(S_(dst), psm[:, srcc:srcc + NH]), reads=[Rpsm], writes=[Rsm])
                P.op("dve", lambda h: h.tensor_tensor(S_(c_bv), ip, S_(c_an), ALU.add), reads=[Rgts, Rsm], writes=[Rsm])
                P.op("dve", lambda h: h.tensor_tensor(Bd[:], self.identf[:].unsqueeze(1).to_broadcast([128, NH, 128]), bc3(S_(c_bv)), ALU.mult),
                     reads=[Rsm, self.Ridf], writes=[RBd])
                P.op("pe", lambda h: h.matmul(pB[:, 0:QW], self.onesf[:], Bd[:].rearrange("p a c -> p (a c)"), start=True, stop=True),
                     reads=[RBd, self.Ronesf], writes=[RpB])
                P.op("dve", lambda h: h.tensor_reduce(sm[:, c_mxB:c_mxB + 2 * NH].rearrange("p (a c) -> p a c", a=NH),
                                                      pB[:, 0:QW].rearrange("p (a c s) -> p a c s", a=NH, c=2), AX.X, ALU.max),
                     reads=[RpB], writes=[Rsm])
                P.op("dve", lambda h: h.tensor_tensor(Dm[:].rearrange("p a c -> p (a c)"), pB[:, 0:QW], self.mask6[:, 0:QW], ALU.add),
                     reads=[RpB, self.Rmask6], writes=[RDm])
                P.op("dve", lambda h: h.tensor_tensor(Dm[:], Dm[:], bc3(S_(c_an)), ALU.subtract), reads=[RDm, Rsm], writes=[RDm])
                P.op("dve", lambda h: h.tensor_reduce(S_(c_rmD), Dm[:], AX.X, ALU.max), reads=[RDm], writes=[Rsm])
                mxB = sm[:, c_mxB:c_mxB + 2 * NH].rearrange("p (a c) -> p a c", c=2)
                m0 = self.mst[:, 0:NH]
                P.op("dve", lambda h: h.tensor_tensor(S_(c_m1), m0, mxB[:, :, 0], ALU.max), reads=[Rsm, self.Rm], writes=[Rsm])
                P.op("dve", lambda h: h.tensor_tensor(S_(c_m1), S_(c_m1), S_(c_al0), ALU.subtract), reads=[Rsm], writes=[Rsm])
                P.op("dve", lambda h: h.tensor_tensor(S_(c_m2), S_(c_m1), mxB[:, :, 1], ALU.max), reads=[Rsm], writes=[Rsm])
                P.op("dve", lambda h: h.tensor_tensor(S_(c_m2), S_(c_m2), S_(c_al1), ALU.subtract), reads=[Rsm], writes=[Rsm])
                P.op("dve", lambda h: h.tensor_tensor(S2(d_wc0), m0, S_(c_al0), ALU.subtract), reads=[Rsm, self.Rm], writes=[Rsm2])
                P.op("dve", lambda h: h.tensor_tensor(S2(d_wc0), S2(d_wc0), S_(c_m1), ALU.subtract), reads=[Rsm, Rsm2], writes=[Rsm2])
                P.op("dve", lambda h: h.tensor_tensor(S2(d_wc1), S_(c_m1), S_(c_al1), ALU.subtract), reads=[Rsm, Rsm2], writes=[Rsm2])
                P.op("dve", lambda h: h.tensor_tensor(S2(d_wc1), S2(d_wc1), S_(c_m2), ALU.subtract), reads=[Rsm, Rsm2], writes=[Rsm2])
                P.op("act", lambda h: h.activation(S2(d_wc0), S2(d_wc0), AF.Exp), reads=[Rsm2], writes=[Rsm2])
                P.op("act", lambda h: h.activation(S2(d_wc1), S2(d_wc1), AF.Exp), reads=[Rsm2], writes=[Rsm2])
                P.op("dve", lambda h: h.tensor_copy(sm[0:64, c_msel:c_msel + NH], self.mst[0:64, 0:NH]), reads=[self.Rm, Rsm], writes=[Rsm])
                P.op("dve", lambda h: h.tensor_copy(sm[64:128, c_msel:c_msel + NH], sm[64:128, c_m1:c_m1 + NH]), reads=[Rsm], writes=[Rsm])
                P.op("dve", lambda h: h.tensor_copy(sm[0:64, c_mns:c_mns + NH], sm[0:64, c_m1:c_m1 + NH]), reads=[Rsm], writes=[Rsm])
                P.op("dve", lambda h: h.tensor_copy(sm[64:128, c_mns:c_mns + NH], sm[64:128, c_m2:c_m2 + NH]), reads=[Rsm], writes=[Rsm])
                P.op("dve", lambda h: h.tensor_copy(sm[0:64, c_als:c_als + NH], sm[0:64, c_al0:c_al0 + NH]), reads=[Rsm], writes=[Rsm])
                P.op("dve", lambda h: h.tensor_copy(sm[64:128, c_als:c_als + NH], sm[64:128, c_al1:c_al1 + NH]), reads=[Rsm], writes=[Rsm])
                P.op("dve", lambda h: h.tensor_copy(self.mst[:, 0:NH], S_(c_m2)), reads=[Rsm], writes=[self.Rm])
                P.op("dve", lambda h: h.tensor_tensor(S_(c_int), S_(c_msel), S_(c_an), ALU.subtract), reads=[Rsm], writes=[Rsm])
                P.op("dve", lambda h: h.tensor_tensor(S_(c_mt), S_(c_int), S_(c_rmD), ALU.max), reads=[Rsm], writes=[Rsm])
                P.op("dve", lambda h: h.tensor_tensor(S_(c_wi), S_(c_int), S_(c_mt), ALU.subtract), reads=[Rsm], writes=[Rsm])
                P.op("act", lambda h: h.activation(S_(c_wi), S_(c_wi), AF.Exp), reads=[Rsm], writes=[Rsm])
                P.op("act", lambda h: h.activation(S_(c_emt), S_(c_mt), AF.Exp, scale=-1.0), reads=[Rsm], writes=[Rsm])
                P.op("dve", lambda h: h.tensor_tensor(S_(c_ws), S_(c_bv), S_(c_als), ALU.subtract), reads=[Rsm], writes=[Rsm])
                P.op("dve", lambda h: h.tensor_tensor(S_(c_ws), S_(c_ws), S_(c_mns), ALU.subtract), reads=[Rsm], writes=[Rsm])
                P.op("act", lambda h: h.activation(S_(c_ws), S_(c_ws), AF.Exp), reads=[Rsm], writes=[Rsm])
                P.op("dve", lambda h: h.tensor_scalar(S_(c_wsc0), S_(c_ws), self.cmask[:, 0:1], None, op0=ALU.mult), reads=[Rsm, self.Rcmask], writes=[Rsm])
                P.op("dve", lambda h: h.tensor_scalar(S_(c_wsc1), S_(c_ws), self.cmask[:, 1:2], None, op0=ALU.mult), reads=[Rsm, self.Rcmask], writes=[Rsm])
                P.op("dve", lambda h: h.tensor_tensor(Dm[:], Dm[:], bc3(S_(c_mt)), ALU.subtract), reads=[RDm, Rsm], writes=[RDm])
                P.op("act", lambda h: h.activation(Dm[:], Dm[:], AF.Exp), reads=[RDm], writes=[RDm])
                for (srcb, Rsrcb, dstT, RdstT) in ((qb, Rqb, qTs, RqTs), (kb_, Rkb, kTs, RkTs)):
                    fns = [(lambda h, hh=hh, srcb=srcb: h.transpose(self.tb[:, hh * 128:(hh + 1) * 128], srcb[:, b, hh * 128:(hh + 1) * 128], self.identb[:]))
                           for hh in range(NH)]
                    P.group("pe", fns, reads=[Rsrcb, self.Ridb], writes=[self.Rtb])
                    P.op("act", lambda h, dstT=dstT: h.copy(dstT[:], self.tb[:, 0:QW]), reads=[self.Rtb], writes=[RdstT])
                fns = [(lambda h, hh=hh: h.matmul(pS[:, hh * 128:(hh + 1) * 128], qTs[:, hh * 128:(hh + 1) * 128],
                                                  kTs[:, hh * 128:(hh + 1) * 128], start=True, stop=True)) for hh in range(NH)]
                P.group("pe", fns, reads=[RqTs, RkTs], writes=[RpS])
                P.op("dve", lambda h: h.tensor_tensor(w[:], Dm[:].rearrange("p a c -> p (a c)"), pS[:, 0:QW], ALU.mult), reads=[RDm, RpS], writes=[Rw_])
                fns = [(lambda h, hh=hh: h.transpose(self.tb[:, hh * 128:(hh + 1) * 128], w[:, hh * 128:(hh + 1) * 128], self.identb[:])) for hh in range(NH)]
                P.group("pe", fns, reads=[Rw_, self.Ridb], writes=[self.Rtb])
                P.op("act", lambda h: h.copy(wT[:], self.tb[:, 0:QW]), reads=[self.Rtb], writes=[RwT])
                P.op("dve", lambda h: h.tensor_tensor(qs[:], qb[:, b, :].rearrange("p (a c) -> p a c", a=NH), bc3(S_(c_wi)), ALU.mult),
                     reads=[Rqb, Rsm], writes=[Rqs])
                fns = [(lambda h, hh=hh: h.transpose(self.tb[:, hh * 128:(hh + 1) * 128], qs[:, hh, :], self.identb[:])) for hh in range(NH)]
                P.group("pe", fns, reads=[Rqs, self.Ridb], writes=[self.Rtb])
                tbv = self.tb[:, 0:QW].rearrange("p (a c) -> p a c", a=NH)
                P.op("act", lambda h: h.copy(qsT[:, :, 0, 0:64], tbv[:, :, 0:64]), reads=[self.Rtb], writes=[RqsT])
                P.op("act", lambda h: h.copy(qsT[:, :, 1, 64:128], tbv[:, :, 64:128]), reads=[self.Rtb], writes=[RqsT])
                kb3 = kb_[:, b, :].rearrange("p (a c) -> p a c", a=NH)
                P.op("dve", lambda h: h.tensor_tensor(ksc[0][:], kb3, bc3(S_(c_wsc0)), ALU.mult), reads=[Rkb, Rsm], writes=[Rksc[0]])
                P.op("dve", lambda h: h.tensor_tensor(ksc[1][:], kb3, bc3(S_(c_wsc1)), ALU.mult), reads=[Rkb, Rsm], writes=[Rksc[1]])
                va3 = va[:, b, :].rearrange("p (a c) -> p a c", a=NH)

                def state_update(c, wc_col, Cb_dst, nb_dst, RCb_dst, Rnb_dst):
                    fns = []
                    for hh in range(NH):
                        fns.append(lambda h, hh=hh: h.matmul(pC[hh // 2][:, (hh % 2) * 256:(hh % 2) * 256 + 256], ksc[c][:, hh, :], va3[:, hh, :],
                                                             start=True, stop=True))
                    P.group("pe", fns, reads=[Rksc[c], Rva], writes=RpC)
                    fns = [(lambda h, hh=hh: h.matmul(psm[:, 32 + hh:33 + hh], ksc[c][:, hh, :], self.onesb[:, 0:1], start=True, stop=True)) for hh in range(NH)]
                    P.group("pe", fns, reads=[Rksc[c], self.Ronesb], writes=[Rpsm])
                    for hh in range(NH):
                        P.op("dve", lambda h, hh=hh: h.scalar_tensor_tensor(self.Cst[:, hh, :], self.Cst[:, hh, :], sm2[:, wc_col + hh:wc_col + hh + 1],
                                                                            pC[hh // 2][:, (hh % 2) * 256:(hh % 2) * 256 + 256], op0=ALU.mult, op1=ALU.add),
                             reads=[Rsm2, RpC[hh // 2], self.RC], writes=[self.RC])
                    P.op("dve", lambda h: h.tensor_tensor(S2(d_dnn), self.nst[:, 0:NH], S2(wc_col), ALU.mult), reads=[self.Rn, Rsm2], writes=[Rsm2])
                    P.op("dve", lambda h: h.tensor_tensor(self.nst[:, 0:NH], S2(d_dnn), psm[:, 32:32 + NH], ALU.add), reads=[Rsm2, Rpsm], writes=[self.Rn])
                    P.op("act", lambda h: h.copy(Cb_dst[:], self.Cst[:]), reads=[self.RC], writes=[RCb_dst])
                    P.op("act", lambda h: h.copy(nb_dst[:, 0:NH], self.nst[:, 0:NH]), reads=[self.Rn], writes=[Rnb_dst])

                state_update(0, d_wc0, self.Cb[1], self.nb[1], self.RCb[1], self.Rnb[1])
                for hh in range(NH):
                    o_ = pN[hh // 2][:, (hh % 2) * 256:(hh % 2) * 256 + 256]
                    fns = [lambda h, hh=hh, o_=o_: h.matmul(o_, wT[:, hh * 128:(hh + 1) * 128], va3[:, hh, :], start=True, stop=False),
                           lambda h, hh=hh, o_=o_: h.matmul(o_, qsT[:, hh, 0, :], self.Cb[0][:, hh, :], start=False, stop=False),
                           lambda h, hh=hh, o_=o_: h.matmul(o_, qsT[:, hh, 1, :], self.Cb[1][:, hh, :], start=False, stop=True)]
                    P.group("pe", fns, reads=[RwT, Rva, RqsT, self.RCb[0], self.RCb[1]], writes=[RpN[hh // 2]])
                for hh in range(NH):
                    o_ = psm[:, 40 + hh:41 + hh]
                    fns = [lambda h, hh=hh, o_=o_: h.matmul(o_, wT[:, hh * 128:(hh + 1) * 128], self.onesb[:, 0:1], start=True, stop=False),
                           lambda h, hh=hh, o_=o_: h.matmul(o_, qsT[:, hh, 0, :], self.nb[0][:, hh:hh + 1], start=False, stop=False),
                           lambda h, hh=hh, o_=o_: h.matmul(o_, qsT[:, hh, 1, :], self.nb[1][:, hh:hh + 1], start=False, stop=True)]
                    P.group("pe", fns, reads=[RwT, self.Ronesb, RqsT, self.Rnb[0], self.Rnb[1]], writes=[Rpsm])
                P.op("dve", lambda h: h.tensor_copy(S2(d_den), psm[:, 40:40 + NH]), reads=[Rpsm], writes=[Rsm2])
                P.op("dve", lambda h: h.scalar_tensor_tensor(S2(d_t), S2(d_den), -1.0, S2(d_den), op0=ALU.mult, op1=ALU.max), reads=[Rsm2], writes=[Rsm2])
                P.op("dve", lambda h: h.tensor_tensor(S2(d_t), S2(d_t), S_(c_emt), ALU.max), reads=[Rsm2, Rsm], writes=[Rsm2])
                P.op("dve", lambda h: h.reciprocal(S2(d_rd), S2(d_t)), reads=[Rsm2], writes=[Rsm2])
                for hh in range(NH):
                    P.op("act", lambda h, hh=hh: h.activation(junk[:], pN[hh // 2][:, (hh % 2) * 256:(hh % 2) * 256 + 256], AF.Square,
                                                              accum_out=sm2[:, d_ssq + hh:d_ssq + hh + 1]),
                         reads=[RpN[hh // 2], Rsm2], writes=[Rjunk, Rsm2])
                P.op("dve", lambda h: h.tensor_tensor(S2(d_t), S2(d_rd), S2(d_rd), ALU.mult), reads=[Rsm2], writes=[Rsm2])
                P.op("dve", lambda h: h.tensor_tensor(S2(d_t), S2(d_t), S2(d_ssq), ALU.mult), reads=[Rsm2], writes=[Rsm2])
                P.op("act", lambda h: h.activation(S2(d_t), S2(d_t), AF.Sqrt, bias=EPS, scale=1.0 / 256.0), reads=[Rsm2], writes=[Rsm2])
                P.op("dve", lambda h: h.reciprocal(S2(d_f), S2(d_t)), reads=[Rsm2], writes=[Rsm2])
                P.op("dve", lambda h: h.tensor_tensor(S2(d_f), S2(d_f), S2(d_rd), ALU.mult), reads=[Rsm2], writes=[Rsm2])
                for hh in range(NH):
                    P.op("dve", lambda h, hh=hh: h.scalar_tensor_tensor(mo[:, hh * 256:(hh + 1) * 256], pN[hh // 2][:, (hh % 2) * 256:(hh % 2) * 256 + 256],
                                                                        sm2[:, d_f + hh:d_f + hh + 1], G[:, b, hh * 256:(hh + 1) * 256],
                                                                        op0=ALU.mult, op1=ALU.mult),
                         reads=[RpN[hh // 2], Rsm2, RG], writes=[Rmo])
                fns = [(lambda h, j=j: h.transpose(self.tb[:, j * 128:(j + 1) * 128], mo[:, j * 128:(j + 1) * 128], self.identb[:])) for j in range(6)]
                P.group("pe", fns, reads=[Rmo, self.Ridb], writes=[self.Rtb])
                P.op("dve", lambda h: h.tensor_tensor(
                    self.yT[:, 6:12, b * 128:(b + 1) * 128],
                    self.tb[:, 0:768].rearrange("p (a c) -> p a c", a=6),
                    self.mhgT[:, 0:6].unsqueeze(2).to_broadcast([128, 6, 128]), ALU.mult),
                    reads=[self.Rtb, self.RmhgT], writes=[self.RyT[1]])
                state_update(1, d_wc1, self.Cb[0], self.nb[0], self.RCb[0], self.Rnb[0])
            for b in range(NBLK):
                do_block(b)
            self.barrier()

    def phase_C(self):
        P, l, t = self.P, self.l, self.t
        last_tile = (t == NTILE - 1)
        with contextlib.ExitStack() as ph:
            uT = self.sb(ph, [128, GC, TT], BF16, "uT")
            vvb = self.sb(ph, [128, NBLK, 1024], BF16, "vvb")
            gv = [self.sb(ph, [128, 1024], F32, "gv") for _ in range(NBLK)]
            tmp = [self.sb(ph, [128, TT], F32, "ctmp") for _ in range(2)]
            junk = self.sb(ph, [128, 1024], BF16, "cjunk")
            sm = self.sb(ph, [128, 16], F32, "csm")
            t1 = self.sb(ph, [128, GC, 128], F32, "ct1")
            RuT, Rvvb, Rjunk, Rsm, Rt1 = Region("uT"), Region("vvb"), Region("cjunk"), Region("csm"), Region("ct1")
            Rgv = [Region("gv%d" % i) for i in range(NBLK)]
            Rtmp = [Region("ct0"), Region("ct1")]
            ti = 0
            for g, wb, Rw in self.wstream("in", list(range(G_CU, G_GATE))):
                if g < G_CV:
                    for j in range(2):
                        acc, Racc = self.mm_feat(wb, Rw, j, TT)
                        gi = (g - G_CU) * 2 + j
                        P.op("act", lambda h, acc=acc, gi=gi: h.activation(uT[:, gi, :], acc[:, 0:TT], AF.Gelu), reads=[Racc], writes=[RuT])
                elif g < G_CZ:
                    c0 = (g - G_CV) * 256
                    for b in range(NBLK):
                        acc, Racc = self.mm_tok(wb, Rw, b * 128, 128, 256)
                        P.op("act", lambda h, acc=acc, b=b, c0=c0: h.activation(gv[b][:, c0:c0 + 256], acc[:, 0:256], AF.Gelu), reads=[Racc], writes=[Rgv[b]])
                else:
                    for j in range(2):
                        acc, Racc = self.mm_feat(wb, Rw, j, TT)
                        gi = (g - G_CZ) * 2 + j
                        tm, Rtm = tmp[ti % 2], Rtmp[ti % 2]
                        ti += 1
                        P.op("act", lambda h, acc=acc, tm=tm: h.activation(tm[:], acc[:, 0:TT], AF.Silu), reads=[Racc], writes=[Rtm])
                        P.op("dve", lambda h, tm=tm, gi=gi: h.tensor_tensor(uT[:, gi, :], uT[:, gi, :], tm[:], ALU.mult), reads=[Rtm, RuT], writes=[RuT])

            def do_blockc(b):
                g_ = gv[b]
                self.layernorm(g_, Rgv[b], 128, junk, Rjunk, sm, Rsm)
                P.op("act", lambda h: h.copy(vvb[:, b, :], g_[:]), reads=[Rgv[b]], writes=[Rvvb])
                if last_tile and b == NBLK - 1:
                    P.dma("sp", self.O["cvp"][l], g_[:], reads=[Rgv[b]], writes=[self.R["cvp"]])
                pS_, RpS_ = self.pb[4 + (b % 2)], self.Rpb[4 + (b % 2)]
                fns = [(lambda h, gi=gi: h.matmul(pS_[:, gi * 128:(gi + 1) * 128], vvb[:, b, gi * 128:(gi + 1) * 128],
                                                  self.wmT[:, gi, :], start=True, stop=True)) for gi in range(GC)]
                P.group("pe", fns, reads=[Rvvb, self.RwmT], writes=[RpS_])
                P.op("dve", lambda h: h.tensor_tensor(t1[:].rearrange("p a c -> p (a c)"), pS_[:, 0:GC * 128], self.bsb[:, 0:GC * 128], ALU.add),
                     reads=[RpS_, self.Rbsb], writes=[Rt1])
                P.op("dve", lambda h: h.tensor_tensor(self.yT[:, 12:12 + GC, b * 128:(b + 1) * 128], t1[:], uT[:, :, b * 128:(b + 1) * 128], ALU.mult),
                     reads=[Rt1, RuT], writes=[self.RyT[2]])
            for b in range(NBLK):
                do_blockc(b)
            self.barrier()

    def layernorm(self, g_, Rg, np_, junk, Rjunk, sm, Rsm):
        P = self.P
        P.op("act", lambda h: h.activation(junk[0:np_, :], g_[0:np_, :], AF.Copy, accum_out=sm[0:np_, 0:1]), reads=[Rg], writes=[Rjunk, Rsm])
        P.op("act", lambda h: h.activation(junk[0:np_, :], g_[0:np_, :], AF.Square, accum_out=sm[0:np_, 1:2]), reads=[Rg], writes=[Rjunk, Rsm])
        P.op("dve", lambda h: h.tensor_scalar(sm[0:np_, 2:4], sm[0:np_, 0:2], 1.0 / 1024.0, None, op0=ALU.mult), reads=[Rsm], writes=[Rsm])
        P.op("dve", lambda h: h.tensor_tensor(sm[0:np_, 4:5], sm[0:np_, 2:3], sm[0:np_, 2:3], ALU.mult), reads=[Rsm], writes=[Rsm])
        P.op("dve", lambda h: h.tensor_tensor(sm[0:np_, 5:6], sm[0:np_, 3:4], sm[0:np_, 4:5], ALU.subtract), reads=[Rsm], writes=[Rsm])
        P.op("act", lambda h: h.activation(sm[0:np_, 6:7], sm[0:np_, 5:6], AF.Sqrt, bias=EPS, scale=1.0), reads=[Rsm], writes=[Rsm])
        P.op("dve", lambda h: h.reciprocal(sm[0:np_, 7:8], sm[0:np_, 6:7]), reads=[Rsm], writes=[Rsm])
        P.op("dve", lambda h: h.tensor_scalar(g_[0:np_, :], g_[0:np_, :], sm[0:np_, 2:3], sm[0:np_, 7:8], op0=ALU.subtract, op1=ALU.mult),
             reads=[Rg, Rsm], writes=[Rg])
        P.op("dve", lambda h: h.tensor_tensor(g_[0:np_, :], g_[0:np_, :], self.lngb[0:np_, :], ALU.mult), reads=[Rg, self.Rln], writes=[Rg])
        P.op("dve", lambda h: h.tensor_tensor(g_[0:np_, :], g_[0:np_, :], self.lnbb[0:np_, :], ALU.add), reads=[Rg, self.Rln], writes=[Rg])

    def phase_O(self, dst, Rdst, np_, nblk):
        P = self.P
        with contextlib.ExitStack() as ph:
            hs = [self.sb(ph, [128, nblk, 512], F32, "hs") for _ in range(2)]
            Rhs = [Region("hs0"), Region("hs1")]
            i = 0
            for g, wb, Rw in self.wstream("out", list(range(NG_OUT))):
                h_, Rh_ = hs[i % 2], Rhs[i % 2]
                i += 1
                for b in range(nblk):
                    acc, Racc = self.next_acc()
                    yT = self.yT
                    fns = [(lambda h, kc=kc, acc=acc, b=b, wb=wb: h.matmul(acc[0:np_, 0:512], yT[:, kc, b * np_:(b + 1) * np_], wb[:, kc, 0:512],
                                                                           start=(kc == 0), stop=(kc == KY - 1))) for kc in range(KY)]
                    P.group("pe", fns, reads=[Rw] + self.RyT, writes=[Racc])
                    if b % 2 == 0:
                        P.op("act", lambda h, acc=acc, h_=h_, b=b: h.copy(h_[0:np_, b, :], acc[0:np_, 0:512]), reads=[Racc], writes=[Rh_])
                    else:
                        P.op("dve", lambda h, acc=acc, h_=h_, b=b: h.tensor_copy(h_[0:np_, b, :], acc[0:np_, 0:512]), reads=[Racc], writes=[Rh_])
                P.dma("sp", dst[:, g * 512:(g + 1) * 512].rearrange("(b p) c -> p b c", p=np_), h_[0:np_, :, :], reads=[Rh_], writes=[Rdst])
            self.barrier()

    def phase_final(self, srcA, RsrcA, srcB, RsrcB, out, Rout, np_, nblk):
        P = self.P
        with contextlib.ExitStack() as ph:
            hb0 = self.sb(ph, [128, D], F32, "fhb")
            hb = [hb0, hb0]
            hb2 = self.sb(ph, [128, D], F32, "fhb2")
            fg = self.sb(ph, [128, D], F32, "fg")
            junk = hb2
            stt = [self.sb(ph, [128, 4], F32, "fst") for _ in range(2)]
            Rhb0 = Region("fhb0")
            Rhb = [Rhb0, Rhb0]
            Rhb2 = Region("fhb2")
            Rst = [Region("fst0"), Region("fst1")]
            Rfg, Rj = Region("fg"), Rhb2
            P.dma("sp", fg[:], self.I["fgain"].partition_broadcast(128), writes=[Rfg])
            for b in range(nblk):
                i = b % 2
                h_, s_ = hb[i], stt[i]
                rs = slice(b * np_, (b + 1) * np_)
                P.dma("sp", h_[0:np_, :], srcA[rs, :], reads=[RsrcA], writes=[Rhb[i]])
                P.dma("sp", hb2[0:np_, :], srcB[rs, :], reads=[RsrcB], writes=[Rhb2])
                P.op("dve", lambda h, h_=h_: h.tensor_tensor(h_[0:np_, :], h_[0:np_, :], hb2[0:np_, :], ALU.add), reads=[Rhb[i], Rhb2], writes=[Rhb[i]])
                P.op("act", lambda h, h_=h_, s_=s_: h.activation(junk[0:np_, :], h_[0:np_, :], AF.Square, accum_out=s_[0:np_, 0:1]),
                     reads=[Rhb[i]], writes=[Rj, Rst[i]])
                P.op("act", lambda h, s_=s_: h.activation(s_[0:np_, 1:2], s_[0:np_, 0:1], AF.Sqrt, bias=EPS, scale=1.0 / D), reads=[Rst[i]], writes=[Rst[i]])
                P.op("dve", lambda h, s_=s_: h.reciprocal(s_[0:np_, 2:3], s_[0:np_, 1:2]), reads=[Rst[i]], writes=[Rst[i]])
                P.op("dve", lambda h, h_=h_, s_=s_: h.scalar_tensor_tensor(h_[0:np_, :], h_[0:np_, :], s_[0:np_, 2:3], fg[0:np_, :], op0=ALU.mult, op1=ALU.mult),
                     reads=[Rhb[i], Rst[i], Rfg], writes=[Rhb[i]])
                P.dma("sp", out[rs, :], h_[0:np_, :], reads=[Rhb[i]], writes=[Rout])
            self.barrier()

    def sample_layer(self):
        l, I, S, O, P = self.l, self.I, self.S, self.O, self.P
        if l == 0:
            self.phase_norm(I["xs"], self.R["xin"], 1, NS)
        else:
            self.phase_norm(I["xs"], self.R["xin"], 1, NS, add=(S["reds0"], self.Rred[(0, "s")]), store=(S["hs1"], self.R["hs1"]))
        self.s_phase_A()
        self.s_phase_M()
        self.s_phase_C()
        rp = Region("pos%d" % l)
        rr = Region("reds%d" % l)
        self.Rred[(l, "s")] = rr
        self.phase_O(S["pos%d" % l], rp, NS, 1)
        P.coll(self.st, S["pos%d" % l], S["reds%d" % l], reads=[rp], writes=[rr])

    def s_proj(self, wb, Rw, ncols, evac):
        acc, Racc = self.mm_tok(wb, Rw, 0, NS, ncols)
        evac(acc, Racc)

    def s_to_yT(self, srcb, Rsrcb, kc0, n):
        P = self.P
        fns = [(lambda h, j=j: h.transpose(self.tb[:, j * NS:(j + 1) * NS], srcb[0:NS, j * 128:(j + 1) * 128], self.identb[0:NS, 0:NS])) for j in range(n)]
        P.group("pe", fns, reads=[Rsrcb, self.Ridb], writes=[self.Rtb])
        reg = self.RyT[0] if kc0 == 0 else (self.RyT[1] if kc0 == 6 else self.RyT[2])
        P.op("act", lambda h: h.copy(self.yT[:, kc0:kc0 + n, 0:NS], self.tb[:, 0:n * NS].rearrange("p (a c) -> p a c", a=n)), reads=[self.Rtb], writes=[reg])

    def s_phase_A(self):
        P, l, I, S, O = self.P, self.l, self.I, self.S, self.O
        NP = NS * KV
        with contextlib.ExitStack() as ph:
            qs_ = self.sb(ph, [NS, 768], F32, "sq")
            kn = self.sb(ph, [NS, 256], F32, "skn")
            vn = self.sb(ph, [NS, 256], F32, "svn")
            za = self.sb(ph, [NS, 768], F32, "sza")
            Rq, Rkn, Rvn, Rza = Region("sq"), Region("skn"), Region("svn"), Region("sza")
            for g, wb, Rw in self.wstream("in", list(range(G_AQ, G_MQK))):
                def evac(acc, Racc, g=g):
                    if g < G_AK:
                        c0 = (g - G_AQ) * 256
                        P.op("act", lambda h: h.activation(qs_[:, c0:c0 + 256], acc[0:NS, 0:256], AF.Copy, scale=QSCALE), reads=[Racc], writes=[Rq])
                    elif g < G_AV:
                        P.op("dve", lambda h: h.tensor_copy(kn[:], acc[0:NS, 0:256]), reads=[Racc], writes=[Rkn])
                    elif g < G_AZ:
                        P.op("dve", lambda h: h.tensor_copy(vn[:], acc[0:NS, 0:256]), reads=[Racc], writes=[Rvn])
                    else:
                        c0 = (g - G_AZ) * 256
                        P.op("act", lambda h: h.activation(za[:, c0:c0 + 256], acc[0:NS, 0:256], AF.Silu), reads=[Racc], writes=[Rza])
                self.s_proj(wb, Rw, 256, evac)
            Rb = self.R["bnc"]
            P.dma("sp", O["ks"][l], kn[:], reads=[Rkn], writes=[self.R["ks"]])
            P.dma("sp", O["vs"][l], vn[:], reads=[Rvn], writes=[self.R["vs"]])
            P.dma("sp", S["bq"], qs_[:], reads=[Rq], writes=[Rb])
            P.dma("sp", S["bk"], kn[:], reads=[Rkn], writes=[Rb])
            P.dma("sp", S["bv"], vn[:], reads=[Rvn], writes=[Rb])
            q4 = self.sb(ph, [NP, 3, 128], F32, "q4")
            k4 = self.sb(ph, [NP, 128], F32, "k4")
            v4 = self.sb(ph, [NP, 128], F32, "v4")
            R4 = Region("qkv4")
            P.dma("sp", q4[:].rearrange("p a b -> p (a b)"), S["bq"].rearrange("b (kv x) -> (b kv) x", kv=KV), reads=[Rb], writes=[R4])
            P.dma("sp", k4[:], S["bk"].rearrange("b (kv x) -> (b kv) x", kv=KV), reads=[Rb], writes=[R4])
            P.dma("sp", v4[:], S["bv"].rearrange("b (kv x) -> (b kv) x", kv=KV), reads=[Rb], writes=[R4])
            slp = self.sb(ph, [NP, 3], F32, "slp")
            sk4 = self.sb(ph, [NP, 3], F32, "sk4")
            dist = self.sb(ph, [NP, 129], F32, "dist")
            Rc = Region("sAc")
            P.dma("sp", slp[:], I["c_slp"], writes=[Rc])
            P.dma("sp", sk4[:], I["sk4"][l], writes=[Rc])
            P.dma("sp", dist[:], I["c_dist"].partition_broadcast(NP), writes=[Rc])
            lg = self.sb(ph, [NP, 3, 129], F32, "lg")
            ab = self.sb(ph, [NP, 3, 129], F32, "ab")
            Rlg, Rab = Region("lg"), Region("ab")
            P.op("dve", lambda h: h.tensor_tensor(ab[:], dist[:].unsqueeze(1).to_broadcast([NP, 3, 129]), slp[:].unsqueeze(2).to_broadcast([NP, 3, 129]), ALU.mult),
                 reads=[Rc], writes=[Rab])
            KC = 16
            kvb = self.sb(ph, [NP, KC, 128], F32, "kvb")
            tmp = self.sb(ph, [NP, KC * 128], F32, "stmp")
            Rkvb, Rtmp = Region("kvb"), Region("stmp")
            for c in range(128 // KC):
                P.dma("sp", kvb[:].rearrange("p a b -> p (a b)"), I["ck"][l][:, c * KC * 128:(c + 1) * KC * 128], writes=[Rkvb])
                for g3 in range(3):
                    P.op("dve", lambda h, g3=g3: h.tensor_tensor(tmp[:].rearrange("p (a b) -> p a b", a=KC), kvb[:], q4[:, g3, :].unsqueeze(1).to_broadcast([NP, KC, 128]), ALU.mult),
                         reads=[Rkvb, R4], writes=[Rtmp])
                    P.op("dve", lambda h, g3=g3, c=c: h.tensor_reduce(lg[:, g3, c * KC:(c + 1) * KC], tmp[:].rearrange("p (a b) -> p a b", a=KC), AX.X, ALU.add),
                         reads=[Rtmp], writes=[Rlg])
            P.op("dve", lambda h: h.tensor_tensor(tmp[:, 0:384].rearrange("p (a b) -> p a b", a=3), q4[:], k4[:].unsqueeze(1).to_broadcast([NP, 3, 128]), ALU.mult),
                 reads=[R4], writes=[Rtmp])
            P.op("dve", lambda h: h.tensor_reduce(lg[:, :, 128], tmp[:, 0:384].rearrange("p (a b) -> p a b", a=3), AX.X, ALU.add), reads=[Rtmp], writes=[Rlg])
            P.op("dve", lambda h: h.tensor_tensor(lg[:], lg[:], ab[:], ALU.subtract), reads=[Rlg, Rab], writes=[Rlg])
            sm = self.sb(ph, [NP, 24], F32, "sAsm")
            Rsm = Region("sAsm")
            P.op("dve", lambda h: h.tensor_reduce(sm[:, 0:3], lg[:], AX.X, ALU.max), reads=[Rlg], writes=[Rsm])
            P.op("dve", lambda h: h.tensor_tensor(sm[:, 0:3], sm[:, 0:3], sk4[:], ALU.max), reads=[Rsm, Rc], writes=[Rsm])
            P.op("dve", lambda h: h.tensor_tensor(lg[:], lg[:], sm[:, 0:3].unsqueeze(2).to_broadcast([NP, 3, 129]), ALU.subtract), reads=[Rlg, Rsm], writes=[Rlg])
            P.op("act", lambda h: h.activation(lg[:], lg[:], AF.Exp), reads=[Rlg], writes=[Rlg])
            P.op("dve", lambda h: h.tensor_reduce(sm[:, 3:6], lg[:], AX.X, ALU.add), reads=[Rlg], writes=[Rsm])
            P.op("dve", lambda h: h.tensor_tensor(sm[:, 6:9], sk4[:], sm[:, 0:3], ALU.subtract), reads=[Rsm, Rc], writes=[Rsm])
            P.op("act", lambda h: h.activation(sm[:, 6:9], sm[:, 6:9], AF.Exp), reads=[Rsm], writes=[Rsm])
            P.op("dve", lambda h: h.tensor_tensor(sm[:, 3:6], sm[:, 3:6], sm[:, 6:9], ALU.add), reads=[Rsm], writes=[Rsm])
            P.op("dve", lambda h: h.reciprocal(sm[:, 9:12], sm[:, 3:6]), reads=[Rsm], writes=[Rsm])
            P.op("dve", lambda h: h.tensor_tensor(lg[:], lg[:], sm[:, 9:12].unsqueeze(2).to_broadcast([NP, 3, 129]), ALU.mult), reads=[Rlg, Rsm], writes=[Rlg])
            o4 = self.sb(ph, [NP, 3, 128], F32, "o4")
            o4t = self.sb(ph, [NP, 3, 128], F32, "o4t")
            Ro4, Ro4t = Region("o4"), Region("o4t")
            for g3 in range(3):
                P.op("dve", lambda h, g3=g3: h.tensor_scalar(o4[:, g3, :], v4[:], lg[:, g3, 128:129], None, op0=ALU.mult), reads=[R4, Rlg], writes=[Ro4])
            for c in range(128 // KC):
                P.dma("sp", kvb[:].rearrange("p a b -> p (a b)"), I["cv"][l][:, c * KC * 128:(c + 1) * KC * 128], writes=[Rkvb])
                for g3 in range(3):
                    P.op("dve", lambda h, g3=g3, c=c: h.tensor_tensor(tmp[:].rearrange("p (d s) -> p d s", s=KC), kvb[:].rearrange("p s d -> p d s"),
                                                                      lg[:, g3, c * KC:(c + 1) * KC].unsqueeze(1).to_broadcast([NP, 128, KC]), ALU.mult),
                         reads=[Rkvb, Rlg], writes=[Rtmp])
                    P.op("dve", lambda h, g3=g3: h.tensor_reduce(o4t[:, g3, :], tmp[:].rearrange("p (d s) -> p d s", s=KC), AX.X, ALU.add), reads=[Rtmp], writes=[Ro4t])
                    P.op("dve", lambda h, g3=g3: h.tensor_tensor(o4[:, g3, :], o4[:, g3, :], o4t[:, g3, :], ALU.add), reads=[Ro4, Ro4t], writes=[Ro4])
            P.dma("sp", S["bo"].rearrange("b (kv x) -> (b kv) x", kv=KV), o4[:].rearrange("p a b -> p (a b)"), reads=[Ro4], writes=[Rb])
            ao = self.sb(ph, [NS, 768], F32, "ao")
            yb = self.sb(ph, [NS, 768], BF16, "syb")
            Rao, Ryb = Region("ao"), Region("syb")
            P.dma("sp", ao[:], S["bo"], reads=[Rb], writes=[Rao])
            P.op("dve", lambda h: h.tensor_tensor(yb[:], ao[:], za[:], ALU.mult), reads=[Rao, Rza], writes=[Ryb])
            self.s_to_yT(yb, Ryb, 0, 6)
            self.barrier()

    def s_phase_M(self):
        P, l, I, S, O = self.P, self.l, self.I, self.S, self.O
        NH = HM
        QW, VW = NH * 128, NH * 256
        with contextlib.ExitStack() as ph:
            q = self.sb(ph, [NS, QW], F32, "smq")
            k = self.sb(ph, [NS, QW], F32, "smk")
            v = self.sb(ph, [NS, VW], BF16, "smv")
            G = self.sb(ph, [NS, VW], F32, "smG")
            gts = self.sb(ph, [NS, 2 * NH], F32, "smg")
            tm = self.sb(ph, [NS, 256], F32, "smt")
            Rq, Rk, Rv, RG, Rg, Rtm = (Region(n) for n in ("smq", "smk", "smv", "smG", "smg", "smt"))
            for g, wb, Rw in self.wstream("in", list(range(G_MQK, G_CU)) + [G_GATE]):
                def evac(acc, Racc, g=g):
                    if g == G_GATE:
                        P.op("dve", lambda h: h.tensor_tensor(gts[:], acc[0:NS, 0:2 * NH], self.bifb[0:NS, l * 2 * NH:(l + 1) * 2 * NH], ALU.add), reads=[Racc, self.Rbifb], writes=[Rg])
                    elif g < G_MV:
                        for j in range(2):
                            ch = (g - G_MQK) * 2 + j
                            if ch < NH:
                                P.op("act", lambda h, ch=ch, j=j: h.copy(q[:, ch * 128:(ch + 1) * 128], acc[0:NS, j * 128:(j + 1) * 128]), reads=[Racc], writes=[Rq])
                            else:
                                P.op("act", lambda h, ch=ch, j=j: h.activation(k[:, (ch - NH) * 128:(ch - NH + 1) * 128], acc[0:NS, j * 128:(j + 1) * 128], AF.Copy, scale=QSCALE),
                                     reads=[Racc], writes=[Rk])
                    elif g < G_MO:
                        c0 = (g - G_MV) * 256
                        P.op("dve", lambda h: h.tensor_copy(v[:, c0:c0 + 256], acc[0:NS, 0:256]), reads=[Racc], writes=[Rv])
                    elif g < G_MZ:
                        c0 = (g - G_MO) * 256
                        P.op("act", lambda h: h.activation(G[:, c0:c0 + 256], acc[0:NS, 0:256], AF.Sigmoid), reads=[Racc], writes=[RG])
                    else:
                        c0 = (g - G_MZ) * 256
                        P.op("act", lambda h: h.activation(tm[:], acc[0:NS, 0:256], AF.Silu), reads=[Racc], writes=[Rtm])
                        P.op("dve", lambda h: h.tensor_tensor(G[:, c0:c0 + 256], G[:, c0:c0 + 256], tm[:], ALU.mult), reads=[Rtm, RG], writes=[RG])
                self.s_proj(wb, Rw, 2 * NH if g == G_GATE else 256, evac)
            sm = self.sb(ph, [NS, 96], F32, "smsm")
            Rsm = Region("smsm")
            n0 = self.sb(ph, [NS, QW], F32, "smn")
            Rn0 = Region("smn")
            P.dma("sp", n0[:], I["sn"][l], writes=[Rn0])
            P.dma("sp", sm[:, 0:NH], I["sm"][l], writes=[Rsm])
            with contextlib.ExitStack() as ph2:
                mhgb = self.sb(ph2, [NS, VW], F32, "mhgb")
                Rmh = Region("mhgb")
                P.dma("sp", mhgb[:], I["mhg"][l].partition_broadcast(NS), writes=[Rmh])
                P.op("dve", lambda h, mhgb=mhgb: h.tensor_tensor(G[:], G[:], mhgb[:], ALU.mult), reads=[RG, Rmh], writes=[RG])
                self.barrier()
            c_m0, c_sp, c_int, c_mt, c_wq, c_wi, c_qk, c_w, c_qn, c_den, c_t, c_rd, c_ssq, c_f, c_emt = [6 * i for i in range(15)]
            s_ = lambda c: sm[:, c:c + NH]
            ip, fp = gts[:, 0:NH], gts[:, NH:2 * NH]
            P.op("act", lambda h: h.activation(s_(c_sp), fp, AF.Exp, scale=-1.0), reads=[Rg], writes=[Rsm])
            P.op("act", lambda h: h.activation(s_(c_sp), s_(c_sp), AF.Ln, bias=1.0), reads=[Rsm], writes=[Rsm])
            P.op("dve", lambda h: h.tensor_tensor(s_(c_int), s_(c_m0), s_(c_sp), ALU.subtract), reads=[Rsm], writes=[Rsm])
            P.op("dve", lambda h: h.tensor_tensor(s_(c_mt), s_(c_int), ip, ALU.max), reads=[Rsm, Rg], writes=[Rsm])
            P.op("dve", lambda h: h.tensor_tensor(s_(c_wq), ip, s_(c_mt), ALU.subtract), reads=[Rsm, Rg], writes=[Rsm])
            P.op("dve", lambda h: h.tensor_tensor(s_(c_wi), s_(c_int), s_(c_mt), ALU.subtract), reads=[Rsm], writes=[Rsm])
            P.op("act", lambda h: h.activation(s_(c_wq), s_(c_wq), AF.Exp), reads=[Rsm], writes=[Rsm])
            P.op("act", lambda h: h.activation(s_(c_wi), s_(c_wi), AF.Exp), reads=[Rsm], writes=[Rsm])
            P.op("act", lambda h: h.activation(s_(c_emt), s_(c_mt), AF.Exp, scale=-1.0), reads=[Rsm], writes=[Rsm])
            P.dma("sp", O["ms"][l], s_(c_mt), reads=[Rsm], writes=[self.R["ms"]])
            big = self.sb(ph, [NS, VW], F32, "smbig")
            Rbig = Region("smbig")
            bc6 = lambda ap, n: ap.unsqueeze(2).to_broadcast([NS, NH, n])
            q3 = q[:].rearrange("p (a b) -> p a b", a=NH)
            k3 = k[:].rearrange("p (a b) -> p a b", a=NH)
            n3 = n0[:].rearrange("p (a b) -> p a b", a=NH)
            b3 = big[:, 0:QW].rearrange("p (a b) -> p a b", a=NH)
            P.op("dve", lambda h: h.tensor_tensor(b3, q3, k3, ALU.mult), reads=[Rq, Rk], writes=[Rbig])
            P.op("dve", lambda h: h.tensor_reduce(s_(c_qk), b3, AX.X, ALU.add), reads=[Rbig], writes=[Rsm])
            P.op("dve", lambda h: h.tensor_tensor(b3, q3, n3, ALU.mult), reads=[Rq, Rn0], writes=[Rbig])
            P.op("dve", lambda h: h.tensor_reduce(s_(c_qn), b3, AX.X, ALU.add), reads=[Rbig], writes=[Rsm])
            P.op("dve", lambda h: h.tensor_tensor(s_(c_w), s_(c_wq), s_(c_qk), ALU.mult), reads=[Rsm], writes=[Rsm])
            P.op("dve", lambda h: h.tensor_tensor(s_(c_den), s_(c_wi), s_(c_qn), ALU.mult), reads=[Rsm], writes=[Rsm])
            P.op("dve", lambda h: h.tensor_tensor(s_(c_den), s_(c_den), s_(c_w), ALU.add), reads=[Rsm], writes=[Rsm])
            P.op("dve", lambda h: h.scalar_tensor_tensor(s_(c_t), s_(c_den), -1.0, s_(c_den), op0=ALU.mult, op1=ALU.max), reads=[Rsm], writes=[Rsm])
            P.op("dve", lambda h: h.tensor_tensor(s_(c_t), s_(c_t), s_(c_emt), ALU.max), reads=[Rsm], writes=[Rsm])
            P.op("dve", lambda h: h.reciprocal(s_(c_rd), s_(c_t)), reads=[Rsm], writes=[Rsm])
            ksc = self.sb(ph, [NS, QW], F32, "smksc")
            Rksc = Region("smksc")
            ksc3 = ksc[:].rearrange("p (a b) -> p a b", a=NH)
            P.op("dve", lambda h: h.tensor_tensor(ksc3, k3, bc6(s_(c_wq), 128), ALU.mult), reads=[Rk, Rsm], writes=[Rksc])
            P.op("dve", lambda h: h.tensor_tensor(n3, n3, bc6(s_(c_wi), 128), ALU.mult), reads=[Rn0, Rsm], writes=[Rn0])
            P.op("dve", lambda h: h.tensor_tensor(n0[:], n0[:], ksc[:], ALU.add), reads=[Rn0, Rksc], writes=[Rn0])
            P.dma("sp", O["ns"][l], n0[:], reads=[Rn0], writes=[self.R["ns"]])
            qT = self.sb(ph, [128, NH, NS], F32, "smqT")
            RqT = Region("smqT")
            pq, Rpq = self.pb[0], self.Rpb[0]
            fns = [(lambda h, hh=hh: h.transpose(pq[:, hh * NS:(hh + 1) * NS], q[0:NS, hh * 128:(hh + 1) * 128], self.identf[0:NS, 0:NS])) for hh in range(NH)]
            P.group("pe", fns, reads=[Rq, self.Ridf], writes=[Rpq])
            P.op("act", lambda h: h.copy(qT[:].rearrange("p a b -> p (a b)"), pq[:, 0:NH * NS]), reads=[Rpq], writes=[RqT])
            i32 = self.sb(ph, [128, NS, NS], F32, "i32")
            Ri32 = Region("i32")
            P.dma("sp", i32[:].rearrange("p a b -> p (a b)"), I["c_i32"], writes=[Ri32])
            Wd = self.sb(ph, [NS, NS, NH], F32, "smWd")
            RWd = Region("smWd")
            P.op("dve", lambda h: h.tensor_tensor(Wd[:], self.identf[0:NS, 0:NS].unsqueeze(2).to_broadcast([NS, NS, NH]), s_(c_wi).unsqueeze(1).to_broadcast([NS, NS, NH]), ALU.mult),
                 reads=[Rsm, self.Ridf], writes=[RWd])
            pw, Rpw = self.pb[1], self.Rpb[1]
            P.op("pe", lambda h: h.matmul(pw[:, 0:NS * NH], self.onesf[0:NS, :], Wd[:].rearrange("p a b -> p (a b)"), start=True, stop=True), reads=[RWd, self.Ronesf], writes=[Rpw])
            wcb = self.sb(ph, [128, NS, NH], F32, "smwcb")
            Rwcb = Region("smwcb")
            P.op("act", lambda h: h.copy(wcb[:].rearrange("p a b -> p (a b)"), pw[:, 0:NS * NH]), reads=[Rpw], writes=[Rwcb])
            vb, Rvb = v, Rv
            Qm = self.sb(ph, [128, NS, NS], F32, "smQm")
            Km = self.sb(ph, [NS, NS, 128], BF16, "smKm")
            RQm, RKm = Region("smQm"), Region("smKm")
            CB = 8
            Cc = self.sb(ph, [128, CB, 256], F32, "smCc")
            RCc = Region("smCc")
            pR = [self.pb[4], self.pb[5]]
            RpR = [self.Rpb[4], self.Rpb[5]]
            pD = [self.pb[2], self.pb[3]]
            RpD = [self.Rpb[2], self.Rpb[3]]

            def do_head(hh):
                P.op("dve", lambda h: h.tensor_tensor(Qm[:], qT[:, hh, :].unsqueeze(2).to_broadcast([128, NS, NS]), i32[:], ALU.mult), reads=[RqT, Ri32], writes=[RQm])
                P.op("dve", lambda h: h.tensor_tensor(Km[:], self.identf[0:NS, 0:NS].unsqueeze(2).to_broadcast([NS, NS, 128]),
                                                      ksc[:, hh * 128:(hh + 1) * 128].unsqueeze(1).to_broadcast([NS, NS, 128]), ALU.mult),
                     reads=[Rksc, self.Ridf], writes=[RKm])
                oR = pR[hh // 2][0:NS, (hh % 2) * 256:(hh % 2) * 256 + 256]
                for cb in range(NS // CB):
                    P.dma("sp", Cc[:], I["sC"][l, cb * CB:(cb + 1) * CB, hh].rearrange("b d v -> d b v"), writes=[RCc])
                    for j in range(CB):
                        bp = cb * CB + j
                        P.op("pe", lambda h, j=j, bp=bp: h.matmul(oR, Qm[:, bp, :], Cc[:, j, :], start=(bp == 0), stop=(bp == NS - 1)),
                             reads=[RQm, RCc], writes=[RpR[hh // 2]])
                    for j in range(CB):
                        bp = cb * CB + j
                        pd, Rpd = pD[j % 2], RpD[j % 2]
                        P.op("pe", lambda h, j=j, bp=bp, pd=pd: h.matmul(pd[:, 0:256], Km[:, bp, :], vb[:, hh * 256:(hh + 1) * 256], start=True, stop=True),
                             reads=[RKm, Rvb], writes=[Rpd])
                        P.op("dve", lambda h, j=j, bp=bp, pd=pd: h.scalar_tensor_tensor(Cc[:, j, :], Cc[:, j, :], wcb[:, bp, hh:hh + 1], pd[:, 0:256], op0=ALU.mult, op1=ALU.add),
                             reads=[Rpd, Rwcb, RCc], writes=[RCc])
                    P.dma("sp", O["Cs"][l, cb * CB:(cb + 1) * CB, hh].rearrange("b d v -> d b v"), Cc[:], reads=[RCc], writes=[self.R["Cs"]])
            for hh in range(NH):
                do_head(hh)
            num = big
            junk = self.sb(ph, [NS, 256], F32, "smjunk")
            Rjunk = Region("smjunk")
            for hh in range(NH):
                sl = slice(hh * 256, (hh + 1) * 256)
                P.op("dve", lambda h, hh=hh, sl=sl: h.tensor_scalar(num[:, sl], v[:, sl], sm[:, c_w + hh:c_w + hh + 1], None, op0=ALU.mult), reads=[Rv, Rsm], writes=[Rbig])
                P.op("dve", lambda h, hh=hh, sl=sl: h.scalar_tensor_tensor(num[:, sl], pR[hh // 2][0:NS, (hh % 2) * 256:(hh % 2) * 256 + 256], sm[:, c_wi + hh:c_wi + hh + 1],
                                                                           num[:, sl], op0=ALU.mult, op1=ALU.add),
                     reads=[RpR[hh // 2], Rsm, Rbig], writes=[Rbig])
                P.op("act", lambda h, hh=hh, sl=sl: h.activation(junk[:], num[:, sl], AF.Square, accum_out=sm[:, c_ssq + hh:c_ssq + hh + 1]), reads=[Rbig, Rsm], writes=[Rjunk, Rsm])
            P.op("dve", lambda h: h.tensor_tensor(s_(c_t), s_(c_rd), s_(c_rd), ALU.mult), reads=[Rsm], writes=[Rsm])
            P.op("dve", lambda h: h.tensor_tensor(s_(c_t), s_(c_t), s_(c_ssq), ALU.mult), reads=[Rsm], writes=[Rsm])
            P.op("act", lambda h: h.activation(s_(c_t), s_(c_t), AF.Sqrt, bias=EPS, scale=1.0 / 256.0), reads=[Rsm], writes=[Rsm])
            P.op("dve", lambda h: h.reciprocal(s_(c_f), s_(c_t)), reads=[Rsm], writes=[Rsm])
            P.op("dve", lambda h: h.tensor_tensor(s_(c_f), s_(c_f), s_(c_rd), ALU.mult), reads=[Rsm], writes=[Rsm])
            num3 = num[:].rearrange("p (a b) -> p a b", a=NH)
            P.op("dve", lambda h: h.tensor_tensor(num3, num3, bc6(s_(c_f), 256), ALU.mult), reads=[Rbig, Rsm], writes=[Rbig])
            yb = self.sb(ph, [NS, VW], BF16, "smyb")
            Ryb = Region("smyb")
            P.op("dve", lambda h: h.tensor_tensor(yb[:], num[:], G[:], ALU.mult), reads=[Rbig, RG], writes=[Ryb])
            self.s_to_yT(yb, Ryb, 6, 6)
            self.barrier()

    def s_phase_C(self):
        P, l, I, S, O = self.P, self.l, self.I, self.S, self.O
        CW = GC * 128
        with contextlib.ExitStack() as ph:
            u = self.sb(ph, [NS, CW], F32, "scu")
            vv = self.sb(ph, [NS, 1024], F32, "scv")
            tm = self.sb(ph, [NS, 256], F32, "sct")
            Ru, Rvv, Rtm = Region("scu"), Region("scv"), Region("sct")
            for g, wb, Rw in self.wstream("in", list(range(G_CU, G_GATE))):
                def evac(acc, Racc, g=g):
                    if g < G_CV:
                        c0 = (g - G_CU) * 256
                        P.op("act", lambda h: h.activation(u[:, c0:c0 + 256], acc[0:NS, 0:256], AF.Gelu), reads=[Racc], writes=[Ru])
                    elif g < G_CZ:
                        c0 = (g - G_CV) * 256
                        P.op("act", lambda h: h.activation(vv[:, c0:c0 + 256], acc[0:NS, 0:256], AF.Gelu), reads=[Racc], writes=[Rvv])
                    else:
                        c0 = (g - G_CZ) * 256
                        P.op("act", lambda h: h.activation(tm[:], acc[0:NS, 0:256], AF.Silu), reads=[Racc], writes=[Rtm])
                        P.op("dve", lambda h: h.tensor_tensor(u[:, c0:c0 + 256], u[:, c0:c0 + 256], tm[:], ALU.mult), reads=[Rtm, Ru], writes=[Ru])
                self.s_proj(wb, Rw, 256, evac)
            junk = self.sb(ph, [NS, 1024], BF16, "scj")
            sm = self.sb(ph, [NS, 16], F32, "scsm")
            Rj, Rsm = Region("scj"), Region("scsm")
            self.layernorm(vv, Rvv, NS, junk, Rj, sm, Rsm)
            P.dma("sp", O["cvs"][l], vv[:], reads=[Rvv], writes=[self.R["cvs"]])
            wb8 = self.sb(ph, [NS, 16], F32, "scw8")
            Rw8 = Region("scw8")
            P.dma("sp", wb8[:, 0:GC], I["ws00"][l * GC:(l + 1) * GC].partition_broadcast(NS), writes=[Rw8])
            P.dma("sp", wb8[:, 8:8 + GC], I["bs0"][l * GC:(l + 1) * GC].partition_broadcast(NS), writes=[Rw8])
            v3 = vv[:, 0:CW].rearrange("p (a b) -> p a b", a=GC)
            P.op("dve", lambda h: h.tensor_tensor(v3, v3, wb8[:, 0:GC].unsqueeze(2).to_broadcast([NS, GC, 128]), ALU.mult), reads=[Rvv, Rw8], writes=[Rvv])
            P.op("dve", lambda h: h.tensor_tensor(v3, v3, wb8[:, 8:8 + GC].unsqueeze(2).to_broadcast([NS, GC, 128]), ALU.add), reads=[Rvv, Rw8], writes=[Rvv])
            yb = self.sb(ph, [NS, CW], BF16, "scyb")
            Ryb = Region("scyb")
            P.op("dve", lambda h: h.tensor_tensor(yb[:], vv[:, 0:CW], u[:], ALU.mult), reads=[Rvv, Ru], writes=[Ryb])
            self.s_to_yT(yb, Ryb, 12, GC)
            self.barrier()


def _consts():
    c = {}
    c["c_ident"] = np.eye(128, dtype=np.float32)
    q = np.arange(128)[:, None]
    s = np.arange(256)[None, :]
    dist = q + 128 - s
    valid = (dist >= 0) & (dist <= 128)
    c["c_dm"] = np.where(valid, dist, 1.0e9).astype(np.float32)
    c["c_dm0"] = c["c_dm"].copy()
    t = np.arange(128)[:, None]
    s2 = np.arange(128)[None, :]
    ok = ((t // 64) == (s2 // 64)) & (s2 <= t)
    m1 = np.where(ok, 0.0, -30000.0).astype(np.float32)
    c["c_mask6"] = np.tile(m1, (1, 3))
    c["c_tri2"] = ok.T.astype(np.float32).copy()
    onesc = np.zeros((128, 2, 128), np.float32)
    onesc[0:64, 0, :] = 1.0
    onesc[64:128, 1, :] = 1.0
    c["c_onesc"] = onesc.reshape(128, 256)
    cm = np.zeros((128, 2), np.float32)
    cm[0:64, 0] = 1.0
    cm[64:128, 1] = 1.0
    c["c_cmask"] = cm
    c["c_tril"] = (np.arange(128)[None, :] >= np.arange(128)[:, None]).astype(np.float32)
    c["c_dist"] = np.concatenate([np.arange(128, 0, -1), [0]]).astype(np.float32)
    c["c_i32"] = np.tile(np.eye(NS, dtype=np.float32).reshape(1, NS * NS), (128, 1))
    return c


_NC_CACHE = {}
_OFF = dict(aq=0, ak=1536, av=2048, az=2560, mq=4096, mk=4864, mv=5632, mi=7168, mf=7174, mo=7180, mz=8716, cu=10252, cv=11276, cz=12300)


def _cols(c):
    r = lambda base, n, w: list(range(base + c * w, base + c * w + w)) if n is None else None
    idx = []
    idx += list(range(_OFF["aq"] + c * 768, _OFF["aq"] + (c + 1) * 768))
    idx += list(range(_OFF["ak"] + c * 256, _OFF["ak"] + (c + 1) * 256))
    idx += list(range(_OFF["av"] + c * 256, _OFF["av"] + (c + 1) * 256))
    idx += list(range(_OFF["az"] + c * 768, _OFF["az"] + (c + 1) * 768))
    idx += list(range(_OFF["mq"] + c * 384, _OFF["mq"] + (c + 1) * 384))
    idx += list(range(_OFF["mk"] + c * 384, _OFF["mk"] + (c + 1) * 384))
    idx += list(range(_OFF["mv"] + c * 768, _OFF["mv"] + (c + 1) * 768))
    idx += list(range(_OFF["mo"] + c * 768, _OFF["mo"] + (c + 1) * 768))
    idx += list(range(_OFF["mz"] + c * 768, _OFF["mz"] + (c + 1) * 768))
    idx += list(range(_OFF["cu"] + c * 512, _OFF["cu"] + (c + 1) * 512))
    idx += list(range(_OFF["cv"] + c * 512, _OFF["cv"] + (c + 1) * 512))
    idx += list(range(_OFF["cv"] + (1 - c) * 512, _OFF["cv"] + (2 - c) * 512))
    idx += list(range(_OFF["cz"] + c * 512, _OFF["cz"] + (c + 1) * 512))
    idx += list(range(_OFF["mi"] + c * 3, _OFF["mi"] + (c + 1) * 3))
    idx += list(range(_OFF["mf"] + c * 3, _OFF["mf"] + (c + 1) * 3))
    return np.array(idx)


def kernel(x_prompt, x_sample, cache_win_k, cache_win_v, state_mlstm_C, state_mlstm_n, state_mlstm_m,
           norm_gain, w_in, b_if, attn_sinks, m_head_gain, c_ln_gain, c_ln_bias, c_w_s, c_b_s, w_out, final_gain, _ncores=8):
    f = lambda a: np.ascontiguousarray(np.asarray(a, dtype=np.float32))
    w_in, w_out = f(w_in), f(w_out)
    xp = f(x_prompt)
    base = dict(_consts())
    base["xs"] = f(x_sample).reshape(NS, D)
    base["gainT"] = np.ascontiguousarray(f(norm_gain).reshape(2, 32, 128).transpose(0, 2, 1))
    base["fgain"] = f(final_gain)
    half = []
    for c in range(2):
        m = {}
        idx = _cols(c)
        wp = np.zeros((2, D, NG_IN * 256), np.float32)
        wp[:, :, 0:idx.size] = w_in[:, :, idx]
        m["w_in"] = np.ascontiguousarray(wp.reshape(2, 32, 128, NG_IN, 256).transpose(0, 3, 2, 1, 4)).reshape(2, NG_IN, 128, 32 * 256)
        rows = np.concatenate([np.arange(c * 768, (c + 1) * 768), 1536 + np.arange(c * 768, (c + 1) * 768), 3072 + np.arange(c * 512, (c + 1) * 512)])
        wo = w_out[:, rows, :]
        m["w_out"] = np.ascontiguousarray(wo.reshape(2, KY, 128, NG_OUT, 512).transpose(0, 3, 2, 1, 4)).reshape(2, NG_OUT, 128, KY * 512)
        bi = f(b_if)
        m["bif"] = np.ascontiguousarray(bi[:, :, c * 3:(c + 1) * 3]).reshape(12)
        sk = f(attn_sinks)[:, c * 6:(c + 1) * 6]
        m["sinks"] = np.ascontiguousarray(sk).reshape(12)
        m["nslp"] = -np.array(SLOPES[c * 6:(c + 1) * 6], np.float32)
        mh = f(m_head_gain)[:, c * 768:(c + 1) * 768]
        m["mhg"] = np.ascontiguousarray(mh)
        m["mhgT"] = np.ascontiguousarray(mh.reshape(2, 6, 128).transpose(0, 2, 1))
        perm = np.concatenate([np.arange(c * 512, (c + 1) * 512), np.arange((1 - c) * 512, (2 - c) * 512)])
        m["lng"] = np.ascontiguousarray(f(c_ln_gain)[:, perm])
        m["lnb"] = np.ascontiguousarray(f(c_ln_bias)[:, perm])
        ws = f(c_w_s)[:, c * 4:(c + 1) * 4]
        m["wsT"] = np.ascontiguousarray(ws.transpose(0, 3, 1, 2)).reshape(2, 128, GC * 128)
        bsv = f(c_b_s)[:, c * 4:(c + 1) * 4]
        m["bs"] = np.ascontiguousarray(bsv).reshape(2, GC * 128)
        m["ws00"] = np.ascontiguousarray(ws[:, :, 0, 0]).reshape(2 * GC)
        m["bs0"] = np.ascontiguousarray(bsv[:, :, 0]).reshape(2 * GC)
        ck = f(cache_win_k)[:, :, :, c * 2:(c + 1) * 2]
        cvv = f(cache_win_v)[:, :, :, c * 2:(c + 1) * 2]
        m["ck"] = np.ascontiguousarray(ck.transpose(0, 1, 3, 2, 4)).reshape(2, 64, 128 * 128)
        m["cv"] = np.ascontiguousarray(cvv.transpose(0, 1, 3, 2, 4)).reshape(2, 64, 128 * 128)
        m["sk4"] = np.ascontiguousarray(np.tile(sk.reshape(2, 1, 2, 3), (1, 32, 1, 1)).reshape(2, 64, 3))
        m["c_slp"] = np.tile(np.array(SLOPES[c * 6:(c + 1) * 6], np.float32).reshape(2, 3), (32, 1))
        m["sC"] = np.ascontiguousarray(f(state_mlstm_C)[:, :, c * 3:(c + 1) * 3])
        m["sn"] = np.ascontiguousarray(f(state_mlstm_n)[:, :, c * 3:(c + 1) * 3]).reshape(2, NS, HM * 128)
        m["sm"] = np.ascontiguousarray(f(state_mlstm_m)[:, :, c * 3:(c + 1) * 3])
        half.append(m)
    in_maps = []
    for core in range(_ncores):
        m = dict(base)
        m.update(half[core % 2])
        m["xp"] = xp[(core // 2) % 4]
        in_maps.append(m)
    if "nc" not in _NC_CACHE:
        _NC_CACHE["nc"] = KB().build()
    nc = _NC_CACHE["nc"]
    res = run_bass_kernel_spmd(nc, in_maps, core_ids=list(range(_ncores)))
    r = list(res.results)
    while len(r) < 8:
        r = r + r[0:2]
    pr = lambda b: (r[2 * b], r[2 * b + 1])
    y_prompt = np.stack([r[2 * b]["yp"] for b in range(4)]).reshape(4, SEQ, D)
    y_sample = r[0]["ys"].reshape(NS, 1, D)
    cat = lambda key, b, shp, ax: np.concatenate([pr(b)[0][key].reshape(shp), pr(b)[1][key].reshape(shp)], axis=ax)
    kp = np.stack([cat("kp", b, (2, 128, 2, 128), 2) for b in range(4)], axis=1)
    vp = np.stack([cat("vp", b, (2, 128, 2, 128), 2) for b in range(4)], axis=1)
    ks = cat("ks", 0, (2, NS, 1, 2, 128), 3)
    vs = cat("vs", 0, (2, NS, 1, 2, 128), 3)
    Cp = np.stack([np.concatenate([x["Cp"].reshape(2, 128, 3, 256).transpose(0, 2, 1, 3) for x in pr(b)], axis=1) for b in range(4)], axis=1)
    npp = np.stack([np.concatenate([x["np"].transpose(0, 2, 1) for x in pr(b)], axis=1) for b in range(4)], axis=1)
    mp = np.stack([np.concatenate([x["mp"].reshape(2, 3) for x in pr(b)], axis=1) for b in range(4)], axis=1)
    Cs = np.concatenate([r[0]["Cs"], r[1]["Cs"]], axis=2)
    ns = np.concatenate([r[0]["ns"].reshape(2, NS, 3, 128), r[1]["ns"].reshape(2, NS, 3, 128)], axis=2)
    ms = np.concatenate([r[0]["ms"], r[1]["ms"]], axis=2)
    cvp = np.stack([r[2 * b]["cvp"] for b in range(4)], axis=1)
    cvs = r[0]["cvs"].reshape(2, NS, 1, 1024)
    outs = (y_prompt, y_sample, kp, vp, ks, vs, Cp, npp, mp, Cs, ns, ms, cvp, cvs)
    return tuple(np.ascontiguousarray(o, dtype=np.float32) for o in outs)
```

```python
import contextlib
import numpy as np
import concourse.bass as bass
import concourse.mybir as mybir
from concourse.bass_utils import run_bass_kernel_spmd

F32 = mybir.dt.float32
BF16 = mybir.dt.bfloat16
ALU = mybir.AluOpType
AF = mybir.ActivationFunctionType
AX = mybir.AxisListType

D = 4096
SEQ = 2048
NS = 32
DIN = 13324
NG_IN = 29
NG_OUT = 8
TT = 512
NBLK = TT // 128
NTILE = SEQ // TT
EPS = 1e-6
SLOPES = [2.0 ** (-8.0 * (h + 1) / 12.0) for h in range(12)]
QSCALE = 128.0 ** -0.5
G_AQ, G_AK, G_AV, G_AZ = 0, 3, 4, 5
G_MQK, G_MV, G_MO, G_MZ = 8, 11, 14, 17
G_CU, G_CV, G_CZ, G_GATE = 20, 22, 26, 28
HA, KV, HM, GC = 6, 2, 3, 4
KY = 16
PAIRS = [[0, 1], [2, 3], [4, 5], [6, 7]]
SBUF_LIMIT = 182 * 1024


class Region:
    __slots__ = ("name", "last_w", "readers", "excl")

    def __init__(self, name, excl=False):
        self.name = name
        self.last_w = None
        self.readers = []
        self.excl = excl


class EngineCtx:
    def __init__(self, name, sem):
        self.name = name
        self.sem = sem
        self.count = 0
        self.known = {}
        self.instrs = []


class Prog:
    def __init__(self, nc, stack, n_dma_sems=24):
        self.nc = nc
        self.eng = {}
        self.sems = {}
        for name in ("pe", "act", "dve", "pool", "sp"):
            sem = stack.enter_context(nc.semaphore("s_" + name))
            self.eng[name] = EngineCtx(name, sem)
            self.sems["e_" + name] = sem
        self.dma_pool = {}
        for q in ("sp", "pool"):
            lst = []
            for i in range(n_dma_sems if q == "sp" else 3):
                key = "d_%s_%d" % (q, i)
                self.sems[key] = stack.enter_context(nc.semaphore(key))
                lst.append([key, 0])
            self.dma_pool[q] = [lst, 0]
        self.n_instr = 0

    def _need(self, e, tok, waits):
        if tok is None:
            return
        key, val, src = tok
        if src == "pe" and e.name == "pe":
            return
        if e.known.get(key, 0) >= val:
            return
        if waits.get(key, 0) < val:
            waits[key] = val

    def _deps(self, e, reads, writes):
        waits = {}
        for r in reads:
            if r.excl:
                writes = list(writes) + [r]
                continue
            self._need(e, r.last_w, waits)
        for r in writes:
            self._need(e, r.last_w, waits)
            for t in r.readers:
                self._need(e, t, waits)
        return waits

    def _commit(self, tok, reads, writes):
        for r in reads:
            if r.excl:
                r.last_w = tok
                r.readers = []
                continue
            r.readers.append(tok)
            if len(r.readers) > 48:
                best = {}
                for t in r.readers:
                    if best.get(t[0], (None, -1))[1] < t[1]:
                        best[t[0]] = t
                r.readers = list(best.values())
        for r in writes:
            r.last_w = tok
            r.readers = []

    def _emit_waits(self, e, waits):
        for key, val in waits.items():
            e.instrs.append(("wait", self.sems[key], val))
            e.known[key] = val

    def op(self, engname, fn, reads=(), writes=()):
        return self.group(engname, [fn], reads, writes)

    def group(self, engname, fns, reads=(), writes=()):
        e = self.eng[engname]
        self._emit_waits(e, self._deps(e, reads, writes))
        e.count += 1
        tok = ("e_" + engname, e.count, engname)
        for fn in fns[:-1]:
            e.instrs.append(("op", fn, None, 0))
        e.instrs.append(("op", fns[-1], e.sem, 1))
        self._commit(tok, reads, writes)
        self.n_instr += len(fns)
        return tok

    def dma(self, qname, out, in_, reads=(), writes=()):
        e = self.eng[qname]
        waits = self._deps(e, reads, writes)
        pool, idx = self.dma_pool[qname]
        ent = pool[idx % len(pool)]
        self.dma_pool[qname][1] = idx + 1
        key, cum = ent
        if cum > 0 and e.known.get(key, 0) < cum and waits.get(key, 0) < cum:
            waits[key] = cum
        self._emit_waits(e, waits)
        ent[1] = cum + 16
        tok = (key, cum + 16, "dma")

        def fn(h, out=out, in_=in_):
            return h.dma_start(out=out, in_=in_)
        e.instrs.append(("op", fn, self.sems[key], 16))
        self._commit(tok, reads, writes)
        self.n_instr += 1
        return tok

    def coll(self, stack, ins_ap, outs_ap, reads=(), writes=()):
        e = self.eng["pool"]
        self._emit_waits(e, self._deps(e, reads, writes))
        key = "cc_%d" % len([k for k in self.sems if k.startswith("cc_")])
        sem = stack.enter_context(self.nc.semaphore(key))
        self.sems[key] = sem
        tok = (key, 1, "cc")

        def fn(h, ins_ap=ins_ap, outs_ap=outs_ap):
            return h.collective_compute("AllReduce", ALU.add, replica_groups=PAIRS, ins=[ins_ap], outs=[outs_ap])
        e.instrs.append(("op", fn, sem, 1))
        e.instrs.append(("wait", sem, 1))
        e.known[key] = 1
        self._commit(tok, reads, writes)
        return tok

    def wait_tok(self, engname, tok):
        e = self.eng[engname]
        waits = {}
        self._need(e, tok, waits)
        self._emit_waits(e, waits)

    def barrier(self, bar_out, bar_in, bar_region):
        sp = self.eng["sp"]
        waits = {}
        snap = {}
        for name in ("pe", "act", "dve"):
            c = self.eng[name].count
            snap["e_" + name] = c
            if c > 0:
                self._need(sp, ("e_" + name, c, name), waits)
        for key, cum in self.dma_pool["sp"][0]:
            snap[key] = cum
            if cum > 0:
                self._need(sp, (key, cum, "dma"), waits)
        self._emit_waits(sp, waits)
        tok = self.dma("sp", bar_out, bar_in, writes=[bar_region])
        for name in ("pe", "act", "dve"):
            self.wait_tok(name, tok)
            e = self.eng[name]
            for k, v in snap.items():
                if e.known.get(k, 0) < v:
                    e.known[k] = v

    def final_wait(self, engname, regions):
        e = self.eng[engname]
        waits = {}
        for r in regions:
            self._need(e, r.last_w, waits)
            for t in r.readers:
                self._need(e, t, waits)
        self._emit_waits(e, waits)

    def replay(self):
        nc = self.nc
        with nc.Block() as block:
            def mk(e):
                def body(h):
                    for ins in e.instrs:
                        if ins[0] == "wait":
                            h.wait_ge(ins[1], ins[2])
                        elif ins[2] is None:
                            ins[1](h)
                        else:
                            ins[1](h).then_inc(ins[2], ins[3])
                return body
            block.tensor(mk(self.eng["pe"]))
            block.scalar(mk(self.eng["act"]))
            block.vector(mk(self.eng["dve"]))
            block.gpsimd(mk(self.eng["pool"]))
            block.sync(mk(self.eng["sp"]))


class KB:
    def __init__(self, do_sample=True):
        self.do_sample = do_sample
        self.nc = bass.Bass("TRN2", target_bir_lowering=False)
        self.uid = 0
        self.sb_bytes = 0
        self.sb_peak = 0

    def sb(self, stack, shape, dt, name="t"):
        self.uid += 1
        n = 1
        for s in shape[1:]:
            n *= s
        nbytes = n * (4 if dt == F32 else 2)
        self.sb_bytes += nbytes
        self.sb_peak = max(self.sb_peak, self.sb_bytes)
        assert self.sb_bytes <= SBUF_LIMIT, ("SBUF overflow", name, self.sb_bytes)
        t = stack.enter_context(self.nc.sbuf_tensor("%s_%d" % (name, self.uid), list(shape), dt))

        def rel():
            self.sb_bytes -= nbytes
        stack.callback(rel)
        return t

    def dram_in(self, name, shape, dt=F32):
        return self.nc.dram_tensor(name, list(shape), dt, kind="ExternalInput").ap()

    def dram_out(self, name, shape, dt=F32):
        return self.nc.dram_tensor(name, list(shape), dt, kind="ExternalOutput").ap()

    def dram_scr(self, name, shape, dt=F32):
        return self.nc.dram_tensor(name, list(shape), dt, kind="Internal").ap()

    def build(self):
        nc = self.nc
        I = {}
        I["xp"] = self.dram_in("xp", [SEQ, D])
        I["xs"] = self.dram_in("xs", [NS, D])
        I["w_in"] = self.dram_in("w_in", [2, NG_IN, 128, 32 * 256])
        I["w_out"] = self.dram_in("w_out", [2, NG_OUT, 128, KY * 512])
        I["gainT"] = self.dram_in("gainT", [2, 128, 32])
        I["fgain"] = self.dram_in("fgain", [D])
        I["bif"] = self.dram_in("bif", [12])
        I["sinks"] = self.dram_in("sinks", [12])
        I["nslp"] = self.dram_in("nslp", [HA])
        I["mhgT"] = self.dram_in("mhgT", [2, 128, 6])
        I["mhg"] = self.dram_in("mhg", [2, 768])
        I["lng"] = self.dram_in("lng", [2, 1024])
        I["lnb"] = self.dram_in("lnb", [2, 1024])
        I["wsT"] = self.dram_in("wsT", [2, 128, GC * 128])
        I["bs"] = self.dram_in("bs", [2, GC * 128])
        I["ws00"] = self.dram_in("ws00", [2 * GC])
        I["bs0"] = self.dram_in("bs0", [2 * GC])
        I["ck"] = self.dram_in("ck", [2, 64, 128 * 128])
        I["cv"] = self.dram_in("cv", [2, 64, 128 * 128])
        I["sk4"] = self.dram_in("sk4", [2, 64, 3])
        I["c_slp"] = self.dram_in("c_slp", [64, 3])
        I["sC"] = self.dram_in("sC", [2, NS, HM, 128, 256])
        I["sn"] = self.dram_in("sn", [2, NS, HM * 128])
        I["sm"] = self.dram_in("sm", [2, NS, HM])
        I["c_ident"] = self.dram_in("c_ident", [128, 128])
        I["c_dm"] = self.dram_in("c_dm", [128, 256])
        I["c_dm0"] = self.dram_in("c_dm0", [128, 256])
        I["c_mask6"] = self.dram_in("c_mask6", [128, 384])
        I["c_tri2"] = self.dram_in("c_tri2", [128, 128])
        I["c_onesc"] = self.dram_in("c_onesc", [128, 256])
        I["c_cmask"] = self.dram_in("c_cmask", [128, 2])
        I["c_tril"] = self.dram_in("c_tril", [128, 128])
        I["c_dist"] = self.dram_in("c_dist", [129])
        I["c_i32"] = self.dram_in("c_i32", [128, NS * NS])
        O = {}
        O["yp"] = self.dram_out("yp", [SEQ, D])
        O["ys"] = self.dram_out("ys", [NS, D])
        O["kp"] = self.dram_out("kp", [2, 128, 256])
        O["vp"] = self.dram_out("vp", [2, 128, 256])
        O["ks"] = self.dram_out("ks", [2, NS, 256])
        O["vs"] = self.dram_out("vs", [2, NS, 256])
        O["Cp"] = self.dram_out("Cp", [2, 128, HM * 256])
        O["np"] = self.dram_out("np", [2, 128, HM])
        O["mp"] = self.dram_out("mp", [2, 1, HM])
        O["Cs"] = self.dram_out("Cs", [2, NS, HM, 128, 256])
        O["ns"] = self.dram_out("ns", [2, NS, HM * 128])
        O["ms"] = self.dram_out("ms", [2, NS, HM])
        O["cvp"] = self.dram_out("cvp", [2, 128, 1024])
        O["cvs"] = self.dram_out("cvs", [2, NS, 1024])
        self.I, self.O = I, O
        S = {}
        S["wb_in"] = self.dram_scr("wb_in", [2, NG_IN, 128, 32 * 256], BF16)
        S["wb_out"] = self.dram_scr("wb_out", [2, NG_OUT, 128, KY * 512], BF16)
        S["h1"] = self.dram_scr("h1", [SEQ, D])
        S["hs1"] = self.dram_scr("hs1", [NS, D])
        for l in range(2):
            S["po%d" % l] = self.dram_scr("po%d" % l, [SEQ, D])
            S["red%d" % l] = self.dram_scr("red%d" % l, [SEQ, D])
            S["pos%d" % l] = self.dram_scr("pos%d" % l, [NS, D])
            S["reds%d" % l] = self.dram_scr("reds%d" % l, [NS, D])
        S["bar"] = self.dram_scr("bar", [2, 16])
        S["bq"] = self.dram_scr("bq", [NS, 768])
        S["bk"] = self.dram_scr("bk", [NS, 256])
        S["bv"] = self.dram_scr("bv", [NS, 256])
        S["bo"] = self.dram_scr("bo", [NS, 768])
        self.S = S

        with contextlib.ExitStack() as st:
            self.st = st
            P = self.P = Prog(nc, st)
            self.R = {}
            for k in list(O.keys()) + ["h1", "hs1", "bar", "xin", "bnc"]:
                self.R[k] = Region(k)
            self.Rw = {}
            self.pb = [st.enter_context(nc.psum_tensor("pb%d" % i, [128, 512], F32)) for i in range(7)]
            self.Rpb = [Region("pb%d" % i, excl=True) for i in range(7)]
            self.tb = st.enter_context(nc.psum_tensor("tb0", [128, 1024], BF16))
            self.Rtb = Region("tb0", excl=True)
            self.acc_i = 0
            self.emit_all()
            P.replay()
        return nc

    def const(self, name, shape, dt, src_ap):
        t = self.sb(self.st, shape, dt, name)
        r = Region(name)
        self.P.dma("sp", t[:], src_ap, writes=[r])
        return t, r

    def barrier(self):
        self.P.barrier(self.S["bar"][0:1, :], self.S["bar"][1:2, :], self.R["bar"])

    def next_acc(self):
        i = self.acc_i % 4
        self.acc_i += 1
        return self.pb[i], self.Rpb[i]

    def convert_weights(self, l):
        P, S, I = self.P, self.S, self.I
        for g in range(NG_IN):
            r = Region("wbin%d_%d" % (l, g))
            self.Rw[("in", l, g)] = r
            P.dma("pool", S["wb_in"][l, g], I["w_in"][l, g], writes=[r])
        for g in range(NG_OUT):
            r = Region("wbout%d_%d" % (l, g))
            self.Rw[("out", l, g)] = r
            P.dma("pool", S["wb_out"][l, g], I["w_out"][l, g], writes=[r])

    def emit_all(self):
        P, nc, I, O, S, st = self.P, self.nc, self.I, self.O, self.S, self.st
        sbp = lambda shape, dt, name: self.sb(st, shape, dt, name)
        bz = sbp([2, 16], F32, "barz")
        Rbz = Region("barz")
        P.op("dve", lambda h: h.memset(bz[:], 0.0), writes=[Rbz])
        P.dma("sp", S["bar"], bz[:], reads=[Rbz], writes=[self.R["bar"]])
        self.convert_weights(0)
        self.convert_weights(1)
        self.build_wseq()
        identf, Ridf = self.const("identf", [128, 128], F32, I["c_ident"])
        self.identf, self.Ridf = identf, Ridf
        identb = sbp([128, 128], BF16, "identb")
        Ridb = Region("identb")
        P.op("dve", lambda h: h.tensor_copy(identb[:], identf[:]), reads=[Ridf], writes=[Ridb])
        self.identb, self.Ridb = identb, Ridb
        self.dm, self.Rdm = self.const("dm", [128, 256], F32, I["c_dm"])
        self.dm0, self.Rdm0 = self.const("dm0", [128, 256], F32, I["c_dm0"])
        self.mask6, self.Rmask6 = self.const("mask6", [128, 384], F32, I["c_mask6"])
        self.tri2, self.Rtri2 = self.const("tri2", [128, 128], F32, I["c_tri2"])
        self.onesc, self.Ronesc = self.const("onesc", [128, 256], F32, I["c_onesc"])
        self.cmask, self.Rcmask = self.const("cmask", [128, 2], F32, I["c_cmask"])
        self.bifb, self.Rbifb = self.const("bifb", [128, 12], F32, I["bif"].partition_broadcast(128))
        self.sinkb, self.Rsinkb = self.const("sinkb", [128, 12], F32, I["sinks"].partition_broadcast(128))
        self.nslp, self.Rnslp = self.const("nslp", [128, HA], F32, I["nslp"].partition_broadcast(128))
        onesf = sbp([128, 128], F32, "onesf")
        self.Ronesf = Region("onesf")
        P.op("dve", lambda h: h.memset(onesf[:], 1.0), writes=[self.Ronesf])
        self.onesf = onesf
        onesb = sbp([128, 2], BF16, "onesb")
        self.Ronesb = Region("onesb")
        P.op("dve", lambda h: h.memset(onesb[:], 1.0), writes=[self.Ronesb])
        self.onesb = onesb
        tril, Rtril = self.const("tril", [128, 128], F32, I["c_tril"])
        self.hnT = sbp([128, 32, TT], BF16, "hnT")
        self.RhnT = Region("hnT")
        self.yT = sbp([128, KY, TT], BF16, "yT")
        self.RyT = [Region("yT_A"), Region("yT_M"), Region("yT_C")]
        self.wbuf = [sbp([128, 32 * 256], BF16, "wbuf") for _ in range(3)]
        self.Rwbuf = [Region("wbuf%d" % i) for i in range(3)]
        self.w_i = 0
        self.Cst = sbp([128, HM, 256], F32, "Cst")
        self.Cb = [sbp([128, HM, 256], BF16, "Cb") for _ in range(2)]
        self.nst = sbp([128, 8], F32, "nst")
        self.nb = [sbp([128, 8], BF16, "nb") for _ in range(2)]
        self.mst = sbp([128, 8], F32, "mst")
        self.RC = Region("Cst")
        self.RCb = [Region("Cb0"), Region("Cb1")]
        self.Rn = Region("nst")
        self.Rnb = [Region("nb0"), Region("nb1")]
        self.Rm = Region("mst")
        self.kcar = sbp([128, KV, 128], BF16, "kcar")
        self.vcar = sbp([128, KV * 128], BF16, "vcar")
        self.Rkcar, self.Rvcar = Region("kcar"), Region("vcar")
        self.gainT = sbp([128, 32], F32, "gainT")
        self.RgainT = Region("gainT")
        self.mhgT = sbp([128, 6], F32, "mhgT")
        self.RmhgT = Region("mhgT")
        self.wmT = sbp([128, GC, 128], BF16, "wmT")
        self.RwmT = Region("wmT")
        self.bsb = sbp([128, GC * 128], F32, "bsb")
        self.Rbsb = Region("bsb")
        self.lngb = sbp([128, 1024], F32, "lngb")
        self.lnbb = sbp([128, 1024], F32, "lnbb")
        self.Rln = Region("ln")
        self.Rpo = {}
        self.Rred = {}

        for l in range(2):
            self.l = l
            P.dma("sp", self.gainT[:], I["gainT"][l], writes=[self.RgainT])
            P.dma("sp", self.mhgT[:], I["mhgT"][l], writes=[self.RmhgT])
            P.dma("sp", self.bsb[:], I["bs"][l].partition_broadcast(128), writes=[self.Rbsb])
            P.dma("sp", self.lngb[:], I["lng"][l].partition_broadcast(128), writes=[self.Rln])
            P.dma("sp", self.lnbb[:], I["lnb"][l].partition_broadcast(128), writes=[self.Rln])
            with contextlib.ExitStack() as ph:
                wtmp = self.sb(ph, [128, GC, 128], F32, "wtmp")
                Rwtmp = Region("wtmp")
                P.dma("sp", wtmp[:].rearrange("p a b -> p (a b)"), I["wsT"][l], writes=[Rwtmp])
                P.op("dve", lambda h, wtmp=wtmp: h.tensor_tensor(self.wmT[:], wtmp[:], tril[:].unsqueeze(1).to_broadcast([128, GC, 128]), ALU.mult),
                     reads=[Rwtmp, Rtril], writes=[self.RwmT])
                self.barrier()
            P.op("dve", lambda h: h.memset(self.Cst[:], 0.0), writes=[self.RC])
            P.op("dve", lambda h: h.memset(self.Cb[0][:], 0.0), writes=[self.RCb[0]])
            P.op("dve", lambda h: h.memset(self.nst[:], 0.0), writes=[self.Rn])
            P.op("dve", lambda h: h.memset(self.nb[0][:], 0.0), writes=[self.Rnb[0]])
            P.op("dve", lambda h: h.memset(self.mst[:], 0.0), writes=[self.Rm])
            for t in range(NTILE):
                self.t = t
                rows = slice(t * TT, (t + 1) * TT)
                if l == 0:
                    self.phase_norm(I["xp"][rows, :], self.R["xin"], NBLK, 128)
                else:
                    self.phase_norm(I["xp"][rows, :], self.R["xin"], NBLK, 128,
                                    add=(S["red0"][rows, :], self.Rred[(0, t)]), store=(S["h1"][rows, :], self.R["h1"]))
                self.phase_A()
                self.phase_M()
                self.phase_C()
                rp = Region("po%d_%d" % (l, t))
                rr = Region("red%d_%d" % (l, t))
                self.Rpo[(l, t)], self.Rred[(l, t)] = rp, rr
                self.phase_O(S["po%d" % l][rows, :], rp, 128, NBLK)
                for cpart in range(TT // 256):
                    crow = slice(t * TT + cpart * 256, t * TT + (cpart + 1) * 256)
                    P.coll(st, S["po%d" % l][crow, :], S["red%d" % l][crow, :], reads=[rp], writes=[rr])
            P.dma("sp", O["Cp"][l], self.Cst[:].rearrange("p a b -> p (a b)"), reads=[self.RC], writes=[self.R["Cp"]])
            P.dma("sp", O["np"][l], self.nst[:, 0:HM], reads=[self.Rn], writes=[self.R["np"]])
            P.dma("sp", O["mp"][l], self.mst[0:1, 0:HM], reads=[self.Rm], writes=[self.R["mp"]])
            if self.do_sample:
                self.sample_layer()
        for t in range(NTILE):
            rows = slice(t * TT, (t + 1) * TT)
            self.phase_final(S["h1"][rows, :], self.R["h1"], S["red1"][rows, :], self.Rred[(1, t)], O["yp"][rows, :], self.R["yp"], 128, NBLK)
        if self.do_sample:
            self.phase_final(S["hs1"], self.R["hs1"], S["reds1"], self.Rred[(1, "s")], O["ys"], self.R["ys"], NS, 1)
        outs = [self.R[k] for k in O.keys()]
        P.final_wait("sp", outs)

    def build_wseq(self):
        per = ([("in", g) for g in range(G_AQ, G_MQK)] + [("in", g) for g in list(range(G_MQK, G_CU)) + [G_GATE]]
               + [("in", g) for g in range(G_CU, G_GATE)] + [("out", g) for g in range(NG_OUT)])
        seq = []
        for l in range(2):
            for _ in range(NTILE + (1 if self.do_sample else 0)):
                seq += [(k, l, g) for (k, g) in per]
        self.wseq = seq
        self.w_issued = 0
        self.w_cons = 0

    def wstream(self, kind, groups):
        for g in groups:
            idx = self.w_cons
            assert self.wseq[idx] == (kind, self.l, g), (self.wseq[idx], kind, self.l, g)
            while self.w_issued < min(len(self.wseq), idx + 3):
                k2, l2, g2 = self.wseq[self.w_issued]
                i2 = self.w_issued % 3
                src = self.S["wb_in"] if k2 == "in" else self.S["wb_out"]
                self.P.dma("sp", self.wbuf[i2][:], src[l2, g2], reads=[self.Rw[(k2, l2, g2)]], writes=[self.Rwbuf[i2]])
                self.w_issued += 1
            self.w_cons += 1
            i = idx % 3
            if kind == "in":
                yield g, self.wbuf[i][:].rearrange("p (k c) -> p k c", k=32), self.Rwbuf[i]
            else:
                yield g, self.wbuf[i][:].rearrange("p (k c) -> p k c", k=KY), self.Rwbuf[i]

    def mm_feat(self, wbuf, Rw, j, ntok):
        acc, Racc = self.next_acc()
        hnT = self.hnT
        fns = [(lambda h, kc=kc: h.matmul(acc[:, 0:ntok], wbuf[:, kc, j * 128:(j + 1) * 128], hnT[:, kc, 0:ntok],
                                          start=(kc == 0), stop=(kc == 31))) for kc in range(32)]
        self.P.group("pe", fns, reads=[Rw, self.RhnT], writes=[Racc])
        return acc, Racc

    def mm_tok(self, wbuf, Rw, t0, nt, ncols):
        acc, Racc = self.next_acc()
        hnT = self.hnT
        fns = [(lambda h, kc=kc: h.matmul(acc[0:nt, 0:ncols], hnT[:, kc, t0:t0 + nt], wbuf[:, kc, 0:ncols],
                                          start=(kc == 0), stop=(kc == 31))) for kc in range(32)]
        self.P.group("pe", fns, reads=[Rw, self.RhnT], writes=[Racc])
        return acc, Racc

    def phase_norm(self, src, Rsrc, nblk, np_, add=None, store=None):
        P = self.P
        with contextlib.ExitStack() as ph:
            hb0 = self.sb(ph, [128, D], F32, "hb")
            hb = [hb0, hb0]
            hb2 = self.sb(ph, [128, D], F32, "hb2") if add is not None else None
            hn = self.sb(ph, [128, D], BF16, "hn")
            junk = hn
            stt = [self.sb(ph, [128, 4], F32, "nst") for _ in range(2)]
            Rhb0 = Region("hb0")
            Rhb = [Rhb0, Rhb0]
            Rhb2 = Region("hb2")
            Rhn = Region("hn")
            Rst = [Region("st0"), Region("st1")]
            Rj = Rhn
            for b in range(nblk):
                i = b % 2
                h_, n_, s_ = hb[i], hn, stt[i]
                rs = slice(b * np_, (b + 1) * np_)
                P.dma("sp", h_[0:np_, :], src[rs, :], reads=[Rsrc], writes=[Rhb[i]])
                if add is not None:
                    P.dma("sp", hb2[0:np_, :], add[0][rs, :], reads=[add[1]], writes=[Rhb2])
                    P.op("dve", lambda h, h_=h_: h.tensor_tensor(h_[0:np_, :], h_[0:np_, :], hb2[0:np_, :], ALU.add), reads=[Rhb[i], Rhb2], writes=[Rhb[i]])
                    if store is not None:
                        P.dma("sp", store[0][rs, :], h_[0:np_, :], reads=[Rhb[i]], writes=[store[1]])
                P.op("act", lambda h, h_=h_, s_=s_: h.activation(junk[0:np_, :], h_[0:np_, :], AF.Square, accum_out=s_[0:np_, 0:1]),
                     reads=[Rhb[i]], writes=[Rj, Rst[i]])
                P.op("act", lambda h, s_=s_: h.activation(s_[0:np_, 1:2], s_[0:np_, 0:1], AF.Sqrt, bias=EPS, scale=1.0 / D),
                     reads=[Rst[i]], writes=[Rst[i]])
                P.op("dve", lambda h, s_=s_: h.reciprocal(s_[0:np_, 2:3], s_[0:np_, 1:2]), reads=[Rst[i]], writes=[Rst[i]])
                P.op("dve", lambda h, h_=h_, n_=n_, s_=s_: h.tensor_scalar(n_[0:np_, :], h_[0:np_, :], s_[0:np_, 2:3], None, op0=ALU.mult),
                     reads=[Rhb[i], Rst[i]], writes=[Rhn])
                for q in range(4):
                    fns = [(lambda h, kc=kc, n_=n_: h.transpose(self.tb[:, (kc % 8) * 128:(kc % 8) * 128 + np_],
                                                                n_[0:np_, kc * 128:(kc + 1) * 128], self.identb[0:np_, 0:np_]))
                           for kc in range(q * 8, q * 8 + 8)]
                    P.group("pe", fns, reads=[Rhn, self.Ridb], writes=[self.Rtb])
                    P.op("dve", lambda h, q=q, b=b: h.tensor_tensor(
                        self.hnT[:, q * 8:(q + 1) * 8, b * np_:(b + 1) * np_],
                        self.tb[:, :].rearrange("p (a c) -> p a c", a=8)[:, :, 0:np_],
                        self.gainT[:, q * 8:(q + 1) * 8].unsqueeze(2).to_broadcast([128, 8, np_]), ALU.mult),
                        reads=[self.Rtb, self.RgainT], writes=[self.RhnT])
            self.barrier()

    def phase_A(self):
        P, l, t = self.P, self.l, self.t
        first_tile = (t == 0)
        last_tile = (t == NTILE - 1)
        KW = KV * 128
        with contextlib.ExitStack() as ph:
            qT = self.sb(ph, [128, HA, TT], BF16, "qT")
            kT = self.sb(ph, [128, KV, TT + 128], BF16, "kT")
            vt = self.sb(ph, [128, NBLK + 1, KW], BF16, "vt")
            zT = self.sb(ph, [128, HA, TT], BF16, "zT")
            ost = self.sb(ph, [128, 2, KW], F32, "ost")
            RqT, RkT, Rvt, RzT, Rost = Region("qT"), Region("kT"), Region("vt"), Region("zT"), Region("ost")
            if not first_tile:
                P.op("act", lambda h: h.copy(kT[:, :, 0:128], self.kcar[:]), reads=[self.Rkcar], writes=[RkT])
                P.op("act", lambda h: h.copy(vt[:, 0, :], self.vcar[:]), reads=[self.Rvcar], writes=[Rvt])
            for g, wb, Rw in self.wstream("in", list(range(G_AQ, G_MQK))):
                if g < G_AK:
                    for j in range(2):
                        acc, Racc = self.mm_feat(wb, Rw, j, TT)
                        hd = (g - G_AQ) * 2 + j
                        P.op("act", lambda h, acc=acc, hd=hd: h.activation(qT[:, hd, :], acc[:, 0:TT], AF.Copy, scale=QSCALE),
                             reads=[Racc], writes=[RqT])
                elif g < G_AV:
                    for j in range(2):
                        acc, Racc = self.mm_feat(wb, Rw, j, TT)
                        P.op("dve", lambda h, acc=acc, j=j: h.tensor_copy(kT[:, j, 128:128 + TT], acc[:, 0:TT]), reads=[Racc], writes=[RkT])
                    if last_tile:
                        acc, Racc = self.mm_tok(wb, Rw, TT - 128, 128, 256)
                        P.op("dve", lambda h, acc=acc: h.tensor_copy(ost[:, 0, :], acc[:, 0:256]), reads=[Racc], writes=[Rost])
                elif g < G_AZ:
                    for b in range(NBLK):
                        acc, Racc = self.mm_tok(wb, Rw, b * 128, 128, 256)
                        P.op("act", lambda h, acc=acc, b=b: h.copy(vt[:, b + 1, :], acc[:, 0:256]), reads=[Racc], writes=[Rvt])
                        if last_tile and b == NBLK - 1:
                            P.op("dve", lambda h, acc=acc: h.tensor_copy(ost[:, 1, :], acc[:, 0:256]), reads=[Racc], writes=[Rost])
                else:
                    for j in range(2):
                        acc, Racc = self.mm_feat(wb, Rw, j, TT)
                        hd = (g - G_AZ) * 2 + j
                        P.op("act", lambda h, acc=acc, hd=hd: h.activation(zT[:, hd, :], acc[:, 0:TT], AF.Silu), reads=[Racc], writes=[RzT])
            if last_tile:
                P.dma("sp", self.O["kp"][l], ost[:, 0, :], reads=[Rost], writes=[self.R["kp"]])
                P.dma("sp", self.O["vp"][l], ost[:, 1, :], reads=[Rost], writes=[self.R["vp"]])
            P.op("act", lambda h: h.copy(self.kcar[:], kT[:, :, TT:TT + 128]), reads=[RkT], writes=[self.Rkcar])
            P.op("act", lambda h: h.copy(self.vcar[:], vt[:, NBLK, :]), reads=[Rvt], writes=[self.Rvcar])
            NB_ = HA
            L = [self.sb(ph, [128, 256], F32, "L") for _ in range(NB_)]
            Pn = [self.sb(ph, [128, 256], BF16, "Pn") for _ in range(NB_)]
            PT = [self.sb(ph, [128, 2, 128], BF16, "PT") for _ in range(NB_)]
            sm = [self.sb(ph, [128, 8], F32, "asm") for _ in range(NB_)]
            RL = [Region("L%d" % i) for i in range(NB_)]
            RPn = [Region("Pn%d" % i) for i in range(NB_)]
            RPT = [Region("PT%d" % i) for i in range(NB_)]
            Rsm = [Region("sm%d" % i) for i in range(NB_)]

            def attn_block(b):
                gfirst = first_tile and b == 0
                nk = 128 if gfirst else 256
                koff = b * 128 + (128 if gfirst else 0)
                dmo = 128 if gfirst else 0
                nkb = nk // 128
                vb0 = b + (1 if gfirst else 0)
                for hd in range(HA):
                    kv = hd // 3
                    pS, RpS = self.pb[4 + hd % 3], self.Rpb[4 + hd % 3]
                    L_, sm_ = L[hd], sm[hd]
                    sk = self.sinkb[:, l * HA + hd:l * HA + hd + 1]
                    ns_ = self.nslp[:, hd:hd + 1]
                    P.op("pe", lambda h, pS=pS, hd=hd, kv=kv: h.matmul(
                        pS[:, 0:nk], qT[:, hd, b * 128:(b + 1) * 128], kT[:, kv, koff:koff + nk], start=True, stop=True),
                        reads=[RqT, RkT], writes=[RpS])
                    P.op("dve", lambda h, pS=pS, L_=L_, ns_=ns_: h.scalar_tensor_tensor(
                        L_[:, 0:nk], self.dm[:, dmo:dmo + nk], ns_, pS[:, 0:nk], op0=ALU.mult, op1=ALU.add),
                        reads=[RpS, self.Rdm, self.Rnslp], writes=[RL[hd]])
                    P.op("dve", lambda h, L_=L_, sm_=sm_: h.tensor_reduce(sm_[:, 0:1], L_[:, 0:nk], AX.X, ALU.max),
                         reads=[RL[hd]], writes=[Rsm[hd]])
                    P.op("dve", lambda h, sm_=sm_, sk=sk: h.tensor_scalar(sm_[:, 1:2], sm_[:, 0:1], sk, -1.0, op0=ALU.max, op1=ALU.mult),
                         reads=[Rsm[hd], self.Rsinkb], writes=[Rsm[hd]])
                for hd in range(HA):
                    L_, sm_ = L[hd], sm[hd]
                    sk = self.sinkb[:, l * HA + hd:l * HA + hd + 1]
                    P.op("act", lambda h, L_=L_, sm_=sm_: h.activation(L_[:, 0:nk], L_[:, 0:nk], AF.Exp, bias=sm_[:, 1:2], scale=1.0,
                                                                       accum_out=sm_[:, 2:3]),
                         reads=[RL[hd], Rsm[hd]], writes=[RL[hd], Rsm[hd]])
                    P.op("act", lambda h, sm_=sm_, sk=sk: h.activation(sm_[:, 3:4], sk, AF.Exp, bias=sm_[:, 1:2], scale=1.0),
                         reads=[Rsm[hd], self.Rsinkb], writes=[Rsm[hd]])
                for hd in range(HA):
                    L_, Pn_, sm_ = L[hd], Pn[hd], sm[hd]
                    P.op("dve", lambda h, sm_=sm_: h.tensor_tensor(sm_[:, 4:5], sm_[:, 2:3], sm_[:, 3:4], ALU.add), reads=[Rsm[hd]], writes=[Rsm[hd]])
                    P.op("dve", lambda h, sm_=sm_: h.reciprocal(sm_[:, 5:6], sm_[:, 4:5]), reads=[Rsm[hd]], writes=[Rsm[hd]])
                    P.op("dve", lambda h, L_=L_, Pn_=Pn_, sm_=sm_: h.tensor_scalar(Pn_[:, 0:nk], L_[:, 0:nk], sm_[:, 5:6], None, op0=ALU.mult),
                         reads=[RL[hd], Rsm[hd]], writes=[RPn[hd]])
                for hd in range(HA):
                    Pn_, PT_ = Pn[hd], PT[hd]
                    fns = [(lambda h, kb=kb, Pn_=Pn_: h.transpose(self.tb[:, kb * 128:(kb + 1) * 128], Pn_[:, kb * 128:(kb + 1) * 128], self.identb[:]))
                           for kb in range(nkb)]
                    P.group("pe", fns, reads=[RPn[hd], self.Ridb], writes=[self.Rtb])
                    P.op("act", lambda h, PT_=PT_: h.copy(PT_[:, 0:nkb, :], self.tb[:, 0:nkb * 128].rearrange("p (a c) -> p a c", a=nkb)),
                         reads=[self.Rtb], writes=[RPT[hd]])
                for hd in range(HA):
                    kv = hd // 3
                    PT_ = PT[hd]
                    pO, RpO = self.pb[hd % 4], self.Rpb[hd % 4]
                    fns = [(lambda h, kb=kb, pO=pO, PT_=PT_, kv=kv: h.matmul(
                        pO[:, 0:128], vt[:, vb0 + kb, kv * 128:(kv + 1) * 128], PT_[:, kb, :], start=(kb == 0), stop=(kb == nkb - 1)))
                        for kb in range(nkb)]
                    P.group("pe", fns, reads=[Rvt, RPT[hd]], writes=[RpO])
                    P.op("dve", lambda h, pO=pO, hd=hd: h.tensor_tensor(self.yT[:, hd, b * 128:(b + 1) * 128], pO[:, 0:128],
                                                                         zT[:, hd, b * 128:(b + 1) * 128], ALU.mult),
                         reads=[RpO, RzT], writes=[self.RyT[0]])
            for b in range(NBLK):
                attn_block(b)
            self.barrier()

    def phase_M(self):
        P, l, t = self.P, self.l, self.t
        NH = HM
        QW = NH * 128
        VW = NH * 256
        with contextlib.ExitStack() as ph:
            qb = self.sb(ph, [128, NBLK, QW], BF16, "qb")
            kb_ = self.sb(ph, [128, NBLK, QW], BF16, "kb")
            va = self.sb(ph, [128, NBLK, VW], BF16, "va")
            G = self.sb(ph, [128, NBLK, VW], BF16, "G")
            gts = self.sb(ph, [128, NBLK, 2 * NH], F32, "gts")
            tmp = [self.sb(ph, [128, 256], F32, "mtmp") for _ in range(2)]
            Rqb, Rkb, Rva, RG, Rgts = Region("qb"), Region("kb"), Region("va"), Region("G"), Region("gts")
            Rtmp = [Region("mt0"), Region("mt1")]
            groups = list(range(G_MQK, G_CU)) + [G_GATE]
            ti = 0
            for g, wb, Rw in self.wstream("in", groups):
                ncols = 2 * NH if g == G_GATE else 256
                for b in range(NBLK):
                    acc, Racc = self.mm_tok(wb, Rw, b * 128, 128, ncols)
                    if g == G_GATE:
                        P.op("dve", lambda h, acc=acc, b=b: h.tensor_tensor(gts[:, b, :], acc[:, 0:2 * NH], self.bifb[:, l * 2 * NH:(l + 1) * 2 * NH], ALU.add),
                             reads=[Racc, self.Rbifb], writes=[Rgts])
                    elif g < G_MV:
                        for j in range(2):
                            ch = (g - G_MQK) * 2 + j
                            if ch < NH:
                                P.op("act", lambda h, acc=acc, b=b, ch=ch, j=j: h.copy(qb[:, b, ch * 128:(ch + 1) * 128], acc[:, j * 128:(j + 1) * 128]),
                                     reads=[Racc], writes=[Rqb])
                            else:
                                P.op("act", lambda h, acc=acc, b=b, ch=ch, j=j: h.activation(kb_[:, b, (ch - NH) * 128:(ch - NH + 1) * 128],
                                                                                              acc[:, j * 128:(j + 1) * 128], AF.Copy, scale=QSCALE),
                                     reads=[Racc], writes=[Rkb])
                    elif g < G_MO:
                        c0 = (g - G_MV) * 256
                        P.op("dve", lambda h, acc=acc, b=b, c0=c0: h.tensor_copy(va[:, b, c0:c0 + 256], acc[:, 0:256]), reads=[Racc], writes=[Rva])
                    elif g < G_MZ:
                        c0 = (g - G_MO) * 256
                        P.op("act", lambda h, acc=acc, b=b, c0=c0: h.activation(G[:, b, c0:c0 + 256], acc[:, 0:256], AF.Sigmoid), reads=[Racc], writes=[RG])
                    else:
                        c0 = (g - G_MZ) * 256
                        tm, Rtm = tmp[ti % 2], Rtmp[ti % 2]
                        ti += 1
                        P.op("act", lambda h, acc=acc, tm=tm: h.activation(tm[:], acc[:, 0:256], AF.Silu), reads=[Racc], writes=[Rtm])
                        P.op("dve", lambda h, tm=tm, b=b, c0=c0: h.tensor_tensor(G[:, b, c0:c0 + 256], G[:, b, c0:c0 + 256], tm[:], ALU.mult),
                             reads=[Rtm, RG], writes=[RG])
            sm = self.sb(ph, [128, 128], F32, "msm")
            Bd = self.sb(ph, [128, NH, 128], F32, "Bd")
            Dm = self.sb(ph, [128, NH, 128], F32, "Dm")
            w = self.sb(ph, [128, QW], BF16, "w")
            wT = self.sb(ph, [128, QW], BF16, "wT")
            qTs = self.sb(ph, [128, QW], BF16, "qTs")
            kTs = self.sb(ph, [128, QW], BF16, "kTs")
            qs = self.sb(ph, [128, NH, 128], BF16, "qs")
            qsT = self.sb(ph, [128, NH, 2, 128], BF16, "qsT")
            ksc = [self.sb(ph, [128, NH, 128], BF16, "ksc") for _ in range(2)]
            mo = self.sb(ph, [128, VW], BF16, "mo")
            junk = self.sb(ph, [128, 256], BF16, "mjunk")
            Rsm, RBd, RDm, Rw_, RwT, RqTs, RkTs, Rqs, RqsT, Rmo, Rjunk = (Region(n) for n in
                ("msm", "Bd", "Dm", "w", "wT", "qTs", "kTs", "qs", "qsT", "mo", "mjunk"))
            Rksc = [Region("ksc0"), Region("ksc1")]
            P.op("dve", lambda h: h.memset(qsT[:], 0.0), writes=[RqsT])
            psm, Rpsm = self.pb[0], self.Rpb[0]
            pB, RpB = self.pb[1], self.Rpb[1]
            pS, RpS = self.pb[2], self.Rpb[2]
            pC = [self.pb[1], self.pb[2]]
            RpC = [self.Rpb[1], self.Rpb[2]]
            pN = [self.pb[3], self.pb[4]]
            RpN = [self.Rpb[3], self.Rpb[4]]
            c_e1, c_sp, c_an, c_al0, c_al1, c_bv = 0, 6, 12, 20, 28, 36
            c_mxB, c_rmD, c_m1, c_m2, c_msel, c_mns, c_als = 42, 54, 60, 66, 72, 78, 84
            c_int, c_mt, c_wi, c_emt, c_ws, c_wsc0, c_wsc1 = 90, 96, 102, 108, 114, 120, 0
            sm2 = self.sb(ph, [128, 64], F32, "msm2")
            Rsm2 = Region("msm2")
            d_wc0, d_wc1, d_den, d_dn, d_t, d_rd, d_ssq, d_f, d_dnn = 0, 6, 12, 18, 24, 30, 36, 42, 48
            S_ = lambda c, n=NH: sm[:, c:c + n]
            S2 = lambda c, n=NH: sm2[:, c:c + n]
            bc3 = lambda ap: ap.unsqueeze(2).to_broadcast([128, NH, 128])

            def do_block(b):
                ip = gts[:, b, 0:NH]
                fp = gts[:, b, NH:2 * NH]
                P.op("act", lambda h: h.activation(S_(c_e1), fp, AF.Exp, scale=-1.0), reads=[Rgts], writes=[Rsm])
                P.op("act", lambda h: h.activation(S_(c_sp), S_(c_e1), AF.Ln, bias=1.0), reads=[Rsm], writes=[Rsm])
                fns = [lambda h: h.matmul(psm[:, 0:NH], self.tri2[:], S_(c_sp), start=True, stop=True),
                       lambda h: h.matmul(psm[:, 8:8 + NH], self.onesc[:, 0:128], S_(c_sp), start=True, stop=True),
                       lambda h: h.matmul(psm[:, 16:16 + NH], self.onesc[:, 128:256], S_(c_sp), start=True, stop=True)]
                P.group("pe", fns, reads=[Rsm, self.Rtri2, self.Ronesc], writes=[Rpsm])
                for (dst, srcc) in ((c_an, 0), (c_al0, 8), (c_al1, 16)):
                    P.op("dve", lambda h, dst=dst, srcc=srcc: h.tensor_copy(S_(dst), psm[:, srcc:srcc + NH]), reads=[Rpsm], writes=[Rsm])
                P.op("dve", lambda h: h.tensor_tensor(S_(c_bv), ip, S_(c_an), ALU.add), reads=[Rgts, Rsm], writes=[Rsm])
                P.op("dve", lambda h: h.tensor_tensor(Bd[:], self.identf[:].unsqueeze(1).to_broadcast([128, NH, 128]), bc3(S_(c_bv)), ALU.mult),
                     reads=[Rsm, self.Ridf], writes=[RBd])
                P.op("pe", lambda h: h.matmul(pB[:, 0:QW], self.onesf[:], Bd[:].rearrange("p a c -> p (a c)"), start=True, stop=True),
                     reads=[RBd, self.Ronesf], writes=[RpB])
                P.op("dve", lambda h: h.tensor_reduce(sm[:, c_mxB:c_mxB + 2 * NH].rearrange("p (a c) -> p a c", a=NH),
                                                      pB[:, 0:QW].rearrange("p (a c s) -> p a c s", a=NH, c=2), AX.X, ALU.max),
                     reads=[RpB], writes=[Rsm])
                P.op("dve", lambda h: h.tensor_tensor(Dm[:].rearrange("p a c -> p (a c)"), pB[:, 0:QW], self.mask6[:, 0:QW], ALU.add),
                     reads=[RpB, self.Rmask6], writes=[RDm])
                P.op("dve", lambda h: h.tensor_tensor(Dm[:], Dm[:], bc3(S_(c_an)), ALU.subtract), reads=[RDm, Rsm], writes=[RDm])
                P.op("dve", lambda h: h.tensor_reduce(S_(c_rmD), Dm[:], AX.X, ALU.max), reads=[RDm], writes=[Rsm])
                mxB = sm[:, c_mxB:c_mxB + 2 * NH].rearrange("p (a c) -> p a c", c=2)
                m0 = self.mst[:, 0:NH]
                P.op("dve", lambda h: h.tensor_tensor(S_(c_m1), m0, mxB[:, :, 0], ALU.max), reads=[Rsm, self.Rm], writes=[Rsm])
                P.op("dve", lambda h: h.tensor_tensor(S_(c_m1), S_(c_m1), S_(c_al0), ALU.subtract), reads=[Rsm], writes=[Rsm])
                P.op("dve", lambda h: h.tensor_tensor(S_(c_m2), S_(c_m1), mxB[:, :, 1], ALU.max), reads=[Rsm], writes=[Rsm])
                P.op("dve", lambda h: h.tensor_tensor(S_(c_m2), S_(c_m2), S_(c_al1), ALU.subtract), reads=[Rsm], writes=[Rsm])
                P.op("dve", lambda h: h.tensor_tensor(S2(d_wc0), m0, S_(c_al0), ALU.subtract), reads=[Rsm, self.Rm], writes=[Rsm2])
                P.op("dve", lambda h: h.tensor_tensor(S2(d_wc0), S2(d_wc0), S_(c_m1), ALU.subtract), reads=[Rsm, Rsm2], writes=[Rsm2])
                P.op("dve", lambda h: h.tensor_tensor(S2(d_wc1), S_(c_m1), S_(c_al1), ALU.subtract), reads=[Rsm, Rsm2], writes=[Rsm2])
                P.op("dve", lambda h: h.tensor_tensor(S2(d_wc1), S2(d_wc1), S_(c_m2), ALU.subtract), reads=[Rsm, Rsm2], writes=[Rsm2])
                P.op("act", lambda h: h.activation(S2(d_wc0), S2(d_wc0), AF.Exp), reads=[Rsm2], writes=[Rsm2])
                P.op("act", lambda h: h.activation(S2(d_wc1), S2(d_wc1), AF.Exp), reads=[Rsm2], writes=[Rsm2])
                P.op("dve", lambda h: h.tensor_copy(sm[0:64, c_msel:c_msel + NH], self.mst[0:64, 0:NH]), reads=[self.Rm, Rsm], writes=[Rsm])
                P.op("dve", lambda h: h.tensor_copy(sm[64:128, c_msel:c_msel + NH], sm[64:128, c_m1:c_m1 + NH]), reads=[Rsm], writes=[Rsm])
                P.op("dve", lambda h: h.tensor_copy(sm[0:64, c_mns:c_mns + NH], sm[0:64, c_m1:c_m1 + NH]), reads=[Rsm], writes=[Rsm])
                P.op("dve", lambda h: h.tensor_copy(sm[64:128, c_mns:c_mns + NH], sm[64:128, c_m2:c_m2 + NH]), reads=[Rsm], writes=[Rsm])
                P.op("dve", lambda h: h.tensor_copy(sm[0:64, c_als:c_als + NH], sm[0:64, c_al0:c_al0 + NH]), reads=[Rsm], writes=[Rsm])
                P.op("dve", lambda h: h.tensor_copy(sm[64:128, c_als:c_als + NH], sm[64:128, c_al1:c_al1 + NH]), reads=[Rsm], writes=[Rsm])
                P.op("dve", lambda h: h.tensor_copy(self.mst[:, 0:NH], S_(c_m2)), reads=[Rsm], writes=[self.Rm])
                P.op("dve", lambda h: h.tensor_tensor(S_(c_int), S_(c_msel), S_(c_an), ALU.subtract), reads=[Rsm], writes=[Rsm])
                P.op("dve", lambda h: h.tensor_tensor(S_(c_mt), S_(c_int), S_(c_rmD), ALU.max), reads=[Rsm], writes=[Rsm])
                P.op("dve", lambda h: h.tensor_tensor(S_(c_wi), S_(c_int), S_(c_mt), ALU.subtract), reads=[Rsm], writes=[Rsm])
                P.op("act", lambda h: h.activation(S_(c_wi), S_(c_wi), AF.Exp), reads=[Rsm], writes=[Rsm])
                P.op("act", lambda h: h.activation(S_(c_emt), S_(c_mt), AF.Exp, scale=-1.0), reads=[Rsm], writes=[Rsm])
                P.op("dve", lambda h: h.tensor_tensor(S_(c_ws), S_(c_bv), S_(c_als), ALU.subtract), reads=[Rsm], writes=[Rsm])
                P.op("dve", lambda h: h.tensor_tensor(S_(c_ws), S_(c_ws), S_(c_mns), ALU.subtract), reads=[Rsm], writes=[Rsm])
                P.op("act", lambda h: h.activation(S_(c_ws), S_(c_ws), AF.Exp), reads=[Rsm], writes=[Rsm])
                P.op("dve", lambda h: h.tensor_scalar(S_(c_wsc0), S_(c_ws), self.cmask[:, 0:1], None, op0=ALU.mult), reads=[Rsm, self.Rcmask], writes=[Rsm])
                P.op("dve", lambda h: h.tensor_scalar(S_(c_wsc1), S_(c_ws), self.cmask[:, 1:2], None, op0=ALU.mult), reads=[Rsm, self.Rcmask], writes=[Rsm])
                P.op("dve", lambda h: h.tensor_tensor(Dm[:], Dm[:], bc3(S_(c_mt)), ALU.subtract), reads=[RDm, Rsm], writes=[RDm])
                P.op("act", lambda h: h.activation(Dm[:], Dm[:], AF.Exp), reads=[RDm], writes=[RDm])
                for (srcb, Rsrcb, dstT, RdstT) in ((qb, Rqb, qTs, RqTs), (kb_, Rkb, kTs, RkTs)):
                    fns = [(lambda h, hh=hh, srcb=srcb: h.transpose(self.tb[:, hh * 128:(hh + 1) * 128], srcb[:, b, hh * 128:(hh + 1) * 128], self.identb[:]))
                           for hh in range(NH)]
                    P.group("pe", fns, reads=[Rsrcb, self.Ridb], writes=[self.Rtb])
                    P.op("act", lambda h, dstT=dstT: h.copy(dstT[:], self.tb[:, 0:QW]), reads=[self.Rtb], writes=[RdstT])
                fns = [(lambda h, hh=hh: h.matmul(pS[:, hh * 128:(hh + 1) * 128], qTs[:, hh * 128:(hh + 1) * 128],
                                                  kTs[:, hh * 128:(hh + 1) * 128], start=True, stop=True)) for hh in range(NH)]
                P.group("pe", fns, reads=[RqTs, RkTs], writes=[RpS])
                P.op("dve", lambda h: h.tensor_tensor(w[:], Dm[:].rearrange("p a c -> p (a c)"), pS[:, 0:QW], ALU.mult), reads=[RDm, RpS], writes=[Rw_])
                fns = [(lambda h, hh=hh: h.transpose(self.tb[:, hh * 128:(hh + 1) * 128], w[:, hh * 128:(hh + 1) * 128], self.identb[:])) for hh in range(NH)]
                P.group("pe", fns, reads=[Rw_, self.Ridb], writes=[self.Rtb])
                P.op("act", lambda h: h.copy(wT[:], self.tb[:, 0:QW]), reads=[self.Rtb], writes=[RwT])
                P.op("dve", lambda h: h.tensor_tensor(qs[:], qb[:, b, :].rearrange("p (a c) -> p a c", a=NH), bc3(S_(c_wi)), ALU.mult),
                     reads=[Rqb, Rsm], writes=[Rqs])
                fns = [(lambda h, hh=hh: h.transpose(self.tb[:, hh * 128:(hh + 1) * 128], qs[:, hh, :], self.identb[:])) for hh in range(NH)]
                P.group("pe", fns, reads=[Rqs, self.Ridb], writes=[self.Rtb])
                tbv = self.tb[:, 0:QW].rearrange("p (a c) -> p a c", a=NH)
                P.op("act", lambda h: h.copy(qsT[:, :, 0, 0:64], tbv[:, :, 0:64]), reads=[self.Rtb], writes=[RqsT])
                P.op("act", lambda h: h.copy(qsT[:, :, 1, 64:128], tbv[:, :, 64:128]), reads=[self.Rtb], writes=[RqsT])
                kb3 = kb_[:, b, :].rearrange("p (a c) -> p a c", a=NH)
                P.op("dve", lambda h: h.tensor_tensor(ksc[0][:], kb3, bc3(S_(c_wsc0)), ALU.mult), reads=[Rkb, Rsm], writes=[Rksc[0]])
                P.op("dve", lambda h: h.tensor_tensor(ksc[1][:], kb3, bc3(S_(c_wsc1)), ALU.mult), reads=[Rkb, Rsm], writes=[Rksc[1]])
                va3 = va[:, b, :].rearrange("p (a c) -> p a c", a=NH)

                def state_update(c, wc_col, Cb_dst, nb_dst, RCb_dst, Rnb_dst):
                    fns = []
                    for hh in range(NH):
                        fns.append(lambda h, hh=hh: h.matmul(pC[hh // 2][:, (hh % 2) * 256:(hh % 2) * 256 + 256], ksc[c][:, hh, :], va3[:, hh, :],
                                                             start=True, stop=True))
                    P.group("pe", fns, reads=[Rksc[c], Rva], writes=RpC)
                    fns = [(lambda h, hh=hh: h.matmul(psm[:, 32 + hh:33 + hh], ksc[c][:, hh, :], self.onesb[:, 0:1], start=True, stop=True)) for hh in range(NH)]
                    P.group("pe", fns, reads=[Rksc[c], self.Ronesb], writes=[Rpsm])
                    for hh in range(NH):
                        P.op("dve", lambda h, hh=hh: h.scalar_tensor_tensor(self.Cst[:, hh, :], self.Cst[:, hh, :], sm2[:, wc_col + hh:wc_col + hh + 1],
                                                                            pC[hh // 2][:, (hh % 2) * 256:(hh % 2) * 256 + 256], op0=ALU.mult, op1=ALU.add),
                             reads=[Rsm2, RpC[hh // 2], self.RC], writes=[self.RC])
                    P.op("dve", lambda h: h.tensor_tensor(S2(d_dnn), self.nst[:, 0:NH], S2(wc_col), ALU.mult), reads=[self.Rn, Rsm2], writes=[Rsm2])
                    P.op("dve", lambda h: h.tensor_tensor(self.nst[:, 0:NH], S2(d_dnn), psm[:, 32:32 + NH], ALU.add), reads=[Rsm2, Rpsm], writes=[self.Rn])
                    P.op("act", lambda h: h.copy(Cb_dst[:], self.Cst[:]), reads=[self.RC], writes=[RCb_dst])
                    P.op("act", lambda h: h.copy(nb_dst[:, 0:NH], self.nst[:, 0:NH]), reads=[self.Rn], writes=[Rnb_dst])

                state_update(0, d_wc0, self.Cb[1], self.nb[1], self.RCb[1], self.Rnb[1])
                for hh in range(NH):
                    o_ = pN[hh // 2][:, (hh % 2) * 256:(hh % 2) * 256 + 256]
                    fns = [lambda h, hh=hh, o_=o_: h.matmul(o_, wT[:, hh * 128:(hh + 1) * 128], va3[:, hh, :], start=True, stop=False),
                           lambda h, hh=hh, o_=o_: h.matmul(o_, qsT[:, hh, 0, :], self.Cb[0][:, hh, :], start=False, stop=False),
                           lambda h, hh=hh, o_=o_: h.matmul(o_, qsT[:, hh, 1, :], self.Cb[1][:, hh, :], start=False, stop=True)]
                    P.group("pe", fns, reads=[RwT, Rva, RqsT, self.RCb[0], self.RCb[1]], writes=[RpN[hh // 2]])
                for hh in range(NH):
                    o_ = psm[:, 40 + hh:41 + hh]
                    fns = [lambda h, hh=hh, o_=o_: h.matmul(o_, wT[:, hh * 128:(hh + 1) * 128], self.onesb[:, 0:1], start=True, stop=False),
                           lambda h, hh=hh, o_=o_: h.matmul(o_, qsT[:, hh, 0, :], self.nb[0][:, hh:hh + 1], start=False, stop=False),
                           lambda h, hh=hh, o_=o_: h.matmul(o_, qsT[:, hh, 1, :], self.nb[1][:, hh:hh + 1], start=False, stop=True)]
                    P.group("pe", fns, reads=[RwT, self.Ronesb, RqsT, self.Rnb[0], self.Rnb[1]], writes=[Rpsm])
                P.op("dve", lambda h: h.tensor_copy(S2(d_den), psm[:, 40:40 + NH]), reads=[Rpsm], writes=[Rsm2])
                P.op("dve", lambda h: h.scalar_tensor_tensor(S2(d_t), S2(d_den), -1.0, S2(d_den), op0=ALU.mult, op1=ALU.max), reads=[Rsm2], writes=[Rsm2])
                P.op("dve", lambda h: h.tensor_tensor(S2(d_t), S2(d_t), S_(c_emt), ALU.max), reads=[Rsm2, Rsm], writes=[Rsm2])
                P.op("dve", lambda h: h.reciprocal(S2(d_rd), S2(d_t)), reads=[Rsm2], writes=[Rsm2])
                for hh in range(NH):
                    P.op("act", lambda h, hh=hh: h.activation(junk[:], pN[hh // 2][:, (hh % 2) * 256:(hh % 2) * 256 + 256], AF.Square,
                                                              accum_out=sm2[:, d_ssq + hh:d_ssq + hh + 1]),
                         reads=[RpN[hh // 2], Rsm2], writes=[Rjunk, Rsm2])
                P.op("dve", lambda h: h.tensor_tensor(S2(d_t), S2(d_rd), S2(d_rd), ALU.mult), reads=[Rsm2], writes=[Rsm2])
                P.op("dve", lambda h: h.tensor_tensor(S2(d_t), S2(d_t), S2(d_ssq), ALU.mult), reads=[Rsm2], writes=[Rsm2])
                P.op("act", lambda h: h.activation(S2(d_t), S2(d_t), AF.Sqrt, bias=EPS, scale=1.0 / 256.0), reads=[Rsm2], writes=[Rsm2])
                P.op("dve", lambda h: h.reciprocal(S2(d_f), S2(d_t)), reads=[Rsm2], writes=[Rsm2])
                P.op("dve", lambda h: h.tensor_tensor(S2(d_f), S2(d_f), S2(d_rd), ALU.mult), reads=[Rsm2], writes=[Rsm2])
                for hh in range(NH):
                    P.op("dve", lambda h, hh=hh: h.scalar_tensor_tensor(mo[:, hh * 256:(hh + 1) * 256], pN[hh // 2][:, (hh % 2) * 256:(hh % 2) * 256 + 256],
                                                                        sm2[:, d_f + hh:d_f + hh + 1], G[:, b, hh * 256:(hh + 1) * 256],
                                                                        op0=ALU.mult, op1=ALU.mult),
                         reads=[RpN[hh // 2], Rsm2, RG], writes=[Rmo])
                fns = [(lambda h, j=j: h.transpose(self.tb[:, j * 128:(j + 1) * 128], mo[:, j * 128:(j + 1) * 128], self.identb[:])) for j in range(6)]
                P.group("pe", fns, reads=[Rmo, self.Ridb], writes=[self.Rtb])
                P.op("dve", lambda h: h.tensor_tensor(
                    self.yT[:, 6:12, b * 128:(b + 1) * 128],
                    self.tb[:, 0:768].rearrange("p (a c) -> p a c", a=6),
                    self.mhgT[:, 0:6].unsqueeze(2).to_broadcast([128, 6, 128]), ALU.mult),
                    reads=[self.Rtb, self.RmhgT], writes=[self.RyT[1]])
                state_update(1, d_wc1, self.Cb[0], self.nb[0], self.RCb[0], self.Rnb[0])
            for b in range(NBLK):
                do_block(b)
            self.barrier()

    def phase_C(self):
        P, l, t = self.P, self.l, self.t
        last_tile = (t == NTILE - 1)
        with contextlib.ExitStack() as ph:
            uT = self.sb(ph, [128, GC, TT], BF16, "uT")
            vvb = self.sb(ph, [128, NBLK, 1024], BF16, "vvb")
            gv = [self.sb(ph, [128, 1024], F32, "gv") for _ in range(NBLK)]
            tmp = [self.sb(ph, [128, TT], F32, "ctmp") for _ in range(2)]
            junk = self.sb(ph, [128, 1024], BF16, "cjunk")
            sm = self.sb(ph, [128, 16], F32, "csm")
            t1 = self.sb(ph, [128, GC, 128], F32, "ct1")
            RuT, Rvvb, Rjunk, Rsm, Rt1 = Region("uT"), Region("vvb"), Region("cjunk"), Region("csm"), Region("ct1")
            Rgv = [Region("gv%d" % i) for i in range(NBLK)]
            Rtmp = [Region("ct0"), Region("ct1")]
            ti = 0
            for g, wb, Rw in self.wstream("in", list(range(G_CU, G_GATE))):
                if g < G_CV:
                    for j in range(2):
                        acc, Racc = self.mm_feat(wb, Rw, j, TT)
                        gi = (g - G_CU) * 2 + j
                        P.op("act", lambda h, acc=acc, gi=gi: h.activation(uT[:, gi, :], acc[:, 0:TT], AF.Gelu), reads=[Racc], writes=[RuT])
                elif g < G_CZ:
                    c0 = (g - G_CV) * 256
                    for b in range(NBLK):
                        acc, Racc = self.mm_tok(wb, Rw, b * 128, 128, 256)
                        P.op("act", lambda h, acc=acc, b=b, c0=c0: h.activation(gv[b][:, c0:c0 + 256], acc[:, 0:256], AF.Gelu), reads=[Racc], writes=[Rgv[b]])
                else:
                    for j in range(2):
                        acc, Racc = self.mm_feat(wb, Rw, j, TT)
                        gi = (g - G_CZ) * 2 + j
                        tm, Rtm = tmp[ti % 2], Rtmp[ti % 2]
                        ti += 1
                        P.op("act", lambda h, acc=acc, tm=tm: h.activation(tm[:], acc[:, 0:TT], AF.Silu), reads=[Racc], writes=[Rtm])
                        P.op("dve", lambda h, tm=tm, gi=gi: h.tensor_tensor(uT[:, gi, :], uT[:, gi, :], tm[:], ALU.mult), reads=[Rtm, RuT], writes=[RuT])

            def do_blockc(b):
                g_ = gv[b]
                self.layernorm(g_, Rgv[b], 128, junk, Rjunk, sm, Rsm)
                P.op("act", lambda h: h.copy(vvb[:, b, :], g_[:]), reads=[Rgv[b]], writes=[Rvvb])
                if last_tile and b == NBLK - 1:
                    P.dma("sp", self.O["cvp"][l], g_[:], reads=[Rgv[b]], writes=[self.R["cvp"]])
                pS_, RpS_ = self.pb[4 + (b % 2)], self.Rpb[4 + (b % 2)]
                fns = [(lambda h, gi=gi: h.matmul(pS_[:, gi * 128:(gi + 1) * 128], vvb[:, b, gi * 128:(gi + 1) * 128],
                                                  self.wmT[:, gi, :], start=True, stop=True)) for gi in range(GC)]
                P.group("pe", fns, reads=[Rvvb, self.RwmT], writes=[RpS_])
                P.op("dve", lambda h: h.tensor_tensor(t1[:].rearrange("p a c -> p (a c)"), pS_[:, 0:GC * 128], self.bsb[:, 0:GC * 128], ALU.add),
                     reads=[RpS_, self.Rbsb], writes=[Rt1])
                P.op("dve", lambda h: h.tensor_tensor(self.yT[:, 12:12 + GC, b * 128:(b + 1) * 128], t1[:], uT[:, :, b * 128:(b + 1) * 128], ALU.mult),
                     reads=[Rt1, RuT], writes=[self.RyT[2]])
            for b in range(NBLK):
                do_blockc(b)
            self.barrier()

    def layernorm(self, g_, Rg, np_, junk, Rjunk, sm, Rsm):
        P = self.P
        P.op("act", lambda h: h.activation(junk[0:np_, :], g_[0:np_, :], AF.Copy, accum_out=sm[0:np_, 0:1]), reads=[Rg], writes=[Rjunk, Rsm])
        P.op("act", lambda h: h.activation(junk[0:np_, :], g_[0:np_, :], AF.Square, accum_out=sm[0:np_, 1:2]), reads=[Rg], writes=[Rjunk, Rsm])
        P.op("dve", lambda h: h.tensor_scalar(sm[0:np_, 2:4], sm[0:np_, 0:2], 1.0 / 1024.0, None, op0=ALU.mult), reads=[Rsm], writes=[Rsm])
        P.op("dve", lambda h: h.tensor_tensor(sm[0:np_, 4:5], sm[0:np_, 2:3], sm[0:np_, 2:3], ALU.mult), reads=[Rsm], writes=[Rsm])
        P.op("dve", lambda h: h.tensor_tensor(sm[0:np_, 5:6], sm[0:np_, 3:4], sm[0:np_, 4:5], ALU.subtract), reads=[Rsm], writes=[Rsm])
        P.op("act", lambda h: h.activation(sm[0:np_, 6:7], sm[0:np_, 5:6], AF.Sqrt, bias=EPS, scale=1.0), reads=[Rsm], writes=[Rsm])
        P.op("dve", lambda h: h.reciprocal(sm[0:np_, 7:8], sm[0:np_, 6:7]), reads=[Rsm], writes=[Rsm])
        P.op("dve", lambda h: h.tensor_scalar(g_[0:np_, :], g_[0:np_, :], sm[0:np_, 2:3], sm[0:np_, 7:8], op0=ALU.subtract, op1=ALU.mult),
             reads=[Rg, Rsm], writes=[Rg])
        P.op("dve", lambda h: h.tensor_tensor(g_[0:np_, :], g_[0:np_, :], self.lngb[0:np_, :], ALU.mult), reads=[Rg, self.Rln], writes=[Rg])
        P.op("dve", lambda h: h.tensor_tensor(g_[0:np_, :], g_[0:np_, :], self.lnbb[0:np_, :], ALU.add), reads=[Rg, self.Rln], writes=[Rg])

    def phase_O(self, dst, Rdst, np_, nblk):
        P = self.P
        with contextlib.ExitStack() as ph:
            hs = [self.sb(ph, [128, nblk, 512], F32, "hs") for _ in range(2)]
            Rhs = [Region("hs0"), Region("hs1")]
            i = 0
            for g, wb, Rw in self.wstream("out", list(range(NG_OUT))):
                h_, Rh_ = hs[i % 2], Rhs[i % 2]
                i += 1
                for b in range(nblk):
                    acc, Racc = self.next_acc()
                    yT = self.yT
                    fns = [(lambda h, kc=kc, acc=acc, b=b, wb=wb: h.matmul(acc[0:np_, 0:512], yT[:, kc, b * np_:(b + 1) * np_], wb[:, kc, 0:512],
                                                                           start=(kc == 0), stop=(kc == KY - 1))) for kc in range(KY)]
                    P.group("pe", fns, reads=[Rw] + self.RyT, writes=[Racc])
                    if b % 2 == 0:
                        P.op("act", lambda h, acc=acc, h_=h_, b=b: h.copy(h_[0:np_, b, :], acc[0:np_, 0:512]), reads=[Racc], writes=[Rh_])
                    else:
                        P.op("dve", lambda h, acc=acc, h_=h_, b=b: h.tensor_copy(h_[0:np_, b, :], acc[0:np_, 0:512]), reads=[Racc], writes=[Rh_])
                P.dma("sp", dst[:, g * 512:(g + 1) * 512].rearrange("(b p) c -> p b c", p=np_), h_[0:np_, :, :], reads=[Rh_], writes=[Rdst])
            self.barrier()

    def phase_final(self, srcA, RsrcA, srcB, RsrcB, out, Rout, np_, nblk):
        P = self.P
        with contextlib.ExitStack() as ph:
            hb0 = self.sb(ph, [128, D], F32, "fhb")
            hb = [hb0, hb0]
            hb2 = self.sb(ph, [128, D], F32, "fhb2")
            fg = self.sb(ph, [128, D], F32, "fg")
            junk = hb2
            stt = [self.sb(ph, [128, 4], F32, "fst") for _ in range(2)]
            Rhb0 = Region("fhb0")
            Rhb = [Rhb0, Rhb0]
            Rhb2 = Region("fhb2")
            Rst = [Region("fst0"), Region("fst1")]
            Rfg, Rj = Region("fg"), Rhb2
            P.dma("sp", fg[:], self.I["fgain"].partition_broadcast(128), writes=[Rfg])
            for b in range(nblk):
                i = b % 2
                h_, s_ = hb[i], stt[i]
                rs = slice(b * np_, (b + 1) * np_)
                P.dma("sp", h_[0:np_, :], srcA[rs, :], reads=[RsrcA], writes=[Rhb[i]])
                P.dma("sp", hb2[0:np_, :], srcB[rs, :], reads=[RsrcB], writes=[Rhb2])
                P.op("dve", lambda h, h_=h_: h.tensor_tensor(h_[0:np_, :], h_[0:np_, :], hb2[0:np_, :], ALU.add), reads=[Rhb[i], Rhb2], writes=[Rhb[i]])
                P.op("act", lambda h, h_=h_, s_=s_: h.activation(junk[0:np_, :], h_[0:np_, :], AF.Square, accum_out=s_[0:np_, 0:1]),
                     reads=[Rhb[i]], writes=[Rj, Rst[i]])
                P.op("act", lambda h, s_=s_: h.activation(s_[0:np_, 1:2], s_[0:np_, 0:1], AF.Sqrt, bias=EPS, scale=1.0 / D), reads=[Rst[i]], writes=[Rst[i]])
                P.op("dve", lambda h, s_=s_: h.reciprocal(s_[0:np_, 2:3], s_[0:np_, 1:2]), reads=[Rst[i]], writes=[Rst[i]])
                P.op("dve", lambda h, h_=h_, s_=s_: h.scalar_tensor_tensor(h_[0:np_, :], h_[0:np_, :], s_[0:np_, 2:3], fg[0:np_, :], op0=ALU.mult, op1=ALU.mult),
                     reads=[Rhb[i], Rst[i], Rfg], writes=[Rhb[i]])
                P.dma("sp", out[rs, :], h_[0:np_, :], reads=[Rhb[i]], writes=[Rout])
            self.barrier()

    def sample_layer(self):
        l, I, S, O, P = self.l, self.I, self.S, self.O, self.P
        if l == 0:
            self.phase_norm(I["xs"], self.R["xin"], 1, NS)
        else:
            self.phase_norm(I["xs"], self.R["xin"], 1, NS, add=(S["reds0"], self.Rred[(0, "s")]), store=(S["hs1"], self.R["hs1"]))
        self.s_phase_A()
        self.s_phase_M()
        self.s_phase_C()
        rp = Region("pos%d" % l)
        rr = Region("reds%d" % l)
        self.Rred[(l, "s")] = rr
        self.phase_O(S["pos%d" % l], rp, NS, 1)
        P.coll(self.st, S["pos%d" % l], S["reds%d" % l], reads=[rp], writes=[rr])

    def s_proj(self, wb, Rw, ncols, evac):
        acc, Racc = self.mm_tok(wb, Rw, 0, NS, ncols)
        evac(acc, Racc)

    def s_to_yT(self, srcb, Rsrcb, kc0, n):
        P = self.P
        fns = [(lambda h, j=j: h.transpose(self.tb[:, j * NS:(j + 1) * NS], srcb[0:NS, j * 128:(j + 1) * 128], self.identb[0:NS, 0:NS])) for j in range(n)]
        P.group("pe", fns, reads=[Rsrcb, self.Ridb], writes=[self.Rtb])
        reg = self.RyT[0] if kc0 == 0 else (self.RyT[1] if kc0 == 6 else self.RyT[2])
        P.op("act", lambda h: h.copy(self.yT[:, kc0:kc0 + n, 0:NS], self.tb[:, 0:n * NS].rearrange("p (a c) -> p a c", a=n)), reads=[self.Rtb], writes=[reg])

    def s_phase_A(self):
        P, l, I, S, O = self.P, self.l, self.I, self.S, self.O
        NP = NS * KV
        with contextlib.ExitStack() as ph:
            qs_ = self.sb(ph, [NS, 768], F32, "sq")
            kn = self.sb(ph, [NS, 256], F32, "skn")
            vn = self.sb(ph, [NS, 256], F32, "svn")
            za = self.sb(ph, [NS, 768], F32, "sza")
            Rq, Rkn, Rvn, Rza = Region("sq"), Region("skn"), Region("svn"), Region("sza")
            for g, wb, Rw in self.wstream("in", list(range(G_AQ, G_MQK))):
                def evac(acc, Racc, g=g):
                    if g < G_AK:
                        c0 = (g - G_AQ) * 256
                        P.op("act", lambda h: h.activation(qs_[:, c0:c0 + 256], acc[0:NS, 0:256], AF.Copy, scale=QSCALE), reads=[Racc], writes=[Rq])
                    elif g < G_AV:
                        P.op("dve", lambda h: h.tensor_copy(kn[:], acc[0:NS, 0:256]), reads=[Racc], writes=[Rkn])
                    elif g < G_AZ:
                        P.op("dve", lambda h: h.tensor_copy(vn[:], acc[0:NS, 0:256]), reads=[Racc], writes=[Rvn])
                    else:
                        c0 = (g - G_AZ) * 256
                        P.op("act", lambda h: h.activation(za[:, c0:c0 + 256], acc[0:NS, 0:256], AF.Silu), reads=[Racc], writes=[Rza])
                self.s_proj(wb, Rw, 256, evac)
            Rb = self.R["bnc"]
            P.dma("sp", O["ks"][l], kn[:], reads=[Rkn], writes=[self.R["ks"]])
            P.dma("sp", O["vs"][l], vn[:], reads=[Rvn], writes=[self.R["vs"]])
            P.dma("sp", S["bq"], qs_[:], reads=[Rq], writes=[Rb])
            P.dma("sp", S["bk"], kn[:], reads=[Rkn], writes=[Rb])
            P.dma("sp", S["bv"], vn[:], reads=[Rvn], writes=[Rb])
            q4 = self.sb(ph, [NP, 3, 128], F32, "q4")
            k4 = self.sb(ph, [NP, 128], F32, "k4")
            v4 = self.sb(ph, [NP, 128], F32, "v4")
            R4 = Region("qkv4")
            P.dma("sp", q4[:].rearrange("p a b -> p (a b)"), S["bq"].rearrange("b (kv x) -> (b kv) x", kv=KV), reads=[Rb], writes=[R4])
            P.dma("sp", k4[:], S["bk"].rearrange("b (kv x) -> (b kv) x", kv=KV), reads=[Rb], writes=[R4])
            P.dma("sp", v4[:], S["bv"].rearrange("b (kv x) -> (b kv) x", kv=KV), reads=[Rb], writes=[R4])
            slp = self.sb(ph, [NP, 3], F32, "slp")
            sk4 = self.sb(ph, [NP, 3], F32, "sk4")
            dist = self.sb(ph, [NP, 129], F32, "dist")
            Rc = Region("sAc")
            P.dma("sp", slp[:], I["c_slp"], writes=[Rc])
            P.dma("sp", sk4[:], I["sk4"][l], writes=[Rc])
            P.dma("sp", dist[:], I["c_dist"].partition_broadcast(NP), writes=[Rc])
            lg = self.sb(ph, [NP, 3, 129], F32, "lg")
            ab = self.sb(ph, [NP, 3, 129], F32, "ab")
            Rlg, Rab = Region("lg"), Region("ab")
            P.op("dve", lambda h: h.tensor_tensor(ab[:], dist[:].unsqueeze(1).to_broadcast([NP, 3, 129]), slp[:].unsqueeze(2).to_broadcast([NP, 3, 129]), ALU.mult),
                 reads=[Rc], writes=[Rab])
            KC = 16
            kvb = self.sb(ph, [NP, KC, 128], F32, "kvb")
            tmp = self.sb(ph, [NP, KC * 128], F32, "stmp")
            Rkvb, Rtmp = Region("kvb"), Region("stmp")
            for c in range(128 // KC):
                P.dma("sp", kvb[:].rearrange("p a b -> p (a b)"), I["ck"][l][:, c * KC * 128:(c + 1) * KC * 128], writes=[Rkvb])
                for g3 in range(3):
                    P.op("dve", lambda h, g3=g3: h.tensor_tensor(tmp[:].rearrange("p (a b) -> p a b", a=KC), kvb[:], q4[:, g3, :].unsqueeze(1).to_broadcast([NP, KC, 128]), ALU.mult),
                         reads=[Rkvb, R4], writes=[Rtmp])
                    P.op("dve", lambda h, g3=g3, c=c: h.tensor_reduce(lg[:, g3, c * KC:(c + 1) * KC], tmp[:].rearrange("p (a b) -> p a b", a=KC), AX.X, ALU.add),
                         reads=[Rtmp], writes=[Rlg])
            P.op("dve", lambda h: h.tensor_tensor(tmp[:, 0:384].rearrange("p (a b) -> p a b", a=3), q4[:], k4[:].unsqueeze(1).to_broadcast([NP, 3, 128]), ALU.mult),
                 reads=[R4], writes=[Rtmp])
            P.op("dve", lambda h: h.tensor_reduce(lg[:, :, 128], tmp[:, 0:384].rearrange("p (a b) -> p a b", a=3), AX.X, ALU.add), reads=[Rtmp], writes=[Rlg])
            P.op("dve", lambda h: h.tensor_tensor(lg[:], lg[:], ab[:], ALU.subtract), reads=[Rlg, Rab], writes=[Rlg])
            sm = self.sb(ph, [NP, 24], F32, "sAsm")
            Rsm = Region("sAsm")
            P.op("dve", lambda h: h.tensor_reduce(sm[:, 0:3], lg[:], AX.X, ALU.max), reads=[Rlg], writes=[Rsm])
            P.op("dve", lambda h: h.tensor_tensor(sm[:, 0:3], sm[:, 0:3], sk4[:], ALU.max), reads=[Rsm, Rc], writes=[Rsm])
            P.op("dve", lambda h: h.tensor_tensor(lg[:], lg[:], sm[:, 0:3].unsqueeze(2).to_broadcast([NP, 3, 129]), ALU.subtract), reads=[Rlg, Rsm], writes=[Rlg])
            P.op("act", lambda h: h.activation(lg[:], lg[:], AF.Exp), reads=[Rlg], writes=[Rlg])
            P.op("dve", lambda h: h.tensor_reduce(sm[:, 3:6], lg[:], AX.X, ALU.add), reads=[Rlg], writes=[Rsm])
            P.op("dve", lambda h: h.tensor_tensor(sm[:, 6:9], sk4[:], sm[:, 0:3], ALU.subtract), reads=[Rsm, Rc], writes=[Rsm])
            P.op("act", lambda h: h.activation(sm[:, 6:9], sm[:, 6:9], AF.Exp), reads=[Rsm], writes=[Rsm])
            P.op("dve", lambda h: h.tensor_tensor(sm[:, 3:6], sm[:, 3:6], sm[:, 6:9], ALU.add), reads=[Rsm], writes=[Rsm])
            P.op("dve", lambda h: h.reciprocal(sm[:, 9:12], sm[:, 3:6]), reads=[Rsm], writes=[Rsm])
            P.op("dve", lambda h: h.tensor_tensor(lg[:], lg[:], sm[:, 9:12].unsqueeze(2).to_broadcast([NP, 3, 129]), ALU.mult), reads=[Rlg, Rsm], writes=[Rlg])
            o4 = self.sb(ph, [NP, 3, 128], F32, "o4")
            o4t = self.sb(ph, [NP, 3, 128], F32, "o4t")
            Ro4, Ro4t = Region("o4"), Region("o4t")
            for g3 in range(3):
                P.op("dve", lambda h, g3=g3: h.tensor_scalar(o4[:, g3, :], v4[:], lg[:, g3, 128:129], None, op0=ALU.mult), reads=[R4, Rlg], writes=[Ro4])
            for c in range(128 // KC):
                P.dma("sp", kvb[:].rearrange("p a b -> p (a b)"), I["cv"][l][:, c * KC * 128:(c + 1) * KC * 128], writes=[Rkvb])
                for g3 in range(3):
                    P.op("dve", lambda h, g3=g3, c=c: h.tensor_tensor(tmp[:].rearrange("p (d s) -> p d s", s=KC), kvb[:].rearrange("p s d -> p d s"),
                                                                      lg[:, g3, c * KC:(c + 1) * KC].unsqueeze(1).to_broadcast([NP, 128, KC]), ALU.mult),
                         reads=[Rkvb, Rlg], writes=[Rtmp])
                    P.op("dve", lambda h, g3=g3: h.tensor_reduce(o4t[:, g3, :], tmp[:].rearrange("p (d s) -> p d s", s=KC), AX.X, ALU.add), reads=[Rtmp], writes=[Ro4t])
                    P.op("dve", lambda h, g3=g3: h.tensor_tensor(o4[:, g3, :], o4[:, g3, :], o4t[:, g3, :], ALU.add), reads=[Ro4, Ro4t], writes=[Ro4])
            P.dma("sp", S["bo"].rearrange("b (kv x) -> (b kv) x", kv=KV), o4[:].rearrange("p a b -> p (a b)"), reads=[Ro4], writes=[Rb])
            ao = self.sb(ph, [NS, 768], F32, "ao")
            yb = self.sb(ph, [NS, 768], BF16, "syb")
            Rao, Ryb = Region("ao"), Region("syb")
            P.dma("sp", ao[:], S["bo"], reads=[Rb], writes=[Rao])
            P.op("dve", lambda h: h.tensor_tensor(yb[:], ao[:], za[:], ALU.mult), reads=[Rao, Rza], writes=[Ryb])
            self.s_to_yT(yb, Ryb, 0, 6)
            self.barrier()

    def s_phase_M(self):
        P, l, I, S, O = self.P, self.l, self.I, self.S, self.O
        NH = HM
        QW, VW = NH * 128, NH * 256
        with contextlib.ExitStack() as ph:
            q = self.sb(ph, [NS, QW], F32, "smq")
            k = self.sb(ph, [NS, QW], F32, "smk")
            v = self.sb(ph, [NS, VW], BF16, "smv")
            G = self.sb(ph, [NS, VW], F32, "smG")
            gts = self.sb(ph, [NS, 2 * NH], F32, "smg")
            tm = self.sb(ph, [NS, 256], F32, "smt")
            Rq, Rk, Rv, RG, Rg, Rtm = (Region(n) for n in ("smq", "smk", "smv", "smG", "smg", "smt"))
            for g, wb, Rw in self.wstream("in", list(range(G_MQK, G_CU)) + [G_GATE]):
                def evac(acc, Racc, g=g):
                    if g == G_GATE:
                        P.op("dve", lambda h: h.tensor_tensor(gts[:], acc[0:NS, 0:2 * NH], self.bifb[0:NS, l * 2 * NH:(l + 1) * 2 * NH], ALU.add), reads=[Racc, self.Rbifb], writes=[Rg])
                    elif g < G_MV:
                        for j in range(2):
                            ch = (g - G_MQK) * 2 + j
                            if ch < NH:
                                P.op("act", lambda h, ch=ch, j=j: h.copy(q[:, ch * 128:(ch + 1) * 128], acc[0:NS, j * 128:(j + 1) * 128]), reads=[Racc], writes=[Rq])
                            else:
                                P.op("act", lambda h, ch=ch, j=j: h.activation(k[:, (ch - NH) * 128:(ch - NH + 1) * 128], acc[0:NS, j * 128:(j + 1) * 128], AF.Copy, scale=QSCALE),
                                     reads=[Racc], writes=[Rk])
                    elif g < G_MO:
                        c0 = (g - G_MV) * 256
                        P.op("dve", lambda h: h.tensor_copy(v[:, c0:c0 + 256], acc[0:NS, 0:256]), reads=[Racc], writes=[Rv])
                    elif g < G_MZ:
                        c0 = (g - G_MO) * 256
                        P.op("act", lambda h: h.activation(G[:, c0:c0 + 256], acc[0:NS, 0:256], AF.Sigmoid), reads=[Racc], writes=[RG])
                    else:
                        c0 = (g - G_MZ) * 256
                        P.op("act", lambda h: h.activation(tm[:], acc[0:NS, 0:256], AF.Silu), reads=[Racc], writes=[Rtm])
                        P.op("dve", lambda h: h.tensor_tensor(G[:, c0:c0 + 256], G[:, c0:c0 + 256], tm[:], ALU.mult), reads=[Rtm, RG], writes=[RG])
                self.s_proj(wb, Rw, 2 * NH if g == G_GATE else 256, evac)
            sm = self.sb(ph, [NS, 96], F32, "smsm")
            Rsm = Region("smsm")
            n0 = self.sb(ph, [NS, QW], F32, "smn")
            Rn0 = Region("smn")
            P.dma("sp", n0[:], I["sn"][l], writes=[Rn0])
            P.dma("sp", sm[:, 0:NH], I["sm"][l], writes=[Rsm])
            with contextlib.ExitStack() as ph2:
                mhgb = self.sb(ph2, [NS, VW], F32, "mhgb")
                Rmh = Region("mhgb")
                P.dma("sp", mhgb[:], I["mhg"][l].partition_broadcast(NS), writes=[Rmh])
                P.op("dve", lambda h, mhgb=mhgb: h.tensor_tensor(G[:], G[:], mhgb[:], ALU.mult), reads=[RG, Rmh], writes=[RG])
                self.barrier()
            c_m0, c_sp, c_int, c_mt, c_wq, c_wi, c_qk, c_w, c_qn, c_den, c_t, c_rd, c_ssq, c_f, c_emt = [6 * i for i in range(15)]
            s_ = lambda c: sm[:, c:c + NH]
            ip, fp = gts[:, 0:NH], gts[:, NH:2 * NH]
            P.op("act", lambda h: h.activation(s_(c_sp), fp, AF.Exp, scale=-1.0), reads=[Rg], writes=[Rsm])
            P.op("act", lambda h: h.activation(s_(c_sp), s_(c_sp), AF.Ln, bias=1.0), reads=[Rsm], writes=[Rsm])
            P.op("dve", lambda h: h.tensor_tensor(s_(c_int), s_(c_m0), s_(c_sp), ALU.subtract), reads=[Rsm], writes=[Rsm])
            P.op("dve", lambda h: h.tensor_tensor(s_(c_mt), s_(c_int), ip, ALU.max), reads=[Rsm, Rg], writes=[Rsm])
            P.op("dve", lambda h: h.tensor_tensor(s_(c_wq), ip, s_(c_mt), ALU.subtract), reads=[Rsm, Rg], writes=[Rsm])
            P.op("dve", lambda h: h.tensor_tensor(s_(c_wi), s_(c_int), s_(c_mt), ALU.subtract), reads=[Rsm], writes=[Rsm])
            P.op("act", lambda h: h.activation(s_(c_wq), s_(c_wq), AF.Exp), reads=[Rsm], writes=[Rsm])
            P.op("act", lambda h: h.activation(s_(c_wi), s_(c_wi), AF.Exp), reads=[Rsm], writes=[Rsm])
            P.op("act", lambda h: h.activation(s_(c_emt), s_(c_mt), AF.Exp, scale=-1.0), reads=[Rsm], writes=[Rsm])
            P.dma("sp", O["ms"][l], s_(c_mt), reads=[Rsm], writes=[self.R["ms"]])
            big = self.sb(ph, [NS, VW], F32, "smbig")
            Rbig = Region("smbig")
            bc6 = lambda ap, n: ap.unsqueeze(2).to_broadcast([NS, NH, n])
            q3 = q[:].rearrange("p (a b) -> p a b", a=NH)
            k3 = k[:].rearrange("p (a b) -> p a b", a=NH)
            n3 = n0[:].rearrange("p (a b) -> p a b", a=NH)
            b3 = big[:, 0:QW].rearrange("p (a b) -> p a b", a=NH)
            P.op("dve", lambda h: h.tensor_tensor(b3, q3, k3, ALU.mult), reads=[Rq, Rk], writes=[Rbig])
            P.op("dve", lambda h: h.tensor_reduce(s_(c_qk), b3, AX.X, ALU.add), reads=[Rbig], writes=[Rsm])
            P.op("dve", lambda h: h.tensor_tensor(b3, q3, n3, ALU.mult), reads=[Rq, Rn0], writes=[Rbig])
            P.op("dve", lambda h: h.tensor_reduce(s_(c_qn), b3, AX.X, ALU.add), reads=[Rbig], writes=[Rsm])
            P.op("dve", lambda h: h.tensor_tensor(s_(c_w), s_(c_wq), s_(c_qk), ALU.mult), reads=[Rsm], writes=[Rsm])
            P.op("dve", lambda h: h.tensor_tensor(s_(c_den), s_(c_wi), s_(c_qn), ALU.mult), reads=[Rsm], writes=[Rsm])
            P.op("dve", lambda h: h.tensor_tensor(s_(c_den), s_(c_den), s_(c_w), ALU.add), reads=[Rsm], writes=[Rsm])
            P.op("dve", lambda h: h.scalar_tensor_tensor(s_(c_t), s_(c_den), -1.0, s_(c_den), op0=ALU.mult, op1=ALU.max), reads=[Rsm], writes=[Rsm])
            P.op("dve", lambda h: h.tensor_tensor(s_(c_t), s_(c_t), s_(c_emt), ALU.max), reads=[Rsm], writes=[Rsm])
            P.op("dve", lambda h: h.reciprocal(s_(c_rd), s_(c_t)), reads=[Rsm], writes=[Rsm])
            ksc = self.sb(ph, [NS, QW], F32, "smksc")
            Rksc = Region("smksc")
            ksc3 = ksc[:].rearrange("p (a b) -> p a b", a=NH)
            P.op("dve", lambda h: h.tensor_tensor(ksc3, k3, bc6(s_(c_wq), 128), ALU.mult), reads=[Rk, Rsm], writes=[Rksc])
            P.op("dve", lambda h: h.tensor_tensor(n3, n3, bc6(s_(c_wi), 128), ALU.mult), reads=[Rn0, Rsm], writes=[Rn0])
            P.op("dve", lambda h: h.tensor_tensor(n0[:], n0[:], ksc[:], ALU.add), reads=[Rn0, Rksc], writes=[Rn0])
            P.dma("sp", O["ns"][l], n0[:], reads=[Rn0], writes=[self.R["ns"]])
            qT = self.sb(ph, [128, NH, NS], F32, "smqT")
            RqT = Region("smqT")
            pq, Rpq = self.pb[0], self.Rpb[0]
            fns = [(lambda h, hh=hh: h.transpose(pq[:, hh * NS:(hh + 1) * NS], q[0:NS, hh * 128:(hh + 1) * 128], self.identf[0:NS, 0:NS])) for hh in range(NH)]
            P.group("pe", fns, reads=[Rq, self.Ridf], writes=[Rpq])
            P.op("act", lambda h: h.copy(qT[:].rearrange("p a b -> p (a b)"), pq[:, 0:NH * NS]), reads=[Rpq], writes=[RqT])
            i32 = self.sb(ph, [128, NS, NS], F32, "i32")
            Ri32 = Region("i32")
            P.dma("sp", i32[:].rearrange("p a b -> p (a b)"), I["c_i32"], writes=[Ri32])
            Wd = self.sb(ph, [NS, NS, NH], F32, "smWd")
            RWd = Region("smWd")
            P.op("dve", lambda h: h.tensor_tensor(Wd[:], self.identf[0:NS, 0:NS].unsqueeze(2).to_broadcast([NS, NS, NH]), s_(c_wi).unsqueeze(1).to_broadcast([NS, NS, NH]), ALU.mult),
                 reads=[Rsm, self.Ridf], writes=[RWd])
            pw, Rpw = self.pb[1], self.Rpb[1]
            P.op("pe", lambda h: h.matmul(pw[:, 0:NS * NH], self.onesf[0:NS, :], Wd[:].rearrange("p a b -> p (a b)"), start=True, stop=True), reads=[RWd, self.Ronesf], writes=[Rpw])
            wcb = self.sb(ph, [128, NS, NH], F32, "smwcb")
            Rwcb = Region("smwcb")
            P.op("act", lambda h: h.copy(wcb[:].rearrange("p a b -> p (a b)"), pw[:, 0:NS * NH]), reads=[Rpw], writes=[Rwcb])
            vb, Rvb = v, Rv
            Qm = self.sb(ph, [128, NS, NS], F32, "smQm")
            Km = self.sb(ph, [NS, NS, 128], BF16, "smKm")
            RQm, RKm = Region("smQm"), Region("smKm")
            CB = 8
            Cc = self.sb(ph, [128, CB, 256], F32, "smCc")
            RCc = Region("smCc")
            pR = [self.pb[4], self.pb[5]]
            RpR = [self.Rpb[4], self.Rpb[5]]
            pD = [self.pb[2], self.pb[3]]
            RpD = [self.Rpb[2], self.Rpb[3]]

            def do_head(hh):
                P.op("dve", lambda h: h.tensor_tensor(Qm[:], qT[:, hh, :].unsqueeze(2).to_broadcast([128, NS, NS]), i32[:], ALU.mult), reads=[RqT, Ri32], writes=[RQm])
                P.op("dve", lambda h: h.tensor_tensor(Km[:], self.identf[0:NS, 0:NS].unsqueeze(2).to_broadcast([NS, NS, 128]),
                                                      ksc[:, hh * 128:(hh + 1) * 128].unsqueeze(1).to_broadcast([NS, NS, 128]), ALU.mult),
                     reads=[Rksc, self.Ridf], writes=[RKm])
                oR = pR[hh // 2][0:NS, (hh % 2) * 256:(hh % 2) * 256 + 256]
                for cb in range(NS // CB):
                    P.dma("sp", Cc[:], I["sC"][l, cb * CB:(cb + 1) * CB, hh].rearrange("b d v -> d b v"), writes=[RCc])
                    for j in range(CB):
                        bp = cb * CB + j
                        P.op("pe", lambda h, j=j, bp=bp: h.matmul(oR, Qm[:, bp, :], Cc[:, j, :], start=(bp == 0), stop=(bp == NS - 1)),
                             reads=[RQm, RCc], writes=[RpR[hh // 2]])
                    for j in range(CB):
                        bp = cb * CB + j
                        pd, Rpd = pD[j % 2], RpD[j % 2]
                        P.op("pe", lambda h, j=j, bp=bp, pd=pd: h.matmul(pd[:, 0:256], Km[:, bp, :], vb[:, hh * 256:(hh + 1) * 256], start=True, stop=True),
                             reads=[RKm, Rvb], writes=[Rpd])
                        P.op("dve", lambda h, j=j, bp=bp, pd=pd: h.scalar_tensor_tensor(Cc[:, j, :], Cc[:, j, :], wcb[:, bp, hh:hh + 1], pd[:, 0:256], op0=ALU.mult, op1=ALU.add),
                             reads=[Rpd, Rwcb, RCc], writes=[RCc])
                    P.dma("sp", O["Cs"][l, cb * CB:(cb + 1) * CB, hh].rearrange("b d v -> d b v"), Cc[:], reads=[RCc], writes=[self.R["Cs"]])
            for hh in range(NH):
                do_head(hh)
            num = big
            junk = self.sb(ph, [NS, 256], F32, "smjunk")
            Rjunk = Region("smjunk")
            for hh in range(NH):
                sl = slice(hh * 256, (hh + 1) * 256)
                P.op("dve", lambda h, hh=hh, sl=sl: h.tensor_scalar(num[:, sl], v[:, sl], sm[:, c_w + hh:c_w + hh + 1], None, op0=ALU.mult), reads=[Rv, Rsm], writes=[Rbig])
                P.op("dve", lambda h, hh=hh, sl=sl: h.scalar_tensor_tensor(num[:, sl], pR[hh // 2][0:NS, (hh % 2) * 256:(hh % 2) * 256 + 256], sm[:, c_wi + hh:c_wi + hh + 1],
                                                                           num[:, sl], op0=ALU.mult, op1=ALU.add),
                     reads=[RpR[hh // 2], Rsm, Rbig], writes=[Rbig])
                P.op("act", lambda h, hh=hh, sl=sl: h.activation(junk[:], num[:, sl], AF.Square, accum_out=sm[:, c_ssq + hh:c_ssq + hh + 1]), reads=[Rbig, Rsm], writes=[Rjunk, Rsm])
            P.op("dve", lambda h: h.tensor_tensor(s_(c_t), s_(c_rd), s_(c_rd), ALU.mult), reads=[Rsm], writes=[Rsm])
            P.op("dve", lambda h: h.tensor_tensor(s_(c_t), s_(c_t), s_(c_ssq), ALU.mult), reads=[Rsm], writes=[Rsm])
            P.op("act", lambda h: h.activation(s_(c_t), s_(c_t), AF.Sqrt, bias=EPS, scale=1.0 / 256.0), reads=[Rsm], writes=[Rsm])
            P.op("dve", lambda h: h.reciprocal(s_(c_f), s_(c_t)), reads=[Rsm], writes=[Rsm])
            P.op("dve", lambda h: h.tensor_tensor(s_(c_f), s_(c_f), s_(c_rd), ALU.mult), reads=[Rsm], writes=[Rsm])
            num3 = num[:].rearrange("p (a b) -> p a b", a=NH)
            P.op("dve", lambda h: h.tensor_tensor(num3, num3, bc6(s_(c_f), 256), ALU.mult), reads=[Rbig, Rsm], writes=[Rbig])
            yb = self.sb(ph, [NS, VW], BF16, "smyb")
            Ryb = Region("smyb")
            P.op("dve", lambda h: h.tensor_tensor(yb[:], num[:], G[:], ALU.mult), reads=[Rbig, RG], writes=[Ryb])
            self.s_to_yT(yb, Ryb, 6, 6)
            self.barrier()

    def s_phase_C(self):
        P, l, I, S, O = self.P, self.l, self.I, self.S, self.O
        CW = GC * 128
        with contextlib.ExitStack() as ph:
            u = self.sb(ph, [NS, CW], F32, "scu")
            vv = self.sb(ph, [NS, 1024], F32, "scv")
            tm = self.sb(ph, [NS, 256], F32, "sct")
            Ru, Rvv, Rtm = Region("scu"), Region("scv"), Region("sct")
            for g, wb, Rw in self.wstream("in", list(range(G_CU, G_GATE))):
                def evac(acc, Racc, g=g):
                    if g < G_CV:
                        c0 = (g - G_CU) * 256
                        P.op("act", lambda h: h.activation(u[:, c0:c0 + 256], acc[0:NS, 0:256], AF.Gelu), reads=[Racc], writes=[Ru])
                    elif g < G_CZ:
                        c0 = (g - G_CV) * 256
                        P.op("act", lambda h: h.activation(vv[:, c0:c0 + 256], acc[0:NS, 0:256], AF.Gelu), reads=[Racc], writes=[Rvv])
                    else:
                        c0 = (g - G_CZ) * 256
                        P.op("act", lambda h: h.activation(tm[:], acc[0:NS, 0:256], AF.Silu), reads=[Racc], writes=[Rtm])
                        P.op("dve", lambda h: h.tensor_tensor(u[:, c0:c0 + 256], u[:, c0:c0 + 256], tm[:], ALU.mult), reads=[Rtm, Ru], writes=[Ru])
                self.s_proj(wb, Rw, 256, evac)
            junk = self.sb(ph, [NS, 1024], BF16, "scj")
            sm = self.sb(ph, [NS, 16], F32, "scsm")
            Rj, Rsm = Region("scj"), Region("scsm")
            self.layernorm(vv, Rvv, NS, junk, Rj, sm, Rsm)
            P.dma("sp", O["cvs"][l], vv[:], reads=[Rvv], writes=[self.R["cvs"]])
            wb8 = self.sb(ph, [NS, 16], F32, "scw8")
            Rw8 = Region("scw8")
            P.dma("sp", wb8[:, 0:GC], I["ws00"][l * GC:(l + 1) * GC].partition_broadcast(NS), writes=[Rw8])
            P.dma("sp", wb8[:, 8:8 + GC], I["bs0"][l * GC:(l + 1) * GC].partition_broadcast(NS), writes=[Rw8])
            v3 = vv[:, 0:CW].rearrange("p (a b) -> p a b", a=GC)
            P.op("dve", lambda h: h.tensor_tensor(v3, v3, wb8[:, 0:GC].unsqueeze(2).to_broadcast([NS, GC, 128]), ALU.mult), reads=[Rvv, Rw8], writes=[Rvv])
            P.op("dve", lambda h: h.tensor_tensor(v3, v3, wb8[:, 8:8 + GC].unsqueeze(2).to_broadcast([NS, GC, 128]), ALU.add), reads=[Rvv, Rw8], writes=[Rvv])
            yb = self.sb(ph, [NS, CW], BF16, "scyb")
            Ryb = Region("scyb")
            P.op("dve", lambda h: h.tensor_tensor(yb[:], vv[:, 0:CW], u[:], ALU.mult), reads=[Rvv, Ru], writes=[Ryb])
            self.s_to_yT(yb, Ryb, 12, GC)
            self.barrier()


def _consts():
    c = {}
    c["c_ident"] = np.eye(128, dtype=np.float32)
    q = np.arange(128)[:, None]
    s = np.arange(256)[None, :]
    dist = q + 128 - s
    valid = (dist >= 0) & (dist <= 128)
    c["c_dm"] = np.where(valid, dist, 1.0e9).astype(np.float32)
    c["c_dm0"] = c["c_dm"].copy()
    t = np.arange(128)[:, None]
    s2 = np.arange(128)[None, :]
    ok = ((t // 64) == (s2 // 64)) & (s2 <= t)
    m1 = np.where(ok, 0.0, -30000.0).astype(np.float32)
    c["c_mask6"] = np.tile(m1, (1, 3))
    c["c_tri2"] = ok.T.astype(np.float32).copy()
    onesc = np.zeros((128, 2, 128), np.float32)
    onesc[0:64, 0, :] = 1.0
    onesc[64:128, 1, :] = 1.0
    c["c_onesc"] = onesc.reshape(128, 256)
    cm = np.zeros((128, 2), np.float32)
    cm[0:64, 0] = 1.0
    cm[64:128, 1] = 1.0
    c["c_cmask"] = cm
    c["c_tril"] = (np.arange(128)[None, :] >= np.arange(128)[:, None]).astype(np.float32)
    c["c_dist"] = np.concatenate([np.arange(128, 0, -1), [0]]).astype(np.float32)
    c["c_i32"] = np.tile(np.eye(NS, dtype=np.float32).reshape(1, NS * NS), (128, 1))
    return c


_NC_CACHE = {}
_OFF = dict(aq=0, ak=1536, av=2048, az=2560, mq=4096, mk=4864, mv=5632, mi=7168, mf=7174, mo=7180, mz=8716, cu=10252, cv=11276, cz=12300)


def _cols(c):
    r = lambda base, n, w: list(range(base + c * w, base + c * w + w)) if n is None else None
    idx = []
    idx += list(range(_OFF["aq"] + c * 768, _OFF["aq"] + (c + 1) * 768))
    idx += list(range(_OFF["ak"] + c * 256, _OFF["ak"] + (c + 1) * 256))
    idx += list(range(_OFF["av"] + c * 256, _OFF["av"] + (c + 1) * 256))
    idx += list(range(_OFF["az"] + c * 768, _OFF["az"] + (c + 1) * 768))
    idx += list(range(_OFF["mq"] + c * 384, _OFF["mq"] + (c + 1) * 384))
    idx += list(range(_OFF["mk"] + c * 384, _OFF["mk"] + (c + 1) * 384))
    idx += list(range(_OFF["mv"] + c * 768, _OFF["mv"] + (c + 1) * 768))
    idx += list(range(_OFF["mo"] + c * 768, _OFF["mo"] + (c + 1) * 768))
    idx += list(range(_OFF["mz"] + c * 768, _OFF["mz"] + (c + 1) * 768))
    idx += list(range(_OFF["cu"] + c * 512, _OFF["cu"] + (c + 1) * 512))
    idx += list(range(_OFF["cv"] + c * 512, _OFF["cv"] + (c + 1) * 512))
    idx += list(range(_OFF["cv"] + (1 - c) * 512, _OFF["cv"] + (2 - c) * 512))
    idx += list(range(_OFF["cz"] + c * 512, _OFF["cz"] + (c + 1) * 512))
    idx += list(range(_OFF["mi"] + c * 3, _OFF["mi"] + (c + 1) * 3))
    idx += list(range(_OFF["mf"] + c * 3, _OFF["mf"] + (c + 1) * 3))
    return np.array(idx)


def kernel(x_prompt, x_sample, cache_win_k, cache_win_v, state_mlstm_C, state_mlstm_n, state_mlstm_m,
           norm_gain, w_in, b_if, attn_sinks, m_head_gain, c_ln_gain, c_ln_bias, c_w_s, c_b_s, w_out, final_gain, _ncores=8):
    f = lambda a: np.ascontiguousarray(np.asarray(a, dtype=np.float32))
    w_in, w_out = f(w_in), f(w_out)
    xp = f(x_prompt)
    base = dict(_consts())
    base["xs"] = f(x_sample).reshape(NS, D)
    base["gainT"] = np.ascontiguousarray(f(norm_gain).reshape(2, 32, 128).transpose(0, 2, 1))
    base["fgain"] = f(final_gain)
    half = []
    for c in range(2):
        m = {}
        idx = _cols(c)
        wp = np.zeros((2, D, NG_IN * 256), np.float32)
        wp[:, :, 0:idx.size] = w_in[:, :, idx]
        m["w_in"] = np.ascontiguousarray(wp.reshape(2, 32, 128, NG_IN, 256).transpose(0, 3, 2, 1, 4)).reshape(2, NG_IN, 128, 32 * 256)
        rows = np.concatenate([np.arange(c * 768, (c + 1) * 768), 1536 + np.arange(c * 768, (c + 1) * 768), 3072 + np.arange(c * 512, (c + 1) * 512)])
        wo = w_out[:, rows, :]
        m["w_out"] = np.ascontiguousarray(wo.reshape(2, KY, 128, NG_OUT, 512).transpose(0, 3, 2, 1, 4)).reshape(2, NG_OUT, 128, KY * 512)
        bi = f(b_if)
        m["bif"] = np.ascontiguousarray(bi[:, :, c * 3:(c + 1) * 3]).reshape(12)
        sk = f(attn_sinks)[:, c * 6:(c + 1) * 6]
        m["sinks"] = np.ascontiguousarray(sk).reshape(12)
        m["nslp"] = -np.array(SLOPES[c * 6:(c + 1) * 6], np.float32)
        mh = f(m_head_gain)[:, c * 768:(c + 1) * 768]
        m["mhg"] = np.ascontiguousarray(mh)
        m["mhgT"] = np.ascontiguousarray(mh.reshape(2, 6, 128).transpose(0, 2, 1))
        perm = np.concatenate([np.arange(c * 512, (c + 1) * 512), np.arange((1 - c) * 512, (2 - c) * 512)])
        m["lng"] = np.ascontiguousarray(f(c_ln_gain)[:, perm])
        m["lnb"] = np.ascontiguousarray(f(c_ln_bias)[:, perm])
        ws = f(c_w_s)[:, c * 4:(c + 1) * 4]
        m["wsT"] = np.ascontiguousarray(ws.transpose(0, 3, 1, 2)).reshape(2, 128, GC * 128)
        bsv = f(c_b_s)[:, c * 4:(c + 1) * 4]
        m["bs"] = np.ascontiguousarray(bsv).reshape(2, GC * 128)
        m["ws00"] = np.ascontiguousarray(ws[:, :, 0, 0]).reshape(2 * GC)
        m["bs0"] = np.ascontiguousarray(bsv[:, :, 0]).reshape(2 * GC)
        ck = f(cache_win_k)[:, :, :, c * 2:(c + 1) * 2]
        cvv = f(cache_win_v)[:, :, :, c * 2:(c + 1) * 2]
        m["ck"] = np.ascontiguousarray(ck.transpose(0, 1, 3, 2, 4)).reshape(2, 64, 128 * 128)
        m["cv"] = np.ascontiguousarray(cvv.transpose(0, 1, 3, 2, 4)).reshape(2, 64, 128 * 128)
        m["sk4"] = np.ascontiguousarray(np.tile(sk.reshape(2, 1, 2, 3), (1, 32, 1, 1)).reshape(2, 64, 3))
        m["c_slp"] = np.tile(np.array(SLOPES[c * 6:(c + 1) * 6], np.float32).reshape(2, 3), (32, 1))
        m["sC"] = np.ascontiguousarray(f(state_mlstm_C)[:, :, c * 3:(c + 1) * 3])
        m["sn"] = np.ascontiguousarray(f(state_mlstm_n)[:, :, c * 3:(c + 1) * 3]).reshape(2, NS, HM * 128)
        m["sm"] = np.ascontiguousarray(f(state_mlstm_m)[:, :, c * 3:(c + 1) * 3])
        half.append(m)
    in_maps = []
    for core in range(_ncores):
        m = dict(base)
        m.update(half[core % 2])
        m["xp"] = xp[(core // 2) % 4]
        in_maps.append(m)
    if "nc" not in _NC_CACHE:
        _NC_CACHE["nc"] = KB().build()
    nc = _NC_CACHE["nc"]
    res = run_bass_kernel_spmd(nc, in_maps, core_ids=list(range(_ncores)))
    r = list(res.results)
    while len(r) < 8:
        r = r + r[0:2]
    pr = lambda b: (r[2 * b], r[2 * b + 1])
    y_prompt = np.stack([r[2 * b]["yp"] for b in range(4)]).reshape(4, SEQ, D)
    y_sample = r[0]["ys"].reshape(NS, 1, D)
    cat = lambda key, b, shp, ax: np.concatenate([pr(b)[0][key].reshape(shp), pr(b)[1][key].reshape(shp)], axis=ax)
    kp = np.stack([cat("kp", b, (2, 128, 2, 128), 2) for b in range(4)], axis=1)
    vp = np.stack([cat("vp", b, (2, 128, 2, 128), 2) for b in range(4)], axis=1)
    ks = cat("ks", 0, (2, NS, 1, 2, 128), 3)
    vs = cat("vs", 0, (2, NS, 1, 2, 128), 3)
    Cp = np.stack([np.concatenate([x["Cp"].reshape(2, 128, 3, 256).transpose(0, 2, 1, 3) for x in pr(b)], axis=1) for b in range(4)], axis=1)
    npp = np.stack([np.concatenate([x["np"].transpose(0, 2, 1) for x in pr(b)], axis=1) for b in range(4)], axis=1)
    mp = np.stack([np.concatenate([x["mp"].reshape(2, 3) for x in pr(b)], axis=1) for b in range(4)], axis=1)
    Cs = np.concatenate([r[0]["Cs"], r[1]["Cs"]], axis=2)
    ns = np.concatenate([r[0]["ns"].reshape(2, NS, 3, 128), r[1]["ns"].reshape(2, NS, 3, 128)], axis=2)
    ms = np.concatenate([r[0]["ms"], r[1]["ms"]], axis=2)
    cvp = np.stack([r[2 * b]["cvp"] for b in range(4)], axis=1)
    cvs = r[0]["cvs"].reshape(2, NS, 1, 1024)
    outs = (y_prompt, y_sample, kp, vp, ks, vs, Cp, npp, mp, Cs, ns, ms, cvp, cvs)
    return tuple(np.ascontiguousarray(o, dtype=np.float32) for o in outs)
```
